# Optimizing a Trainium2 kernel written in Bass

```python
import math
import jax, jax.numpy as jnp
from jax import lax
import numpy as np

D_MODEL = 1024
BATCH = 8
SEQ = 2048
DEPTH = 2

N_META = 16
CHUNK = 128
D_MIX = D_MODEL
RET_HEADS = 4
RET_DK = 128
RET_DV = 128
RET_WIDTH = RET_HEADS * RET_DV
RET_THETA = 10000.0
DIFF_HEADS = 4
DIFF_D = 64
DIFF_DV = 2 * DIFF_D
DIFF_WIDTH = DIFF_HEADS * DIFF_DV
ROPE_THETA = 500000.0
ROPE_DIMS = DIFF_D // 4
D_FF = 2816
EPS = 1e-6

IN_SIZES = (
    RET_HEADS * RET_DK,
    RET_HEADS * RET_DK,
    RET_WIDTH,
    RET_WIDTH,
    DIFF_HEADS * 2 * DIFF_D,
    DIFF_HEADS * 2 * DIFF_D,
    DIFF_WIDTH,
)
D_IN = sum(IN_SIZES)

kernel_name = "hymba_retnet_diffattn_macaron"


def rmsnorm(x, g):
    xf = x.astype(jnp.float32)
    y = xf * lax.rsqrt(jnp.mean(xf * xf, axis=-1, keepdims=True) + EPS)
    return (y * g.astype(jnp.float32)).astype(x.dtype)


def swiglu(x, w_gate, w_up, w_down):
    return (jax.nn.silu(x @ w_gate) * (x @ w_up)) @ w_down


def retention_rotate(x, pos):
    d = x.shape[-1]
    angle = RET_THETA ** (-jnp.linspace(0.0, 1.0, d // 2, dtype=jnp.float32))
    freqs = pos[:, None] * angle[None, :]
    cos = jnp.cos(freqs)[None, :, None, :]
    sin = jnp.sin(freqs)[None, :, None, :]
    x1, x2 = x[..., 0::2], x[..., 1::2]
    out = jnp.stack([x1 * cos - x2 * sin, x1 * sin + x2 * cos], axis=-1)
    return out.reshape(x.shape)


def partial_rope(x, pos):
    r = ROPE_DIMS
    inv = ROPE_THETA ** (-jnp.arange(0, r, 2, dtype=jnp.float32) / r)
    freqs = pos[:, None] * inv[None, :]
    emb = jnp.concatenate([freqs, freqs], axis=-1)
    cos = jnp.cos(emb)[None, :, None, None, :]
    sin = jnp.sin(emb)[None, :, None, None, :]
    xr, xp = x[..., :r], x[..., r:]
    rot = jnp.concatenate([-xr[..., r // 2:], xr[..., :r // 2]], axis=-1)
    return jnp.concatenate([xr * cos + rot * sin, xp], axis=-1)


def retention_group(q, k, v, g, gn_w):
    B, T = q.shape[0], q.shape[1]
    f32 = jnp.float32
    pos = jnp.arange(T, dtype=f32)
    q = retention_rotate(q.reshape(B, T, RET_HEADS, RET_DK).astype(f32), pos)
    k = retention_rotate(k.reshape(B, T, RET_HEADS, RET_DK).astype(f32), pos) * (RET_DK ** -0.5)
    v = v.reshape(B, T, RET_HEADS, RET_DV).astype(f32)

    n_chunks = -(-T // CHUNK)
    pad = n_chunks * CHUNK - T

    def to_chunks(t):
        t = jnp.pad(t, ((0, 0), (pad, 0), (0, 0), (0, 0)))
        return t.reshape(B, n_chunks, CHUNK, t.shape[2], t.shape[3]).transpose(0, 3, 1, 2, 4)

    qc, kc, vc = to_chunks(q), to_chunks(k), to_chunks(v)

    log_gamma = jnp.log(1.0 - 2.0 ** (-5.0 - jnp.arange(RET_HEADS, dtype=f32)))
    idx = jnp.arange(CHUNK, dtype=f32)
    rel = idx[:, None] - idx[None, :]
    decay_intra = jnp.where(rel >= 0, jnp.exp(log_gamma[:, None, None] * jnp.maximum(rel, 0.0)), 0.0)
    k_decay = jnp.exp(log_gamma[:, None] * (CHUNK - 1 - idx)[None, :])
    q_decay = jnp.exp(log_gamma[:, None] * (idx + 1.0)[None, :])
    chunk_decay = jnp.exp(log_gamma * CHUNK)

    scores = jnp.einsum('bhnid,bhnjd->bhnij', qc, kc) * decay_intra[None, :, None]
    o_intra = jnp.einsum('bhnij,bhnje->bhnie', scores, vc)

    chunk_state = jnp.einsum('bhnjd,bhnje,hj->bhnde', kc, vc, k_decay)

    def step(R, s):
        return R * chunk_decay[None, :, None, None] + s, R

    R0 = jnp.zeros((B, RET_HEADS, RET_DK, RET_DV), f32)
    _, R_prev = lax.scan(step, R0, jnp.moveaxis(chunk_state, 2, 0))
    R_prev = jnp.moveaxis(R_prev, 0, 2)
    o_cross = jnp.einsum('bhnid,bhnde,hi->bhnie', qc, R_prev, q_decay)

    o = (o_intra + o_cross).transpose(0, 2, 3, 1, 4).reshape(B, n_chunks * CHUNK, RET_HEADS, RET_DV)[:, pad:]
    mu = jnp.mean(o, axis=-1, keepdims=True)
    var = jnp.mean((o - mu) ** 2, axis=-1, keepdims=True)
    o = ((o - mu) * lax.rsqrt(var + EPS)).reshape(B, T, RET_WIDTH) * gn_w.astype(f32)
    return (jax.nn.silu(g.astype(f32)) * o).astype(g.dtype)


def diff_attention_group(q, k, v, subln_w, lq1, lk1, lq2, lk2, lambda_init):
    B, T = q.shape[0], q.shape[1]
    f32 = jnp.float32
    pos = jnp.arange(T, dtype=f32)
    q = partial_rope(q.reshape(B, T, DIFF_HEADS, 2, DIFF_D).astype(f32), pos)
    k = partial_rope(k.reshape(B, T, DIFF_HEADS, 2, DIFF_D).astype(f32), pos)
    v = v.reshape(B, T, DIFF_HEADS, DIFF_DV).astype(f32)

    lam = (jnp.exp(jnp.sum(lq1.astype(f32) * lk1.astype(f32)))
           - jnp.exp(jnp.sum(lq2.astype(f32) * lk2.astype(f32))) + lambda_init)

    n_blocks = -(-T // CHUNK)
    pad = n_blocks * CHUNK - T
    q = jnp.pad(q, ((0, 0), (pad, 0), (0, 0), (0, 0), (0, 0))).transpose(0, 2, 3, 1, 4)
    k = jnp.pad(k, ((0, 0), (pad, 0), (0, 0), (0, 0), (0, 0))).transpose(0, 2, 3, 1, 4)
    v = jnp.pad(v, ((0, 0), (pad, 0), (0, 0), (0, 0))).transpose(0, 2, 1, 3)
    scale = DIFF_D ** -0.5

    outs = []
    for nb in range(n_blocks):
        n_keys = (nb + 1) * CHUNK
        qb = q[:, :, :, nb * CHUNK:(nb + 1) * CHUNK]
        s = jnp.einsum('bhmid,bhmjd->bhmij', qb, k[:, :, :, :n_keys]) * scale
        qi = nb * CHUNK + jnp.arange(CHUNK)
        kj = jnp.arange(n_keys)
        allowed = (kj[None, :] <= qi[:, None]) & (kj[None, :] >= pad)
        p = jax.nn.softmax(jnp.where(allowed, s, -1e30), axis=-1)
        a = p[:, :, 0] - lam * p[:, :, 1]
        outs.append(jnp.einsum('bhij,bhje->bhie', a, v[:, :, :n_keys]))
    o = jnp.concatenate(outs, axis=2)[:, :, pad:].transpose(0, 2, 1, 3)
    o = o * lax.rsqrt(jnp.mean(o * o, axis=-1, keepdims=True) + EPS) * subln_w.astype(f32)
    o = o * (1.0 - lambda_init)
    return o.reshape(B, T, DIFF_WIDTH).astype(q.dtype)


def setup_inputs(seed: int = 0) -> dict:
    key = jax.random.key(seed)
    ks = jax.random.split(key, 24)
    f32 = jnp.float32
    nrm = lambda k, shape, s: jax.random.normal(k, shape, f32) * s
    gain = lambda k, shape: 1.0 + 0.1 * jax.random.normal(k, shape, f32)
    return {
        "x": jax.random.normal(ks[0], (BATCH, SEQ, D_MODEL), f32),
        "meta_tokens": nrm(ks[1], (N_META, D_MODEL), 1.0),
        "ffn1_norm": gain(ks[2], (DEPTH, D_MODEL)),
        "ffn1_w_gate": nrm(ks[3], (DEPTH, D_MODEL, D_FF), D_MODEL ** -0.5),
        "ffn1_w_up": nrm(ks[4], (DEPTH, D_MODEL, D_FF), D_MODEL ** -0.5),
        "ffn1_w_down": nrm(ks[5], (DEPTH, D_FF, D_MODEL), D_FF ** -0.5),
        "mix_norm": gain(ks[6], (DEPTH, D_MODEL)),
        "w_in": nrm(ks[7], (DEPTH, D_MODEL, D_IN), D_MODEL ** -0.5),
        "ret_gn_w": gain(ks[8], (DEPTH, RET_WIDTH)),
        "diff_subln_w": gain(ks[9], (DEPTH, DIFF_DV)),
        "diff_lambda_q1": nrm(ks[10], (DEPTH, DIFF_D), 0.1),
        "diff_lambda_k1": nrm(ks[11], (DEPTH, DIFF_D), 0.1),
        "diff_lambda_q2": nrm(ks[12], (DEPTH, DIFF_D), 0.1),
        "diff_lambda_k2": nrm(ks[13], (DEPTH, DIFF_D), 0.1),
        "w_out": nrm(ks[14], (DEPTH, D_MIX, D_MODEL), D_MIX ** -0.5),
        "ffn2_norm": gain(ks[15], (DEPTH, D_MODEL)),
        "ffn2_w_gate": nrm(ks[16], (DEPTH, D_MODEL, D_FF), D_MODEL ** -0.5),
        "ffn2_w_up": nrm(ks[17], (DEPTH, D_MODEL, D_FF), D_MODEL ** -0.5),
        "ffn2_w_down": nrm(ks[18], (DEPTH, D_FF, D_MODEL), D_FF ** -0.5),
        "final_norm": gain(ks[19], (D_MODEL,)),
    }


def reference(x, meta_tokens, ffn1_norm, ffn1_w_gate, ffn1_w_up, ffn1_w_down,
              mix_norm, w_in, ret_gn_w, diff_subln_w, diff_lambda_q1, diff_lambda_k1,
              diff_lambda_q2, diff_lambda_k2, w_out, ffn2_norm, ffn2_w_gate,
              ffn2_w_up, ffn2_w_down, final_norm):
    B = x.shape[0]
    meta = jnp.broadcast_to(meta_tokens.astype(x.dtype)[None], (B, N_META, D_MODEL))
    h = jnp.concatenate([meta, x], axis=1)
    offs = np.cumsum((0,) + IN_SIZES)

    for l in range(DEPTH):
        h = h + 0.5 * swiglu(rmsnorm(h, ffn1_norm[l]), ffn1_w_gate[l], ffn1_w_up[l], ffn1_w_down[l])

        proj = rmsnorm(h, mix_norm[l]) @ w_in[l]
        parts = [proj[..., int(offs[i]):int(offs[i + 1])] for i in range(len(IN_SIZES))]
        rq, rk, rv, rg, dq, dk, dv = parts
        lambda_init = 0.8 - 0.6 * math.exp(-0.3 * l)
        y_ret = retention_group(rq, rk, rv, rg, ret_gn_w[l])
        y_diff = diff_attention_group(dq, dk, dv, diff_subln_w[l], diff_lambda_q1[l],
                                      diff_lambda_k1[l], diff_lambda_q2[l], diff_lambda_k2[l],
                                      lambda_init)
        h = h + jnp.concatenate([y_ret, y_diff], axis=-1) @ w_out[l]

        h = h + 0.5 * swiglu(rmsnorm(h, ffn2_norm[l]), ffn2_w_gate[l], ffn2_w_up[l], ffn2_w_down[l])

    h = rmsnorm(h, final_norm)
    return h[:, N_META:]
```

```python
import math
import numpy as np
import ml_dtypes
import concourse.bass as bass
import concourse.mybir as mybir
from concourse.bass_utils import run_bass_kernel_spmd

F32 = mybir.dt.float32
BF16 = mybir.dt.bfloat16
AF = mybir.ActivationFunctionType
ALU = mybir.AluOpType

D = 1024
SEQ = 2048
NMETA = 16
PAD = 112
TP = 2176
NCH = 17
DFF = 2816
NF = 22
NFH = 11
EPS = 1e-6
GROUPS = [(0, 128)] + [(128 + 512 * g, 512) for g in range(4)]
NSLOT = 11
NW = 4
WCOLS = 2048
SAME_ENGINE_SYNC = True
DBG_HACK = 0
WARM = 0
ROT_ADD_ENG = 'dve'
DBG_MIX = None

CB_ONES, CB_ONESPAD, CB_IDENT, CB_PERMR, CB_PERMD, CB_CMASK = [i * 128 for i in range(6)]
CB_N = 6 * 128
CF_DMASK = 0
CF_QDEC = 512
CF_KDEC = 1024
CF_PVEC = 1028
PV_NORM = 0
PV_GN = 56
PV_SW = 64
CF_EPS = CF_PVEC + 66
CF_ONE = CF_EPS + 3
CF_N = CF_ONE + 1


class Prog:
    def __init__(self, nc):
        self.nc = nc
        self.ops = []
        self.dma_sems = {}

    def op(self, eng, fn, reads=(), writes=()):
        self.ops.append(dict(eng=eng, fn=fn, reads=list(reads), writes=list(writes), dma=None))

    def dma(self, eng, fn, slot, reads=(), writes=()):
        self.ops.append(dict(eng=eng, fn=fn, reads=list(reads), writes=list(writes), dma=slot))

    def build(self):
        nc = self.nc
        ops = self.ops
        engs = ['pe', 'act', 'dve', 'pool', 'sp']
        last_w = {}
        readers = {}
        for i, o in enumerate(ops):
            deps = set()
            key = ('dma', i) if o['dma'] is not None else o['eng']
            for r in o['reads']:
                if r in last_w:
                    deps.add(last_w[r])
                if r[0] in ('ps', 'pt'):
                    for k2, i2 in readers.get(r, {}).items():
                        if k2 != key:
                            deps.add(i2)
            for w in o['writes']:
                if w in last_w:
                    deps.add(last_w[w])
                rd = readers.get(w)
                if rd:
                    deps.update(rd.values())
            deps.discard(i)
            o['deps'] = deps
            for r in o['reads']:
                readers.setdefault(r, {})[key] = i
            for w in o['writes']:
                last_w[w] = i
                readers[w] = {}
        signal = set()
        for i, o in enumerate(ops):
            for d in o['deps']:
                p = ops[d]
                if p['dma'] is not None:
                    continue
                if p['eng'] != o['eng'] or (SAME_ENGINE_SYNC and o['eng'] != 'pe'):
                    signal.add(d)
        sem = {e: nc.alloc_semaphore('s_' + e) for e in engs}
        cnt = {e: 0 for e in engs}
        for i, o in enumerate(ops):
            if o['dma'] is not None:
                slot = o['dma']
                if slot not in self.dma_sems:
                    self.dma_sems[slot] = [nc.alloc_semaphore('d_' + str(slot)), 0]
                ent = self.dma_sems[slot]
                ent[1] += 16
                o['tok'] = (ent[0], ent[1])
            elif i in signal:
                cnt[o['eng']] += 1
                o['tok'] = (sem[o['eng']], cnt[o['eng']])
            else:
                o['tok'] = None
        waited = {e: {} for e in engs}
        for i, o in enumerate(ops):
            need = {}
            for d in o['deps']:
                p = ops[d]
                if p['dma'] is None and p['eng'] == o['eng'] and not (SAME_ENGINE_SYNC and o['eng'] != 'pe'):
                    continue
                tok = p['tok']
                assert tok is not None
                k = tok[0]
                if need.get(k, (None, 0))[1] < tok[1]:
                    need[k] = tok
            ws = []
            for k, tok in need.items():
                if waited[o['eng']].get(k, 0) < tok[1]:
                    waited[o['eng']][k] = tok[1]
                    ws.append(tok)
            o['waits'] = ws
        per = {e: [o for o in ops if o['eng'] == e] for e in engs}
        self.stats = {e: len(per[e]) for e in engs}
        self.stats['signals'] = dict(cnt)

        def run(e, lst):
            for o in lst:
                for (s, v) in o['waits']:
                    e.wait_ge(s, v)
                if o['fn'] is None:
                    continue
                ins = o['fn'](e)
                if o['tok'] is not None:
                    ins.then_inc(o['tok'][0], 16 if o['dma'] is not None else 1)

        with nc.Block() as block:
            @block.tensor
            def _(e):
                run(e, per['pe'])

            @block.scalar
            def _(e):
                run(e, per['act'])

            @block.vector
            def _(e):
                run(e, per['dve'])

            @block.gpsimd
            def _(e):
                run(e, per['pool'])

            @block.sync
            def _(e):
                run(e, per['sp'])


def build_program(stop=None, dump=None):
    nc = bass.Bass("TRN2", target_bir_lowering=False)
    P = Prog(nc)

    def din(name, shape, dt=F32):
        return nc.dram_tensor(name, list(shape), dt, kind="ExternalInput").ap()

    xT_d = din("xT", [128, 8, SEQ])
    meta_d = din("metaT", [128, 8, NMETA])
    cb_d = din("cb", [128, CB_N], BF16)
    cf_d = din("cf", [128, CF_N])
    lam_d = din("lam", [128, 2 * 4 * 64])
    tabs_d = din("tabs", [4, 128, TP])
    wgu_d = din("wgu", [2, 2, NF, 128, 2048])
    wd_d = din("wd", [2, 2, 2, 8, 128, NFH * 128])
    wmix_d = din("wmix", [2, 8, 2, 128, 2048])
    wout_d = din("wout", [2, 8, 128, 1024])
    out_d = nc.dram_tensor("outT", [128, 8, SEQ], F32, kind="ExternalOutput").ap()
    dbg_d = None
    if dump is not None:
        dbg_d = nc.dram_tensor("dbg", [128, 8, TP], F32, kind="ExternalOutput").ap()

    sb = nc.alloc_sbuf_tensor
    hT = sb("hT", [128, 8, TP], F32)
    xnT = sb("xnT", [128, 8, TP], BF16)
    SL = sb("slots", [128, NSLOT, TP], BF16)
    wsl = [sb(f"wsl{i}", [128, WCOLS], BF16) for i in range(NW)]
    tabt = [sb(f"tab{i}", [128, 2, 512], F32) for i in range(2)]
    cb = sb("cbs", [128, CB_N], BF16)
    cf = sb("cfs", [128, CF_N], F32)
    lamin = sb("lamin", [128, 512], F32)
    lamw = sb("lamw", [128, 64], F32)
    lamv = sb("lamv", [128, 8], F32)
    NR = 3
    sqr = [sb(f"sq{i}", [128, 512], BF16) for i in range(NR)]
    fa = [sb(f"fa{i}", [128, 512], F32) for i in range(4)]
    fb = [sb(f"fb{i}", [128, 512], F32) for i in range(4)]
    ba = [sb(f"ba{i}", [128, 512], BF16) for i in range(4)]
    sm = [sb(f"sm{i}", [128, 128], BF16) for i in range(6)]
    Rf = [sb(f"Rf{i}", [128, 128], F32) for i in range(2)]

    PS = [nc.alloc_psum_tensor(f"ps{i}", [128, 512], F32) for i in range(7)]
    PT = nc.alloc_psum_tensor("pst", [128, 1024], BF16)

    SBANK = [(PS[0], ('ps', 0)), (PS[1], ('ps', 1)), (PT[:, :].bitcast(F32), ('pt',))]
    ctr = {}

    def rr(name, n):
        v = ctr.get(name, 0)
        ctr[name] = v + 1
        return v % n

    def ring(name, lst):
        i = rr(name, len(lst))
        return lst[i], (name, i)

    ones = cb[:, CB_ONES:CB_ONES + 128]
    ones_pad = cb[:, CB_ONESPAD:CB_ONESPAD + 128]
    ident = cb[:, CB_IDENT:CB_IDENT + 128]
    cmask = cb[:, CB_CMASK:CB_CMASK + 128]
    RC = ('const',)
    epsc = cf[:, CF_EPS:CF_EPS + 1]
    onec = cf[:, CF_ONE:CF_ONE + 1]
    epsl = [cf[:, CF_EPS + 1 + l_:CF_EPS + 2 + l_] for l_ in range(2)]

    P.dma('sp', lambda e: e.dma_start(out=cb[:, :], in_=cb_d), 'const', writes=[RC])
    P.dma('sp', lambda e: e.dma_start(out=cf[:, :], in_=cf_d), 'const', writes=[RC])
    P.dma('sp', lambda e: e.dma_start(out=lamin[:, :], in_=lam_d), 'const', writes=[RC])
    hregs = lambda c: [('h', c, g) for g in range(5)]
    P.op('dve', lambda e: e.memset(hT[:, :, 0:PAD], 0.0), writes=[('h', c, 0) for c in range(8)])
    for c in range(8):
        P.dma('sp', lambda e, c=c: e.dma_start(out=hT[:, c, 128:TP], in_=xT_d[:, c, :]), 'xin',
              writes=[('h', c, g) for g in range(1, 5)])
    P.dma('sp', lambda e: e.dma_start(out=hT[:, :, PAD:128], in_=meta_d), 'xin',
          reads=[('h', c, 0) for c in range(8)], writes=[('h', c, 0) for c in range(8)])

    def wload(src_ap, ncols):
        i = rr('w', NW)
        t = wsl[i]
        reg = ('w', i)
        P.dma('pool', lambda e: e.dma_start(out=t[:, 0:ncols], in_=src_ap, max_dma_last_dim=8192),
              ('w', i), writes=[reg])
        return t, reg

    def tload(k, s, n):
        i = rr('tab', 2)
        t = tabt[i]
        regs = [('tab', i, 0), ('tab', i, 1)]
        P.dma('sp', lambda e: e.dma_start(out=t[:, 0, 0:n], in_=tabs_d[2 * k, :, s:s + n]), ('tab', i, 0), writes=[regs[0]])
        P.dma('sp', lambda e: e.dma_start(out=t[:, 1, 0:n], in_=tabs_d[2 * k + 1, :, s:s + n]), ('tab', i, 1), writes=[regs[1]])
        return t, regs

    def rmsnorm_group(pvcol, final, g, s, n):
        ss = PS[6]
        for c in range(8):
            sq, sqreg = ring('sq', sqr)
            P.op('act', lambda e, sq=sq, c=c: e.activation(out=sq[:, 0:n], in_=hT[:, c, s:s + n], func=AF.Square),
                 reads=[('h', c, g)], writes=[sqreg])
            P.op('pe', lambda e, sq=sq, c=c: e.matmul(ss[:, 0:n], lhsT=ones, rhs=sq[:, 0:n], start=(c == 0), stop=(c == 7)),
                 reads=[sqreg, RC], writes=[('ps', 6)])
        rt, rtreg = ring('fa', fa)
        P.op('act', lambda e: e.activation(out=rt[:, 0:n], in_=ss[:, 0:n], func=AF.Ln, scale=1.0 / D, bias=epsc),
             reads=[('ps', 6), RC], writes=[rtreg])
        rs, rsreg = ring('fb', fb)
        P.op('act', lambda e: e.activation(out=rs[:, 0:n], in_=rt[:, 0:n], func=AF.Exp, scale=-0.5), reads=[rtreg], writes=[rsreg])
        for c in range(8):
            gcol = cf[:, CF_PVEC + pvcol + c:CF_PVEC + pvcol + c + 1]
            dst = hT if final else xnT
            wreg_ = ('h', c, g) if final else ('xn', c, g)
            P.op('dve', lambda e, c=c, gcol=gcol, dst=dst: e.scalar_tensor_tensor(
                out=dst[:, c, s:s + n], in0=hT[:, c, s:s + n], scalar=gcol, in1=rs[:, 0:n], op0=ALU.mult, op1=ALU.mult),
                reads=[('h', c, g), rsreg, RC], writes=[wreg_])

    def rmsnorm(pvcol, final=False):
        for g, (s, n) in enumerate(GROUPS):
            rmsnorm_group(pvcol, final, g, s, n)

    NM = 128 - PAD

    def resid_add(ps, psreg, c, g, scale):
        s, n = GROUPS[g]
        if g == 0:
            s, n = PAD, NM
        P.op('dve', lambda e: e.scalar_tensor_tensor(out=hT[:, c, s:s + n], in0=ps[:, 0:n], scalar=scale,
                                                      in1=hT[:, c, s:s + n], op0=ALU.mult, op1=ALU.add),
             reads=[psreg, ('h', c, g)], writes=[('h', c, g)])

    def ffn_gu_group(wt, wreg, fi, g, s, n):
        if g == 0:
            s, n = PAD, NM
        k = rr('ffn_gu', 2)
        gp, up = PS[k], PS[2 + k]
        for kc in range(8):
            P.op('pe', lambda e, kc=kc: e.matmul(gp[:, 0:n], lhsT=wt[:, kc * 128:(kc + 1) * 128],
                                               rhs=xnT[:, kc, s:s + n], start=(kc == 0), stop=(kc == 7)),
                 reads=[wreg, ('xn', kc, g)], writes=[('ps', k)])
        for kc in range(8):
            P.op('pe', lambda e, kc=kc: e.matmul(up[:, 0:n], lhsT=wt[:, 1024 + kc * 128:1024 + (kc + 1) * 128],
                                               rhs=xnT[:, kc, s:s + n], start=(kc == 0), stop=(kc == 7)),
                 reads=[wreg, ('xn', kc, g)], writes=[('ps', 2 + k)])
        sg, sgreg = ring('fa', fa)
        P.op('act', lambda e: e.activation(out=sg[:, 0:n], in_=gp[:, 0:n], func=AF.Silu),
             reads=[('ps', k)], writes=[sgreg])
        P.op('dve', lambda e: e.tensor_tensor(out=SL[:, fi, s:s + n], in0=up[:, 0:n], in1=sg[:, 0:n], op=ALU.mult),
             reads=[('ps', 2 + k), sgreg], writes=[('S', fi, g)])

    def mm_out_group(wt, wreg, nk, c, g, s, n, scale):
        if g == 0:
            s, n = PAD, NM
        k = 4 + rr('ffn_o', 2)
        op_ = PS[k]
        for fi in range(nk):
            P.op('pe', lambda e, fi=fi: e.matmul(op_[:, 0:n], lhsT=wt[:, fi * 128:(fi + 1) * 128],
                                               rhs=SL[:, fi, s:s + n], start=(fi == 0), stop=(fi == nk - 1)),
                 reads=[wreg, ('S', fi, g)], writes=[('ps', k)])
        resid_add(op_, ('ps', k), c, g, scale)

    def ffn(l, j):
        rmsnorm(PV_NORM + (l * 3 + (0 if j == 0 else 2)) * 8)
        for half in range(2):
            for fi in range(NFH):
                wt, wreg = wload(wgu_d[l, j, half * NFH + fi], 2048)
                for g, (s, n) in enumerate(GROUPS):
                    ffn_gu_group(wt, wreg, fi, g, s, n)
            for c in range(8):
                wt, wreg = wload(wd_d[l, j, half, c], NFH * 128)
                for g, (s, n) in enumerate(GROUPS):
                    mm_out_group(wt, wreg, NFH, c, g, s, n, 0.5)

    tabs_for = {}

    def project_rot_group(which, wt, wreg, dst_slot, permcol, g, split=None):
        perm = cb[:, permcol:permcol + 128]
        s, n = GROUPS[g]
        tt, tregs = tabs_for[g]
        pk = rr('proj', 2)
        pp, sp_ = PS[pk], PS[2 + pk]
        for kc in range(8):
            P.op('pe', lambda e, kc=kc: e.matmul(pp[:, 0:n], lhsT=wt[:, which * 1024 + kc * 128:which * 1024 + (kc + 1) * 128],
                                               rhs=xnT[:, kc, s:s + n], start=(kc == 0), stop=(kc == 7)),
                 reads=[wreg, ('xn', kc, g)], writes=[('ps', pk)])
        qb, qbreg = ring('ba', ba)
        P.op('act', lambda e: e.activation(out=qb[:, 0:n], in_=pp[:, 0:n], func=AF.Identity), reads=[('ps', pk)], writes=[qbreg])
        if DBG_MIX == 'R1':
            return
        P.op('pe', lambda e: e.matmul(sp_[:, 0:n], lhsT=perm, rhs=qb[:, 0:n], start=True, stop=True),
             reads=[qbreg, RC], writes=[('ps', 2 + pk)])
        if DBG_MIX == 'R2':
            return
        t1, t1reg = ring('fa', fa)
        t2, t2reg = ring('fb', fb)
        if DBG_HACK == 1:
            P.op('dve', lambda e: e.tensor_tensor(out=t1[:, 0:n], in0=pp[:, 0:n], in1=cf[:, 0:n], op=ALU.mult),
                 reads=[('ps', pk), RC], writes=[t1reg])
            P.op('dve', lambda e: e.tensor_tensor(out=t2[:, 0:n], in0=sp_[:, 0:n], in1=cf[:, 512:512 + n], op=ALU.mult),
                 reads=[('ps', 2 + pk), RC], writes=[t2reg])
            return
        P.op('dve', lambda e: e.tensor_tensor(out=t1[:, 0:n], in0=pp[:, 0:n], in1=tt[:, 0, 0:n], op=ALU.mult),
             reads=[('ps', pk)] + tregs, writes=[t1reg])
        P.op('dve', lambda e: e.tensor_tensor(out=t2[:, 0:n], in0=sp_[:, 0:n], in1=tt[:, 1, 0:n], op=ALU.mult),
             reads=[('ps', 2 + pk)] + tregs, writes=[t2reg])
        if DBG_MIX == 'R3':
            return
        if split is None:
            P.op(ROT_ADD_ENG, lambda e: e.tensor_tensor(out=SL[:, dst_slot, s:s + n], in0=t1[:, 0:n], in1=t2[:, 0:n], op=ALU.add),
                 reads=[t1reg, t2reg], writes=[('S', dst_slot, g)])
        else:
            for m in range(2):
                P.op('dve', lambda e, m=m: e.tensor_tensor(out=SL[64 * m:64 * m + 64, split[m], s:s + n], in0=t1[64 * m:64 * m + 64, 0:n],
                                                         in1=t2[64 * m:64 * m + 64, 0:n], op=ALU.add),
                     reads=[t1reg, t2reg], writes=[('S', split[m], g)])

    def project_v_group(wt, wreg, vslot, g, s, n):
        pk = rr('proj', 2)
        pp = PS[pk]
        for ci in range(n // 128):
            cs = s + ci * 128
            for kc in range(8):
                P.op('pe', lambda e, kc=kc, ci=ci, cs=cs: e.matmul(pp[:, ci * 128:(ci + 1) * 128], lhsT=xnT[:, kc, cs:cs + 128],
                                                                 rhs=wt[:, kc * 128:(kc + 1) * 128], start=(kc == 0), stop=(kc == 7)),
                     reads=[wreg, ('xn', kc, g)], writes=[('ps', pk)])
        P.op('act', lambda e: e.activation(out=SL[:, vslot, s:s + n], in_=pp[:, 0:n], func=AF.Identity),
             reads=[('ps', pk)], writes=[('S', vslot, g)])

    def project_v(wt, wreg, vslot):
        for g, (s, n) in enumerate(GROUPS):
            project_v_group(wt, wreg, vslot, g, s, n)

    def stats_rstd(src, srcreg, n, scale, bias_col, o0=0):
        sq, sqreg = ring('ba', ba)
        P.op('dve', lambda e: e.tensor_tensor(out=sq[:, o0:n], in0=src[:, o0:n], in1=src[:, o0:n], op=ALU.mult), reads=[srcreg], writes=[sqreg])
        P.op('pe', lambda e: e.matmul(PS[6][:, o0:n], lhsT=ones, rhs=sq[:, o0:n], start=True, stop=True),
             reads=[sqreg, RC], writes=[('ps', 6)])
        sd, sdreg = ring('fb', fb)
        P.op('act', lambda e: e.activation(out=sd[:, o0:n], in_=PS[6][:, o0:n], func=AF.Ln, scale=scale, bias=bias_col),
             reads=[('ps', 6), RC], writes=[sdreg])
        rs, rsreg = ring('fb', fb)
        P.op('act', lambda e: e.activation(out=rs[:, o0:n], in_=sd[:, o0:n], func=AF.Exp, scale=-0.5), reads=[sdreg], writes=[rsreg])
        return rs, rsreg

    def retention_attention(l, h, u, ks, vs, wvg, wvgreg):
        gam = 1.0 - 2.0 ** (-5.0 - h)
        cd = gam ** 128
        dmask = cf[:, CF_DMASK + h * 128:CF_DMASK + (h + 1) * 128]
        qdec = cf[:, CF_QDEC + h * 128:CF_QDEC + (h + 1) * 128]
        kdec = cf[:, CF_KDEC + h:CF_KDEC + h + 1]
        gnw = cf[:, CF_PVEC + PV_GN + l * 4 + h:CF_PVEC + PV_GN + l * 4 + h + 1]
        RBs = 18 - ks
        grp = lambda n: 0 if n == 0 else 1 + (n - 1) // 4
        P.op('dve', lambda e: e.memset(SL[:, RBs, 0:128], 0.0), writes=[('S', RBs, 0)])
        P.op('dve', lambda e: e.memset(Rf[0][:, :], 0.0), writes=[('Rf', 0)])
        def emit_gate(g, bank=6):
            s, n = GROUPS[g]
            gp = PS[bank]
            for kc in range(8):
                P.op('pe', lambda e, kc=kc: e.matmul(gp[:, 0:n], lhsT=wvg[:, 1024 + kc * 128:1024 + (kc + 1) * 128],
                                                   rhs=xnT[:, kc, s:s + n], start=(kc == 0), stop=(kc == 7)),
                     reads=[wvgreg, ('xn', kc, g)], writes=[('ps', bank)])
            sg, sgreg = ring('fb', fb)
            P.op('act', lambda e: e.activation(out=sg[:, 0:n], in_=gp[:, 0:n], func=AF.Exp, scale=-1.0), reads=[('ps', bank)], writes=[sgreg])
            P.op('act', lambda e: e.activation(out=sg[:, 0:n], in_=sg[:, 0:n], func=AF.Ln, bias=onec), reads=[sgreg, RC], writes=[sgreg])
            P.op('act', lambda e: e.activation(out=sg[:, 0:n], in_=sg[:, 0:n], func=AF.Exp, scale=-1.0), reads=[sgreg], writes=[sgreg])
            gs, gsreg = ring('sq', sqr)
            P.op('dve', lambda e: e.tensor_tensor(out=gs[:, 0:n], in0=gp[:, 0:n], in1=sg[:, 0:n], op=ALU.mult),
                 reads=[('ps', bank), sgreg], writes=[gsreg])
            gate[g] = (gs, gsreg)

        for r in range(4):
            for i in range(4):
                n = 4 * r + i
                P.op('pe', lambda e, n=n, i=i: e.transpose(PT[:, i * 128:(i + 1) * 128], SL[:, ks, n * 128:(n + 1) * 128], ident),
                     reads=[('S', ks, grp(n)), RC], writes=[('pt',)])
            kd, kdreg = ring('ba', ba)
            P.op('act', lambda e, kd=kd: e.activation(out=kd[:, 0:512], in_=PT[:, 0:512], func=AF.Identity, scale=kdec),
                 reads=[('pt',), RC], writes=[kdreg])
            for i in range(4):
                n = 4 * r + i
                P.op('pe', lambda e, n=n, i=i, r=r, kd=kd: e.matmul(PS[r][:, i * 128:(i + 1) * 128], lhsT=kd[:, i * 128:(i + 1) * 128],
                                                                  rhs=SL[:, vs, n * 128:(n + 1) * 128], start=True, stop=True),
                     reads=[kdreg, ('S', vs, grp(n))], writes=[('ps', r)])
        gate = {}
        for n in range(16):
            if n in (0, 5, 10):
                emit_gate(n // 5, (6, 4, 5)[n // 5])
            r, i = divmod(n, 4)
            P.op('dve', lambda e, n=n, r=r, i=i: e.scalar_tensor_tensor(out=Rf[(n + 1) % 2][:, :], in0=Rf[n % 2][:, :], scalar=cd,
                                                                      in1=PS[r][:, i * 128:(i + 1) * 128], op0=ALU.mult, op1=ALU.add),
                 reads=[('Rf', n % 2), ('ps', r)], writes=[('Rf', (n + 1) % 2)])
            P.op('act', lambda e, n=n: e.activation(out=SL[:, RBs, (n + 1) * 128:(n + 2) * 128], in_=Rf[(n + 1) % 2][:, :], func=AF.Identity),
                 reads=[('Rf', (n + 1) % 2)], writes=[('S', RBs, grp(n + 1))])
        deferred = []

        def emit_ST(n):
            cs = n * 128
            g = grp(n)
            bk = n % 2
            P.op('pe', lambda e: e.matmul(PS[bk][:, 0:128], lhsT=SL[:, ks, cs:cs + 128], rhs=SL[:, u, cs:cs + 128],
                                          start=True, stop=True), reads=[('S', u, g), ('S', ks, g)], writes=[('ps', bk)])
            stm, stmreg = ring('sm', sm)
            P.op('dve', lambda e: e.tensor_tensor(out=stm[:, :], in0=PS[bk][:, 0:128], in1=dmask, op=ALU.mult),
                 reads=[('ps', bk), RC], writes=[stmreg])
            qd, qdreg = ring('sm', sm)
            P.op('dve', lambda e: e.tensor_tensor(out=qd[:, :], in0=SL[:, u, cs:cs + 128], in1=qdec, op=ALU.mult),
                 reads=[('S', u, g), RC], writes=[qdreg])
            return stm, stmreg, qd, qdreg

        def emit_O(n, st):
            stm, stmreg, qd, qdreg = st
            cs = n * 128
            g = grp(n)
            ci = 0 if n == 0 else (n - 1) % 4
            OBk = 4 + g % 2
            OB = PS[OBk]
            P.op('pe', lambda e: e.matmul(OB[:, ci * 128:(ci + 1) * 128], lhsT=SL[:, vs, cs:cs + 128], rhs=stm[:, :],
                                          start=True, stop=False), reads=[('S', vs, g), stmreg], writes=[('ps', OBk)])
            P.op('pe', lambda e: e.matmul(OB[:, ci * 128:(ci + 1) * 128], lhsT=SL[:, RBs, cs:cs + 128], rhs=qd[:, :],
                                          start=False, stop=True), reads=[('S', RBs, g), qdreg], writes=[('ps', OBk)])
            if n == 0 or ci == 3:
                while deferred:
                    deferred.pop(0)()
                deferred.append(lambda: finalize(g, OBk))

        def finalize(g, OBk):
            s, n = GROUPS[g]
            OB = PS[OBk]
            ob, obreg = ring('ba', ba)
            P.op('act', lambda e: e.activation(out=ob[:, 0:n], in_=OB[:, 0:n], func=AF.Identity), reads=[('ps', OBk)], writes=[obreg])
            osq, osqreg = ring('ba', ba)
            P.op('act', lambda e: e.activation(out=osq[:, 0:n], in_=OB[:, 0:n], func=AF.Square), reads=[('ps', OBk)], writes=[osqreg])
            P.op('pe', lambda e: e.matmul(PS[2][:, 0:n], lhsT=ones, rhs=ob[:, 0:n], start=True, stop=True),
                 reads=[obreg, RC], writes=[('ps', 2)])
            P.op('pe', lambda e: e.matmul(PS[3][:, 0:n], lhsT=ones, rhs=osq[:, 0:n], start=True, stop=True),
                 reads=[osqreg, RC], writes=[('ps', 3)])
            mean, meanreg = ring('fa', fa)
            P.op('dve', lambda e: e.tensor_scalar(out=mean[:, 0:n], in0=PS[2][:, 0:n], scalar1=1.0 / 128, scalar2=None, op0=ALU.mult),
                 reads=[('ps', 2)], writes=[meanreg])
            nms, nmsreg = ring('fa', fa)
            P.op('dve', lambda e: e.scalar_tensor_tensor(out=nms[:, 0:n], in0=mean[:, 0:n], scalar=-1.0, in1=mean[:, 0:n],
                                                          op0=ALU.mult, op1=ALU.mult), reads=[meanreg], writes=[nmsreg])
            P.op('dve', lambda e: e.scalar_tensor_tensor(out=nms[:, 0:n], in0=PS[3][:, 0:n], scalar=1.0 / 128, in1=nms[:, 0:n],
                                                          op0=ALU.mult, op1=ALU.add), reads=[('ps', 3), nmsreg], writes=[nmsreg])
            rs, rsreg = ring('fb', fb)
            P.op('act', lambda e: e.activation(out=rs[:, 0:n], in_=nms[:, 0:n], func=AF.Ln, bias=epsc), reads=[nmsreg, RC], writes=[rsreg])
            P.op('act', lambda e: e.activation(out=rs[:, 0:n], in_=rs[:, 0:n], func=AF.Exp, scale=-0.5), reads=[rsreg], writes=[rsreg])
            cen, cenreg = ring('fa', fa)
            P.op('dve', lambda e: e.tensor_tensor(out=cen[:, 0:n], in0=OB[:, 0:n], in1=mean[:, 0:n], op=ALU.subtract),
                 reads=[('ps', OBk), meanreg], writes=[cenreg])
            P.op('dve', lambda e: e.scalar_tensor_tensor(out=cen[:, 0:n], in0=cen[:, 0:n], scalar=gnw, in1=rs[:, 0:n],
                                                          op0=ALU.mult, op1=ALU.mult),
                 reads=[cenreg, rsreg, RC], writes=[cenreg])
            gs, gsreg = gate[g]
            P.op('dve', lambda e: e.tensor_tensor(out=SL[:, u, s:s + n], in0=cen[:, 0:n], in1=gs[:, 0:n], op=ALU.mult),
                 reads=[cenreg, gsreg], writes=[('S', u, g)])

        pending = None
        for n in range(NCH):
            if n in (6, 10):
                emit_gate(grp(n) + 1)
            st = emit_ST(n)
            if pending is not None:
                emit_O(*pending)
            pending = (n, st)
        emit_O(*pending)
        while deferred:
            deferred.pop(0)()

    def retention_unit(l, h):
        u = h
        ks = 8 + 2 * (u % 2)
        vs = 9
        wqk, wqkreg = wload(wmix_d[l, u, 0], 2048)
        wvg, wvgreg = wload(wmix_d[l, u, 1], 2048)
        for g, (s, n) in enumerate(GROUPS):
            tabs_for[g] = tload(0, s, n)
            project_rot_group(0, wqk, wqkreg, u, CB_PERMR, g)
            project_rot_group(1, wqk, wqkreg, ks, CB_PERMR, g)
        if DBG_MIX in ('B1', 'R1', 'R2', 'R3'):
            return
        project_v(wvg, wvgreg, vs)
        if DBG_MIX == 'B':
            return
        retention_attention(l, h, u, ks, vs, wvg, wvgreg)

    def diff_attention(l, u, ks, vs, om):
        neglam = lamv[:, 4 + l:5 + l]
        sw = cf[:, CF_PVEC + PV_SW + l:CF_PVEC + PV_SW + l + 1]
        blocks = []
        for g, (s, n) in enumerate(GROUPS):
            last = (s + n) // 128 - 1
            for m in range(2):
                for jb in range(last + 1):
                    blocks.append((g, m, jb, last))
        odset = {}
        tres = {}
        deferred = []
        post = []

        def emit_S(blk):
            g, m, jb, last = blk
            s, n = GROUPS[g]
            off = max(0, jb * 128 - s)
            nq = n - off
            diag = jb * 128 >= s
            sk = rr('dst', 3)
            SP_, sreg = SBANK[sk]
            kg = 0 if jb == 0 else 1 + (jb - 1) // 4
            P.op('pe', lambda e: e.matmul(SP_[:, 0:nq], lhsT=SL[:, ks[m], jb * 128:(jb + 1) * 128],
                                          rhs=SL[:, u, s + off:s + n], start=True, stop=(not diag)),
                 reads=[('S', ks[m], kg), ('S', u, g)], writes=[sreg])
            if diag:
                P.op('pe', lambda e: e.matmul(SP_[:, 0:128], lhsT=ident, rhs=cmask, start=False, stop=True),
                     reads=[RC], writes=[sreg])
            return sk, off, nq, kg

        def emit_rest(blk, sinfo):
            g, m, jb, last = blk
            sk, off, nq, kg = sinfo
            SP_, sreg = SBANK[sk]
            if (g, m) not in odset:
                odset[(g, m)] = 2 + 2 * rr('dod', 2)
            OBk = odset[(g, m)]
            DBk = OBk + 1
            OB, DB = PS[OBk], PS[DBk]
            pt, ptreg = ring('ba', ba)
            P.op('act', lambda e: e.activation(out=pt[:, 0:nq], in_=SP_[:, 0:nq], func=AF.Exp, scale=0.125),
                 reads=[sreg], writes=[ptreg])
            P.op('pe', lambda e: e.matmul(OB[:, off:off + nq], lhsT=SL[:, vs, jb * 128:(jb + 1) * 128], rhs=pt[:, 0:nq],
                                          start=(jb == 0), stop=(jb == last)),
                 reads=[('S', vs, kg), ptreg], writes=[('ps', OBk)])
            P.op('pe', lambda e: e.matmul(DB[:, off:off + nq], lhsT=(ones_pad if jb == 0 else ones), rhs=pt[:, 0:nq],
                                          start=(jb == 0), stop=(jb == last)),
                 reads=[RC, ptreg], writes=[('ps', DBk)])
            if jb == last:
                post.append([2, lambda: normalize(g, m, OBk)])

        def normalize(g, m, OBk):
            DBk = OBk + 1
            OB, DB = PS[OBk], PS[DBk]
            s, n = GROUPS[g]
            o0 = PAD if g == 0 else 0
            r, rreg = ring('fb', fb)
            P.op('act', lambda e: e.activation(out=r[:, o0:n], in_=DB[:, o0:n], func=AF.Ln), reads=[('ps', DBk)], writes=[rreg])
            P.op('act', lambda e: e.activation(out=r[:, o0:n], in_=r[:, o0:n], func=AF.Exp, scale=-1.0), reads=[rreg], writes=[rreg])
            t, treg_ = ring('fa', fa)
            P.op('dve', lambda e: e.tensor_tensor(out=t[:, o0:n], in0=OB[:, o0:n], in1=r[:, o0:n], op=ALU.mult),
                 reads=[('ps', OBk), rreg], writes=[treg_])
            tres[(g, m)] = (t, treg_)
            if m == 0:
                while deferred:
                    deferred.pop(0)()
            else:
                deferred.append(lambda g=g: finalize(g))

        def tick(force=False):
            for ent in list(post):
                ent[0] -= 1
                if force or ent[0] <= 0:
                    post.remove(ent)
                    ent[1]()

        def finalize(g):
            s, n = GROUPS[g]
            o0 = PAD if g == 0 else 0
            (t0, t0reg), (t1, t1reg) = tres[(g, 0)], tres[(g, 1)]
            o, oreg = ring('fa', fa)
            P.op('dve', lambda e: e.scalar_tensor_tensor(out=o[:, o0:n], in0=t1[:, o0:n], scalar=neglam, in1=t0[:, o0:n],
                                                          op0=ALU.mult, op1=ALU.add),
                 reads=[t0reg, t1reg, ('lamv',)], writes=[oreg])
            rs, rsreg = stats_rstd(o, oreg, n, 1.0 / (128 * om * om), epsl[l], o0)
            P.op('dve', lambda e: e.scalar_tensor_tensor(out=SL[:, u, s + o0:s + n], in0=o[:, o0:n], scalar=sw, in1=rs[:, o0:n],
                                                          op0=ALU.mult, op1=ALU.mult),
                 reads=[oreg, rsreg, RC], writes=[('S', u, g)])

        for _ in range(WARM):
            P.op('pe', lambda e: e.matmul(PS[6][:, 0:512], lhsT=ones, rhs=cb[:, 0:512], start=True, stop=True),
                 reads=[RC], writes=[('ps', 6)])
        pend = []
        for blk in blocks:
            sinfo = emit_S(blk)
            pend.append((blk, sinfo))
            if len(pend) > 2:
                emit_rest(*pend.pop(0))
                tick()
        while pend:
            emit_rest(*pend.pop(0))
            tick()
        while post:
            tick(force=True)
        while deferred:
            deferred.pop(0)()

    def diff_unit(l, h, lam_init):
        u = 4 + h
        ks = (8, 10)
        vs = 9
        wqk, wqkreg = wload(wmix_d[l, u, 0], 2048)
        wv, wvreg = wload(wmix_d[l, u, 1, :, 0:1024], 1024)
        if h == 0:
            P.op('dve', lambda e: e.memset(SL[64:128, ks[0], :], 0.0), writes=[('S', ks[0], g) for g in range(5)])
            P.op('dve', lambda e: e.memset(SL[0:64, ks[1], :], 0.0), writes=[('S', ks[1], g) for g in range(5)])
        for g, (s, n) in enumerate(GROUPS):
            tabs_for[g] = tload(1, s, n)
            project_rot_group(0, wqk, wqkreg, u, CB_PERMD, g)
            project_rot_group(1, wqk, wqkreg, None, CB_PERMD, g, split=ks)
        project_v(wv, wvreg, vs)
        diff_attention(l, u, ks, vs, 1.0 - lam_init)

    def lam_compute(l):
        lam_init = 0.8 - 0.6 * math.exp(-0.3 * l)
        for t in range(2):
            a = lamin[:, (l * 4 + 2 * t) * 64:(l * 4 + 2 * t + 1) * 64]
            b = lamin[:, (l * 4 + 2 * t + 1) * 64:(l * 4 + 2 * t + 2) * 64]
            P.op('dve', lambda e, a=a, b=b: e.tensor_tensor(out=lamw[:, :], in0=a, in1=b, op=ALU.mult),
                 reads=[RC, ('lamw',)], writes=[('lamw',)])
            P.op('dve', lambda e, t=t: e.tensor_reduce(out=lamv[:, t:t + 1], in_=lamw[:, :], axis=mybir.AxisListType.X, op=ALU.add),
                 reads=[('lamw',), ('lamv',)], writes=[('lamv',)])
            P.op('act', lambda e, t=t: e.activation(out=lamv[:, 2 + t:3 + t], in_=lamv[:, t:t + 1], func=AF.Exp),
                 reads=[('lamv',)], writes=[('lamv',)])
        P.op('dve', lambda e: e.scalar_tensor_tensor(out=lamv[:, 4 + l:5 + l], in0=lamv[:, 3:4], scalar=-lam_init, in1=lamv[:, 2:3],
                                                      op0=ALU.add, op1=ALU.subtract),
             reads=[('lamv',)], writes=[('lamv',)])
        return lam_init

    def mixer(l):
        rmsnorm(PV_NORM + (l * 3 + 1) * 8)
        lam_init = lam_compute(l)
        if DBG_MIX == 'A':
            return
        for h in range(4):
            retention_unit(l, h)
            if DBG_MIX in ('B', 'B1', 'C', 'R1', 'R2', 'R3'):
                return
        if DBG_MIX == 'D':
            return
        for h in range(4):
            diff_unit(l, h, lam_init)
            if DBG_MIX == 'F':
                return
        for c in range(8):
            wt, wreg = wload(wout_d[l, c], 1024)
            for g, (s, n) in enumerate(GROUPS):
                mm_out_group(wt, wreg, 8, c, g, s, n, 1.0)

    stages = []
    for l in range(2):
        stages += [('ffn', l, 0), ('mix', l), ('ffn', l, 1)]
    nst = len(stages) if stop is None else stop
    for st in stages[:nst]:
        if st[0] == 'ffn':
            ffn(st[1], st[2])
        else:
            mixer(st[1])
    if stop is None:
        rmsnorm(PV_NORM + 48, final=True)

    outregs = []
    for c in range(8):
        P.dma('sp', lambda e, c=c: e.dma_start(out=out_d[:, c, :], in_=hT[:, c, 128:TP]), 'out',
              reads=[('h', c, g) for g in range(1, 5)], writes=[('out', c)])
        outregs.append(('out', c))
    if dump == 'h':
        for c in range(8):
            P.dma('sp', lambda e, c=c: e.dma_start(out=dbg_d[:, c, :], in_=hT[:, c, :]), 'out',
                  reads=[('h', c, g) for g in range(5)], writes=[('dbg', c)])
            outregs.append(('dbg', c))
    P.op('sp', None, reads=outregs)
    P.build()
    return nc, P


def _bf(a):
    return np.asarray(a, dtype=np.float32).astype(ml_dtypes.bfloat16)


def _const_tables():
    pos = (np.arange(TP, dtype=np.float32) - np.float32(PAD))
    angle = (np.float32(10000.0) ** (-np.linspace(0.0, 1.0, 64, dtype=np.float32))).astype(np.float32)
    fr = (pos[None, :] * np.repeat(angle, 2)[:, None]).astype(np.float32)
    cosr = np.cos(fr).astype(np.float32)
    sgn = np.where(np.arange(128) % 2 == 0, -1.0, 1.0).astype(np.float32)[:, None]
    sinr = (np.sin(fr) * sgn).astype(np.float32)
    r = 16
    inv = (np.float32(500000.0) ** (-np.arange(0, r, 2, dtype=np.float32) / r)).astype(np.float32)
    cosd = np.ones((128, TP), np.float32)
    sind = np.zeros((128, TP), np.float32)
    for p in range(128):
        dd = p % 64
        if dd < r:
            f = (pos * inv[dd % 8]).astype(np.float32)
            cosd[p] = np.cos(f)
            sind[p] = np.sin(f) * (-1.0 if dd < 8 else 1.0)
    tabs = np.stack([cosr, sinr, cosd, sind]).astype(np.float32)
    cbm = np.zeros((128, CB_N), np.float32)
    cbm[:, CB_ONES:CB_ONES + 128] = 1.0
    cbm[PAD:, CB_ONESPAD:CB_ONESPAD + 128] = 1.0
    cbm[:, CB_IDENT:CB_IDENT + 128] = np.eye(128)
    for m in range(128):
        src = m + 1 if m % 2 == 0 else m - 1
        cbm[src, CB_PERMR + m] = 1.0
        dd = m % 64
        if dd < 8:
            cbm[m + 8, CB_PERMD + m] = 1.0
        elif dd < 16:
            cbm[m - 8, CB_PERMD + m] = 1.0
    jj = np.arange(128)[:, None]
    ii = np.arange(128)[None, :]
    cbm[:, CB_CMASK:CB_CMASK + 128] = np.where(jj <= ii, 0.0, -30000.0)
    cfm = np.zeros((128, CF_N), np.float32)
    cfm[:, CF_EPS] = EPS
    cfm[:, CF_ONE] = 1.0
    for l_ in range(2):
        om_ = 1.0 - (0.8 - 0.6 * math.exp(-0.3 * l_))
        cfm[:, CF_EPS + 1 + l_] = EPS / (om_ * om_)
    for h in range(4):
        lg = math.log(1.0 - 2.0 ** (-5.0 - h))
        rel = (ii - jj).astype(np.float64)
        dm = np.where(rel >= 0, np.exp(lg * np.maximum(rel, 0.0)), 0.0) * (128.0 ** -0.5)
        cfm[:, CF_DMASK + h * 128:CF_DMASK + (h + 1) * 128] = dm
        cfm[:, CF_QDEC + h * 128:CF_QDEC + (h + 1) * 128] = np.exp(lg * (np.arange(128) + 1.0))[None, :]
        cfm[:, CF_KDEC + h] = np.exp(lg * (127.0 - np.arange(128))) * (128.0 ** -0.5)
    return tabs, cbm, cfm


def _prep(inputs):
    f = lambda k: np.asarray(inputs[k], dtype=np.float32)
    tabs, cbm, cfm = _const_tables()
    cvec = lambda v: np.ascontiguousarray(v.reshape(-1, 128).T)
    norms = [f("ffn1_norm"), f("mix_norm"), f("ffn2_norm")]
    for l in range(2):
        for w in range(3):
            c0 = CF_PVEC + PV_NORM + (l * 3 + w) * 8
            cfm[:, c0:c0 + 8] = cvec(norms[w][l])
        cfm[:, CF_PVEC + PV_GN + l * 4:CF_PVEC + PV_GN + l * 4 + 4] = cvec(f("ret_gn_w")[l])
        cfm[:, CF_PVEC + PV_SW + l] = f("diff_subln_w")[l]
    cfm[:, CF_PVEC + PV_NORM + 48:CF_PVEC + PV_NORM + 56] = cvec(f("final_norm"))
    lam = np.stack([f("diff_lambda_q1"), f("diff_lambda_k1"), f("diff_lambda_q2"), f("diff_lambda_k2")], axis=1)
    lam = np.ascontiguousarray(np.broadcast_to(lam.reshape(1, 512), (128, 512)))

    def fm_tile(w):
        nc_ = w.shape[1] // 128
        return w.reshape(8, 128, nc_, 128).transpose(2, 1, 0, 3)

    wgu = np.empty((2, 2, NF, 128, 2, 8, 128), np.float32)
    wd = np.empty((2, 2, 2, 8, 128, NFH, 128), np.float32)
    for l in range(2):
        for j, (kg, ku, kd_) in enumerate([("ffn1_w_gate", "ffn1_w_up", "ffn1_w_down"), ("ffn2_w_gate", "ffn2_w_up", "ffn2_w_down")]):
            wgu[l, j, :, :, 0] = fm_tile(f(kg)[l])
            wgu[l, j, :, :, 1] = fm_tile(f(ku)[l])
            wd[l, j] = f(kd_)[l].reshape(2, NFH, 128, 8, 128).transpose(0, 3, 2, 1, 4)
    win = f("w_in")
    wmix = np.zeros((2, 8, 2, 128, 2, 8, 128), np.float32)
    for l in range(2):
        t = fm_tile(win[l])
        for h in range(4):
            wmix[l, h, 0, :, 0] = t[h]
            wmix[l, h, 0, :, 1] = t[4 + h]
            wmix[l, h, 1, :, 0] = t[8 + h]
            wmix[l, h, 1, :, 1] = t[12 + h]
            wmix[l, 4 + h, 0, :, 0] = t[16 + h]
            wmix[l, 4 + h, 0, :, 1] = t[20 + h]
            wmix[l, 4 + h, 1, :, 0] = t[24 + h]
    wout = np.stack([fm_tile(f("w_out")[l]) for l in range(2)])
    shared = {
        "metaT": np.ascontiguousarray(f("meta_tokens").reshape(NMETA, 8, 128).transpose(2, 1, 0)),
        "cb": _bf(cbm), "cf": cfm, "lam": lam, "tabs": tabs,
        "wgu": wgu.reshape(2, 2, NF, 128, 2048), "wd": wd.reshape(2, 2, 2, 8, 128, NFH * 128),
        "wmix": wmix.reshape(2, 8, 2, 128, 2048), "wout": np.ascontiguousarray(wout.reshape(2, 8, 128, 1024)),
    }
    x = f("x")
    xTs = [np.ascontiguousarray(x[b].reshape(SEQ, 8, 128).transpose(2, 1, 0)) for b in range(x.shape[0])]
    return shared, xTs


def run(inputs, cores=None, stop=None, dump=None, trace=False):
    shared, xTs = _prep(inputs)
    cores = list(range(8)) if cores is None else cores
    nc, P = build_program(stop=stop, dump=dump)
    in_maps = [dict(shared, xT=xTs[b]) for b in cores]
    res = run_bass_kernel_spmd(nc, in_maps, core_ids=list(range(len(cores))), trace=trace)
    return res, P


def kernel(**inputs):
    res, _ = run(inputs)
    outs = [np.asarray(r["outT"], dtype=np.float32) for r in res.results]
    out = np.stack([o.transpose(2, 1, 0).reshape(SEQ, D) for o in outs])
    return np.ascontiguousarray(out.astype(np.float32))
```

```python
import math
import numpy as np
import ml_dtypes
import concourse.bass as bass
import concourse.mybir as mybir
from concourse.bass_utils import run_bass_kernel_spmd

F32 = mybir.dt.float32
BF16 = mybir.dt.bfloat16
AF = mybir.ActivationFunctionType
ALU = mybir.AluOpType

D = 1024
SEQ = 2048
NMETA = 16
PAD = 112
TP = 2176
NCH = 17
DFF = 2816
NF = 22
NFH = 11
EPS = 1e-6
GROUPS = [(0, 128)] + [(128 + 512 * g, 512) for g in range(4)]
NSLOT = 11
NW = 4
WCOLS = 2048
SAME_ENGINE_SYNC = True
DBG_HACK = 0
WARM = 0
ROT_ADD_ENG = 'dve'
DBG_MIX = None

CB_ONES, CB_ONESPAD, CB_IDENT, CB_PERMR, CB_PERMD, CB_CMASK = [i * 128 for i in range(6)]
CB_N = 6 * 128
CF_DMASK = 0
CF_QDEC = 512
CF_KDEC = 1024
CF_PVEC = 1028
PV_NORM = 0
PV_GN = 56
PV_SW = 64
CF_EPS = CF_PVEC + 66
CF_ONE = CF_EPS + 3
CF_N = CF_ONE + 1


class Prog:
    def __init__(self, nc):
        self.nc = nc
        self.ops = []
        self.dma_sems = {}

    def op(self, eng, fn, reads=(), writes=()):
        self.ops.append(dict(eng=eng, fn=fn, reads=list(reads), writes=list(writes), dma=None))

    def dma(self, eng, fn, slot, reads=(), writes=()):
        self.ops.append(dict(eng=eng, fn=fn, reads=list(reads), writes=list(writes), dma=slot))

    def build(self):
        nc = self.nc
        ops = self.ops
        engs = ['pe', 'act', 'dve', 'pool', 'sp']
        last_w = {}
        readers = {}
        for i, o in enumerate(ops):
            deps = set()
            key = ('dma', i) if o['dma'] is not None else o['eng']
            for r in o['reads']:
                if r in last_w:
                    deps.add(last_w[r])
                if r[0] in ('ps', 'pt'):
                    for k2, i2 in readers.get(r, {}).items():
                        if k2 != key:
                            deps.add(i2)
            for w in o['writes']:
                if w in last_w:
                    deps.add(last_w[w])
                rd = readers.get(w)
                if rd:
                    deps.update(rd.values())
            deps.discard(i)
            o['deps'] = deps
            for r in o['reads']:
                readers.setdefault(r, {})[key] = i
            for w in o['writes']:
                last_w[w] = i
                readers[w] = {}
        signal = set()
        for i, o in enumerate(ops):
            for d in o['deps']:
                p = ops[d]
                if p['dma'] is not None:
                    continue
                if p['eng'] != o['eng'] or (SAME_ENGINE_SYNC and o['eng'] != 'pe'):
                    signal.add(d)
        sem = {e: nc.alloc_semaphore('s_' + e) for e in engs}
        cnt = {e: 0 for e in engs}
        for i, o in enumerate(ops):
            if o['dma'] is not None:
                slot = o['dma']
                if slot not in self.dma_sems:
                    self.dma_sems[slot] = [nc.alloc_semaphore('d_' + str(slot)), 0]
                ent = self.dma_sems[slot]
                ent[1] += 16
                o['tok'] = (ent[0], ent[1])
            elif i in signal:
                cnt[o['eng']] += 1
                o['tok'] = (sem[o['eng']], cnt[o['eng']])
            else:
                o['tok'] = None
        waited = {e: {} for e in engs}
        for i, o in enumerate(ops):
            need = {}
            for d in o['deps']:
                p = ops[d]
                if p['dma'] is None and p['eng'] == o['eng'] and not (SAME_ENGINE_SYNC and o['eng'] != 'pe'):
                    continue
                tok = p['tok']
                assert tok is not None
                k = tok[0]
                if need.get(k, (None, 0))[1] < tok[1]:
                    need[k] = tok
            ws = []
            for k, tok in need.items():
                if waited[o['eng']].get(k, 0) < tok[1]:
                    waited[o['eng']][k] = tok[1]
                    ws.append(tok)
            o['waits'] = ws
        per = {e: [o for o in ops if o['eng'] == e] for e in engs}
        self.stats = {e: len(per[e]) for e in engs}
        self.stats['signals'] = dict(cnt)

        def run(e, lst):
            for o in lst:
                for (s, v) in o['waits']:
                    e.wait_ge(s, v)
                if o['fn'] is None:
                    continue
                ins = o['fn'](e)
                if o['tok'] is not None:
                    ins.then_inc(o['tok'][0], 16 if o['dma'] is not None else 1)

        with nc.Block() as block:
            @block.tensor
            def _(e):
                run(e, per['pe'])

            @block.scalar
            def _(e):
                run(e, per['act'])

            @block.vector
            def _(e):
                run(e, per['dve'])

            @block.gpsimd
            def _(e):
                run(e, per['pool'])

            @block.sync
            def _(e):
                run(e, per['sp'])


def build_program(stop=None, dump=None):
    nc = bass.Bass("TRN2", target_bir_lowering=False)
    P = Prog(nc)

    def din(name, shape, dt=F32):
        return nc.dram_tensor(name, list(shape), dt, kind="ExternalInput").ap()

    xT_d = din("xT", [128, 8, SEQ])
    meta_d = din("metaT", [128, 8, NMETA])
    cb_d = din("cb", [128, CB_N], BF16)
    cf_d = din("cf", [128, CF_N])
    lam_d = din("lam", [128, 2 * 4 * 64])
    tabs_d = din("tabs", [4, 128, TP])
    wgu_d = din("wgu", [2, 2, NF, 128, 2048])
    wd_d = din("wd", [2, 2, 2, 8, 128, NFH * 128])
    wmix_d = din("wmix", [2, 8, 2, 128, 2048])
    wout_d = din("wout", [2, 8, 128, 1024])
    out_d = nc.dram_tensor("outT", [128, 8, SEQ], F32, kind="ExternalOutput").ap()
    dbg_d = None
    if dump is not None:
        dbg_d = nc.dram_tensor("dbg", [128, 8, TP], F32, kind="ExternalOutput").ap()

    sb = nc.alloc_sbuf_tensor
    hT = sb("hT", [128, 8, TP], F32)
    xnT = sb("xnT", [128, 8, TP], BF16)
    SL = sb("slots", [128, NSLOT, TP], BF16)
    wsl = [sb(f"wsl{i}", [128, WCOLS], BF16) for i in range(NW)]
    tabt = [sb(f"tab{i}", [128, 2, 512], F32) for i in range(2)]
    cb = sb("cbs", [128, CB_N], BF16)
    cf = sb("cfs", [128, CF_N], F32)
    lamin = sb("lamin", [128, 512], F32)
    lamw = sb("lamw", [128, 64], F32)
    lamv = sb("lamv", [128, 8], F32)
    NR = 3
    sqr = [sb(f"sq{i}", [128, 512], BF16) for i in range(NR)]
    fa = [sb(f"fa{i}", [128, 512], F32) for i in range(4)]
    fb = [sb(f"fb{i}", [128, 512], F32) for i in range(4)]
    ba = [sb(f"ba{i}", [128, 512], BF16) for i in range(4)]
    sm = [sb(f"sm{i}", [128, 128], BF16) for i in range(6)]
    Rf = [sb(f"Rf{i}", [128, 128], F32) for i in range(2)]

    PS = [nc.alloc_psum_tensor(f"ps{i}", [128, 512], F32) for i in range(7)]
    PT = nc.alloc_psum_tensor("pst", [128, 1024], BF16)

    SBANK = [(PS[0], ('ps', 0)), (PS[1], ('ps', 1)), (PT[:, :].bitcast(F32), ('pt',))]
    ctr = {}

    def rr(name, n):
        v = ctr.get(name, 0)
        ctr[name] = v + 1
        return v % n

    def ring(name, lst):
        i = rr(name, len(lst))
        return lst[i], (name, i)

    ones = cb[:, CB_ONES:CB_ONES + 128]
    ones_pad = cb[:, CB_ONESPAD:CB_ONESPAD + 128]
    ident = cb[:, CB_IDENT:CB_IDENT + 128]
    cmask = cb[:, CB_CMASK:CB_CMASK + 128]
    RC = ('const',)
    epsc = cf[:, CF_EPS:CF_EPS + 1]
    onec = cf[:, CF_ONE:CF_ONE + 1]
    epsl = [cf[:, CF_EPS + 1 + l_:CF_EPS + 2 + l_] for l_ in range(2)]

    P.dma('sp', lambda e: e.dma_start(out=cb[:, :], in_=cb_d), 'const', writes=[RC])
    P.dma('sp', lambda e: e.dma_start(out=cf[:, :], in_=cf_d), 'const', writes=[RC])
    P.dma('sp', lambda e: e.dma_start(out=lamin[:, :], in_=lam_d), 'const', writes=[RC])
    hregs = lambda c: [('h', c, g) for g in range(5)]
    P.op('dve', lambda e: e.memset(hT[:, :, 0:PAD], 0.0), writes=[('h', c, 0) for c in range(8)])
    for c in range(8):
        P.dma('sp', lambda e, c=c: e.dma_start(out=hT[:, c, 128:TP], in_=xT_d[:, c, :]), 'xin',
              writes=[('h', c, g) for g in range(1, 5)])
    P.dma('sp', lambda e: e.dma_start(out=hT[:, :, PAD:128], in_=meta_d), 'xin',
          reads=[('h', c, 0) for c in range(8)], writes=[('h', c, 0) for c in range(8)])

    def wload(src_ap, ncols):
        i = rr('w', NW)
        t = wsl[i]
        reg = ('w', i)
        P.dma('pool', lambda e: e.dma_start(out=t[:, 0:ncols], in_=src_ap, max_dma_last_dim=8192),
              ('w', i), writes=[reg])
        return t, reg

    def tload(k, s, n):
        i = rr('tab', 2)
        t = tabt[i]
        regs = [('tab', i, 0), ('tab', i, 1)]
        P.dma('sp', lambda e: e.dma_start(out=t[:, 0, 0:n], in_=tabs_d[2 * k, :, s:s + n]), ('tab', i, 0), writes=[regs[0]])
        P.dma('sp', lambda e: e.dma_start(out=t[:, 1, 0:n], in_=tabs_d[2 * k + 1, :, s:s + n]), ('tab', i, 1), writes=[regs[1]])
        return t, regs

    def rmsnorm_group(pvcol, final, g, s, n):
        ss = PS[6]
        for c in range(8):
            sq, sqreg = ring('sq', sqr)
            P.op('act', lambda e, sq=sq, c=c: e.activation(out=sq[:, 0:n], in_=hT[:, c, s:s + n], func=AF.Square),
                 reads=[('h', c, g)], writes=[sqreg])
            P.op('pe', lambda e, sq=sq, c=c: e.matmul(ss[:, 0:n], lhsT=ones, rhs=sq[:, 0:n], start=(c == 0), stop=(c == 7)),
                 reads=[sqreg, RC], writes=[('ps', 6)])
        rt, rtreg = ring('fa', fa)
        P.op('act', lambda e: e.activation(out=rt[:, 0:n], in_=ss[:, 0:n], func=AF.Ln, scale=1.0 / D, bias=epsc),
             reads=[('ps', 6), RC], writes=[rtreg])
        rs, rsreg = ring('fb', fb)
        P.op('act', lambda e: e.activation(out=rs[:, 0:n], in_=rt[:, 0:n], func=AF.Exp, scale=-0.5), reads=[rtreg], writes=[rsreg])
        for c in range(8):
            gcol = cf[:, CF_PVEC + pvcol + c:CF_PVEC + pvcol + c + 1]
            dst = hT if final else xnT
            wreg_ = ('h', c, g) if final else ('xn', c, g)
            P.op('dve', lambda e, c=c, gcol=gcol, dst=dst: e.scalar_tensor_tensor(
                out=dst[:, c, s:s + n], in0=hT[:, c, s:s + n], scalar=gcol, in1=rs[:, 0:n], op0=ALU.mult, op1=ALU.mult),
                reads=[('h', c, g), rsreg, RC], writes=[wreg_])

    def rmsnorm(pvcol, final=False):
        for g, (s, n) in enumerate(GROUPS):
            rmsnorm_group(pvcol, final, g, s, n)

    NM = 128 - PAD

    def resid_add(ps, psreg, c, g, scale):
        s, n = GROUPS[g]
        if g == 0:
            s, n = PAD, NM
        P.op('dve', lambda e: e.scalar_tensor_tensor(out=hT[:, c, s:s + n], in0=ps[:, 0:n], scalar=scale,
                                                      in1=hT[:, c, s:s + n], op0=ALU.mult, op1=ALU.add),
             reads=[psreg, ('h', c, g)], writes=[('h', c, g)])

    def ffn_gu_group(wt, wreg, fi, g, s, n):
        if g == 0:
            s, n = PAD, NM
        k = rr('ffn_gu', 2)
        gp, up = PS[k], PS[2 + k]
        for kc in range(8):
            P.op('pe', lambda e, kc=kc: e.matmul(gp[:, 0:n], lhsT=wt[:, kc * 128:(kc + 1) * 128],
                                               rhs=xnT[:, kc, s:s + n], start=(kc == 0), stop=(kc == 7)),
                 reads=[wreg, ('xn', kc, g)], writes=[('ps', k)])
        for kc in range(8):
            P.op('pe', lambda e, kc=kc: e.matmul(up[:, 0:n], lhsT=wt[:, 1024 + kc * 128:1024 + (kc + 1) * 128],
                                               rhs=xnT[:, kc, s:s + n], start=(kc == 0), stop=(kc == 7)),
                 reads=[wreg, ('xn', kc, g)], writes=[('ps', 2 + k)])
        sg, sgreg = ring('fa', fa)
        P.op('act', lambda e: e.activation(out=sg[:, 0:n], in_=gp[:, 0:n], func=AF.Silu),
             reads=[('ps', k)], writes=[sgreg])
        P.op('dve', lambda e: e.tensor_tensor(out=SL[:, fi, s:s + n], in0=up[:, 0:n], in1=sg[:, 0:n], op=ALU.mult),
             reads=[('ps', 2 + k), sgreg], writes=[('S', fi, g)])

    def mm_out_group(wt, wreg, nk, c, g, s, n, scale):
        if g == 0:
            s, n = PAD, NM
        k = 4 + rr('ffn_o', 2)
        op_ = PS[k]
        for fi in range(nk):
            P.op('pe', lambda e, fi=fi: e.matmul(op_[:, 0:n], lhsT=wt[:, fi * 128:(fi + 1) * 128],
                                               rhs=SL[:, fi, s:s + n], start=(fi == 0), stop=(fi == nk - 1)),
                 reads=[wreg, ('S', fi, g)], writes=[('ps', k)])
        resid_add(op_, ('ps', k), c, g, scale)

    def ffn(l, j):
        rmsnorm(PV_NORM + (l * 3 + (0 if j == 0 else 2)) * 8)
        for half in range(2):
            for fi in range(NFH):
                wt, wreg = wload(wgu_d[l, j, half * NFH + fi], 2048)
                for g, (s, n) in enumerate(GROUPS):
                    ffn_gu_group(wt, wreg, fi, g, s, n)
            for c in range(8):
                wt, wreg = wload(wd_d[l, j, half, c], NFH * 128)
                for g, (s, n) in enumerate(GROUPS):
                    mm_out_group(wt, wreg, NFH, c, g, s, n, 0.5)

    tabs_for = {}

    rot_pending = []

    def project_rot_group(which, wt, wreg, dst_slot, permcol, g, split=None):
        perm = cb[:, permcol:permcol + 128]
        s, n = GROUPS[g]
        tt, tregs = tabs_for[g]
        pk = rr('proj', 2)
        pp, sp_ = PS[pk], PS[2 + pk]
        for kc in range(8):
            P.op('pe', lambda e, kc=kc: e.matmul(pp[:, 0:n], lhsT=wt[:, which * 1024 + kc * 128:which * 1024 + (kc + 1) * 128],
                                               rhs=xnT[:, kc, s:s + n], start=(kc == 0), stop=(kc == 7)),
                 reads=[wreg, ('xn', kc, g)], writes=[('ps', pk)])
        qb, qbreg = ring('ba', ba)
        P.op('act', lambda e: e.activation(out=qb[:, 0:n], in_=pp[:, 0:n], func=AF.Identity), reads=[('ps', pk)], writes=[qbreg])
        rot_flush()

        def second():
            P.op('pe', lambda e: e.matmul(sp_[:, 0:n], lhsT=perm, rhs=qb[:, 0:n], start=True, stop=True),
                 reads=[qbreg, RC], writes=[('ps', 2 + pk)])
            t1, t1reg = ring('fa', fa)
            t2, t2reg = ring('fb', fb)
            P.op('dve', lambda e: e.tensor_tensor(out=t1[:, 0:n], in0=pp[:, 0:n], in1=tt[:, 0, 0:n], op=ALU.mult),
                 reads=[('ps', pk)] + tregs, writes=[t1reg])
            P.op('dve', lambda e: e.tensor_tensor(out=t2[:, 0:n], in0=sp_[:, 0:n], in1=tt[:, 1, 0:n], op=ALU.mult),
                 reads=[('ps', 2 + pk)] + tregs, writes=[t2reg])
            if split is None:
                P.op(ROT_ADD_ENG, lambda e: e.tensor_tensor(out=SL[:, dst_slot, s:s + n], in0=t1[:, 0:n], in1=t2[:, 0:n], op=ALU.add),
                     reads=[t1reg, t2reg], writes=[('S', dst_slot, g)])
            else:
                for m in range(2):
                    P.op('dve', lambda e, m=m: e.tensor_tensor(out=SL[64 * m:64 * m + 64, split[m], s:s + n], in0=t1[64 * m:64 * m + 64, 0:n],
                                                             in1=t2[64 * m:64 * m + 64, 0:n], op=ALU.add),
                         reads=[t1reg, t2reg], writes=[('S', split[m], g)])
        rot_pending.append(second)

    def rot_flush():
        while rot_pending:
            rot_pending.pop(0)()

    def project_v_group(wt, wreg, vslot, g, s, n):
        pk = rr('proj', 2)
        pp = PS[pk]
        for ci in range(n // 128):
            cs = s + ci * 128
            for kc in range(8):
                P.op('pe', lambda e, kc=kc, ci=ci, cs=cs: e.matmul(pp[:, ci * 128:(ci + 1) * 128], lhsT=xnT[:, kc, cs:cs + 128],
                                                                 rhs=wt[:, kc * 128:(kc + 1) * 128], start=(kc == 0), stop=(kc == 7)),
                     reads=[wreg, ('xn', kc, g)], writes=[('ps', pk)])
        P.op('act', lambda e: e.activation(out=SL[:, vslot, s:s + n], in_=pp[:, 0:n], func=AF.Identity),
             reads=[('ps', pk)], writes=[('S', vslot, g)])

    def project_v(wt, wreg, vslot):
        for g, (s, n) in enumerate(GROUPS):
            project_v_group(wt, wreg, vslot, g, s, n)

    def stats_rstd(src, srcreg, n, scale, bias_col, o0=0):
        sq, sqreg = ring('ba', ba)
        P.op('dve', lambda e: e.tensor_tensor(out=sq[:, o0:n], in0=src[:, o0:n], in1=src[:, o0:n], op=ALU.mult), reads=[srcreg], writes=[sqreg])
        P.op('pe', lambda e: e.matmul(PS[6][:, o0:n], lhsT=ones, rhs=sq[:, o0:n], start=True, stop=True),
             reads=[sqreg, RC], writes=[('ps', 6)])
        sd, sdreg = ring('fb', fb)
        P.op('act', lambda e: e.activation(out=sd[:, o0:n], in_=PS[6][:, o0:n], func=AF.Ln, scale=scale, bias=bias_col),
             reads=[('ps', 6), RC], writes=[sdreg])
        rs, rsreg = ring('fb', fb)
        P.op('act', lambda e: e.activation(out=rs[:, o0:n], in_=sd[:, o0:n], func=AF.Exp, scale=-0.5), reads=[sdreg], writes=[rsreg])
        return rs, rsreg

    def retention_attention(l, h, u, ks, vs, wvg, wvgreg):
        gam = 1.0 - 2.0 ** (-5.0 - h)
        cd = gam ** 128
        dmask = cf[:, CF_DMASK + h * 128:CF_DMASK + (h + 1) * 128]
        qdec = cf[:, CF_QDEC + h * 128:CF_QDEC + (h + 1) * 128]
        kdec = cf[:, CF_KDEC + h:CF_KDEC + h + 1]
        gnw = cf[:, CF_PVEC + PV_GN + l * 4 + h:CF_PVEC + PV_GN + l * 4 + h + 1]
        RBs = 18 - ks
        grp = lambda n: 0 if n == 0 else 1 + (n - 1) // 4
        P.op('dve', lambda e: e.memset(SL[:, RBs, 0:128], 0.0), writes=[('S', RBs, 0)])
        P.op('dve', lambda e: e.memset(Rf[0][:, :], 0.0), writes=[('Rf', 0)])
        def emit_gate(g, bank=6):
            s, n = GROUPS[g]
            gp = PS[bank]
            for kc in range(8):
                P.op('pe', lambda e, kc=kc: e.matmul(gp[:, 0:n], lhsT=wvg[:, 1024 + kc * 128:1024 + (kc + 1) * 128],
                                                   rhs=xnT[:, kc, s:s + n], start=(kc == 0), stop=(kc == 7)),
                     reads=[wvgreg, ('xn', kc, g)], writes=[('ps', bank)])
            sg, sgreg = ring('fb', fb)
            P.op('act', lambda e: e.activation(out=sg[:, 0:n], in_=gp[:, 0:n], func=AF.Exp, scale=-1.0), reads=[('ps', bank)], writes=[sgreg])
            P.op('act', lambda e: e.activation(out=sg[:, 0:n], in_=sg[:, 0:n], func=AF.Ln, bias=onec), reads=[sgreg, RC], writes=[sgreg])
            P.op('act', lambda e: e.activation(out=sg[:, 0:n], in_=sg[:, 0:n], func=AF.Exp, scale=-1.0), reads=[sgreg], writes=[sgreg])
            gs, gsreg = ring('sq', sqr)
            P.op('dve', lambda e: e.tensor_tensor(out=gs[:, 0:n], in0=gp[:, 0:n], in1=sg[:, 0:n], op=ALU.mult),
                 reads=[('ps', bank), sgreg], writes=[gsreg])
            gate[g] = (gs, gsreg)

        for r in range(4):
            for i in range(4):
                n = 4 * r + i
                P.op('pe', lambda e, n=n, i=i: e.transpose(PT[:, i * 128:(i + 1) * 128], SL[:, ks, n * 128:(n + 1) * 128], ident),
                     reads=[('S', ks, grp(n)), RC], writes=[('pt',)])
            kd, kdreg = ring('ba', ba)
            P.op('act', lambda e, kd=kd: e.activation(out=kd[:, 0:512], in_=PT[:, 0:512], func=AF.Identity, scale=kdec),
                 reads=[('pt',), RC], writes=[kdreg])
            for i in range(4):
                n = 4 * r + i
                P.op('pe', lambda e, n=n, i=i, r=r, kd=kd: e.matmul(PS[r][:, i * 128:(i + 1) * 128], lhsT=kd[:, i * 128:(i + 1) * 128],
                                                                  rhs=SL[:, vs, n * 128:(n + 1) * 128], start=True, stop=True),
                     reads=[kdreg, ('S', vs, grp(n))], writes=[('ps', r)])
        gate = {}
        for n in range(16):
            if n in (0, 5, 10):
                emit_gate(n // 5, (6, 4, 5)[n // 5])
            r, i = divmod(n, 4)
            P.op('dve', lambda e, n=n, r=r, i=i: e.scalar_tensor_tensor(out=Rf[(n + 1) % 2][:, :], in0=Rf[n % 2][:, :], scalar=cd,
                                                                      in1=PS[r][:, i * 128:(i + 1) * 128], op0=ALU.mult, op1=ALU.add),
                 reads=[('Rf', n % 2), ('ps', r)], writes=[('Rf', (n + 1) % 2)])
            P.op('act', lambda e, n=n: e.activation(out=SL[:, RBs, (n + 1) * 128:(n + 2) * 128], in_=Rf[(n + 1) % 2][:, :], func=AF.Identity),
                 reads=[('Rf', (n + 1) % 2)], writes=[('S', RBs, grp(n + 1))])
        deferred = []

        def emit_ST(n):
            cs = n * 128
            g = grp(n)
            bk = n % 2
            P.op('pe', lambda e: e.matmul(PS[bk][:, 0:128], lhsT=SL[:, ks, cs:cs + 128], rhs=SL[:, u, cs:cs + 128],
                                          start=True, stop=True), reads=[('S', u, g), ('S', ks, g)], writes=[('ps', bk)])
            stm, stmreg = ring('sm', sm)
            P.op('dve', lambda e: e.tensor_tensor(out=stm[:, :], in0=PS[bk][:, 0:128], in1=dmask, op=ALU.mult),
                 reads=[('ps', bk), RC], writes=[stmreg])
            qd, qdreg = ring('sm', sm)
            P.op('dve', lambda e: e.tensor_tensor(out=qd[:, :], in0=SL[:, u, cs:cs + 128], in1=qdec, op=ALU.mult),
                 reads=[('S', u, g), RC], writes=[qdreg])
            return stm, stmreg, qd, qdreg

        def emit_O(n, st):
            stm, stmreg, qd, qdreg = st
            cs = n * 128
            g = grp(n)
            ci = 0 if n == 0 else (n - 1) % 4
            OBk = 4 + g % 2
            OB = PS[OBk]
            P.op('pe', lambda e: e.matmul(OB[:, ci * 128:(ci + 1) * 128], lhsT=SL[:, vs, cs:cs + 128], rhs=stm[:, :],
                                          start=True, stop=False), reads=[('S', vs, g), stmreg], writes=[('ps', OBk)])
            P.op('pe', lambda e: e.matmul(OB[:, ci * 128:(ci + 1) * 128], lhsT=SL[:, RBs, cs:cs + 128], rhs=qd[:, :],
                                          start=False, stop=True), reads=[('S', RBs, g), qdreg], writes=[('ps', OBk)])
            if n == 0 or ci == 3:
                while deferred:
                    deferred.pop(0)()
                deferred.append(lambda: finalize(g, OBk))

        def finalize(g, OBk):
            s, n = GROUPS[g]
            OB = PS[OBk]
            ob, obreg = ring('ba', ba)
            P.op('act', lambda e: e.activation(out=ob[:, 0:n], in_=OB[:, 0:n], func=AF.Identity), reads=[('ps', OBk)], writes=[obreg])
            osq, osqreg = ring('ba', ba)
            P.op('act', lambda e: e.activation(out=osq[:, 0:n], in_=OB[:, 0:n], func=AF.Square), reads=[('ps', OBk)], writes=[osqreg])
            P.op('pe', lambda e: e.matmul(PS[2][:, 0:n], lhsT=ones, rhs=ob[:, 0:n], start=True, stop=True),
                 reads=[obreg, RC], writes=[('ps', 2)])
            P.op('pe', lambda e: e.matmul(PS[3][:, 0:n], lhsT=ones, rhs=osq[:, 0:n], start=True, stop=True),
                 reads=[osqreg, RC], writes=[('ps', 3)])
            mean, meanreg = ring('fa', fa)
            P.op('dve', lambda e: e.tensor_scalar(out=mean[:, 0:n], in0=PS[2][:, 0:n], scalar1=1.0 / 128, scalar2=None, op0=ALU.mult),
                 reads=[('ps', 2)], writes=[meanreg])
            nms, nmsreg = ring('fa', fa)
            P.op('dve', lambda e: e.scalar_tensor_tensor(out=nms[:, 0:n], in0=mean[:, 0:n], scalar=-1.0, in1=mean[:, 0:n],
                                                          op0=ALU.mult, op1=ALU.mult), reads=[meanreg], writes=[nmsreg])
            P.op('dve', lambda e: e.scalar_tensor_tensor(out=nms[:, 0:n], in0=PS[3][:, 0:n], scalar=1.0 / 128, in1=nms[:, 0:n],
                                                          op0=ALU.mult, op1=ALU.add), reads=[('ps', 3), nmsreg], writes=[nmsreg])
            rs, rsreg = ring('fb', fb)
            P.op('act', lambda e: e.activation(out=rs[:, 0:n], in_=nms[:, 0:n], func=AF.Ln, bias=epsc), reads=[nmsreg, RC], writes=[rsreg])
            P.op('act', lambda e: e.activation(out=rs[:, 0:n], in_=rs[:, 0:n], func=AF.Exp, scale=-0.5), reads=[rsreg], writes=[rsreg])
            cen, cenreg = ring('fa', fa)
            P.op('dve', lambda e: e.tensor_tensor(out=cen[:, 0:n], in0=OB[:, 0:n], in1=mean[:, 0:n], op=ALU.subtract),
                 reads=[('ps', OBk), meanreg], writes=[cenreg])
            P.op('dve', lambda e: e.scalar_tensor_tensor(out=cen[:, 0:n], in0=cen[:, 0:n], scalar=gnw, in1=rs[:, 0:n],
                                                          op0=ALU.mult, op1=ALU.mult),
                 reads=[cenreg, rsreg, RC], writes=[cenreg])
            gs, gsreg = gate[g]
            P.op('dve', lambda e: e.tensor_tensor(out=SL[:, u, s:s + n], in0=cen[:, 0:n], in1=gs[:, 0:n], op=ALU.mult),
                 reads=[cenreg, gsreg], writes=[('S', u, g)])

        pending = None
        for n in range(NCH):
            if n in (6, 10):
                emit_gate(grp(n) + 1)
            st = emit_ST(n)
            if pending is not None:
                emit_O(*pending)
            pending = (n, st)
        emit_O(*pending)
        while deferred:
            deferred.pop(0)()

    def retention_unit(l, h):
        u = h
        ks = 8 + 2 * (u % 2)
        vs = 9
        wqk, wqkreg = wload(wmix_d[l, u, 0], 2048)
        wvg, wvgreg = wload(wmix_d[l, u, 1], 2048)
        for g, (s, n) in enumerate(GROUPS):
            tabs_for[g] = tload(0, s, n)
            project_rot_group(0, wqk, wqkreg, u, CB_PERMR, g)
            project_rot_group(1, wqk, wqkreg, ks, CB_PERMR, g)
        rot_flush()
        project_v(wvg, wvgreg, vs)
        if DBG_MIX == 'B':
            return
        retention_attention(l, h, u, ks, vs, wvg, wvgreg)

    def diff_attention(l, u, ks, vs, om):
        neglam = lamv[:, 4 + l:5 + l]
        sw = cf[:, CF_PVEC + PV_SW + l:CF_PVEC + PV_SW + l + 1]
        blocks = []
        for g, (s, n) in enumerate(GROUPS):
            last = (s + n) // 128 - 1
            for m in range(2):
                for jb in range(last + 1):
                    blocks.append((g, m, jb, last))
        odset = {}
        tres = {}
        deferred = []
        post = []

        def emit_S(blk):
            g, m, jb, last = blk
            s, n = GROUPS[g]
            off = max(0, jb * 128 - s)
            nq = n - off
            diag = jb * 128 >= s
            sk = rr('dst', 3)
            SP_, sreg = SBANK[sk]
            kg = 0 if jb == 0 else 1 + (jb - 1) // 4
            P.op('pe', lambda e: e.matmul(SP_[:, 0:nq], lhsT=SL[:, ks[m], jb * 128:(jb + 1) * 128],
                                          rhs=SL[:, u, s + off:s + n], start=True, stop=(not diag)),
                 reads=[('S', ks[m], kg), ('S', u, g)], writes=[sreg])
            if diag:
                P.op('pe', lambda e: e.matmul(SP_[:, 0:128], lhsT=ident, rhs=cmask, start=False, stop=True),
                     reads=[RC], writes=[sreg])
            return sk, off, nq, kg

        def emit_rest(blk, sinfo):
            g, m, jb, last = blk
            sk, off, nq, kg = sinfo
            SP_, sreg = SBANK[sk]
            if (g, m) not in odset:
                odset[(g, m)] = 2 + 2 * rr('dod', 2)
            OBk = odset[(g, m)]
            DBk = OBk + 1
            OB, DB = PS[OBk], PS[DBk]
            pt, ptreg = ring('ba', ba)
            P.op('act', lambda e: e.activation(out=pt[:, 0:nq], in_=SP_[:, 0:nq], func=AF.Exp, scale=0.125),
                 reads=[sreg], writes=[ptreg])
            P.op('pe', lambda e: e.matmul(OB[:, off:off + nq], lhsT=SL[:, vs, jb * 128:(jb + 1) * 128], rhs=pt[:, 0:nq],
                                          start=(jb == 0), stop=(jb == last)),
                 reads=[('S', vs, kg), ptreg], writes=[('ps', OBk)])
            P.op('pe', lambda e: e.matmul(DB[:, off:off + nq], lhsT=(ones_pad if jb == 0 else ones), rhs=pt[:, 0:nq],
                                          start=(jb == 0), stop=(jb == last)),
                 reads=[RC, ptreg], writes=[('ps', DBk)])
            if jb == last:
                post.append([2, lambda: normalize(g, m, OBk)])

        def normalize(g, m, OBk):
            DBk = OBk + 1
            OB, DB = PS[OBk], PS[DBk]
            s, n = GROUPS[g]
            o0 = PAD if g == 0 else 0
            r, rreg = ring('fb', fb)
            P.op('act', lambda e: e.activation(out=r[:, o0:n], in_=DB[:, o0:n], func=AF.Ln), reads=[('ps', DBk)], writes=[rreg])
            P.op('act', lambda e: e.activation(out=r[:, o0:n], in_=r[:, o0:n], func=AF.Exp, scale=-1.0), reads=[rreg], writes=[rreg])
            t, treg_ = ring('fa', fa)
            P.op('dve', lambda e: e.tensor_tensor(out=t[:, o0:n], in0=OB[:, o0:n], in1=r[:, o0:n], op=ALU.mult),
                 reads=[('ps', OBk), rreg], writes=[treg_])
            tres[(g, m)] = (t, treg_)
            if m == 0:
                while deferred:
                    deferred.pop(0)()
            else:
                deferred.append(lambda g=g: finalize(g))

        def tick(force=False):
            for ent in list(post):
                ent[0] -= 1
                if force or ent[0] <= 0:
                    post.remove(ent)
                    ent[1]()

        def finalize(g):
            s, n = GROUPS[g]
            o0 = PAD if g == 0 else 0
            (t0, t0reg), (t1, t1reg) = tres[(g, 0)], tres[(g, 1)]
            o, oreg = ring('fa', fa)
            P.op('dve', lambda e: e.scalar_tensor_tensor(out=o[:, o0:n], in0=t1[:, o0:n], scalar=neglam, in1=t0[:, o0:n],
                                                          op0=ALU.mult, op1=ALU.add),
                 reads=[t0reg, t1reg, ('lamv',)], writes=[oreg])
            rs, rsreg = stats_rstd(o, oreg, n, 1.0 / (128 * om * om), epsl[l], o0)
            P.op('dve', lambda e: e.scalar_tensor_tensor(out=SL[:, u, s + o0:s + n], in0=o[:, o0:n], scalar=sw, in1=rs[:, o0:n],
                                                          op0=ALU.mult, op1=ALU.mult),
                 reads=[oreg, rsreg, RC], writes=[('S', u, g)])

        for _ in range(WARM):
            P.op('pe', lambda e: e.matmul(PS[6][:, 0:512], lhsT=ones, rhs=cb[:, 0:512], start=True, stop=True),
                 reads=[RC], writes=[('ps', 6)])
        pend = []
        for blk in blocks:
            sinfo = emit_S(blk)
            pend.append((blk, sinfo))
            if len(pend) > 2:
                emit_rest(*pend.pop(0))
                tick()
        while pend:
            emit_rest(*pend.pop(0))
            tick()
        while post:
            tick(force=True)
        while deferred:
            deferred.pop(0)()

    def diff_unit(l, h, lam_init):
        u = 4 + h
        ks = (8, 10)
        vs = 9
        wqk, wqkreg = wload(wmix_d[l, u, 0], 2048)
        wv, wvreg = wload(wmix_d[l, u, 1, :, 0:1024], 1024)
        if h == 0:
            P.op('dve', lambda e: e.memset(SL[64:128, ks[0], :], 0.0), writes=[('S', ks[0], g) for g in range(5)])
            P.op('dve', lambda e: e.memset(SL[0:64, ks[1], :], 0.0), writes=[('S', ks[1], g) for g in range(5)])
        for g, (s, n) in enumerate(GROUPS):
            tabs_for[g] = tload(1, s, n)
            project_rot_group(0, wqk, wqkreg, u, CB_PERMD, g)
            project_rot_group(1, wqk, wqkreg, None, CB_PERMD, g, split=ks)
        rot_flush()
        project_v(wv, wvreg, vs)
        diff_attention(l, u, ks, vs, 1.0 - lam_init)

    def lam_compute(l):
        lam_init = 0.8 - 0.6 * math.exp(-0.3 * l)
        for t in range(2):
            a = lamin[:, (l * 4 + 2 * t) * 64:(l * 4 + 2 * t + 1) * 64]
            b = lamin[:, (l * 4 + 2 * t + 1) * 64:(l * 4 + 2 * t + 2) * 64]
            P.op('dve', lambda e, a=a, b=b: e.tensor_tensor(out=lamw[:, :], in0=a, in1=b, op=ALU.mult),
                 reads=[RC, ('lamw',)], writes=[('lamw',)])
            P.op('dve', lambda e, t=t: e.tensor_reduce(out=lamv[:, t:t + 1], in_=lamw[:, :], axis=mybir.AxisListType.X, op=ALU.add),
                 reads=[('lamw',), ('lamv',)], writes=[('lamv',)])
            P.op('act', lambda e, t=t: e.activation(out=lamv[:, 2 + t:3 + t], in_=lamv[:, t:t + 1], func=AF.Exp),
                 reads=[('lamv',)], writes=[('lamv',)])
        P.op('dve', lambda e: e.scalar_tensor_tensor(out=lamv[:, 4 + l:5 + l], in0=lamv[:, 3:4], scalar=-lam_init, in1=lamv[:, 2:3],
                                                      op0=ALU.add, op1=ALU.subtract),
             reads=[('lamv',)], writes=[('lamv',)])
        return lam_init

    def mixer(l):
        rmsnorm(PV_NORM + (l * 3 + 1) * 8)
        lam_init = lam_compute(l)
        if DBG_MIX == 'A':
            return
        for h in range(4):
            retention_unit(l, h)
            if DBG_MIX in ('B', 'B1', 'C', 'R1', 'R2', 'R3'):
                return
        if DBG_MIX == 'D':
            return
        for h in range(4):
            diff_unit(l, h, lam_init)
            if DBG_MIX == 'F':
                return
        for c in range(8):
            wt, wreg = wload(wout_d[l, c], 1024)
            for g, (s, n) in enumerate(GROUPS):
                mm_out_group(wt, wreg, 8, c, g, s, n, 1.0)

    stages = []
    for l in range(2):
        stages += [('ffn', l, 0), ('mix', l), ('ffn', l, 1)]
    nst = len(stages) if stop is None else stop
    for st in stages[:nst]:
        if st[0] == 'ffn':
            ffn(st[1], st[2])
        else:
            mixer(st[1])
    if stop is None:
        rmsnorm(PV_NORM + 48, final=True)

    outregs = []
    for c in range(8):
        P.dma('sp', lambda e, c=c: e.dma_start(out=out_d[:, c, :], in_=hT[:, c, 128:TP]), 'out',
              reads=[('h', c, g) for g in range(1, 5)], writes=[('out', c)])
        outregs.append(('out', c))
    if dump == 'h':
        for c in range(8):
            P.dma('sp', lambda e, c=c: e.dma_start(out=dbg_d[:, c, :], in_=hT[:, c, :]), 'out',
                  reads=[('h', c, g) for g in range(5)], writes=[('dbg', c)])
            outregs.append(('dbg', c))
    P.op('sp', None, reads=outregs)
    P.build()
    return nc, P


def _bf(a):
    return np.asarray(a, dtype=np.float32).astype(ml_dtypes.bfloat16)


def _const_tables():
    pos = (np.arange(TP, dtype=np.float32) - np.float32(PAD))
    angle = (np.float32(10000.0) ** (-np.linspace(0.0, 1.0, 64, dtype=np.float32))).astype(np.float32)
    fr = (pos[None, :] * np.repeat(angle, 2)[:, None]).astype(np.float32)
    cosr = np.cos(fr).astype(np.float32)
    sgn = np.where(np.arange(128) % 2 == 0, -1.0, 1.0).astype(np.float32)[:, None]
    sinr = (np.sin(fr) * sgn).astype(np.float32)
    r = 16
    inv = (np.float32(500000.0) ** (-np.arange(0, r, 2, dtype=np.float32) / r)).astype(np.float32)
    cosd = np.ones((128, TP), np.float32)
    sind = np.zeros((128, TP), np.float32)
    for p in range(128):
        dd = p % 64
        if dd < r:
            f = (pos * inv[dd % 8]).astype(np.float32)
            cosd[p] = np.cos(f)
            sind[p] = np.sin(f) * (-1.0 if dd < 8 else 1.0)
    tabs = np.stack([cosr, sinr, cosd, sind]).astype(np.float32)
    cbm = np.zeros((128, CB_N), np.float32)
    cbm[:, CB_ONES:CB_ONES + 128] = 1.0
    cbm[PAD:, CB_ONESPAD:CB_ONESPAD + 128] = 1.0
    cbm[:, CB_IDENT:CB_IDENT + 128] = np.eye(128)
    for m in range(128):
        src = m + 1 if m % 2 == 0 else m - 1
        cbm[src, CB_PERMR + m] = 1.0
        dd = m % 64
        if dd < 8:
            cbm[m + 8, CB_PERMD + m] = 1.0
        elif dd < 16:
            cbm[m - 8, CB_PERMD + m] = 1.0
    jj = np.arange(128)[:, None]
    ii = np.arange(128)[None, :]
    cbm[:, CB_CMASK:CB_CMASK + 128] = np.where(jj <= ii, 0.0, -30000.0)
    cfm = np.zeros((128, CF_N), np.float32)
    cfm[:, CF_EPS] = EPS
    cfm[:, CF_ONE] = 1.0
    for l_ in range(2):
        om_ = 1.0 - (0.8 - 0.6 * math.exp(-0.3 * l_))
        cfm[:, CF_EPS + 1 + l_] = EPS / (om_ * om_)
    for h in range(4):
        lg = math.log(1.0 - 2.0 ** (-5.0 - h))
        rel = (ii - jj).astype(np.float64)
        dm = np.where(rel >= 0, np.exp(lg * np.maximum(rel, 0.0)), 0.0) * (128.0 ** -0.5)
        cfm[:, CF_DMASK + h * 128:CF_DMASK + (h + 1) * 128] = dm
        cfm[:, CF_QDEC + h * 128:CF_QDEC + (h + 1) * 128] = np.exp(lg * (np.arange(128) + 1.0))[None, :]
        cfm[:, CF_KDEC + h] = np.exp(lg * (127.0 - np.arange(128))) * (128.0 ** -0.5)
    return tabs, cbm, cfm


def _prep(inputs):
    f = lambda k: np.asarray(inputs[k], dtype=np.float32)
    tabs, cbm, cfm = _const_tables()
    cvec = lambda v: np.ascontiguousarray(v.reshape(-1, 128).T)
    norms = [f("ffn1_norm"), f("mix_norm"), f("ffn2_norm")]
    for l in range(2):
        for w in range(3):
            c0 = CF_PVEC + PV_NORM + (l * 3 + w) * 8
            cfm[:, c0:c0 + 8] = cvec(norms[w][l])
        cfm[:, CF_PVEC + PV_GN + l * 4:CF_PVEC + PV_GN + l * 4 + 4] = cvec(f("ret_gn_w")[l])
        cfm[:, CF_PVEC + PV_SW + l] = f("diff_subln_w")[l]
    cfm[:, CF_PVEC + PV_NORM + 48:CF_PVEC + PV_NORM + 56] = cvec(f("final_norm"))
    lam = np.stack([f("diff_lambda_q1"), f("diff_lambda_k1"), f("diff_lambda_q2"), f("diff_lambda_k2")], axis=1)
    lam = np.ascontiguousarray(np.broadcast_to(lam.reshape(1, 512), (128, 512)))

    def fm_tile(w):
        nc_ = w.shape[1] // 128
        return w.reshape(8, 128, nc_, 128).transpose(2, 1, 0, 3)

    wgu = np.empty((2, 2, NF, 128, 2, 8, 128), np.float32)
    wd = np.empty((2, 2, 2, 8, 128, NFH, 128), np.float32)
    for l in range(2):
        for j, (kg, ku, kd_) in enumerate([("ffn1_w_gate", "ffn1_w_up", "ffn1_w_down"), ("ffn2_w_gate", "ffn2_w_up", "ffn2_w_down")]):
            wgu[l, j, :, :, 0] = fm_tile(f(kg)[l])
            wgu[l, j, :, :, 1] = fm_tile(f(ku)[l])
            wd[l, j] = f(kd_)[l].reshape(2, NFH, 128, 8, 128).transpose(0, 3, 2, 1, 4)
    win = f("w_in")
    wmix = np.zeros((2, 8, 2, 128, 2, 8, 128), np.float32)
    for l in range(2):
        t = fm_tile(win[l])
        for h in range(4):
            wmix[l, h, 0, :, 0] = t[h]
            wmix[l, h, 0, :, 1] = t[4 + h]
            wmix[l, h, 1, :, 0] = t[8 + h]
            wmix[l, h, 1, :, 1] = t[12 + h]
            wmix[l, 4 + h, 0, :, 0] = t[16 + h]
            wmix[l, 4 + h, 0, :, 1] = t[20 + h]
            wmix[l, 4 + h, 1, :, 0] = t[24 + h]
    wout = np.stack([fm_tile(f("w_out")[l]) for l in range(2)])
    shared = {
        "metaT": np.ascontiguousarray(f("meta_tokens").reshape(NMETA, 8, 128).transpose(2, 1, 0)),
        "cb": _bf(cbm), "cf": cfm, "lam": lam, "tabs": tabs,
        "wgu": wgu.reshape(2, 2, NF, 128, 2048), "wd": wd.reshape(2, 2, 2, 8, 128, NFH * 128),
        "wmix": wmix.reshape(2, 8, 2, 128, 2048), "wout": np.ascontiguousarray(wout.reshape(2, 8, 128, 1024)),
    }
    x = f("x")
    xTs = [np.ascontiguousarray(x[b].reshape(SEQ, 8, 128).transpose(2, 1, 0)) for b in range(x.shape[0])]
    return shared, xTs


def run(inputs, cores=None, stop=None, dump=None, trace=False):
    shared, xTs = _prep(inputs)
    cores = list(range(8)) if cores is None else cores
    nc, P = build_program(stop=stop, dump=dump)
    in_maps = [dict(shared, xT=xTs[b]) for b in cores]
    res = run_bass_kernel_spmd(nc, in_maps, core_ids=list(range(len(cores))), trace=trace)
    return res, P


def kernel(**inputs):
    res, _ = run(inputs)
    outs = [np.asarray(r["outT"], dtype=np.float32) for r in res.results]
    out = np.stack([o.transpose(2, 1, 0).reshape(SEQ, D) for o in outs])
    return np.ascontiguousarray(out.astype(np.float32))
```

```python
import math
import numpy as np
import ml_dtypes
import concourse.bass as bass
import concourse.mybir as mybir
from concourse.bass_utils import run_bass_kernel_spmd

F32 = mybir.dt.float32
BF16 = mybir.dt.bfloat16
AF = mybir.ActivationFunctionType
ALU = mybir.AluOpType

D = 1024
SEQ = 2048
NMETA = 16
PAD = 112
TP = 2176
NCH = 17
DFF = 2816
NF = 22
NFH = 11
EPS = 1e-6
GROUPS = [(0, 128)] + [(128 + 512 * g, 512) for g in range(4)]
NSLOT = 11
NW = 4
WCOLS = 2048
SAME_ENGINE_SYNC = True
DBG_HACK = 0
WARM = 0
ROT_ADD_ENG = 'dve'
DBG_MIX = None

CB_ONES, CB_ONESPAD, CB_IDENT, CB_PERMR, CB_PERMD, CB_CMASK, CB_CENT = [i * 128 for i in range(7)]
CB_N = 7 * 128
CF_DMASK = 0
CF_QDEC = 512
CF_KDEC = 1024
CF_PVEC = 1028
PV_NORM = 0
PV_GN = 56
PV_SW = 64
CF_EPS = CF_PVEC + 66
CF_ONE = CF_EPS + 3
CF_N = CF_ONE + 1


class Prog:
    def __init__(self, nc):
        self.nc = nc
        self.ops = []
        self.dma_sems = {}

    def op(self, eng, fn, reads=(), writes=()):
        self.ops.append(dict(eng=eng, fn=fn, reads=list(reads), writes=list(writes), dma=None))

    def dma(self, eng, fn, slot, reads=(), writes=()):
        self.ops.append(dict(eng=eng, fn=fn, reads=list(reads), writes=list(writes), dma=slot))

    def build(self):
        nc = self.nc
        ops = self.ops
        engs = ['pe', 'act', 'dve', 'pool', 'sp']
        last_w = {}
        readers = {}
        for i, o in enumerate(ops):
            deps = set()
            key = ('dma', i) if o['dma'] is not None else o['eng']
            for r in o['reads']:
                if r in last_w:
                    deps.add(last_w[r])
                if r[0] in ('ps', 'pt'):
                    for k2, i2 in readers.get(r, {}).items():
                        if k2 != key:
                            deps.add(i2)
            for w in o['writes']:
                if w in last_w:
                    deps.add(last_w[w])
                rd = readers.get(w)
                if rd:
                    deps.update(rd.values())
            deps.discard(i)
            o['deps'] = deps
            for r in o['reads']:
                readers.setdefault(r, {})[key] = i
            for w in o['writes']:
                last_w[w] = i
                readers[w] = {}
        signal = set()
        for i, o in enumerate(ops):
            for d in o['deps']:
                p = ops[d]
                if p['dma'] is not None:
                    continue
                if p['eng'] != o['eng'] or (SAME_ENGINE_SYNC and o['eng'] != 'pe'):
                    signal.add(d)
        sem = {e: nc.alloc_semaphore('s_' + e) for e in engs}
        cnt = {e: 0 for e in engs}
        for i, o in enumerate(ops):
            if o['dma'] is not None:
                slot = o['dma']
                if slot not in self.dma_sems:
                    self.dma_sems[slot] = [nc.alloc_semaphore('d_' + str(slot)), 0]
                ent = self.dma_sems[slot]
                ent[1] += 16
                o['tok'] = (ent[0], ent[1])
            elif i in signal:
                cnt[o['eng']] += 1
                o['tok'] = (sem[o['eng']], cnt[o['eng']])
            else:
                o['tok'] = None
        waited = {e: {} for e in engs}
        for i, o in enumerate(ops):
            need = {}
            for d in o['deps']:
                p = ops[d]
                if p['dma'] is None and p['eng'] == o['eng'] and not (SAME_ENGINE_SYNC and o['eng'] != 'pe'):
                    continue
                tok = p['tok']
                assert tok is not None
                k = tok[0]
                if need.get(k, (None, 0))[1] < tok[1]:
                    need[k] = tok
            ws = []
            for k, tok in need.items():
                if waited[o['eng']].get(k, 0) < tok[1]:
                    waited[o['eng']][k] = tok[1]
                    ws.append(tok)
            o['waits'] = ws
        per = {e: [o for o in ops if o['eng'] == e] for e in engs}
        self.stats = {e: len(per[e]) for e in engs}
        self.stats['signals'] = dict(cnt)

        def run(e, lst):
            for o in lst:
                for (s, v) in o['waits']:
                    e.wait_ge(s, v)
                if o['fn'] is None:
                    continue
                ins = o['fn'](e)
                if o['tok'] is not None:
                    ins.then_inc(o['tok'][0], 16 if o['dma'] is not None else 1)

        with nc.Block() as block:
            @block.tensor
            def _(e):
                run(e, per['pe'])

            @block.scalar
            def _(e):
                run(e, per['act'])

            @block.vector
            def _(e):
                run(e, per['dve'])

            @block.gpsimd
            def _(e):
                run(e, per['pool'])

            @block.sync
            def _(e):
                run(e, per['sp'])


def build_program(stop=None, dump=None):
    nc = bass.Bass("TRN2", target_bir_lowering=False)
    P = Prog(nc)

    def din(name, shape, dt=F32):
        return nc.dram_tensor(name, list(shape), dt, kind="ExternalInput").ap()

    xT_d = din("xT", [128, 8, SEQ])
    meta_d = din("metaT", [128, 8, NMETA])
    cb_d = din("cb", [128, CB_N], BF16)
    cf_d = din("cf", [128, CF_N])
    lam_d = din("lam", [128, 2 * 4 * 64])
    tabs_d = din("tabs", [4, 128, TP])
    wgu_d = din("wgu", [2, 2, NF, 128, 2048])
    wd_d = din("wd", [2, 2, 2, 8, 128, NFH * 128])
    wmix_d = din("wmix", [2, 8, 2, 128, 2048])
    wout_d = din("wout", [2, 8, 128, 1024])
    out_d = nc.dram_tensor("outT", [128, 8, SEQ], F32, kind="ExternalOutput").ap()
    dbg_d = None
    if dump is not None:
        dbg_d = nc.dram_tensor("dbg", [128, 8, TP], F32, kind="ExternalOutput").ap()

    sb = nc.alloc_sbuf_tensor
    hT = sb("hT", [128, 8, TP], F32)
    xnT = sb("xnT", [128, 8, TP], BF16)
    SL = sb("slots", [128, NSLOT, TP], BF16)
    wsl = [sb(f"wsl{i}", [128, WCOLS], BF16) for i in range(NW)]
    tabt = [sb(f"tab{i}", [128, 2, 512], F32) for i in range(2)]
    cb = sb("cbs", [128, CB_N], BF16)
    cf = sb("cfs", [128, CF_N], F32)
    lamin = sb("lamin", [128, 512], F32)
    lamw = sb("lamw", [128, 64], F32)
    lamv = sb("lamv", [128, 8], F32)
    NR = 3
    sqr = [sb(f"sq{i}", [128, 512], BF16) for i in range(NR)]
    fa = [sb(f"fa{i}", [128, 512], F32) for i in range(4)]
    fb = [sb(f"fb{i}", [128, 512], F32) for i in range(4)]
    ba = [sb(f"ba{i}", [128, 512], BF16) for i in range(4)]
    sm = [sb(f"sm{i}", [128, 128], BF16) for i in range(6)]
    Rf = [sb(f"Rf{i}", [128, 128], F32) for i in range(2)]

    PS = [nc.alloc_psum_tensor(f"ps{i}", [128, 512], F32) for i in range(7)]
    PT = nc.alloc_psum_tensor("pst", [128, 1024], BF16)

    SBANK = [(PS[0], ('ps', 0)), (PS[1], ('ps', 1)), (PT[:, :].bitcast(F32), ('pt',))]
    ctr = {}

    def rr(name, n):
        v = ctr.get(name, 0)
        ctr[name] = v + 1
        return v % n

    def ring(name, lst):
        i = rr(name, len(lst))
        return lst[i], (name, i)

    ones = cb[:, CB_ONES:CB_ONES + 128]
    ones_pad = cb[:, CB_ONESPAD:CB_ONESPAD + 128]
    ident = cb[:, CB_IDENT:CB_IDENT + 128]
    cmask = cb[:, CB_CMASK:CB_CMASK + 128]
    RC = ('const',)
    epsc = cf[:, CF_EPS:CF_EPS + 1]
    onec = cf[:, CF_ONE:CF_ONE + 1]
    epsl = [cf[:, CF_EPS + 1 + l_:CF_EPS + 2 + l_] for l_ in range(2)]

    P.dma('sp', lambda e: e.dma_start(out=cb[:, :], in_=cb_d), 'const', writes=[RC])
    P.dma('sp', lambda e: e.dma_start(out=cf[:, :], in_=cf_d), 'const', writes=[RC])
    P.dma('sp', lambda e: e.dma_start(out=lamin[:, :], in_=lam_d), 'const', writes=[RC])
    hregs = lambda c: [('h', c, g) for g in range(5)]
    P.op('dve', lambda e: e.memset(hT[:, :, 0:PAD], 0.0), writes=[('h', c, 0) for c in range(8)])
    for c in range(8):
        P.dma('sp', lambda e, c=c: e.dma_start(out=hT[:, c, 128:TP], in_=xT_d[:, c, :]), 'xin',
              writes=[('h', c, g) for g in range(1, 5)])
    P.dma('sp', lambda e: e.dma_start(out=hT[:, :, PAD:128], in_=meta_d), 'xin',
          reads=[('h', c, 0) for c in range(8)], writes=[('h', c, 0) for c in range(8)])

    def wload(src_ap, ncols):
        i = rr('w', NW)
        t = wsl[i]
        reg = ('w', i)
        P.dma('pool', lambda e: e.dma_start(out=t[:, 0:ncols], in_=src_ap, max_dma_last_dim=8192),
              ('w', i), writes=[reg])
        return t, reg

    def tload(k, s, n):
        i = rr('tab', 2)
        t = tabt[i]
        regs = [('tab', i, 0), ('tab', i, 1)]
        P.dma('sp', lambda e: e.dma_start(out=t[:, 0, 0:n], in_=tabs_d[2 * k, :, s:s + n]), ('tab', i, 0), writes=[regs[0]])
        P.dma('sp', lambda e: e.dma_start(out=t[:, 1, 0:n], in_=tabs_d[2 * k + 1, :, s:s + n]), ('tab', i, 1), writes=[regs[1]])
        return t, regs

    def rmsnorm_group(pvcol, final, g, s, n):
        ss = PS[6]
        for c in range(8):
            sq, sqreg = ring('sq', sqr)
            P.op('act', lambda e, sq=sq, c=c: e.activation(out=sq[:, 0:n], in_=hT[:, c, s:s + n], func=AF.Square),
                 reads=[('h', c, g)], writes=[sqreg])
            P.op('pe', lambda e, sq=sq, c=c: e.matmul(ss[:, 0:n], lhsT=ones, rhs=sq[:, 0:n], start=(c == 0), stop=(c == 7)),
                 reads=[sqreg, RC], writes=[('ps', 6)])
        rt, rtreg = ring('fa', fa)
        P.op('act', lambda e: e.activation(out=rt[:, 0:n], in_=ss[:, 0:n], func=AF.Ln, scale=1.0 / D, bias=epsc),
             reads=[('ps', 6), RC], writes=[rtreg])
        rs, rsreg = ring('fb', fb)
        P.op('act', lambda e: e.activation(out=rs[:, 0:n], in_=rt[:, 0:n], func=AF.Exp, scale=-0.5), reads=[rtreg], writes=[rsreg])
        for c in range(8):
            gcol = cf[:, CF_PVEC + pvcol + c:CF_PVEC + pvcol + c + 1]
            dst = hT if final else xnT
            wreg_ = ('h', c, g) if final else ('xn', c, g)
            P.op('dve', lambda e, c=c, gcol=gcol, dst=dst: e.scalar_tensor_tensor(
                out=dst[:, c, s:s + n], in0=hT[:, c, s:s + n], scalar=gcol, in1=rs[:, 0:n], op0=ALU.mult, op1=ALU.mult),
                reads=[('h', c, g), rsreg, RC], writes=[wreg_])

    def rmsnorm(pvcol, final=False):
        for g, (s, n) in enumerate(GROUPS):
            rmsnorm_group(pvcol, final, g, s, n)

    NM = 128 - PAD

    def resid_add(ps, psreg, c, g, scale):
        s, n = GROUPS[g]
        if g == 0:
            s, n = PAD, NM
        P.op('dve', lambda e: e.scalar_tensor_tensor(out=hT[:, c, s:s + n], in0=ps[:, 0:n], scalar=scale,
                                                      in1=hT[:, c, s:s + n], op0=ALU.mult, op1=ALU.add),
             reads=[psreg, ('h', c, g)], writes=[('h', c, g)])

    def ffn_gu_group(wt, wreg, fi, g, s, n):
        if g == 0:
            s, n = PAD, NM
        k = rr('ffn_gu', 2)
        gp, up = PS[k], PS[2 + k]
        for kc in range(8):
            P.op('pe', lambda e, kc=kc: e.matmul(gp[:, 0:n], lhsT=wt[:, kc * 128:(kc + 1) * 128],
                                               rhs=xnT[:, kc, s:s + n], start=(kc == 0), stop=(kc == 7)),
                 reads=[wreg, ('xn', kc, g)], writes=[('ps', k)])
        for kc in range(8):
            P.op('pe', lambda e, kc=kc: e.matmul(up[:, 0:n], lhsT=wt[:, 1024 + kc * 128:1024 + (kc + 1) * 128],
                                               rhs=xnT[:, kc, s:s + n], start=(kc == 0), stop=(kc == 7)),
                 reads=[wreg, ('xn', kc, g)], writes=[('ps', 2 + k)])
        sg, sgreg = ring('fa', fa)
        P.op('act', lambda e: e.activation(out=sg[:, 0:n], in_=gp[:, 0:n], func=AF.Silu),
             reads=[('ps', k)], writes=[sgreg])
        P.op('dve', lambda e: e.tensor_tensor(out=SL[:, fi, s:s + n], in0=up[:, 0:n], in1=sg[:, 0:n], op=ALU.mult),
             reads=[('ps', 2 + k), sgreg], writes=[('S', fi, g)])

    def mm_out_group(wt, wreg, nk, c, g, s, n, scale):
        if g == 0:
            s, n = PAD, NM
        k = 4 + rr('ffn_o', 2)
        op_ = PS[k]
        for fi in range(nk):
            P.op('pe', lambda e, fi=fi: e.matmul(op_[:, 0:n], lhsT=wt[:, fi * 128:(fi + 1) * 128],
                                               rhs=SL[:, fi, s:s + n], start=(fi == 0), stop=(fi == nk - 1)),
                 reads=[wreg, ('S', fi, g)], writes=[('ps', k)])
        resid_add(op_, ('ps', k), c, g, scale)

    def ffn(l, j):
        rmsnorm(PV_NORM + (l * 3 + (0 if j == 0 else 2)) * 8)
        for half in range(2):
            for fi in range(NFH):
                wt, wreg = wload(wgu_d[l, j, half * NFH + fi], 2048)
                for g, (s, n) in enumerate(GROUPS):
                    ffn_gu_group(wt, wreg, fi, g, s, n)
            for c in range(8):
                wt, wreg = wload(wd_d[l, j, half, c], NFH * 128)
                for g, (s, n) in enumerate(GROUPS):
                    mm_out_group(wt, wreg, NFH, c, g, s, n, 0.5)

    tabs_for = {}

    rot_pending = []

    def project_rot_group(which, wt, wreg, dst_slot, permcol, g, split=None):
        perm = cb[:, permcol:permcol + 128]
        s, n = GROUPS[g]
        tt, tregs = tabs_for[g]
        pk = rr('proj', 2)
        pp, sp_ = PS[pk], PS[2 + pk]
        for kc in range(8):
            P.op('pe', lambda e, kc=kc: e.matmul(pp[:, 0:n], lhsT=wt[:, which * 1024 + kc * 128:which * 1024 + (kc + 1) * 128],
                                               rhs=xnT[:, kc, s:s + n], start=(kc == 0), stop=(kc == 7)),
                 reads=[wreg, ('xn', kc, g)], writes=[('ps', pk)])
        qb, qbreg = ring('ba', ba)
        P.op('act', lambda e: e.activation(out=qb[:, 0:n], in_=pp[:, 0:n], func=AF.Identity), reads=[('ps', pk)], writes=[qbreg])
        rot_flush()

        def second():
            P.op('pe', lambda e: e.matmul(sp_[:, 0:n], lhsT=perm, rhs=qb[:, 0:n], start=True, stop=True),
                 reads=[qbreg, RC], writes=[('ps', 2 + pk)])
            t1, t1reg = ring('fa', fa)
            t2, t2reg = ring('fb', fb)
            P.op('dve', lambda e: e.tensor_tensor(out=t1[:, 0:n], in0=pp[:, 0:n], in1=tt[:, 0, 0:n], op=ALU.mult),
                 reads=[('ps', pk)] + tregs, writes=[t1reg])
            P.op('dve', lambda e: e.tensor_tensor(out=t2[:, 0:n], in0=sp_[:, 0:n], in1=tt[:, 1, 0:n], op=ALU.mult),
                 reads=[('ps', 2 + pk)] + tregs, writes=[t2reg])
            if split is None:
                P.op(ROT_ADD_ENG, lambda e: e.tensor_tensor(out=SL[:, dst_slot, s:s + n], in0=t1[:, 0:n], in1=t2[:, 0:n], op=ALU.add),
                     reads=[t1reg, t2reg], writes=[('S', dst_slot, g)])
            else:
                for m in range(2):
                    P.op('dve', lambda e, m=m: e.tensor_tensor(out=SL[64 * m:64 * m + 64, split[m], s:s + n], in0=t1[64 * m:64 * m + 64, 0:n],
                                                             in1=t2[64 * m:64 * m + 64, 0:n], op=ALU.add),
                         reads=[t1reg, t2reg], writes=[('S', split[m], g)])
        rot_pending.append(second)

    def rot_flush():
        while rot_pending:
            rot_pending.pop(0)()

    def project_v_group(wt, wreg, vslot, g, s, n):
        pk = rr('proj', 2)
        pp = PS[pk]
        for ci in range(n // 128):
            cs = s + ci * 128
            for kc in range(8):
                P.op('pe', lambda e, kc=kc, ci=ci, cs=cs: e.matmul(pp[:, ci * 128:(ci + 1) * 128], lhsT=xnT[:, kc, cs:cs + 128],
                                                                 rhs=wt[:, kc * 128:(kc + 1) * 128], start=(kc == 0), stop=(kc == 7)),
                     reads=[wreg, ('xn', kc, g)], writes=[('ps', pk)])
        P.op('act', lambda e: e.activation(out=SL[:, vslot, s:s + n], in_=pp[:, 0:n], func=AF.Identity),
             reads=[('ps', pk)], writes=[('S', vslot, g)])

    def project_v(wt, wreg, vslot):
        for g, (s, n) in enumerate(GROUPS):
            project_v_group(wt, wreg, vslot, g, s, n)

    def stats_rstd(src, srcreg, n, scale, bias_col, o0=0):
        sq, sqreg = ring('ba', ba)
        P.op('dve', lambda e: e.tensor_tensor(out=sq[:, o0:n], in0=src[:, o0:n], in1=src[:, o0:n], op=ALU.mult), reads=[srcreg], writes=[sqreg])
        P.op('pe', lambda e: e.matmul(PS[6][:, o0:n], lhsT=ones, rhs=sq[:, o0:n], start=True, stop=True),
             reads=[sqreg, RC], writes=[('ps', 6)])
        sd, sdreg = ring('fb', fb)
        P.op('act', lambda e: e.activation(out=sd[:, o0:n], in_=PS[6][:, o0:n], func=AF.Ln, scale=scale, bias=bias_col),
             reads=[('ps', 6), RC], writes=[sdreg])
        rs, rsreg = ring('fb', fb)
        P.op('act', lambda e: e.activation(out=rs[:, o0:n], in_=sd[:, o0:n], func=AF.Exp, scale=-0.5), reads=[sdreg], writes=[rsreg])
        return rs, rsreg

    def retention_attention(l, h, u, ks, vs, wvg, wvgreg):
        gam = 1.0 - 2.0 ** (-5.0 - h)
        cd = gam ** 128
        dmask = cf[:, CF_DMASK + h * 128:CF_DMASK + (h + 1) * 128]
        qdec = cf[:, CF_QDEC + h * 128:CF_QDEC + (h + 1) * 128]
        kdec = cf[:, CF_KDEC + h:CF_KDEC + h + 1]
        gnw = cf[:, CF_PVEC + PV_GN + l * 4 + h:CF_PVEC + PV_GN + l * 4 + h + 1]
        RBs = 18 - ks
        grp = lambda n: 0 if n == 0 else 1 + (n - 1) // 4
        P.op('dve', lambda e: e.memset(SL[:, RBs, 0:128], 0.0), writes=[('S', RBs, 0)])
        P.op('dve', lambda e: e.memset(Rf[0][:, :], 0.0), writes=[('Rf', 0)])
        def emit_gate(g, bank=6):
            s, n = GROUPS[g]
            gp = PS[bank]
            for kc in range(8):
                P.op('pe', lambda e, kc=kc: e.matmul(gp[:, 0:n], lhsT=wvg[:, 1024 + kc * 128:1024 + (kc + 1) * 128],
                                                   rhs=xnT[:, kc, s:s + n], start=(kc == 0), stop=(kc == 7)),
                     reads=[wvgreg, ('xn', kc, g)], writes=[('ps', bank)])
            sg, sgreg = ring('fb', fb)
            P.op('act', lambda e: e.activation(out=sg[:, 0:n], in_=gp[:, 0:n], func=AF.Exp, scale=-1.0), reads=[('ps', bank)], writes=[sgreg])
            P.op('act', lambda e: e.activation(out=sg[:, 0:n], in_=sg[:, 0:n], func=AF.Ln, bias=onec), reads=[sgreg, RC], writes=[sgreg])
            P.op('act', lambda e: e.activation(out=sg[:, 0:n], in_=sg[:, 0:n], func=AF.Exp, scale=-1.0), reads=[sgreg], writes=[sgreg])
            gs, gsreg = ring('sq', sqr)
            P.op('dve', lambda e: e.tensor_tensor(out=gs[:, 0:n], in0=gp[:, 0:n], in1=sg[:, 0:n], op=ALU.mult),
                 reads=[('ps', bank), sgreg], writes=[gsreg])
            gate[g] = (gs, gsreg)

        for r in range(4):
            for i in range(4):
                n = 4 * r + i
                P.op('pe', lambda e, n=n, i=i: e.transpose(PT[:, i * 128:(i + 1) * 128], SL[:, ks, n * 128:(n + 1) * 128], ident),
                     reads=[('S', ks, grp(n)), RC], writes=[('pt',)])
            kd, kdreg = ring('ba', ba)
            P.op('act', lambda e, kd=kd: e.activation(out=kd[:, 0:512], in_=PT[:, 0:512], func=AF.Identity, scale=kdec),
                 reads=[('pt',), RC], writes=[kdreg])
            for i in range(4):
                n = 4 * r + i
                P.op('pe', lambda e, n=n, i=i, r=r, kd=kd: e.matmul(PS[r][:, i * 128:(i + 1) * 128], lhsT=kd[:, i * 128:(i + 1) * 128],
                                                                  rhs=SL[:, vs, n * 128:(n + 1) * 128], start=True, stop=True),
                     reads=[kdreg, ('S', vs, grp(n))], writes=[('ps', r)])
        gate = {}
        for n in range(16):
            if n in (0, 5, 10):
                emit_gate(n // 5, (6, 4, 5)[n // 5])
            r, i = divmod(n, 4)
            P.op('dve', lambda e, n=n, r=r, i=i: e.scalar_tensor_tensor(out=Rf[(n + 1) % 2][:, :], in0=Rf[n % 2][:, :], scalar=cd,
                                                                      in1=PS[r][:, i * 128:(i + 1) * 128], op0=ALU.mult, op1=ALU.add),
                 reads=[('Rf', n % 2), ('ps', r)], writes=[('Rf', (n + 1) % 2)])
            P.op('act', lambda e, n=n: e.activation(out=SL[:, RBs, (n + 1) * 128:(n + 2) * 128], in_=Rf[(n + 1) % 2][:, :], func=AF.Identity),
                 reads=[('Rf', (n + 1) % 2)], writes=[('S', RBs, grp(n + 1))])
        deferred = []

        def emit_ST(n):
            cs = n * 128
            g = grp(n)
            bk = n % 2
            P.op('pe', lambda e: e.matmul(PS[bk][:, 0:128], lhsT=SL[:, ks, cs:cs + 128], rhs=SL[:, u, cs:cs + 128],
                                          start=True, stop=True), reads=[('S', u, g), ('S', ks, g)], writes=[('ps', bk)])
            stm, stmreg = ring('sm', sm)
            P.op('dve', lambda e: e.tensor_tensor(out=stm[:, :], in0=PS[bk][:, 0:128], in1=dmask, op=ALU.mult),
                 reads=[('ps', bk), RC], writes=[stmreg])
            qd, qdreg = ring('sm', sm)
            P.op('dve', lambda e: e.tensor_tensor(out=qd[:, :], in0=SL[:, u, cs:cs + 128], in1=qdec, op=ALU.mult),
                 reads=[('S', u, g), RC], writes=[qdreg])
            return stm, stmreg, qd, qdreg

        def emit_O(n, st):
            stm, stmreg, qd, qdreg = st
            cs = n * 128
            g = grp(n)
            ci = 0 if n == 0 else (n - 1) % 4
            OBk = 4 + g % 2
            OB = PS[OBk]
            P.op('pe', lambda e: e.matmul(OB[:, ci * 128:(ci + 1) * 128], lhsT=SL[:, vs, cs:cs + 128], rhs=stm[:, :],
                                          start=True, stop=False), reads=[('S', vs, g), stmreg], writes=[('ps', OBk)])
            P.op('pe', lambda e: e.matmul(OB[:, ci * 128:(ci + 1) * 128], lhsT=SL[:, RBs, cs:cs + 128], rhs=qd[:, :],
                                          start=False, stop=True), reads=[('S', RBs, g), qdreg], writes=[('ps', OBk)])
            if n == 0 or ci == 3:
                while deferred:
                    deferred.pop(0)()
                deferred.append(lambda: finalize(g, OBk))

        def finalize(g, OBk):
            s, n = GROUPS[g]
            OB = PS[OBk]
            cent = cb[:, CB_CENT:CB_CENT + 128]
            ob, obreg = ring('ba', ba)
            P.op('act', lambda e: e.activation(out=ob[:, 0:n], in_=OB[:, 0:n], func=AF.Identity), reads=[('ps', OBk)], writes=[obreg])
            P.op('pe', lambda e: e.matmul(PS[2][:, 0:n], lhsT=cent, rhs=ob[:, 0:n], start=True, stop=True),
                 reads=[obreg, RC], writes=[('ps', 2)])
            csq, csqreg = ring('ba', ba)
            P.op('act', lambda e: e.activation(out=csq[:, 0:n], in_=PS[2][:, 0:n], func=AF.Square), reads=[('ps', 2)], writes=[csqreg])
            P.op('pe', lambda e: e.matmul(PS[3][:, 0:n], lhsT=ones, rhs=csq[:, 0:n], start=True, stop=True),
                 reads=[csqreg, RC], writes=[('ps', 3)])
            rs, rsreg = ring('fb', fb)
            P.op('act', lambda e: e.activation(out=rs[:, 0:n], in_=PS[3][:, 0:n], func=AF.Ln, scale=1.0 / 128, bias=epsc),
                 reads=[('ps', 3), RC], writes=[rsreg])
            P.op('act', lambda e: e.activation(out=rs[:, 0:n], in_=rs[:, 0:n], func=AF.Exp, scale=-0.5), reads=[rsreg], writes=[rsreg])
            nrm, nrmreg = ring('fa', fa)
            P.op('dve', lambda e: e.scalar_tensor_tensor(out=nrm[:, 0:n], in0=PS[2][:, 0:n], scalar=gnw, in1=rs[:, 0:n],
                                                          op0=ALU.mult, op1=ALU.mult),
                 reads=[('ps', 2), rsreg, RC], writes=[nrmreg])
            gs, gsreg = gate[g]
            P.op('dve', lambda e: e.tensor_tensor(out=SL[:, u, s:s + n], in0=nrm[:, 0:n], in1=gs[:, 0:n], op=ALU.mult),
                 reads=[nrmreg, gsreg], writes=[('S', u, g)])

        pending = None
        for n in range(NCH):
            if n in (6, 10):
                emit_gate(grp(n) + 1)
            st = emit_ST(n)
            if pending is not None:
                emit_O(*pending)
            pending = (n, st)
        emit_O(*pending)
        while deferred:
            deferred.pop(0)()

    def retention_unit(l, h):
        u = h
        ks = 8 + 2 * (u % 2)
        vs = 9
        wqk, wqkreg = wload(wmix_d[l, u, 0], 2048)
        wvg, wvgreg = wload(wmix_d[l, u, 1], 2048)
        for g, (s, n) in enumerate(GROUPS):
            tabs_for[g] = tload(0, s, n)
            project_rot_group(0, wqk, wqkreg, u, CB_PERMR, g)
            project_rot_group(1, wqk, wqkreg, ks, CB_PERMR, g)
        rot_flush()
        project_v(wvg, wvgreg, vs)
        if DBG_MIX == 'B':
            return
        retention_attention(l, h, u, ks, vs, wvg, wvgreg)

    def diff_attention(l, u, ks, vs, om):
        neglam = lamv[:, 4 + l:5 + l]
        sw = cf[:, CF_PVEC + PV_SW + l:CF_PVEC + PV_SW + l + 1]
        blocks = []
        for g, (s, n) in enumerate(GROUPS):
            last = (s + n) // 128 - 1
            for m in range(2):
                for jb in range(last + 1):
                    blocks.append((g, m, jb, last))
        odset = {}
        tres = {}
        deferred = []
        post = []

        def emit_S(blk):
            g, m, jb, last = blk
            s, n = GROUPS[g]
            off = max(0, jb * 128 - s)
            nq = n - off
            diag = jb * 128 >= s
            sk = rr('dst', 3)
            SP_, sreg = SBANK[sk]
            kg = 0 if jb == 0 else 1 + (jb - 1) // 4
            P.op('pe', lambda e: e.matmul(SP_[:, 0:nq], lhsT=SL[:, ks[m], jb * 128:(jb + 1) * 128],
                                          rhs=SL[:, u, s + off:s + n], start=True, stop=(not diag)),
                 reads=[('S', ks[m], kg), ('S', u, g)], writes=[sreg])
            if diag:
                P.op('pe', lambda e: e.matmul(SP_[:, 0:128], lhsT=ident, rhs=cmask, start=False, stop=True),
                     reads=[RC], writes=[sreg])
            return sk, off, nq, kg

        def emit_rest(blk, sinfo):
            g, m, jb, last = blk
            sk, off, nq, kg = sinfo
            SP_, sreg = SBANK[sk]
            if (g, m) not in odset:
                odset[(g, m)] = 2 + 2 * rr('dod', 2)
                while any(ent[2] == odset[(g, m)] for ent in post):
                    for ent in list(post):
                        if ent[2] == odset[(g, m)]:
                            post.remove(ent)
                            ent[1]()
            OBk = odset[(g, m)]
            DBk = OBk + 1
            OB, DB = PS[OBk], PS[DBk]
            pt, ptreg = ring('ba', ba)
            P.op('act', lambda e: e.activation(out=pt[:, 0:nq], in_=SP_[:, 0:nq], func=AF.Exp, scale=0.125),
                 reads=[sreg], writes=[ptreg])
            P.op('pe', lambda e: e.matmul(OB[:, off:off + nq], lhsT=SL[:, vs, jb * 128:(jb + 1) * 128], rhs=pt[:, 0:nq],
                                          start=(jb == 0), stop=(jb == last)),
                 reads=[('S', vs, kg), ptreg], writes=[('ps', OBk)])
            P.op('pe', lambda e: e.matmul(DB[:, off:off + nq], lhsT=(ones_pad if jb == 0 else ones), rhs=pt[:, 0:nq],
                                          start=(jb == 0), stop=(jb == last)),
                 reads=[RC, ptreg], writes=[('ps', DBk)])
            if jb == last:
                post.append([2, lambda: normalize_act(g, m, OBk), OBk])

        def normalize_act(g, m, OBk):
            DBk = OBk + 1
            DB = PS[DBk]
            s, n = GROUPS[g]
            o0 = PAD if g == 0 else 0
            r, rreg = ring('fb', fb)
            P.op('act', lambda e: e.activation(out=r[:, o0:n], in_=DB[:, o0:n], func=AF.Ln), reads=[('ps', DBk)], writes=[rreg])
            P.op('act', lambda e: e.activation(out=r[:, o0:n], in_=r[:, o0:n], func=AF.Exp, scale=-1.0), reads=[rreg], writes=[rreg])
            post.append([3, lambda: normalize_dve(g, m, OBk, r, rreg), OBk])

        def normalize_dve(g, m, OBk, r, rreg):
            OB = PS[OBk]
            s, n = GROUPS[g]
            o0 = PAD if g == 0 else 0
            t, treg_ = ring('fa', fa)
            P.op('dve', lambda e: e.tensor_tensor(out=t[:, o0:n], in0=OB[:, o0:n], in1=r[:, o0:n], op=ALU.mult),
                 reads=[('ps', OBk), rreg], writes=[treg_])
            tres[(g, m)] = (t, treg_)
            if m == 1:
                post.append([1, lambda: finalize1(g), None])

        def finalize1(g):
            s, n = GROUPS[g]
            o0 = PAD if g == 0 else 0
            (t0, t0reg), (t1, t1reg) = tres[(g, 0)], tres[(g, 1)]
            o, oreg = ring('fa', fa)
            P.op('dve', lambda e: e.scalar_tensor_tensor(out=o[:, o0:n], in0=t1[:, o0:n], scalar=neglam, in1=t0[:, o0:n],
                                                          op0=ALU.mult, op1=ALU.add),
                 reads=[t0reg, t1reg, ('lamv',)], writes=[oreg])
            sq, sqreg = ring('ba', ba)
            P.op('dve', lambda e: e.tensor_tensor(out=sq[:, o0:n], in0=o[:, o0:n], in1=o[:, o0:n], op=ALU.mult), reads=[oreg], writes=[sqreg])
            P.op('pe', lambda e: e.matmul(PS[6][:, o0:n], lhsT=ones, rhs=sq[:, o0:n], start=True, stop=True),
                 reads=[sqreg, RC], writes=[('ps', 6)])
            post.append([4, lambda: finalize2(g, o, oreg), None])

        def finalize2(g, o, oreg):
            s, n = GROUPS[g]
            o0 = PAD if g == 0 else 0
            rs, rsreg = ring('fb', fb)
            P.op('act', lambda e: e.activation(out=rs[:, o0:n], in_=PS[6][:, o0:n], func=AF.Ln, scale=1.0 / (128 * om * om), bias=epsl[l]),
                 reads=[('ps', 6), RC], writes=[rsreg])
            P.op('act', lambda e: e.activation(out=rs[:, o0:n], in_=rs[:, o0:n], func=AF.Exp, scale=-0.5), reads=[rsreg], writes=[rsreg])
            P.op('dve', lambda e: e.scalar_tensor_tensor(out=SL[:, u, s + o0:s + n], in0=o[:, o0:n], scalar=sw, in1=rs[:, o0:n],
                                                          op0=ALU.mult, op1=ALU.mult),
                 reads=[oreg, rsreg, RC], writes=[('S', u, g)])

        def tick(force=False):
            for ent in list(post):
                ent[0] -= 1
                if force or ent[0] <= 0:
                    post.remove(ent)
                    ent[1]()

        pend = []
        for blk in blocks:
            sinfo = emit_S(blk)
            pend.append((blk, sinfo))
            if len(pend) > 2:
                emit_rest(*pend.pop(0))
                tick()
        while pend:
            emit_rest(*pend.pop(0))
            tick()
        while post:
            tick(force=True)

    def diff_unit(l, h, lam_init):
        u = 4 + h
        ks = (8, 10)
        vs = 9
        wqk, wqkreg = wload(wmix_d[l, u, 0], 2048)
        wv, wvreg = wload(wmix_d[l, u, 1, :, 0:1024], 1024)
        if h == 0:
            P.op('dve', lambda e: e.memset(SL[64:128, ks[0], :], 0.0), writes=[('S', ks[0], g) for g in range(5)])
            P.op('dve', lambda e: e.memset(SL[0:64, ks[1], :], 0.0), writes=[('S', ks[1], g) for g in range(5)])
        for g, (s, n) in enumerate(GROUPS):
            tabs_for[g] = tload(1, s, n)
            project_rot_group(0, wqk, wqkreg, u, CB_PERMD, g)
            project_rot_group(1, wqk, wqkreg, None, CB_PERMD, g, split=ks)
        rot_flush()
        project_v(wv, wvreg, vs)
        diff_attention(l, u, ks, vs, 1.0 - lam_init)

    def lam_compute(l):
        lam_init = 0.8 - 0.6 * math.exp(-0.3 * l)
        for t in range(2):
            a = lamin[:, (l * 4 + 2 * t) * 64:(l * 4 + 2 * t + 1) * 64]
            b = lamin[:, (l * 4 + 2 * t + 1) * 64:(l * 4 + 2 * t + 2) * 64]
            P.op('dve', lambda e, a=a, b=b: e.tensor_tensor(out=lamw[:, :], in0=a, in1=b, op=ALU.mult),
                 reads=[RC, ('lamw',)], writes=[('lamw',)])
            P.op('dve', lambda e, t=t: e.tensor_reduce(out=lamv[:, t:t + 1], in_=lamw[:, :], axis=mybir.AxisListType.X, op=ALU.add),
                 reads=[('lamw',), ('lamv',)], writes=[('lamv',)])
            P.op('act', lambda e, t=t: e.activation(out=lamv[:, 2 + t:3 + t], in_=lamv[:, t:t + 1], func=AF.Exp),
                 reads=[('lamv',)], writes=[('lamv',)])
        P.op('dve', lambda e: e.scalar_tensor_tensor(out=lamv[:, 4 + l:5 + l], in0=lamv[:, 3:4], scalar=-lam_init, in1=lamv[:, 2:3],
                                                      op0=ALU.add, op1=ALU.subtract),
             reads=[('lamv',)], writes=[('lamv',)])
        return lam_init

    def mixer(l):
        rmsnorm(PV_NORM + (l * 3 + 1) * 8)
        lam_init = lam_compute(l)
        if DBG_MIX == 'A':
            return
        for h in range(4):
            retention_unit(l, h)
            if DBG_MIX in ('B', 'B1', 'C', 'R1', 'R2', 'R3'):
                return
        if DBG_MIX == 'D':
            return
        for h in range(4):
            diff_unit(l, h, lam_init)
            if DBG_MIX == 'F':
                return
        for c in range(8):
            wt, wreg = wload(wout_d[l, c], 1024)
            for g, (s, n) in enumerate(GROUPS):
                mm_out_group(wt, wreg, 8, c, g, s, n, 1.0)

    stages = []
    for l in range(2):
        stages += [('ffn', l, 0), ('mix', l), ('ffn', l, 1)]
    nst = len(stages) if stop is None else stop
    for st in stages[:nst]:
        if st[0] == 'ffn':
            ffn(st[1], st[2])
        else:
            mixer(st[1])
    if stop is None:
        rmsnorm(PV_NORM + 48, final=True)

    outregs = []
    for c in range(8):
        P.dma('sp', lambda e, c=c: e.dma_start(out=out_d[:, c, :], in_=hT[:, c, 128:TP]), 'out',
              reads=[('h', c, g) for g in range(1, 5)], writes=[('out', c)])
        outregs.append(('out', c))
    if dump == 'h':
        for c in range(8):
            P.dma('sp', lambda e, c=c: e.dma_start(out=dbg_d[:, c, :], in_=hT[:, c, :]), 'out',
                  reads=[('h', c, g) for g in range(5)], writes=[('dbg', c)])
            outregs.append(('dbg', c))
    P.op('sp', None, reads=outregs)
    P.build()
    return nc, P


def _bf(a):
    return np.asarray(a, dtype=np.float32).astype(ml_dtypes.bfloat16)


def _const_tables():
    pos = (np.arange(TP, dtype=np.float32) - np.float32(PAD))
    angle = (np.float32(10000.0) ** (-np.linspace(0.0, 1.0, 64, dtype=np.float32))).astype(np.float32)
    fr = (pos[None, :] * np.repeat(angle, 2)[:, None]).astype(np.float32)
    cosr = np.cos(fr).astype(np.float32)
    sgn = np.where(np.arange(128) % 2 == 0, -1.0, 1.0).astype(np.float32)[:, None]
    sinr = (np.sin(fr) * sgn).astype(np.float32)
    r = 16
    inv = (np.float32(500000.0) ** (-np.arange(0, r, 2, dtype=np.float32) / r)).astype(np.float32)
    cosd = np.ones((128, TP), np.float32)
    sind = np.zeros((128, TP), np.float32)
    for p in range(128):
        dd = p % 64
        if dd < r:
            f = (pos * inv[dd % 8]).astype(np.float32)
            cosd[p] = np.cos(f)
            sind[p] = np.sin(f) * (-1.0 if dd < 8 else 1.0)
    tabs = np.stack([cosr, sinr, cosd, sind]).astype(np.float32)
    cbm = np.zeros((128, CB_N), np.float32)
    cbm[:, CB_ONES:CB_ONES + 128] = 1.0
    cbm[PAD:, CB_ONESPAD:CB_ONESPAD + 128] = 1.0
    cbm[:, CB_IDENT:CB_IDENT + 128] = np.eye(128)
    for m in range(128):
        src = m + 1 if m % 2 == 0 else m - 1
        cbm[src, CB_PERMR + m] = 1.0
        dd = m % 64
        if dd < 8:
            cbm[m + 8, CB_PERMD + m] = 1.0
        elif dd < 16:
            cbm[m - 8, CB_PERMD + m] = 1.0
    jj = np.arange(128)[:, None]
    ii = np.arange(128)[None, :]
    cbm[:, CB_CMASK:CB_CMASK + 128] = np.where(jj <= ii, 0.0, -30000.0)
    cbm[:, CB_CENT:CB_CENT + 128] = np.eye(128) - 1.0 / 128.0
    cfm = np.zeros((128, CF_N), np.float32)
    cfm[:, CF_EPS] = EPS
    cfm[:, CF_ONE] = 1.0
    for l_ in range(2):
        om_ = 1.0 - (0.8 - 0.6 * math.exp(-0.3 * l_))
        cfm[:, CF_EPS + 1 + l_] = EPS / (om_ * om_)
    for h in range(4):
        lg = math.log(1.0 - 2.0 ** (-5.0 - h))
        rel = (ii - jj).astype(np.float64)
        dm = np.where(rel >= 0, np.exp(lg * np.maximum(rel, 0.0)), 0.0) * (128.0 ** -0.5)
        cfm[:, CF_DMASK + h * 128:CF_DMASK + (h + 1) * 128] = dm
        cfm[:, CF_QDEC + h * 128:CF_QDEC + (h + 1) * 128] = np.exp(lg * (np.arange(128) + 1.0))[None, :]
        cfm[:, CF_KDEC + h] = np.exp(lg * (127.0 - np.arange(128))) * (128.0 ** -0.5)
    return tabs, cbm, cfm


def _prep(inputs):
    f = lambda k: np.asarray(inputs[k], dtype=np.float32)
    tabs, cbm, cfm = _const_tables()
    cvec = lambda v: np.ascontiguousarray(v.reshape(-1, 128).T)
    norms = [f("ffn1_norm"), f("mix_norm"), f("ffn2_norm")]
    for l in range(2):
        for w in range(3):
            c0 = CF_PVEC + PV_NORM + (l * 3 + w) * 8
            cfm[:, c0:c0 + 8] = cvec(norms[w][l])
        cfm[:, CF_PVEC + PV_GN + l * 4:CF_PVEC + PV_GN + l * 4 + 4] = cvec(f("ret_gn_w")[l])
        cfm[:, CF_PVEC + PV_SW + l] = f("diff_subln_w")[l]
    cfm[:, CF_PVEC + PV_NORM + 48:CF_PVEC + PV_NORM + 56] = cvec(f("final_norm"))
    lam = np.stack([f("diff_lambda_q1"), f("diff_lambda_k1"), f("diff_lambda_q2"), f("diff_lambda_k2")], axis=1)
    lam = np.ascontiguousarray(np.broadcast_to(lam.reshape(1, 512), (128, 512)))

    def fm_tile(w):
        nc_ = w.shape[1] // 128
        return w.reshape(8, 128, nc_, 128).transpose(2, 1, 0, 3)

    wgu = np.empty((2, 2, NF, 128, 2, 8, 128), np.float32)
    wd = np.empty((2, 2, 2, 8, 128, NFH, 128), np.float32)
    for l in range(2):
        for j, (kg, ku, kd_) in enumerate([("ffn1_w_gate", "ffn1_w_up", "ffn1_w_down"), ("ffn2_w_gate", "ffn2_w_up", "ffn2_w_down")]):
            wgu[l, j, :, :, 0] = fm_tile(f(kg)[l])
            wgu[l, j, :, :, 1] = fm_tile(f(ku)[l])
            wd[l, j] = f(kd_)[l].reshape(2, NFH, 128, 8, 128).transpose(0, 3, 2, 1, 4)
    win = f("w_in")
    wmix = np.zeros((2, 8, 2, 128, 2, 8, 128), np.float32)
    for l in range(2):
        t = fm_tile(win[l])
        for h in range(4):
            wmix[l, h, 0, :, 0] = t[h]
            wmix[l, h, 0, :, 1] = t[4 + h]
            wmix[l, h, 1, :, 0] = t[8 + h]
            wmix[l, h, 1, :, 1] = t[12 + h]
            wmix[l, 4 + h, 0, :, 0] = t[16 + h]
            wmix[l, 4 + h, 0, :, 1] = t[20 + h]
            wmix[l, 4 + h, 1, :, 0] = t[24 + h]
    wout = np.stack([fm_tile(f("w_out")[l]) for l in range(2)])
    shared = {
        "metaT": np.ascontiguousarray(f("meta_tokens").reshape(NMETA, 8, 128).transpose(2, 1, 0)),
        "cb": _bf(cbm), "cf": cfm, "lam": lam, "tabs": tabs,
        "wgu": wgu.reshape(2, 2, NF, 128, 2048), "wd": wd.reshape(2, 2, 2, 8, 128, NFH * 128),
        "wmix": wmix.reshape(2, 8, 2, 128, 2048), "wout": np.ascontiguousarray(wout.reshape(2, 8, 128, 1024)),
    }
    x = f("x")
    xTs = [np.ascontiguousarray(x[b].reshape(SEQ, 8, 128).transpose(2, 1, 0)) for b in range(x.shape[0])]
    return shared, xTs


def run(inputs, cores=None, stop=None, dump=None, trace=False):
    shared, xTs = _prep(inputs)
    cores = list(range(8)) if cores is None else cores
    nc, P = build_program(stop=stop, dump=dump)
    in_maps = [dict(shared, xT=xTs[b]) for b in cores]
    res = run_bass_kernel_spmd(nc, in_maps, core_ids=list(range(len(cores))), trace=trace)
    return res, P


def kernel(**inputs):
    res, _ = run(inputs)
    outs = [np.asarray(r["outT"], dtype=np.float32) for r in res.results]
    out = np.stack([o.transpose(2, 1, 0).reshape(SEQ, D) for o in outs])
    return np.ascontiguousarray(out.astype(np.float32))
```

```python
import math
import numpy as np
import ml_dtypes
import concourse.bass as bass
import concourse.mybir as mybir
from concourse.bass_utils import run_bass_kernel_spmd

F32 = mybir.dt.float32
BF16 = mybir.dt.bfloat16
AF = mybir.ActivationFunctionType
ALU = mybir.AluOpType

D = 1024
SEQ = 2048
NMETA = 16
PAD = 112
TP = 2176
NCH = 17
DFF = 2816
NF = 22
NFH = 11
EPS = 1e-6
GROUPS = [(0, 128)] + [(128 + 512 * g, 512) for g in range(4)]
NSLOT = 11
NW = 4
WCOLS = 2048
SAME_ENGINE_SYNC = True
DBG_HACK = 0
WARM = 0
ROT_ADD_ENG = 'dve'
DBG_MIX = None

CB_ONES, CB_ONESPAD, CB_IDENT, CB_PERMR, CB_PERMD, CB_CMASK, CB_CENT = [i * 128 for i in range(7)]
CB_N = 7 * 128
CF_DMASK = 0
CF_QDEC = 512
CF_KDEC = 1024
CF_PVEC = 1028
PV_NORM = 0
PV_GN = 56
PV_SW = 64
CF_EPS = CF_PVEC + 66
CF_ONE = CF_EPS + 3
CF_N = CF_ONE + 1


class Prog:
    def __init__(self, nc):
        self.nc = nc
        self.ops = []
        self.dma_sems = {}

    def op(self, eng, fn, reads=(), writes=()):
        self.ops.append(dict(eng=eng, fn=fn, reads=list(reads), writes=list(writes), dma=None))

    def dma(self, eng, fn, slot, reads=(), writes=()):
        self.ops.append(dict(eng=eng, fn=fn, reads=list(reads), writes=list(writes), dma=slot))

    def build(self):
        nc = self.nc
        ops = self.ops
        engs = ['pe', 'act', 'dve', 'pool', 'sp']
        last_w = {}
        readers = {}
        for i, o in enumerate(ops):
            deps = set()
            key = ('dma', i) if o['dma'] is not None else o['eng']
            for r in o['reads']:
                if r in last_w:
                    deps.add(last_w[r])
                if r[0] in ('ps', 'pt'):
                    for k2, i2 in readers.get(r, {}).items():
                        if k2 != key:
                            deps.add(i2)
            for w in o['writes']:
                if w in last_w:
                    deps.add(last_w[w])
                rd = readers.get(w)
                if rd:
                    deps.update(rd.values())
            deps.discard(i)
            o['deps'] = deps
            for r in o['reads']:
                readers.setdefault(r, {})[key] = i
            for w in o['writes']:
                last_w[w] = i
                readers[w] = {}
        signal = set()
        for i, o in enumerate(ops):
            for d in o['deps']:
                p = ops[d]
                if p['dma'] is not None:
                    continue
                if p['eng'] != o['eng'] or (SAME_ENGINE_SYNC and o['eng'] != 'pe'):
                    signal.add(d)
        sem = {e: nc.alloc_semaphore('s_' + e) for e in engs}
        cnt = {e: 0 for e in engs}
        for i, o in enumerate(ops):
            if o['dma'] is not None:
                slot = o['dma']
                if slot not in self.dma_sems:
                    self.dma_sems[slot] = [nc.alloc_semaphore('d_' + str(slot)), 0]
                ent = self.dma_sems[slot]
                ent[1] += 16
                o['tok'] = (ent[0], ent[1])
            elif i in signal:
                cnt[o['eng']] += 1
                o['tok'] = (sem[o['eng']], cnt[o['eng']])
            else:
                o['tok'] = None
        waited = {e: {} for e in engs}
        for i, o in enumerate(ops):
            need = {}
            for d in o['deps']:
                p = ops[d]
                if p['dma'] is None and p['eng'] == o['eng'] and not (SAME_ENGINE_SYNC and o['eng'] != 'pe'):
                    continue
                tok = p['tok']
                assert tok is not None
                k = tok[0]
                if need.get(k, (None, 0))[1] < tok[1]:
                    need[k] = tok
            ws = []
            for k, tok in need.items():
                if waited[o['eng']].get(k, 0) < tok[1]:
                    waited[o['eng']][k] = tok[1]
                    ws.append(tok)
            o['waits'] = ws
        per = {e: [o for o in ops if o['eng'] == e] for e in engs}
        self.stats = {e: len(per[e]) for e in engs}
        self.stats['signals'] = dict(cnt)

        def run(e, lst):
            for o in lst:
                for (s, v) in o['waits']:
                    e.wait_ge(s, v)
                if o['fn'] is None:
                    continue
                ins = o['fn'](e)
                if o['tok'] is not None:
                    ins.then_inc(o['tok'][0], 16 if o['dma'] is not None else 1)

        with nc.Block() as block:
            @block.tensor
            def _(e):
                run(e, per['pe'])

            @block.scalar
            def _(e):
                run(e, per['act'])

            @block.vector
            def _(e):
                run(e, per['dve'])

            @block.gpsimd
            def _(e):
                run(e, per['pool'])

            @block.sync
            def _(e):
                run(e, per['sp'])


def build_program(stop=None, dump=None):
    nc = bass.Bass("TRN2", target_bir_lowering=False)
    P = Prog(nc)

    def din(name, shape, dt=F32):
        return nc.dram_tensor(name, list(shape), dt, kind="ExternalInput").ap()

    xT_d = din("xT", [128, 8, SEQ])
    meta_d = din("metaT", [128, 8, NMETA])
    cb_d = din("cb", [128, CB_N], BF16)
    cf_d = din("cf", [128, CF_N])
    lam_d = din("lam", [128, 2 * 4 * 64])
    tabs_d = din("tabs", [4, 128, TP])
    wgu_d = din("wgu", [2, 2, NF, 128, 2048])
    wd_d = din("wd", [2, 2, 2, 8, 128, NFH * 128])
    wmix_d = din("wmix", [2, 8, 2, 128, 2048])
    wout_d = din("wout", [2, 8, 128, 1024])
    out_d = nc.dram_tensor("outT", [128, 8, SEQ], F32, kind="ExternalOutput").ap()
    dbg_d = None
    if dump is not None:
        dbg_d = nc.dram_tensor("dbg", [128, 8, TP], F32, kind="ExternalOutput").ap()

    sb = nc.alloc_sbuf_tensor
    hT = sb("hT", [128, 8, TP], F32)
    xnT = sb("xnT", [128, 8, TP], BF16)
    SL = sb("slots", [128, NSLOT, TP], BF16)
    wsl = [sb(f"wsl{i}", [128, WCOLS], BF16) for i in range(NW)]
    tabt = [sb(f"tab{i}", [128, 2, 512], F32) for i in range(2)]
    cb = sb("cbs", [128, CB_N], BF16)
    cf = sb("cfs", [128, CF_N], F32)
    lamin = sb("lamin", [128, 512], F32)
    lamw = sb("lamw", [128, 64], F32)
    lamv = sb("lamv", [128, 8], F32)
    NR = 3
    sqr = [sb(f"sq{i}", [128, 512], BF16) for i in range(NR)]
    fa = [sb(f"fa{i}", [128, 512], F32) for i in range(4)]
    fb = [sb(f"fb{i}", [128, 512], F32) for i in range(4)]
    ba = [sb(f"ba{i}", [128, 512], BF16) for i in range(6)]
    Rf = [sb(f"Rf{i}", [128, 128], F32) for i in range(2)]

    PS = [nc.alloc_psum_tensor(f"ps{i}", [128, 512], F32) for i in range(7)]
    PT = nc.alloc_psum_tensor("pst", [128, 1024], BF16)

    SBANK = [(PS[0], ('ps', 0)), (PS[1], ('ps', 1)), (PT[:, :].bitcast(F32), ('pt',))]
    ctr = {}

    def rr(name, n):
        v = ctr.get(name, 0)
        ctr[name] = v + 1
        return v % n

    def ring(name, lst):
        i = rr(name, len(lst))
        return lst[i], (name, i)

    ones = cb[:, CB_ONES:CB_ONES + 128]
    ones_pad = cb[:, CB_ONESPAD:CB_ONESPAD + 128]
    ident = cb[:, CB_IDENT:CB_IDENT + 128]
    cmask = cb[:, CB_CMASK:CB_CMASK + 128]
    RC = ('const',)
    epsc = cf[:, CF_EPS:CF_EPS + 1]
    onec = cf[:, CF_ONE:CF_ONE + 1]
    epsl = [cf[:, CF_EPS + 1 + l_:CF_EPS + 2 + l_] for l_ in range(2)]

    P.dma('sp', lambda e: e.dma_start(out=cb[:, :], in_=cb_d), 'const', writes=[RC])
    P.dma('sp', lambda e: e.dma_start(out=cf[:, :], in_=cf_d), 'const', writes=[RC])
    P.dma('sp', lambda e: e.dma_start(out=lamin[:, :], in_=lam_d), 'const', writes=[RC])
    hregs = lambda c: [('h', c, g) for g in range(5)]
    P.op('dve', lambda e: e.memset(hT[:, :, 0:PAD], 0.0), writes=[('h', c, 0) for c in range(8)])
    for c in range(8):
        P.dma('sp', lambda e, c=c: e.dma_start(out=hT[:, c, 128:TP], in_=xT_d[:, c, :]), 'xin',
              writes=[('h', c, g) for g in range(1, 5)])
    P.dma('sp', lambda e: e.dma_start(out=hT[:, :, PAD:128], in_=meta_d), 'xin',
          reads=[('h', c, 0) for c in range(8)], writes=[('h', c, 0) for c in range(8)])

    def wload(src_ap, ncols):
        i = rr('w', NW)
        t = wsl[i]
        reg = ('w', i)
        P.dma('pool', lambda e: e.dma_start(out=t[:, 0:ncols], in_=src_ap, max_dma_last_dim=8192),
              ('w', i), writes=[reg])
        return t, reg

    def tload(k, s, n):
        i = rr('tab', 2)
        t = tabt[i]
        regs = [('tab', i, 0), ('tab', i, 1)]
        P.dma('sp', lambda e: e.dma_start(out=t[:, 0, 0:n], in_=tabs_d[2 * k, :, s:s + n]), ('tab', i, 0), writes=[regs[0]])
        P.dma('sp', lambda e: e.dma_start(out=t[:, 1, 0:n], in_=tabs_d[2 * k + 1, :, s:s + n]), ('tab', i, 1), writes=[regs[1]])
        return t, regs

    def rmsnorm_group(pvcol, final, g, s, n):
        ss = PS[6]
        for c in range(8):
            sq, sqreg = ring('sq', sqr)
            P.op('act', lambda e, sq=sq, c=c: e.activation(out=sq[:, 0:n], in_=hT[:, c, s:s + n], func=AF.Square),
                 reads=[('h', c, g)], writes=[sqreg])
            P.op('pe', lambda e, sq=sq, c=c: e.matmul(ss[:, 0:n], lhsT=ones, rhs=sq[:, 0:n], start=(c == 0), stop=(c == 7)),
                 reads=[sqreg, RC], writes=[('ps', 6)])
        rt, rtreg = ring('fa', fa)
        P.op('act', lambda e: e.activation(out=rt[:, 0:n], in_=ss[:, 0:n], func=AF.Ln, scale=1.0 / D, bias=epsc),
             reads=[('ps', 6), RC], writes=[rtreg])
        rs, rsreg = ring('fb', fb)
        P.op('act', lambda e: e.activation(out=rs[:, 0:n], in_=rt[:, 0:n], func=AF.Exp, scale=-0.5), reads=[rtreg], writes=[rsreg])
        for c in range(8):
            gcol = cf[:, CF_PVEC + pvcol + c:CF_PVEC + pvcol + c + 1]
            dst = hT if final else xnT
            wreg_ = ('h', c, g) if final else ('xn', c, g)
            P.op('dve', lambda e, c=c, gcol=gcol, dst=dst: e.scalar_tensor_tensor(
                out=dst[:, c, s:s + n], in0=hT[:, c, s:s + n], scalar=gcol, in1=rs[:, 0:n], op0=ALU.mult, op1=ALU.mult),
                reads=[('h', c, g), rsreg, RC], writes=[wreg_])

    def rmsnorm(pvcol, final=False):
        for g, (s, n) in enumerate(GROUPS):
            rmsnorm_group(pvcol, final, g, s, n)

    NM = 128 - PAD

    def resid_add(ps, psreg, c, g, scale):
        s, n = GROUPS[g]
        if g == 0:
            s, n = PAD, NM
        P.op('dve', lambda e: e.scalar_tensor_tensor(out=hT[:, c, s:s + n], in0=ps[:, 0:n], scalar=scale,
                                                      in1=hT[:, c, s:s + n], op0=ALU.mult, op1=ALU.add),
             reads=[psreg, ('h', c, g)], writes=[('h', c, g)])

    def ffn_gu_group(wt, wreg, fi, g, s, n):
        if g == 0:
            s, n = PAD, NM
        k = rr('ffn_gu', 2)
        gp, up = PS[k], PS[2 + k]
        for kc in range(8):
            P.op('pe', lambda e, kc=kc: e.matmul(gp[:, 0:n], lhsT=wt[:, kc * 128:(kc + 1) * 128],
                                               rhs=xnT[:, kc, s:s + n], start=(kc == 0), stop=(kc == 7)),
                 reads=[wreg, ('xn', kc, g)], writes=[('ps', k)])
        for kc in range(8):
            P.op('pe', lambda e, kc=kc: e.matmul(up[:, 0:n], lhsT=wt[:, 1024 + kc * 128:1024 + (kc + 1) * 128],
                                               rhs=xnT[:, kc, s:s + n], start=(kc == 0), stop=(kc == 7)),
                 reads=[wreg, ('xn', kc, g)], writes=[('ps', 2 + k)])
        sg, sgreg = ring('fa', fa)
        P.op('act', lambda e: e.activation(out=sg[:, 0:n], in_=gp[:, 0:n], func=AF.Silu),
             reads=[('ps', k)], writes=[sgreg])
        P.op('dve', lambda e: e.tensor_tensor(out=SL[:, fi, s:s + n], in0=up[:, 0:n], in1=sg[:, 0:n], op=ALU.mult),
             reads=[('ps', 2 + k), sgreg], writes=[('S', fi, g)])

    def mm_out_group(wt, wreg, nk, c, g, s, n, scale):
        if g == 0:
            s, n = PAD, NM
        k = 4 + rr('ffn_o', 2)
        op_ = PS[k]
        for fi in range(nk):
            P.op('pe', lambda e, fi=fi: e.matmul(op_[:, 0:n], lhsT=wt[:, fi * 128:(fi + 1) * 128],
                                               rhs=SL[:, fi, s:s + n], start=(fi == 0), stop=(fi == nk - 1)),
                 reads=[wreg, ('S', fi, g)], writes=[('ps', k)])
        resid_add(op_, ('ps', k), c, g, scale)

    def ffn(l, j):
        rmsnorm(PV_NORM + (l * 3 + (0 if j == 0 else 2)) * 8)
        for half in range(2):
            for fi in range(NFH):
                wt, wreg = wload(wgu_d[l, j, half * NFH + fi], 2048)
                for g, (s, n) in enumerate(GROUPS):
                    ffn_gu_group(wt, wreg, fi, g, s, n)
            for c in range(8):
                wt, wreg = wload(wd_d[l, j, half, c], NFH * 128)
                for g, (s, n) in enumerate(GROUPS):
                    mm_out_group(wt, wreg, NFH, c, g, s, n, 0.5)

    tabs_for = {}

    rot_pending = []

    def project_rot_group(which, wt, wreg, dst_slot, permcol, g, split=None):
        perm = cb[:, permcol:permcol + 128]
        s, n = GROUPS[g]
        tt, tregs = tabs_for[g]
        pk = rr('proj', 2)
        pp, sp_ = PS[pk], PS[2 + pk]
        for kc in range(8):
            P.op('pe', lambda e, kc=kc: e.matmul(pp[:, 0:n], lhsT=wt[:, which * 1024 + kc * 128:which * 1024 + (kc + 1) * 128],
                                               rhs=xnT[:, kc, s:s + n], start=(kc == 0), stop=(kc == 7)),
                 reads=[wreg, ('xn', kc, g)], writes=[('ps', pk)])
        qb, qbreg = ring('ba', ba)
        P.op('act', lambda e: e.activation(out=qb[:, 0:n], in_=pp[:, 0:n], func=AF.Identity), reads=[('ps', pk)], writes=[qbreg])
        rot_flush()

        def second():
            P.op('pe', lambda e: e.matmul(sp_[:, 0:n], lhsT=perm, rhs=qb[:, 0:n], start=True, stop=True),
                 reads=[qbreg, RC], writes=[('ps', 2 + pk)])
            t1, t1reg = ring('fa', fa)
            t2, t2reg = ring('fb', fb)
            P.op('dve', lambda e: e.tensor_tensor(out=t1[:, 0:n], in0=pp[:, 0:n], in1=tt[:, 0, 0:n], op=ALU.mult),
                 reads=[('ps', pk)] + tregs, writes=[t1reg])
            P.op('dve', lambda e: e.tensor_tensor(out=t2[:, 0:n], in0=sp_[:, 0:n], in1=tt[:, 1, 0:n], op=ALU.mult),
                 reads=[('ps', 2 + pk)] + tregs, writes=[t2reg])
            if split is None:
                P.op(ROT_ADD_ENG, lambda e: e.tensor_tensor(out=SL[:, dst_slot, s:s + n], in0=t1[:, 0:n], in1=t2[:, 0:n], op=ALU.add),
                     reads=[t1reg, t2reg], writes=[('S', dst_slot, g)])
            else:
                for m in range(2):
                    P.op('dve', lambda e, m=m: e.tensor_tensor(out=SL[64 * m:64 * m + 64, split[m], s:s + n], in0=t1[64 * m:64 * m + 64, 0:n],
                                                             in1=t2[64 * m:64 * m + 64, 0:n], op=ALU.add),
                         reads=[t1reg, t2reg], writes=[('S', split[m], g)])
        rot_pending.append(second)

    def rot_flush():
        while rot_pending:
            rot_pending.pop(0)()

    def project_v_group(wt, wreg, vslot, g, s, n):
        pk = rr('proj', 2)
        pp = PS[pk]
        for ci in range(n // 128):
            cs = s + ci * 128
            for kc in range(8):
                P.op('pe', lambda e, kc=kc, ci=ci, cs=cs: e.matmul(pp[:, ci * 128:(ci + 1) * 128], lhsT=xnT[:, kc, cs:cs + 128],
                                                                 rhs=wt[:, kc * 128:(kc + 1) * 128], start=(kc == 0), stop=(kc == 7)),
                     reads=[wreg, ('xn', kc, g)], writes=[('ps', pk)])
        P.op('act', lambda e: e.activation(out=SL[:, vslot, s:s + n], in_=pp[:, 0:n], func=AF.Identity),
             reads=[('ps', pk)], writes=[('S', vslot, g)])

    def project_v(wt, wreg, vslot):
        for g, (s, n) in enumerate(GROUPS):
            project_v_group(wt, wreg, vslot, g, s, n)

    def stats_rstd(src, srcreg, n, scale, bias_col, o0=0):
        sq, sqreg = ring('ba', ba)
        P.op('dve', lambda e: e.tensor_tensor(out=sq[:, o0:n], in0=src[:, o0:n], in1=src[:, o0:n], op=ALU.mult), reads=[srcreg], writes=[sqreg])
        P.op('pe', lambda e: e.matmul(PS[6][:, o0:n], lhsT=ones, rhs=sq[:, o0:n], start=True, stop=True),
             reads=[sqreg, RC], writes=[('ps', 6)])
        sd, sdreg = ring('fb', fb)
        P.op('act', lambda e: e.activation(out=sd[:, o0:n], in_=PS[6][:, o0:n], func=AF.Ln, scale=scale, bias=bias_col),
             reads=[('ps', 6), RC], writes=[sdreg])
        rs, rsreg = ring('fb', fb)
        P.op('act', lambda e: e.activation(out=rs[:, o0:n], in_=sd[:, o0:n], func=AF.Exp, scale=-0.5), reads=[sdreg], writes=[rsreg])
        return rs, rsreg

    def retention_attention(l, h, u, ks, vs, wvg, wvgreg):
        gam = 1.0 - 2.0 ** (-5.0 - h)
        cd = gam ** 128
        dmask = cf[:, CF_DMASK + h * 128:CF_DMASK + (h + 1) * 128]
        qdec = cf[:, CF_QDEC + h * 128:CF_QDEC + (h + 1) * 128]
        kdec = cf[:, CF_KDEC + h:CF_KDEC + h + 1]
        gnw = cf[:, CF_PVEC + PV_GN + l * 4 + h:CF_PVEC + PV_GN + l * 4 + h + 1]
        RBs = 18 - ks
        grp = lambda n: 0 if n == 0 else 1 + (n - 1) // 4
        P.op('dve', lambda e: e.memset(SL[:, RBs, 0:128], 0.0), writes=[('S', RBs, 0)])
        P.op('dve', lambda e: e.memset(Rf[0][:, :], 0.0), writes=[('Rf', 0)])
        def emit_gate(g, bank=6):
            s, n = GROUPS[g]
            gp, greg = (PTF, ('pt',)) if bank == 'ptf' else (PS[bank], ('ps', bank))
            for kc in range(8):
                P.op('pe', lambda e, kc=kc: e.matmul(gp[:, 0:n], lhsT=wvg[:, 1024 + kc * 128:1024 + (kc + 1) * 128],
                                                   rhs=xnT[:, kc, s:s + n], start=(kc == 0), stop=(kc == 7)),
                     reads=[wvgreg, ('xn', kc, g)], writes=[greg])
            sg, sgreg = ring('fb', fb)
            P.op('act', lambda e: e.activation(out=sg[:, 0:n], in_=gp[:, 0:n], func=AF.Exp, scale=-1.0), reads=[greg], writes=[sgreg])
            P.op('act', lambda e: e.activation(out=sg[:, 0:n], in_=sg[:, 0:n], func=AF.Ln, bias=onec), reads=[sgreg, RC], writes=[sgreg])
            P.op('act', lambda e: e.activation(out=sg[:, 0:n], in_=sg[:, 0:n], func=AF.Exp, scale=-1.0), reads=[sgreg], writes=[sgreg])
            gs, gsreg = ring('sq', sqr)
            P.op('dve', lambda e: e.tensor_tensor(out=gs[:, 0:n], in0=gp[:, 0:n], in1=sg[:, 0:n], op=ALU.mult),
                 reads=[greg, sgreg], writes=[gsreg])
            gate[g] = (gs, gsreg)

        for r in range(4):
            for i in range(4):
                n = 4 * r + i
                P.op('pe', lambda e, n=n, i=i: e.transpose(PT[:, i * 128:(i + 1) * 128], SL[:, ks, n * 128:(n + 1) * 128], ident),
                     reads=[('S', ks, grp(n)), RC], writes=[('pt',)])
            kd, kdreg = ring('ba', ba)
            P.op('act', lambda e, kd=kd: e.activation(out=kd[:, 0:512], in_=PT[:, 0:512], func=AF.Identity, scale=kdec),
                 reads=[('pt',), RC], writes=[kdreg])
            for i in range(4):
                n = 4 * r + i
                P.op('pe', lambda e, n=n, i=i, r=r, kd=kd: e.matmul(PS[r][:, i * 128:(i + 1) * 128], lhsT=kd[:, i * 128:(i + 1) * 128],
                                                                  rhs=SL[:, vs, n * 128:(n + 1) * 128], start=True, stop=True),
                     reads=[kdreg, ('S', vs, grp(n))], writes=[('ps', r)])
        gate = {}
        for n in range(16):
            if n in (0, 5, 10):
                emit_gate(n // 5, (6, 4, 5)[n // 5])
            r, i = divmod(n, 4)
            P.op('dve', lambda e, n=n, r=r, i=i: e.scalar_tensor_tensor(out=Rf[(n + 1) % 2][:, :], in0=Rf[n % 2][:, :], scalar=cd,
                                                                      in1=PS[r][:, i * 128:(i + 1) * 128], op0=ALU.mult, op1=ALU.add),
                 reads=[('Rf', n % 2), ('ps', r)], writes=[('Rf', (n + 1) % 2)])
            P.op('act', lambda e, n=n: e.activation(out=SL[:, RBs, (n + 1) * 128:(n + 2) * 128], in_=Rf[(n + 1) % 2][:, :], func=AF.Identity),
                 reads=[('Rf', (n + 1) % 2)], writes=[('S', RBs, grp(n + 1))])
        def emit_ST(g):
            s, n = GROUPS[g]
            ncx = n // 128
            bk = g % 2
            for ci in range(ncx):
                cs = s + ci * 128
                P.op('pe', lambda e, ci=ci, cs=cs: e.matmul(PS[bk][:, ci * 128:(ci + 1) * 128], lhsT=SL[:, ks, cs:cs + 128],
                                                          rhs=SL[:, u, cs:cs + 128], start=True, stop=True),
                     reads=[('S', u, g), ('S', ks, g)], writes=[('ps', bk)])
            v3 = lambda ap: ap.rearrange("p (c i) -> p c i", c=ncx)
            bc = lambda ap: ap.unsqueeze(1).broadcast_to([128, ncx, 128])
            stm, stmreg = ring('ba', ba)
            P.op('dve', lambda e: e.tensor_tensor(out=v3(stm[:, 0:n]), in0=v3(PS[bk][:, 0:n]), in1=bc(dmask), op=ALU.mult),
                 reads=[('ps', bk), RC], writes=[stmreg])
            qd, qdreg = ring('ba', ba)
            P.op('dve', lambda e: e.tensor_tensor(out=v3(qd[:, 0:n]), in0=v3(SL[:, u, s:s + n]), in1=bc(qdec), op=ALU.mult),
                 reads=[('S', u, g), RC], writes=[qdreg])
            return stm, stmreg, qd, qdreg

        def emit_O(g, st):
            stm, stmreg, qd, qdreg = st
            s, n = GROUPS[g]
            OBk = 4 + g % 2
            OB = PS[OBk]
            for ci in range(n // 128):
                cs = s + ci * 128
                P.op('pe', lambda e, ci=ci, cs=cs: e.matmul(OB[:, ci * 128:(ci + 1) * 128], lhsT=SL[:, vs, cs:cs + 128],
                                                          rhs=stm[:, ci * 128:(ci + 1) * 128], start=True, stop=False),
                     reads=[('S', vs, g), stmreg], writes=[('ps', OBk)])
                P.op('pe', lambda e, ci=ci, cs=cs: e.matmul(OB[:, ci * 128:(ci + 1) * 128], lhsT=SL[:, RBs, cs:cs + 128],
                                                          rhs=qd[:, ci * 128:(ci + 1) * 128], start=False, stop=True),
                     reads=[('S', RBs, g), qdreg], writes=[('ps', OBk)])
            post.append([1, lambda: fin1(g, OBk)])
            tick()

        post = []
        PTF = PT[:, :].bitcast(F32)

        def tick(force=False):
            for ent in list(post):
                ent[0] -= 1
                if force or ent[0] <= 0:
                    post.remove(ent)
                    ent[1]()

        def fin1(g, OBk):
            s, n = GROUPS[g]
            OB = PS[OBk]
            cent = cb[:, CB_CENT:CB_CENT + 128]
            ck = 2 + g % 2
            ob, obreg = ring('ba', ba)
            P.op('act', lambda e: e.activation(out=ob[:, 0:n], in_=OB[:, 0:n], func=AF.Identity), reads=[('ps', OBk)], writes=[obreg])
            P.op('pe', lambda e: e.matmul(PS[ck][:, 0:n], lhsT=cent, rhs=ob[:, 0:n], start=True, stop=True),
                 reads=[obreg, RC], writes=[('ps', ck)])
            post.append([1, lambda: fin2(g, ck)])

        def fin2(g, ck):
            s, n = GROUPS[g]
            csq, csqreg = ring('ba', ba)
            P.op('act', lambda e: e.activation(out=csq[:, 0:n], in_=PS[ck][:, 0:n], func=AF.Square), reads=[('ps', ck)], writes=[csqreg])
            P.op('pe', lambda e: e.matmul(PS[6][:, 0:n], lhsT=ones, rhs=csq[:, 0:n], start=True, stop=True),
                 reads=[csqreg, RC], writes=[('ps', 6)])
            post.append([1, lambda: fin3(g, ck)])

        def fin3(g, ck):
            s, n = GROUPS[g]
            rs, rsreg = ring('fb', fb)
            P.op('act', lambda e: e.activation(out=rs[:, 0:n], in_=PS[6][:, 0:n], func=AF.Ln, scale=1.0 / 128, bias=epsc),
                 reads=[('ps', 6), RC], writes=[rsreg])
            P.op('act', lambda e: e.activation(out=rs[:, 0:n], in_=rs[:, 0:n], func=AF.Exp, scale=-0.5), reads=[rsreg], writes=[rsreg])
            fin4(g, ck, rs, rsreg)

        def fin4(g, ck, rs, rsreg):
            s, n = GROUPS[g]
            nrm, nrmreg = ring('fa', fa)
            P.op('dve', lambda e: e.scalar_tensor_tensor(out=nrm[:, 0:n], in0=PS[ck][:, 0:n], scalar=gnw, in1=rs[:, 0:n],
                                                          op0=ALU.mult, op1=ALU.mult),
                 reads=[('ps', ck), rsreg, RC], writes=[nrmreg])
            gs, gsreg = gate[g]
            P.op('dve', lambda e: e.tensor_tensor(out=SL[:, u, s:s + n], in0=nrm[:, 0:n], in1=gs[:, 0:n], op=ALU.mult),
                 reads=[nrmreg, gsreg], writes=[('S', u, g)])
            done.add(g)

        done = set()

        pending = None
        need_gate = [3, 4]
        for g in range(5):
            if need_gate and (need_gate[0] - 3) in done:
                emit_gate(need_gate.pop(0), 'ptf')
            st = emit_ST(g)
            if pending is not None:
                emit_O(*pending)
            pending = (g, st)
        emit_O(*pending)
        while post or need_gate:
            if need_gate and (need_gate[0] - 3) in done:
                emit_gate(need_gate.pop(0), 'ptf')
            tick(force=True)

    def retention_unit(l, h):
        u = h
        ks = 8 + 2 * (u % 2)
        vs = 9
        wqk, wqkreg = wload(wmix_d[l, u, 0], 2048)
        wvg, wvgreg = wload(wmix_d[l, u, 1], 2048)
        for g, (s, n) in enumerate(GROUPS):
            tabs_for[g] = tload(0, s, n)
            project_rot_group(0, wqk, wqkreg, u, CB_PERMR, g)
            project_rot_group(1, wqk, wqkreg, ks, CB_PERMR, g)
        rot_flush()
        project_v(wvg, wvgreg, vs)
        if DBG_MIX == 'B':
            return
        retention_attention(l, h, u, ks, vs, wvg, wvgreg)

    def diff_attention(l, u, ks, vs, om):
        neglam = lamv[:, 4 + l:5 + l]
        sw = cf[:, CF_PVEC + PV_SW + l:CF_PVEC + PV_SW + l + 1]
        blocks = []
        for g, (s, n) in enumerate(GROUPS):
            last = (s + n) // 128 - 1
            for m in range(2):
                for jb in range(last + 1):
                    blocks.append((g, m, jb, last))
        odset = {}
        tres = {}
        deferred = []
        post = []

        def emit_S(blk):
            g, m, jb, last = blk
            s, n = GROUPS[g]
            off = max(0, jb * 128 - s)
            nq = n - off
            diag = jb * 128 >= s
            sk = rr('dst', 3)
            SP_, sreg = SBANK[sk]
            kg = 0 if jb == 0 else 1 + (jb - 1) // 4
            P.op('pe', lambda e: e.matmul(SP_[:, 0:nq], lhsT=SL[:, ks[m], jb * 128:(jb + 1) * 128],
                                          rhs=SL[:, u, s + off:s + n], start=True, stop=(not diag)),
                 reads=[('S', ks[m], kg), ('S', u, g)], writes=[sreg])
            if diag:
                P.op('pe', lambda e: e.matmul(SP_[:, 0:128], lhsT=ident, rhs=cmask, start=False, stop=True),
                     reads=[RC], writes=[sreg])
            return sk, off, nq, kg

        def emit_rest(blk, sinfo):
            g, m, jb, last = blk
            sk, off, nq, kg = sinfo
            SP_, sreg = SBANK[sk]
            if (g, m) not in odset:
                odset[(g, m)] = 2 + 2 * rr('dod', 2)
                while any(ent[2] == odset[(g, m)] for ent in post):
                    for ent in list(post):
                        if ent[2] == odset[(g, m)]:
                            post.remove(ent)
                            ent[1]()
            OBk = odset[(g, m)]
            DBk = OBk + 1
            OB, DB = PS[OBk], PS[DBk]
            pt, ptreg = ring('ba', ba)
            P.op('act', lambda e: e.activation(out=pt[:, 0:nq], in_=SP_[:, 0:nq], func=AF.Exp, scale=0.125),
                 reads=[sreg], writes=[ptreg])
            P.op('pe', lambda e: e.matmul(OB[:, off:off + nq], lhsT=SL[:, vs, jb * 128:(jb + 1) * 128], rhs=pt[:, 0:nq],
                                          start=(jb == 0), stop=(jb == last)),
                 reads=[('S', vs, kg), ptreg], writes=[('ps', OBk)])
            P.op('pe', lambda e: e.matmul(DB[:, off:off + nq], lhsT=(ones_pad if jb == 0 else ones), rhs=pt[:, 0:nq],
                                          start=(jb == 0), stop=(jb == last)),
                 reads=[RC, ptreg], writes=[('ps', DBk)])
            if jb == last:
                post.append([2, lambda: normalize_act(g, m, OBk), OBk])

        def normalize_act(g, m, OBk):
            DBk = OBk + 1
            DB = PS[DBk]
            s, n = GROUPS[g]
            o0 = PAD if g == 0 else 0
            r, rreg = ring('fb', fb)
            P.op('act', lambda e: e.activation(out=r[:, o0:n], in_=DB[:, o0:n], func=AF.Ln), reads=[('ps', DBk)], writes=[rreg])
            P.op('act', lambda e: e.activation(out=r[:, o0:n], in_=r[:, o0:n], func=AF.Exp, scale=-1.0), reads=[rreg], writes=[rreg])
            post.append([3, lambda: normalize_dve(g, m, OBk, r, rreg), OBk])

        def normalize_dve(g, m, OBk, r, rreg):
            OB = PS[OBk]
            s, n = GROUPS[g]
            o0 = PAD if g == 0 else 0
            t, treg_ = ring('fa', fa)
            P.op('dve', lambda e: e.tensor_tensor(out=t[:, o0:n], in0=OB[:, o0:n], in1=r[:, o0:n], op=ALU.mult),
                 reads=[('ps', OBk), rreg], writes=[treg_])
            tres[(g, m)] = (t, treg_)
            if m == 1:
                post.append([1, lambda: finalize1(g), None])

        def finalize1(g):
            s, n = GROUPS[g]
            o0 = PAD if g == 0 else 0
            (t0, t0reg), (t1, t1reg) = tres[(g, 0)], tres[(g, 1)]
            o, oreg = ring('fa', fa)
            P.op('dve', lambda e: e.scalar_tensor_tensor(out=o[:, o0:n], in0=t1[:, o0:n], scalar=neglam, in1=t0[:, o0:n],
                                                          op0=ALU.mult, op1=ALU.add),
                 reads=[t0reg, t1reg, ('lamv',)], writes=[oreg])
            sq, sqreg = ring('ba', ba)
            P.op('dve', lambda e: e.tensor_tensor(out=sq[:, o0:n], in0=o[:, o0:n], in1=o[:, o0:n], op=ALU.mult), reads=[oreg], writes=[sqreg])
            P.op('pe', lambda e: e.matmul(PS[6][:, o0:n], lhsT=ones, rhs=sq[:, o0:n], start=True, stop=True),
                 reads=[sqreg, RC], writes=[('ps', 6)])
            post.append([4, lambda: finalize2(g, o, oreg), None])

        def finalize2(g, o, oreg):
            s, n = GROUPS[g]
            o0 = PAD if g == 0 else 0
            rs, rsreg = ring('fb', fb)
            P.op('act', lambda e: e.activation(out=rs[:, o0:n], in_=PS[6][:, o0:n], func=AF.Ln, scale=1.0 / (128 * om * om), bias=epsl[l]),
                 reads=[('ps', 6), RC], writes=[rsreg])
            P.op('act', lambda e: e.activation(out=rs[:, o0:n], in_=rs[:, o0:n], func=AF.Exp, scale=-0.5), reads=[rsreg], writes=[rsreg])
            P.op('dve', lambda e: e.scalar_tensor_tensor(out=SL[:, u, s + o0:s + n], in0=o[:, o0:n], scalar=sw, in1=rs[:, o0:n],
                                                          op0=ALU.mult, op1=ALU.mult),
                 reads=[oreg, rsreg, RC], writes=[('S', u, g)])

        def tick(force=False):
            for ent in list(post):
                ent[0] -= 1
                if force or ent[0] <= 0:
                    post.remove(ent)
                    ent[1]()

        pend = []
        for blk in blocks:
            sinfo = emit_S(blk)
            pend.append((blk, sinfo))
            if len(pend) > 2:
                emit_rest(*pend.pop(0))
                tick()
        while pend:
            emit_rest(*pend.pop(0))
            tick()
        while post:
            tick(force=True)

    def diff_unit(l, h, lam_init):
        u = 4 + h
        ks = (8, 10)
        vs = 9
        wqk, wqkreg = wload(wmix_d[l, u, 0], 2048)
        wv, wvreg = wload(wmix_d[l, u, 1, :, 0:1024], 1024)
        if h == 0:
            P.op('dve', lambda e: e.memset(SL[64:128, ks[0], :], 0.0), writes=[('S', ks[0], g) for g in range(5)])
            P.op('dve', lambda e: e.memset(SL[0:64, ks[1], :], 0.0), writes=[('S', ks[1], g) for g in range(5)])
        for g, (s, n) in enumerate(GROUPS):
            tabs_for[g] = tload(1, s, n)
            project_rot_group(0, wqk, wqkreg, u, CB_PERMD, g)
            project_rot_group(1, wqk, wqkreg, None, CB_PERMD, g, split=ks)
        rot_flush()
        project_v(wv, wvreg, vs)
        diff_attention(l, u, ks, vs, 1.0 - lam_init)

    def lam_compute(l):
        lam_init = 0.8 - 0.6 * math.exp(-0.3 * l)
        for t in range(2):
            a = lamin[:, (l * 4 + 2 * t) * 64:(l * 4 + 2 * t + 1) * 64]
            b = lamin[:, (l * 4 + 2 * t + 1) * 64:(l * 4 + 2 * t + 2) * 64]
            P.op('dve', lambda e, a=a, b=b: e.tensor_tensor(out=lamw[:, :], in0=a, in1=b, op=ALU.mult),
                 reads=[RC, ('lamw',)], writes=[('lamw',)])
            P.op('dve', lambda e, t=t: e.tensor_reduce(out=lamv[:, t:t + 1], in_=lamw[:, :], axis=mybir.AxisListType.X, op=ALU.add),
                 reads=[('lamw',), ('lamv',)], writes=[('lamv',)])
            P.op('act', lambda e, t=t: e.activation(out=lamv[:, 2 + t:3 + t], in_=lamv[:, t:t + 1], func=AF.Exp),
                 reads=[('lamv',)], writes=[('lamv',)])
        P.op('dve', lambda e: e.scalar_tensor_tensor(out=lamv[:, 4 + l:5 + l], in0=lamv[:, 3:4], scalar=-lam_init, in1=lamv[:, 2:3],
                                                      op0=ALU.add, op1=ALU.subtract),
             reads=[('lamv',)], writes=[('lamv',)])
        return lam_init

    def mixer(l):
        rmsnorm(PV_NORM + (l * 3 + 1) * 8)
        lam_init = lam_compute(l)
        if DBG_MIX == 'A':
            return
        for h in range(4):
            retention_unit(l, h)
            if DBG_MIX in ('B', 'B1', 'C', 'R1', 'R2', 'R3'):
                return
        if DBG_MIX == 'D':
            return
        for h in range(4):
            diff_unit(l, h, lam_init)
            if DBG_MIX == 'F':
                return
        for c in range(8):
            wt, wreg = wload(wout_d[l, c], 1024)
            for g, (s, n) in enumerate(GROUPS):
                mm_out_group(wt, wreg, 8, c, g, s, n, 1.0)

    stages = []
    for l in range(2):
        stages += [('ffn', l, 0), ('mix', l), ('ffn', l, 1)]
    nst = len(stages) if stop is None else stop
    for st in stages[:nst]:
        if st[0] == 'ffn':
            ffn(st[1], st[2])
        else:
            mixer(st[1])
    if stop is None:
        rmsnorm(PV_NORM + 48, final=True)

    outregs = []
    for c in range(8):
        P.dma('sp', lambda e, c=c: e.dma_start(out=out_d[:, c, :], in_=hT[:, c, 128:TP]), 'out',
              reads=[('h', c, g) for g in range(1, 5)], writes=[('out', c)])
        outregs.append(('out', c))
    if dump == 'h':
        for c in range(8):
            P.dma('sp', lambda e, c=c: e.dma_start(out=dbg_d[:, c, :], in_=hT[:, c, :]), 'out',
                  reads=[('h', c, g) for g in range(5)], writes=[('dbg', c)])
            outregs.append(('dbg', c))
    P.op('sp', None, reads=outregs)
    P.build()
    return nc, P


def _bf(a):
    return np.asarray(a, dtype=np.float32).astype(ml_dtypes.bfloat16)


def _const_tables():
    pos = (np.arange(TP, dtype=np.float32) - np.float32(PAD))
    angle = (np.float32(10000.0) ** (-np.linspace(0.0, 1.0, 64, dtype=np.float32))).astype(np.float32)
    fr = (pos[None, :] * np.repeat(angle, 2)[:, None]).astype(np.float32)
    cosr = np.cos(fr).astype(np.float32)
    sgn = np.where(np.arange(128) % 2 == 0, -1.0, 1.0).astype(np.float32)[:, None]
    sinr = (np.sin(fr) * sgn).astype(np.float32)
    r = 16
    inv = (np.float32(500000.0) ** (-np.arange(0, r, 2, dtype=np.float32) / r)).astype(np.float32)
    cosd = np.ones((128, TP), np.float32)
    sind = np.zeros((128, TP), np.float32)
    for p in range(128):
        dd = p % 64
        if dd < r:
            f = (pos * inv[dd % 8]).astype(np.float32)
            cosd[p] = np.cos(f)
            sind[p] = np.sin(f) * (-1.0 if dd < 8 else 1.0)
    tabs = np.stack([cosr, sinr, cosd, sind]).astype(np.float32)
    cbm = np.zeros((128, CB_N), np.float32)
    cbm[:, CB_ONES:CB_ONES + 128] = 1.0
    cbm[PAD:, CB_ONESPAD:CB_ONESPAD + 128] = 1.0
    cbm[:, CB_IDENT:CB_IDENT + 128] = np.eye(128)
    for m in range(128):
        src = m + 1 if m % 2 == 0 else m - 1
        cbm[src, CB_PERMR + m] = 1.0
        dd = m % 64
        if dd < 8:
            cbm[m + 8, CB_PERMD + m] = 1.0
        elif dd < 16:
            cbm[m - 8, CB_PERMD + m] = 1.0
    jj = np.arange(128)[:, None]
    ii = np.arange(128)[None, :]
    cbm[:, CB_CMASK:CB_CMASK + 128] = np.where(jj <= ii, 0.0, -30000.0)
    cbm[:, CB_CENT:CB_CENT + 128] = np.eye(128) - 1.0 / 128.0
    cfm = np.zeros((128, CF_N), np.float32)
    cfm[:, CF_EPS] = EPS
    cfm[:, CF_ONE] = 1.0
    for l_ in range(2):
        om_ = 1.0 - (0.8 - 0.6 * math.exp(-0.3 * l_))
        cfm[:, CF_EPS + 1 + l_] = EPS / (om_ * om_)
    for h in range(4):
        lg = math.log(1.0 - 2.0 ** (-5.0 - h))
        rel = (ii - jj).astype(np.float64)
        dm = np.where(rel >= 0, np.exp(lg * np.maximum(rel, 0.0)), 0.0) * (128.0 ** -0.5)
        cfm[:, CF_DMASK + h * 128:CF_DMASK + (h + 1) * 128] = dm
        cfm[:, CF_QDEC + h * 128:CF_QDEC + (h + 1) * 128] = np.exp(lg * (np.arange(128) + 1.0))[None, :]
        cfm[:, CF_KDEC + h] = np.exp(lg * (127.0 - np.arange(128))) * (128.0 ** -0.5)
    return tabs, cbm, cfm


def _prep(inputs):
    f = lambda k: np.asarray(inputs[k], dtype=np.float32)
    tabs, cbm, cfm = _const_tables()
    cvec = lambda v: np.ascontiguousarray(v.reshape(-1, 128).T)
    norms = [f("ffn1_norm"), f("mix_norm"), f("ffn2_norm")]
    for l in range(2):
        for w in range(3):
            c0 = CF_PVEC + PV_NORM + (l * 3 + w) * 8
            cfm[:, c0:c0 + 8] = cvec(norms[w][l])
        cfm[:, CF_PVEC + PV_GN + l * 4:CF_PVEC + PV_GN + l * 4 + 4] = cvec(f("ret_gn_w")[l])
        cfm[:, CF_PVEC + PV_SW + l] = f("diff_subln_w")[l]
    cfm[:, CF_PVEC + PV_NORM + 48:CF_PVEC + PV_NORM + 56] = cvec(f("final_norm"))
    lam = np.stack([f("diff_lambda_q1"), f("diff_lambda_k1"), f("diff_lambda_q2"), f("diff_lambda_k2")], axis=1)
    lam = np.ascontiguousarray(np.broadcast_to(lam.reshape(1, 512), (128, 512)))

    def fm_tile(w):
        nc_ = w.shape[1] // 128
        return w.reshape(8, 128, nc_, 128).transpose(2, 1, 0, 3)

    wgu = np.empty((2, 2, NF, 128, 2, 8, 128), np.float32)
    wd = np.empty((2, 2, 2, 8, 128, NFH, 128), np.float32)
    for l in range(2):
        for j, (kg, ku, kd_) in enumerate([("ffn1_w_gate", "ffn1_w_up", "ffn1_w_down"), ("ffn2_w_gate", "ffn2_w_up", "ffn2_w_down")]):
            wgu[l, j, :, :, 0] = fm_tile(f(kg)[l])
            wgu[l, j, :, :, 1] = fm_tile(f(ku)[l])
            wd[l, j] = f(kd_)[l].reshape(2, NFH, 128, 8, 128).transpose(0, 3, 2, 1, 4)
    win = f("w_in")
    wmix = np.zeros((2, 8, 2, 128, 2, 8, 128), np.float32)
    for l in range(2):
        t = fm_tile(win[l])
        for h in range(4):
            wmix[l, h, 0, :, 0] = t[h]
            wmix[l, h, 0, :, 1] = t[4 + h]
            wmix[l, h, 1, :, 0] = t[8 + h]
            wmix[l, h, 1, :, 1] = t[12 + h]
            wmix[l, 4 + h, 0, :, 0] = t[16 + h]
            wmix[l, 4 + h, 0, :, 1] = t[20 + h]
            wmix[l, 4 + h, 1, :, 0] = t[24 + h]
    wout = np.stack([fm_tile(f("w_out")[l]) for l in range(2)])
    shared = {
        "metaT": np.ascontiguousarray(f("meta_tokens").reshape(NMETA, 8, 128).transpose(2, 1, 0)),
        "cb": _bf(cbm), "cf": cfm, "lam": lam, "tabs": tabs,
        "wgu": wgu.reshape(2, 2, NF, 128, 2048), "wd": wd.reshape(2, 2, 2, 8, 128, NFH * 128),
        "wmix": wmix.reshape(2, 8, 2, 128, 2048), "wout": np.ascontiguousarray(wout.reshape(2, 8, 128, 1024)),
    }
    x = f("x")
    xTs = [np.ascontiguousarray(x[b].reshape(SEQ, 8, 128).transpose(2, 1, 0)) for b in range(x.shape[0])]
    return shared, xTs


def run(inputs, cores=None, stop=None, dump=None, trace=False):
    shared, xTs = _prep(inputs)
    cores = list(range(8)) if cores is None else cores
    nc, P = build_program(stop=stop, dump=dump)
    in_maps = [dict(shared, xT=xTs[b]) for b in cores]
    res = run_bass_kernel_spmd(nc, in_maps, core_ids=list(range(len(cores))), trace=trace)
    return res, P


def kernel(**inputs):
    res, _ = run(inputs)
    outs = [np.asarray(r["outT"], dtype=np.float32) for r in res.results]
    out = np.stack([o.transpose(2, 1, 0).reshape(SEQ, D) for o in outs])
    return np.ascontiguousarray(out.astype(np.float32))
```

```python
import math
import numpy as np
import ml_dtypes
import concourse.bass as bass
import concourse.mybir as mybir
from concourse.bass_utils import run_bass_kernel_spmd

F32 = mybir.dt.float32
BF16 = mybir.dt.bfloat16
AF = mybir.ActivationFunctionType
ALU = mybir.AluOpType

D = 1024
SEQ = 2048
NMETA = 16
PAD = 112
TP = 2176
NCH = 17
DFF = 2816
NF = 22
NFH = 11
EPS = 1e-6
GROUPS = [(0, 128)] + [(128 + 512 * g, 512) for g in range(4)]
NSLOT = 11
NW = 4
WCOLS = 2048
SAME_ENGINE_SYNC = True
DBG_HACK = 0
WARM = 0
ROT_ADD_ENG = 'dve'
DBG_MIX = None

CB_ONES, CB_ONESPAD, CB_IDENT, CB_PERMR, CB_PERMD, CB_CMASK, CB_CENT = [i * 128 for i in range(7)]
CB_N = 7 * 128
CF_DMASK = 0
CF_QDEC = 512
CF_KDEC = 1024
CF_PVEC = 1028
PV_NORM = 0
PV_GN = 56
PV_SW = 64
CF_EPS = CF_PVEC + 66
CF_ONE = CF_EPS + 3
CF_N = CF_ONE + 1


class Prog:
    def __init__(self, nc):
        self.nc = nc
        self.ops = []
        self.dma_sems = {}

    def op(self, eng, fn, reads=(), writes=()):
        self.ops.append(dict(eng=eng, fn=fn, reads=list(reads), writes=list(writes), dma=None))

    def dma(self, eng, fn, slot, reads=(), writes=()):
        self.ops.append(dict(eng=eng, fn=fn, reads=list(reads), writes=list(writes), dma=slot))

    def build(self):
        nc = self.nc
        ops = self.ops
        engs = ['pe', 'act', 'dve', 'pool', 'sp']
        last_w = {}
        readers = {}
        for i, o in enumerate(ops):
            deps = set()
            key = ('dma', i) if o['dma'] is not None else o['eng']
            for r in o['reads']:
                if r in last_w:
                    deps.add(last_w[r])
                if r[0] in ('ps', 'pt'):
                    for k2, i2 in readers.get(r, {}).items():
                        if k2 != key:
                            deps.add(i2)
            for w in o['writes']:
                if w in last_w:
                    deps.add(last_w[w])
                rd = readers.get(w)
                if rd:
                    deps.update(rd.values())
            deps.discard(i)
            o['deps'] = deps
            for r in o['reads']:
                readers.setdefault(r, {})[key] = i
            for w in o['writes']:
                last_w[w] = i
                readers[w] = {}
        signal = set()
        for i, o in enumerate(ops):
            for d in o['deps']:
                p = ops[d]
                if p['dma'] is not None:
                    continue
                if p['eng'] != o['eng'] or (SAME_ENGINE_SYNC and o['eng'] != 'pe'):
                    signal.add(d)
        sem = {e: nc.alloc_semaphore('s_' + e) for e in engs}
        cnt = {e: 0 for e in engs}
        for i, o in enumerate(ops):
            if o['dma'] is not None:
                slot = o['dma']
                if slot not in self.dma_sems:
                    self.dma_sems[slot] = [nc.alloc_semaphore('d_' + str(slot)), 0]
                ent = self.dma_sems[slot]
                ent[1] += 16
                o['tok'] = (ent[0], ent[1])
            elif i in signal:
                cnt[o['eng']] += 1
                o['tok'] = (sem[o['eng']], cnt[o['eng']])
            else:
                o['tok'] = None
        waited = {e: {} for e in engs}
        for i, o in enumerate(ops):
            need = {}
            for d in o['deps']:
                p = ops[d]
                if p['dma'] is None and p['eng'] == o['eng'] and not (SAME_ENGINE_SYNC and o['eng'] != 'pe'):
                    continue
                tok = p['tok']
                assert tok is not None
                k = tok[0]
                if need.get(k, (None, 0))[1] < tok[1]:
                    need[k] = tok
            ws = []
            for k, tok in need.items():
                if waited[o['eng']].get(k, 0) < tok[1]:
                    waited[o['eng']][k] = tok[1]
                    ws.append(tok)
            o['waits'] = ws
        per = {e: [o for o in ops if o['eng'] == e] for e in engs}
        self.stats = {e: len(per[e]) for e in engs}
        self.stats['signals'] = dict(cnt)

        def run(e, lst):
            for o in lst:
                for (s, v) in o['waits']:
                    e.wait_ge(s, v)
                if o['fn'] is None:
                    continue
                ins = o['fn'](e)
                if o['tok'] is not None:
                    ins.then_inc(o['tok'][0], 16 if o['dma'] is not None else 1)

        with nc.Block() as block:
            @block.tensor
            def _(e):
                run(e, per['pe'])

            @block.scalar
            def _(e):
                run(e, per['act'])

            @block.vector
            def _(e):
                run(e, per['dve'])

            @block.gpsimd
            def _(e):
                run(e, per['pool'])

            @block.sync
            def _(e):
                run(e, per['sp'])


def build_program(stop=None, dump=None):
    nc = bass.Bass("TRN2", target_bir_lowering=False)
    P = Prog(nc)

    def din(name, shape, dt=F32):
        return nc.dram_tensor(name, list(shape), dt, kind="ExternalInput").ap()

    xT_d = din("xT", [128, 8, SEQ])
    meta_d = din("metaT", [128, 8, NMETA])
    cb_d = din("cb", [128, CB_N], BF16)
    cf_d = din("cf", [128, CF_N])
    lam_d = din("lam", [128, 2 * 4 * 64])
    tabs_d = din("tabs", [4, 128, TP])
    wgu_d = din("wgu", [2, 2, NF, 128, 2048])
    wd_d = din("wd", [2, 2, 2, 8, 128, NFH * 128])
    wmix_d = din("wmix", [2, 8, 2, 128, 2048])
    wout_d = din("wout", [2, 8, 128, 1024])
    out_d = nc.dram_tensor("outT", [128, 8, SEQ], F32, kind="ExternalOutput").ap()
    dbg_d = None
    if dump is not None:
        dbg_d = nc.dram_tensor("dbg", [128, 8, TP], F32, kind="ExternalOutput").ap()

    sb = nc.alloc_sbuf_tensor
    hT = sb("hT", [128, 8, TP], F32)
    xnT = sb("xnT", [128, 8, TP], BF16)
    SL = sb("slots", [128, NSLOT, TP], BF16)
    wsl = [sb(f"wsl{i}", [128, WCOLS], BF16) for i in range(NW)]
    tabt = [sb(f"tab{i}", [128, 2, 512], F32) for i in range(2)]
    cb = sb("cbs", [128, CB_N], BF16)
    cf = sb("cfs", [128, CF_N], F32)
    lamin = sb("lamin", [128, 512], F32)
    lamw = sb("lamw", [128, 64], F32)
    lamv = sb("lamv", [128, 8], F32)
    NR = 3
    sqr = [sb(f"sq{i}", [128, 512], BF16) for i in range(NR)]
    fa = [sb(f"fa{i}", [128, 512], F32) for i in range(4)]
    fb = [sb(f"fb{i}", [128, 512], F32) for i in range(4)]
    ba = [sb(f"ba{i}", [128, 512], BF16) for i in range(6)]
    Rf = [sb(f"Rf{i}", [128, 128], F32) for i in range(2)]

    PS = [nc.alloc_psum_tensor(f"ps{i}", [128, 512], F32) for i in range(7)]
    PT = nc.alloc_psum_tensor("pst", [128, 1024], BF16)

    SBANK = [(PS[0], ('ps', 0)), (PS[1], ('ps', 1)), (PT[:, :].bitcast(F32), ('pt',))]
    ctr = {}

    def rr(name, n):
        v = ctr.get(name, 0)
        ctr[name] = v + 1
        return v % n

    def ring(name, lst):
        i = rr(name, len(lst))
        return lst[i], (name, i)

    ones = cb[:, CB_ONES:CB_ONES + 128]
    ones_pad = cb[:, CB_ONESPAD:CB_ONESPAD + 128]
    ident = cb[:, CB_IDENT:CB_IDENT + 128]
    cmask = cb[:, CB_CMASK:CB_CMASK + 128]
    RC = ('const',)
    epsc = cf[:, CF_EPS:CF_EPS + 1]
    onec = cf[:, CF_ONE:CF_ONE + 1]
    epsl = [cf[:, CF_EPS + 1 + l_:CF_EPS + 2 + l_] for l_ in range(2)]

    P.dma('sp', lambda e: e.dma_start(out=cb[:, :], in_=cb_d), 'const', writes=[RC])
    P.dma('sp', lambda e: e.dma_start(out=cf[:, :], in_=cf_d), 'const', writes=[RC])
    P.dma('sp', lambda e: e.dma_start(out=lamin[:, :], in_=lam_d), 'const', writes=[RC])
    hregs = lambda c: [('h', c, g) for g in range(5)]
    P.op('dve', lambda e: e.memset(hT[:, :, 0:PAD], 0.0), writes=[('h', c, 0) for c in range(8)])
    for c in range(8):
        P.dma('sp', lambda e, c=c: e.dma_start(out=hT[:, c, 128:TP], in_=xT_d[:, c, :]), 'xin',
              writes=[('h', c, g) for g in range(1, 5)])
    P.dma('sp', lambda e: e.dma_start(out=hT[:, :, PAD:128], in_=meta_d), 'xin',
          reads=[('h', c, 0) for c in range(8)], writes=[('h', c, 0) for c in range(8)])

    def wload(src_ap, ncols):
        i = rr('w', NW)
        t = wsl[i]
        reg = ('w', i)
        P.dma('pool', lambda e: e.dma_start(out=t[:, 0:ncols], in_=src_ap, max_dma_last_dim=8192),
              ('w', i), writes=[reg])
        return t, reg

    def tload(k, s, n):
        i = rr('tab', 2)
        t = tabt[i]
        regs = [('tab', i, 0), ('tab', i, 1)]
        P.dma('sp', lambda e: e.dma_start(out=t[:, 0, 0:n], in_=tabs_d[2 * k, :, s:s + n]), ('tab', i, 0), writes=[regs[0]])
        P.dma('sp', lambda e: e.dma_start(out=t[:, 1, 0:n], in_=tabs_d[2 * k + 1, :, s:s + n]), ('tab', i, 1), writes=[regs[1]])
        return t, regs

    def rmsnorm_group(pvcol, final, g, s, n):
        ss = PS[6]
        for c in range(8):
            sq, sqreg = ring('sq', sqr)
            P.op('act', lambda e, sq=sq, c=c: e.activation(out=sq[:, 0:n], in_=hT[:, c, s:s + n], func=AF.Square),
                 reads=[('h', c, g)], writes=[sqreg])
            P.op('pe', lambda e, sq=sq, c=c: e.matmul(ss[:, 0:n], lhsT=ones, rhs=sq[:, 0:n], start=(c == 0), stop=(c == 7)),
                 reads=[sqreg, RC], writes=[('ps', 6)])
        rt, rtreg = ring('fa', fa)
        P.op('act', lambda e: e.activation(out=rt[:, 0:n], in_=ss[:, 0:n], func=AF.Ln, scale=1.0 / D, bias=epsc),
             reads=[('ps', 6), RC], writes=[rtreg])
        rs, rsreg = ring('fb', fb)
        P.op('act', lambda e: e.activation(out=rs[:, 0:n], in_=rt[:, 0:n], func=AF.Exp, scale=-0.5), reads=[rtreg], writes=[rsreg])
        for c in range(8):
            gcol = cf[:, CF_PVEC + pvcol + c:CF_PVEC + pvcol + c + 1]
            dst = hT if final else xnT
            wreg_ = ('h', c, g) if final else ('xn', c, g)
            P.op('dve', lambda e, c=c, gcol=gcol, dst=dst: e.scalar_tensor_tensor(
                out=dst[:, c, s:s + n], in0=hT[:, c, s:s + n], scalar=gcol, in1=rs[:, 0:n], op0=ALU.mult, op1=ALU.mult),
                reads=[('h', c, g), rsreg, RC], writes=[wreg_])

    norm_queue = []

    def rmsnorm(pvcol, final=False, ahead=None):
        for g, (s, n) in enumerate(GROUPS):
            norm_queue.append(lambda g=g, s=s, n=n: rmsnorm_group(pvcol, final, g, s, n))
        for _ in range(len(GROUPS) if ahead is None else ahead):
            norm_pop()

    def norm_pop():
        if norm_queue:
            norm_queue.pop(0)()

    NM = 128 - PAD

    def resid_add(ps, psreg, c, g, scale):
        s, n = GROUPS[g]
        if g == 0:
            s, n = PAD, NM
        P.op('dve', lambda e: e.scalar_tensor_tensor(out=hT[:, c, s:s + n], in0=ps[:, 0:n], scalar=scale,
                                                      in1=hT[:, c, s:s + n], op0=ALU.mult, op1=ALU.add),
             reads=[psreg, ('h', c, g)], writes=[('h', c, g)])

    def ffn_gu_group(wt, wreg, fi, g, s, n):
        if g == 0:
            s, n = PAD, NM
        k = rr('ffn_gu', 2)
        gp, up = PS[k], PS[2 + k]
        for kc in range(8):
            P.op('pe', lambda e, kc=kc: e.matmul(gp[:, 0:n], lhsT=wt[:, kc * 128:(kc + 1) * 128],
                                               rhs=xnT[:, kc, s:s + n], start=(kc == 0), stop=(kc == 7)),
                 reads=[wreg, ('xn', kc, g)], writes=[('ps', k)])
        for kc in range(8):
            P.op('pe', lambda e, kc=kc: e.matmul(up[:, 0:n], lhsT=wt[:, 1024 + kc * 128:1024 + (kc + 1) * 128],
                                               rhs=xnT[:, kc, s:s + n], start=(kc == 0), stop=(kc == 7)),
                 reads=[wreg, ('xn', kc, g)], writes=[('ps', 2 + k)])
        sg, sgreg = ring('fa', fa)
        P.op('act', lambda e: e.activation(out=sg[:, 0:n], in_=gp[:, 0:n], func=AF.Silu),
             reads=[('ps', k)], writes=[sgreg])
        P.op('dve', lambda e: e.tensor_tensor(out=SL[:, fi, s:s + n], in0=up[:, 0:n], in1=sg[:, 0:n], op=ALU.mult),
             reads=[('ps', 2 + k), sgreg], writes=[('S', fi, g)])

    def mm_out_group(wt, wreg, nk, c, g, s, n, scale):
        if g == 0:
            s, n = PAD, NM
        k = 4 + rr('ffn_o', 2)
        op_ = PS[k]
        for fi in range(nk):
            P.op('pe', lambda e, fi=fi: e.matmul(op_[:, 0:n], lhsT=wt[:, fi * 128:(fi + 1) * 128],
                                               rhs=SL[:, fi, s:s + n], start=(fi == 0), stop=(fi == nk - 1)),
                 reads=[wreg, ('S', fi, g)], writes=[('ps', k)])
        resid_add(op_, ('ps', k), c, g, scale)

    def ffn(l, j):
        rmsnorm(PV_NORM + (l * 3 + (0 if j == 0 else 2)) * 8, ahead=2)
        for half in range(2):
            for fi in range(NFH):
                wt, wreg = wload(wgu_d[l, j, half * NFH + fi], 2048)
                for g, (s, n) in enumerate(GROUPS):
                    ffn_gu_group(wt, wreg, fi, g, s, n)
                    norm_pop()
            for c in range(8):
                wt, wreg = wload(wd_d[l, j, half, c], NFH * 128)
                for g, (s, n) in enumerate(GROUPS):
                    mm_out_group(wt, wreg, NFH, c, g, s, n, 0.5)

    tabs_for = {}

    rot_pending = []

    def project_rot_group(which, wt, wreg, dst_slot, permcol, g, split=None):
        perm = cb[:, permcol:permcol + 128]
        s, n = GROUPS[g]
        tt, tregs = tabs_for[g]
        pk = rr('proj', 2)
        pp, sp_ = PS[pk], PS[2 + pk]
        for kc in range(8):
            P.op('pe', lambda e, kc=kc: e.matmul(pp[:, 0:n], lhsT=wt[:, which * 1024 + kc * 128:which * 1024 + (kc + 1) * 128],
                                               rhs=xnT[:, kc, s:s + n], start=(kc == 0), stop=(kc == 7)),
                 reads=[wreg, ('xn', kc, g)], writes=[('ps', pk)])
        qb, qbreg = ring('ba', ba)
        P.op('act', lambda e: e.activation(out=qb[:, 0:n], in_=pp[:, 0:n], func=AF.Identity), reads=[('ps', pk)], writes=[qbreg])
        rot_flush()

        def second():
            P.op('pe', lambda e: e.matmul(sp_[:, 0:n], lhsT=perm, rhs=qb[:, 0:n], start=True, stop=True),
                 reads=[qbreg, RC], writes=[('ps', 2 + pk)])
            t1, t1reg = ring('fa', fa)
            t2, t2reg = ring('fb', fb)
            P.op('dve', lambda e: e.tensor_tensor(out=t1[:, 0:n], in0=pp[:, 0:n], in1=tt[:, 0, 0:n], op=ALU.mult),
                 reads=[('ps', pk)] + tregs, writes=[t1reg])
            P.op('dve', lambda e: e.tensor_tensor(out=t2[:, 0:n], in0=sp_[:, 0:n], in1=tt[:, 1, 0:n], op=ALU.mult),
                 reads=[('ps', 2 + pk)] + tregs, writes=[t2reg])
            if split is None:
                P.op(ROT_ADD_ENG, lambda e: e.tensor_tensor(out=SL[:, dst_slot, s:s + n], in0=t1[:, 0:n], in1=t2[:, 0:n], op=ALU.add),
                     reads=[t1reg, t2reg], writes=[('S', dst_slot, g)])
            else:
                for m in range(2):
                    P.op('dve', lambda e, m=m: e.tensor_tensor(out=SL[64 * m:64 * m + 64, split[m], s:s + n], in0=t1[64 * m:64 * m + 64, 0:n],
                                                             in1=t2[64 * m:64 * m + 64, 0:n], op=ALU.add),
                         reads=[t1reg, t2reg], writes=[('S', split[m], g)])
        rot_pending.append(second)

    def rot_flush():
        while rot_pending:
            rot_pending.pop(0)()

    def project_v_group(wt, wreg, vslot, g, s, n):
        pk = rr('proj', 2)
        pp = PS[pk]
        for ci in range(n // 128):
            cs = s + ci * 128
            for kc in range(8):
                P.op('pe', lambda e, kc=kc, ci=ci, cs=cs: e.matmul(pp[:, ci * 128:(ci + 1) * 128], lhsT=xnT[:, kc, cs:cs + 128],
                                                                 rhs=wt[:, kc * 128:(kc + 1) * 128], start=(kc == 0), stop=(kc == 7)),
                     reads=[wreg, ('xn', kc, g)], writes=[('ps', pk)])
        P.op('act', lambda e: e.activation(out=SL[:, vslot, s:s + n], in_=pp[:, 0:n], func=AF.Identity),
             reads=[('ps', pk)], writes=[('S', vslot, g)])

    def project_v(wt, wreg, vslot):
        for g, (s, n) in enumerate(GROUPS):
            project_v_group(wt, wreg, vslot, g, s, n)

    def stats_rstd(src, srcreg, n, scale, bias_col, o0=0):
        sq, sqreg = ring('ba', ba)
        P.op('dve', lambda e: e.tensor_tensor(out=sq[:, o0:n], in0=src[:, o0:n], in1=src[:, o0:n], op=ALU.mult), reads=[srcreg], writes=[sqreg])
        P.op('pe', lambda e: e.matmul(PS[6][:, o0:n], lhsT=ones, rhs=sq[:, o0:n], start=True, stop=True),
             reads=[sqreg, RC], writes=[('ps', 6)])
        sd, sdreg = ring('fb', fb)
        P.op('act', lambda e: e.activation(out=sd[:, o0:n], in_=PS[6][:, o0:n], func=AF.Ln, scale=scale, bias=bias_col),
             reads=[('ps', 6), RC], writes=[sdreg])
        rs, rsreg = ring('fb', fb)
        P.op('act', lambda e: e.activation(out=rs[:, o0:n], in_=sd[:, o0:n], func=AF.Exp, scale=-0.5), reads=[sdreg], writes=[rsreg])
        return rs, rsreg

    def retention_attention(l, h, u, ks, vs, wvg, wvgreg):
        gam = 1.0 - 2.0 ** (-5.0 - h)
        cd = gam ** 128
        dmask = cf[:, CF_DMASK + h * 128:CF_DMASK + (h + 1) * 128]
        qdec = cf[:, CF_QDEC + h * 128:CF_QDEC + (h + 1) * 128]
        kdec = cf[:, CF_KDEC + h:CF_KDEC + h + 1]
        gnw = cf[:, CF_PVEC + PV_GN + l * 4 + h:CF_PVEC + PV_GN + l * 4 + h + 1]
        RBs = 18 - ks
        grp = lambda n: 0 if n == 0 else 1 + (n - 1) // 4
        P.op('dve', lambda e: e.memset(SL[:, RBs, 0:128], 0.0), writes=[('S', RBs, 0)])
        P.op('dve', lambda e: e.memset(Rf[0][:, :], 0.0), writes=[('Rf', 0)])
        def emit_gate(g, bank=6):
            s, n = GROUPS[g]
            gp, greg = (PTF, ('pt',)) if bank == 'ptf' else (PS[bank], ('ps', bank))
            for kc in range(8):
                P.op('pe', lambda e, kc=kc: e.matmul(gp[:, 0:n], lhsT=wvg[:, 1024 + kc * 128:1024 + (kc + 1) * 128],
                                                   rhs=xnT[:, kc, s:s + n], start=(kc == 0), stop=(kc == 7)),
                     reads=[wvgreg, ('xn', kc, g)], writes=[greg])
            sg, sgreg = ring('fb', fb)
            P.op('act', lambda e: e.activation(out=sg[:, 0:n], in_=gp[:, 0:n], func=AF.Exp, scale=-1.0), reads=[greg], writes=[sgreg])
            P.op('act', lambda e: e.activation(out=sg[:, 0:n], in_=sg[:, 0:n], func=AF.Ln, bias=onec), reads=[sgreg, RC], writes=[sgreg])
            P.op('act', lambda e: e.activation(out=sg[:, 0:n], in_=sg[:, 0:n], func=AF.Exp, scale=-1.0), reads=[sgreg], writes=[sgreg])
            gs, gsreg = ring('sq', sqr)
            P.op('dve', lambda e: e.tensor_tensor(out=gs[:, 0:n], in0=gp[:, 0:n], in1=sg[:, 0:n], op=ALU.mult),
                 reads=[greg, sgreg], writes=[gsreg])
            gate[g] = (gs, gsreg)

        for r in range(4):
            for i in range(4):
                n = 4 * r + i
                P.op('pe', lambda e, n=n, i=i: e.transpose(PT[:, i * 128:(i + 1) * 128], SL[:, ks, n * 128:(n + 1) * 128], ident),
                     reads=[('S', ks, grp(n)), RC], writes=[('pt',)])
            kd, kdreg = ring('ba', ba)
            P.op('act', lambda e, kd=kd: e.activation(out=kd[:, 0:512], in_=PT[:, 0:512], func=AF.Identity, scale=kdec),
                 reads=[('pt',), RC], writes=[kdreg])
            for i in range(4):
                n = 4 * r + i
                P.op('pe', lambda e, n=n, i=i, r=r, kd=kd: e.matmul(PS[r][:, i * 128:(i + 1) * 128], lhsT=kd[:, i * 128:(i + 1) * 128],
                                                                  rhs=SL[:, vs, n * 128:(n + 1) * 128], start=True, stop=True),
                     reads=[kdreg, ('S', vs, grp(n))], writes=[('ps', r)])
        gate = {}
        for n in range(16):
            if n in (0, 5, 10):
                emit_gate(n // 5, (6, 4, 5)[n // 5])
            r, i = divmod(n, 4)
            P.op('dve', lambda e, n=n, r=r, i=i: e.scalar_tensor_tensor(out=Rf[(n + 1) % 2][:, :], in0=Rf[n % 2][:, :], scalar=cd,
                                                                      in1=PS[r][:, i * 128:(i + 1) * 128], op0=ALU.mult, op1=ALU.add),
                 reads=[('Rf', n % 2), ('ps', r)], writes=[('Rf', (n + 1) % 2)])
            P.op('act', lambda e, n=n: e.activation(out=SL[:, RBs, (n + 1) * 128:(n + 2) * 128], in_=Rf[(n + 1) % 2][:, :], func=AF.Identity),
                 reads=[('Rf', (n + 1) % 2)], writes=[('S', RBs, grp(n + 1))])
        def emit_ST(g):
            s, n = GROUPS[g]
            ncx = n // 128
            bk = g % 2
            for ci in range(ncx):
                cs = s + ci * 128
                P.op('pe', lambda e, ci=ci, cs=cs: e.matmul(PS[bk][:, ci * 128:(ci + 1) * 128], lhsT=SL[:, ks, cs:cs + 128],
                                                          rhs=SL[:, u, cs:cs + 128], start=True, stop=True),
                     reads=[('S', u, g), ('S', ks, g)], writes=[('ps', bk)])
            v3 = lambda ap: ap.rearrange("p (c i) -> p c i", c=ncx)
            bc = lambda ap: ap.unsqueeze(1).broadcast_to([128, ncx, 128])
            stm, stmreg = ring('ba', ba)
            P.op('dve', lambda e: e.tensor_tensor(out=v3(stm[:, 0:n]), in0=v3(PS[bk][:, 0:n]), in1=bc(dmask), op=ALU.mult),
                 reads=[('ps', bk), RC], writes=[stmreg])
            qd, qdreg = ring('ba', ba)
            P.op('dve', lambda e: e.tensor_tensor(out=v3(qd[:, 0:n]), in0=v3(SL[:, u, s:s + n]), in1=bc(qdec), op=ALU.mult),
                 reads=[('S', u, g), RC], writes=[qdreg])
            return stm, stmreg, qd, qdreg

        def emit_O(g, st):
            stm, stmreg, qd, qdreg = st
            s, n = GROUPS[g]
            OBk = 4 + g % 2
            OB = PS[OBk]
            for ci in range(n // 128):
                cs = s + ci * 128
                P.op('pe', lambda e, ci=ci, cs=cs: e.matmul(OB[:, ci * 128:(ci + 1) * 128], lhsT=SL[:, vs, cs:cs + 128],
                                                          rhs=stm[:, ci * 128:(ci + 1) * 128], start=True, stop=False),
                     reads=[('S', vs, g), stmreg], writes=[('ps', OBk)])
                P.op('pe', lambda e, ci=ci, cs=cs: e.matmul(OB[:, ci * 128:(ci + 1) * 128], lhsT=SL[:, RBs, cs:cs + 128],
                                                          rhs=qd[:, ci * 128:(ci + 1) * 128], start=False, stop=True),
                     reads=[('S', RBs, g), qdreg], writes=[('ps', OBk)])
            post.append([1, lambda: fin1(g, OBk)])
            tick()

        post = []
        PTF = PT[:, :].bitcast(F32)

        def tick(force=False):
            for ent in list(post):
                ent[0] -= 1
                if force or ent[0] <= 0:
                    post.remove(ent)
                    ent[1]()

        def fin1(g, OBk):
            s, n = GROUPS[g]
            OB = PS[OBk]
            cent = cb[:, CB_CENT:CB_CENT + 128]
            ck = 2 + g % 2
            ob, obreg = ring('ba', ba)
            P.op('act', lambda e: e.activation(out=ob[:, 0:n], in_=OB[:, 0:n], func=AF.Identity), reads=[('ps', OBk)], writes=[obreg])
            P.op('pe', lambda e: e.matmul(PS[ck][:, 0:n], lhsT=cent, rhs=ob[:, 0:n], start=True, stop=True),
                 reads=[obreg, RC], writes=[('ps', ck)])
            post.append([1, lambda: fin2(g, ck)])

        def fin2(g, ck):
            s, n = GROUPS[g]
            csq, csqreg = ring('ba', ba)
            P.op('act', lambda e: e.activation(out=csq[:, 0:n], in_=PS[ck][:, 0:n], func=AF.Square), reads=[('ps', ck)], writes=[csqreg])
            P.op('pe', lambda e: e.matmul(PS[6][:, 0:n], lhsT=ones, rhs=csq[:, 0:n], start=True, stop=True),
                 reads=[csqreg, RC], writes=[('ps', 6)])
            post.append([1, lambda: fin3(g, ck)])

        def fin3(g, ck):
            s, n = GROUPS[g]
            rs, rsreg = ring('fb', fb)
            P.op('act', lambda e: e.activation(out=rs[:, 0:n], in_=PS[6][:, 0:n], func=AF.Ln, scale=1.0 / 128, bias=epsc),
                 reads=[('ps', 6), RC], writes=[rsreg])
            P.op('act', lambda e: e.activation(out=rs[:, 0:n], in_=rs[:, 0:n], func=AF.Exp, scale=-0.5), reads=[rsreg], writes=[rsreg])
            fin4(g, ck, rs, rsreg)

        def fin4(g, ck, rs, rsreg):
            s, n = GROUPS[g]
            nrm, nrmreg = ring('fa', fa)
            P.op('dve', lambda e: e.scalar_tensor_tensor(out=nrm[:, 0:n], in0=PS[ck][:, 0:n], scalar=gnw, in1=rs[:, 0:n],
                                                          op0=ALU.mult, op1=ALU.mult),
                 reads=[('ps', ck), rsreg, RC], writes=[nrmreg])
            gs, gsreg = gate[g]
            P.op('dve', lambda e: e.tensor_tensor(out=SL[:, u, s:s + n], in0=nrm[:, 0:n], in1=gs[:, 0:n], op=ALU.mult),
                 reads=[nrmreg, gsreg], writes=[('S', u, g)])
            done.add(g)

        done = set()

        pending = None
        need_gate = [3, 4]
        for g in range(5):
            if need_gate and (need_gate[0] - 3) in done:
                emit_gate(need_gate.pop(0), 'ptf')
            st = emit_ST(g)
            if pending is not None:
                emit_O(*pending)
            pending = (g, st)
        emit_O(*pending)
        while post or need_gate:
            if need_gate and (need_gate[0] - 3) in done:
                emit_gate(need_gate.pop(0), 'ptf')
            tick(force=True)

    def retention_unit(l, h):
        u = h
        ks = 8 + 2 * (u % 2)
        vs = 9
        wqk, wqkreg = wload(wmix_d[l, u, 0], 2048)
        wvg, wvgreg = wload(wmix_d[l, u, 1], 2048)
        for g, (s, n) in enumerate(GROUPS):
            tabs_for[g] = tload(0, s, n)
            project_rot_group(0, wqk, wqkreg, u, CB_PERMR, g)
            project_rot_group(1, wqk, wqkreg, ks, CB_PERMR, g)
            norm_pop()
        rot_flush()
        project_v(wvg, wvgreg, vs)
        if DBG_MIX == 'B':
            return
        retention_attention(l, h, u, ks, vs, wvg, wvgreg)

    def diff_attention(l, u, ks, vs, om):
        neglam = lamv[:, 4 + l:5 + l]
        sw = cf[:, CF_PVEC + PV_SW + l:CF_PVEC + PV_SW + l + 1]
        blocks = []
        for g, (s, n) in enumerate(GROUPS):
            last = (s + n) // 128 - 1
            for m in range(2):
                for jb in range(last + 1):
                    blocks.append((g, m, jb, last))
        odset = {}
        tres = {}
        deferred = []
        post = []

        def emit_S(blk):
            g, m, jb, last = blk
            s, n = GROUPS[g]
            off = max(0, jb * 128 - s)
            nq = n - off
            diag = jb * 128 >= s
            sk = rr('dst', 3)
            SP_, sreg = SBANK[sk]
            kg = 0 if jb == 0 else 1 + (jb - 1) // 4
            P.op('pe', lambda e: e.matmul(SP_[:, 0:nq], lhsT=SL[:, ks[m], jb * 128:(jb + 1) * 128],
                                          rhs=SL[:, u, s + off:s + n], start=True, stop=(not diag)),
                 reads=[('S', ks[m], kg), ('S', u, g)], writes=[sreg])
            if diag:
                P.op('pe', lambda e: e.matmul(SP_[:, 0:128], lhsT=ident, rhs=cmask, start=False, stop=True),
                     reads=[RC], writes=[sreg])
            return sk, off, nq, kg

        def emit_rest(blk, sinfo):
            g, m, jb, last = blk
            sk, off, nq, kg = sinfo
            SP_, sreg = SBANK[sk]
            if (g, m) not in odset:
                odset[(g, m)] = 2 + 2 * rr('dod', 2)
                while any(ent[2] == odset[(g, m)] for ent in post):
                    for ent in list(post):
                        if ent[2] == odset[(g, m)]:
                            post.remove(ent)
                            ent[1]()
            OBk = odset[(g, m)]
            DBk = OBk + 1
            OB, DB = PS[OBk], PS[DBk]
            pt, ptreg = ring('ba', ba)
            P.op('act', lambda e: e.activation(out=pt[:, 0:nq], in_=SP_[:, 0:nq], func=AF.Exp, scale=0.125),
                 reads=[sreg], writes=[ptreg])
            P.op('pe', lambda e: e.matmul(OB[:, off:off + nq], lhsT=SL[:, vs, jb * 128:(jb + 1) * 128], rhs=pt[:, 0:nq],
                                          start=(jb == 0), stop=(jb == last)),
                 reads=[('S', vs, kg), ptreg], writes=[('ps', OBk)])
            P.op('pe', lambda e: e.matmul(DB[:, off:off + nq], lhsT=(ones_pad if jb == 0 else ones), rhs=pt[:, 0:nq],
                                          start=(jb == 0), stop=(jb == last)),
                 reads=[RC, ptreg], writes=[('ps', DBk)])
            if jb == last:
                post.append([2, lambda: normalize_act(g, m, OBk), OBk])

        def normalize_act(g, m, OBk):
            DBk = OBk + 1
            DB = PS[DBk]
            s, n = GROUPS[g]
            o0 = PAD if g == 0 else 0
            r, rreg = ring('fb', fb)
            P.op('act', lambda e: e.activation(out=r[:, o0:n], in_=DB[:, o0:n], func=AF.Ln), reads=[('ps', DBk)], writes=[rreg])
            P.op('act', lambda e: e.activation(out=r[:, o0:n], in_=r[:, o0:n], func=AF.Exp, scale=-1.0), reads=[rreg], writes=[rreg])
            post.append([3, lambda: normalize_dve(g, m, OBk, r, rreg), OBk])

        def normalize_dve(g, m, OBk, r, rreg):
            OB = PS[OBk]
            s, n = GROUPS[g]
            o0 = PAD if g == 0 else 0
            t, treg_ = ring('fa', fa)
            P.op('dve', lambda e: e.tensor_tensor(out=t[:, o0:n], in0=OB[:, o0:n], in1=r[:, o0:n], op=ALU.mult),
                 reads=[('ps', OBk), rreg], writes=[treg_])
            tres[(g, m)] = (t, treg_)
            if m == 1:
                post.append([1, lambda: finalize1(g), None])

        def finalize1(g):
            s, n = GROUPS[g]
            o0 = PAD if g == 0 else 0
            (t0, t0reg), (t1, t1reg) = tres[(g, 0)], tres[(g, 1)]
            o, oreg = ring('fa', fa)
            P.op('dve', lambda e: e.scalar_tensor_tensor(out=o[:, o0:n], in0=t1[:, o0:n], scalar=neglam, in1=t0[:, o0:n],
                                                          op0=ALU.mult, op1=ALU.add),
                 reads=[t0reg, t1reg, ('lamv',)], writes=[oreg])
            sq, sqreg = ring('ba', ba)
            P.op('dve', lambda e: e.tensor_tensor(out=sq[:, o0:n], in0=o[:, o0:n], in1=o[:, o0:n], op=ALU.mult), reads=[oreg], writes=[sqreg])
            P.op('pe', lambda e: e.matmul(PS[6][:, o0:n], lhsT=ones, rhs=sq[:, o0:n], start=True, stop=True),
                 reads=[sqreg, RC], writes=[('ps', 6)])
            post.append([4, lambda: finalize2(g, o, oreg), None])

        def finalize2(g, o, oreg):
            s, n = GROUPS[g]
            o0 = PAD if g == 0 else 0
            rs, rsreg = ring('fb', fb)
            P.op('act', lambda e: e.activation(out=rs[:, o0:n], in_=PS[6][:, o0:n], func=AF.Ln, scale=1.0 / (128 * om * om), bias=epsl[l]),
                 reads=[('ps', 6), RC], writes=[rsreg])
            P.op('act', lambda e: e.activation(out=rs[:, o0:n], in_=rs[:, o0:n], func=AF.Exp, scale=-0.5), reads=[rsreg], writes=[rsreg])
            P.op('dve', lambda e: e.scalar_tensor_tensor(out=SL[:, u, s + o0:s + n], in0=o[:, o0:n], scalar=sw, in1=rs[:, o0:n],
                                                          op0=ALU.mult, op1=ALU.mult),
                 reads=[oreg, rsreg, RC], writes=[('S', u, g)])

        def tick(force=False):
            for ent in list(post):
                ent[0] -= 1
                if force or ent[0] <= 0:
                    post.remove(ent)
                    ent[1]()

        pend = []
        for blk in blocks:
            sinfo = emit_S(blk)
            pend.append((blk, sinfo))
            if len(pend) > 2:
                emit_rest(*pend.pop(0))
                tick()
        while pend:
            emit_rest(*pend.pop(0))
            tick()
        while post:
            tick(force=True)

    def diff_unit(l, h, lam_init):
        u = 4 + h
        ks = (8, 10)
        vs = 9
        wqk, wqkreg = wload(wmix_d[l, u, 0], 2048)
        wv, wvreg = wload(wmix_d[l, u, 1, :, 0:1024], 1024)
        if h == 0:
            P.op('dve', lambda e: e.memset(SL[64:128, ks[0], :], 0.0), writes=[('S', ks[0], g) for g in range(5)])
            P.op('dve', lambda e: e.memset(SL[0:64, ks[1], :], 0.0), writes=[('S', ks[1], g) for g in range(5)])
        for g, (s, n) in enumerate(GROUPS):
            tabs_for[g] = tload(1, s, n)
            project_rot_group(0, wqk, wqkreg, u, CB_PERMD, g)
            project_rot_group(1, wqk, wqkreg, None, CB_PERMD, g, split=ks)
        rot_flush()
        project_v(wv, wvreg, vs)
        diff_attention(l, u, ks, vs, 1.0 - lam_init)

    def lam_compute(l):
        lam_init = 0.8 - 0.6 * math.exp(-0.3 * l)
        for t in range(2):
            a = lamin[:, (l * 4 + 2 * t) * 64:(l * 4 + 2 * t + 1) * 64]
            b = lamin[:, (l * 4 + 2 * t + 1) * 64:(l * 4 + 2 * t + 2) * 64]
            P.op('dve', lambda e, a=a, b=b: e.tensor_tensor(out=lamw[:, :], in0=a, in1=b, op=ALU.mult),
                 reads=[RC, ('lamw',)], writes=[('lamw',)])
            P.op('dve', lambda e, t=t: e.tensor_reduce(out=lamv[:, t:t + 1], in_=lamw[:, :], axis=mybir.AxisListType.X, op=ALU.add),
                 reads=[('lamw',), ('lamv',)], writes=[('lamv',)])
            P.op('act', lambda e, t=t: e.activation(out=lamv[:, 2 + t:3 + t], in_=lamv[:, t:t + 1], func=AF.Exp),
                 reads=[('lamv',)], writes=[('lamv',)])
        P.op('dve', lambda e: e.scalar_tensor_tensor(out=lamv[:, 4 + l:5 + l], in0=lamv[:, 3:4], scalar=-lam_init, in1=lamv[:, 2:3],
                                                      op0=ALU.add, op1=ALU.subtract),
             reads=[('lamv',)], writes=[('lamv',)])
        return lam_init

    def mixer(l):
        rmsnorm(PV_NORM + (l * 3 + 1) * 8, ahead=2)
        lam_init = lam_compute(l)
        if DBG_MIX == 'A':
            return
        for h in range(4):
            retention_unit(l, h)
            if DBG_MIX in ('B', 'B1', 'C', 'R1', 'R2', 'R3'):
                return
        if DBG_MIX == 'D':
            return
        for h in range(4):
            diff_unit(l, h, lam_init)
            if DBG_MIX == 'F':
                return
        for c in range(8):
            wt, wreg = wload(wout_d[l, c], 1024)
            for g, (s, n) in enumerate(GROUPS):
                mm_out_group(wt, wreg, 8, c, g, s, n, 1.0)

    stages = []
    for l in range(2):
        stages += [('ffn', l, 0), ('mix', l), ('ffn', l, 1)]
    nst = len(stages) if stop is None else stop
    for st in stages[:nst]:
        if st[0] == 'ffn':
            ffn(st[1], st[2])
        else:
            mixer(st[1])
    outregs = []
    if stop is None:
        for g, (s, n) in enumerate(GROUPS):
            rmsnorm_group(PV_NORM + 48, True, g, s, n)
            if g >= 1:
                P.dma('sp', lambda e, s=s, n=n: e.dma_start(out=out_d[:, :, s - 128:s - 128 + n], in_=hT[:, :, s:s + n]), 'out',
                      reads=[('h', c, g) for c in range(8)], writes=[('out', g)])
                outregs.append(('out', g))
    else:
        for c in range(8):
            P.dma('sp', lambda e, c=c: e.dma_start(out=out_d[:, c, :], in_=hT[:, c, 128:TP]), 'out',
                  reads=[('h', c, g) for g in range(1, 5)], writes=[('out', c)])
            outregs.append(('out', c))
    if dump == 'h':
        for c in range(8):
            P.dma('sp', lambda e, c=c: e.dma_start(out=dbg_d[:, c, :], in_=hT[:, c, :]), 'out',
                  reads=[('h', c, g) for g in range(5)], writes=[('dbg', c)])
            outregs.append(('dbg', c))
    P.op('sp', None, reads=outregs)
    P.build()
    return nc, P


def _bf(a):
    return np.asarray(a, dtype=np.float32).astype(ml_dtypes.bfloat16)


def _const_tables():
    pos = (np.arange(TP, dtype=np.float32) - np.float32(PAD))
    angle = (np.float32(10000.0) ** (-np.linspace(0.0, 1.0, 64, dtype=np.float32))).astype(np.float32)
    fr = (pos[None, :] * np.repeat(angle, 2)[:, None]).astype(np.float32)
    cosr = np.cos(fr).astype(np.float32)
    sgn = np.where(np.arange(128) % 2 == 0, -1.0, 1.0).astype(np.float32)[:, None]
    sinr = (np.sin(fr) * sgn).astype(np.float32)
    r = 16
    inv = (np.float32(500000.0) ** (-np.arange(0, r, 2, dtype=np.float32) / r)).astype(np.float32)
    cosd = np.ones((128, TP), np.float32)
    sind = np.zeros((128, TP), np.float32)
    for p in range(128):
        dd = p % 64
        if dd < r:
            f = (pos * inv[dd % 8]).astype(np.float32)
            cosd[p] = np.cos(f)
            sind[p] = np.sin(f) * (-1.0 if dd < 8 else 1.0)
    tabs = np.stack([cosr, sinr, cosd, sind]).astype(np.float32)
    cbm = np.zeros((128, CB_N), np.float32)
    cbm[:, CB_ONES:CB_ONES + 128] = 1.0
    cbm[PAD:, CB_ONESPAD:CB_ONESPAD + 128] = 1.0
    cbm[:, CB_IDENT:CB_IDENT + 128] = np.eye(128)
    for m in range(128):
        src = m + 1 if m % 2 == 0 else m - 1
        cbm[src, CB_PERMR + m] = 1.0
        dd = m % 64
        if dd < 8:
            cbm[m + 8, CB_PERMD + m] = 1.0
        elif dd < 16:
            cbm[m - 8, CB_PERMD + m] = 1.0
    jj = np.arange(128)[:, None]
    ii = np.arange(128)[None, :]
    cbm[:, CB_CMASK:CB_CMASK + 128] = np.where(jj <= ii, 0.0, -30000.0)
    cbm[:, CB_CENT:CB_CENT + 128] = np.eye(128) - 1.0 / 128.0
    cfm = np.zeros((128, CF_N), np.float32)
    cfm[:, CF_EPS] = EPS
    cfm[:, CF_ONE] = 1.0
    for l_ in range(2):
        om_ = 1.0 - (0.8 - 0.6 * math.exp(-0.3 * l_))
        cfm[:, CF_EPS + 1 + l_] = EPS / (om_ * om_)
    for h in range(4):
        lg = math.log(1.0 - 2.0 ** (-5.0 - h))
        rel = (ii - jj).astype(np.float64)
        dm = np.where(rel >= 0, np.exp(lg * np.maximum(rel, 0.0)), 0.0) * (128.0 ** -0.5)
        cfm[:, CF_DMASK + h * 128:CF_DMASK + (h + 1) * 128] = dm
        cfm[:, CF_QDEC + h * 128:CF_QDEC + (h + 1) * 128] = np.exp(lg * (np.arange(128) + 1.0))[None, :]
        cfm[:, CF_KDEC + h] = np.exp(lg * (127.0 - np.arange(128))) * (128.0 ** -0.5)
    return tabs, cbm, cfm


def _prep(inputs):
    f = lambda k: np.asarray(inputs[k], dtype=np.float32)
    tabs, cbm, cfm = _const_tables()
    cvec = lambda v: np.ascontiguousarray(v.reshape(-1, 128).T)
    norms = [f("ffn1_norm"), f("mix_norm"), f("ffn2_norm")]
    for l in range(2):
        for w in range(3):
            c0 = CF_PVEC + PV_NORM + (l * 3 + w) * 8
            cfm[:, c0:c0 + 8] = cvec(norms[w][l])
        cfm[:, CF_PVEC + PV_GN + l * 4:CF_PVEC + PV_GN + l * 4 + 4] = cvec(f("ret_gn_w")[l])
        cfm[:, CF_PVEC + PV_SW + l] = f("diff_subln_w")[l]
    cfm[:, CF_PVEC + PV_NORM + 48:CF_PVEC + PV_NORM + 56] = cvec(f("final_norm"))
    lam = np.stack([f("diff_lambda_q1"), f("diff_lambda_k1"), f("diff_lambda_q2"), f("diff_lambda_k2")], axis=1)
    lam = np.ascontiguousarray(np.broadcast_to(lam.reshape(1, 512), (128, 512)))

    def fm_tile(w):
        nc_ = w.shape[1] // 128
        return w.reshape(8, 128, nc_, 128).transpose(2, 1, 0, 3)

    wgu = np.empty((2, 2, NF, 128, 2, 8, 128), np.float32)
    wd = np.empty((2, 2, 2, 8, 128, NFH, 128), np.float32)
    for l in range(2):
        for j, (kg, ku, kd_) in enumerate([("ffn1_w_gate", "ffn1_w_up", "ffn1_w_down"), ("ffn2_w_gate", "ffn2_w_up", "ffn2_w_down")]):
            wgu[l, j, :, :, 0] = fm_tile(f(kg)[l])
            wgu[l, j, :, :, 1] = fm_tile(f(ku)[l])
            wd[l, j] = f(kd_)[l].reshape(2, NFH, 128, 8, 128).transpose(0, 3, 2, 1, 4)
    win = f("w_in")
    wmix = np.zeros((2, 8, 2, 128, 2, 8, 128), np.float32)
    for l in range(2):
        t = fm_tile(win[l])
        for h in range(4):
            wmix[l, h, 0, :, 0] = t[h]
            wmix[l, h, 0, :, 1] = t[4 + h]
            wmix[l, h, 1, :, 0] = t[8 + h]
            wmix[l, h, 1, :, 1] = t[12 + h]
            wmix[l, 4 + h, 0, :, 0] = t[16 + h]
            wmix[l, 4 + h, 0, :, 1] = t[20 + h]
            wmix[l, 4 + h, 1, :, 0] = t[24 + h]
    wout = np.stack([fm_tile(f("w_out")[l]) for l in range(2)])
    shared = {
        "metaT": np.ascontiguousarray(f("meta_tokens").reshape(NMETA, 8, 128).transpose(2, 1, 0)),
        "cb": _bf(cbm), "cf": cfm, "lam": lam, "tabs": tabs,
        "wgu": wgu.reshape(2, 2, NF, 128, 2048), "wd": wd.reshape(2, 2, 2, 8, 128, NFH * 128),
        "wmix": wmix.reshape(2, 8, 2, 128, 2048), "wout": np.ascontiguousarray(wout.reshape(2, 8, 128, 1024)),
    }
    x = f("x")
    xTs = [np.ascontiguousarray(x[b].reshape(SEQ, 8, 128).transpose(2, 1, 0)) for b in range(x.shape[0])]
    return shared, xTs


def run(inputs, cores=None, stop=None, dump=None, trace=False):
    shared, xTs = _prep(inputs)
    cores = list(range(8)) if cores is None else cores
    nc, P = build_program(stop=stop, dump=dump)
    in_maps = [dict(shared, xT=xTs[b]) for b in cores]
    res = run_bass_kernel_spmd(nc, in_maps, core_ids=list(range(len(cores))), trace=trace)
    return res, P


def kernel(**inputs):
    res, _ = run(inputs)
    outs = [np.asarray(r["outT"], dtype=np.float32) for r in res.results]
    out = np.stack([o.transpose(2, 1, 0).reshape(SEQ, D) for o in outs])
    return np.ascontiguousarray(out.astype(np.float32))
```

```python
import math
import numpy as np
import ml_dtypes
import concourse.bass as bass
import concourse.mybir as mybir
from concourse.bass_utils import run_bass_kernel_spmd

F32 = mybir.dt.float32
BF16 = mybir.dt.bfloat16
AF = mybir.ActivationFunctionType
ALU = mybir.AluOpType

D = 1024
SEQ = 2048
NMETA = 16
PAD = 112
TP = 2176
NCH = 17
DFF = 2816
NF = 22
NFH = 11
EPS = 1e-6
GROUPS = [(0, 128)] + [(128 + 512 * g, 512) for g in range(4)]
NSLOT = 11
NW = 4
WCOLS = 2048
SAME_ENGINE_SYNC = True
DBG_HACK = 0
WARM = 0
ROT_ADD_ENG = 'dve'
DBG_MIX = None

CB_ONES, CB_ONESPAD, CB_IDENT, CB_PERMR, CB_PERMD, CB_CMASK, CB_CENT = [i * 128 for i in range(7)]
CB_N = 7 * 128
CF_DMASK = 0
CF_QDEC = 512
CF_KDEC = 1024
CF_PVEC = 1028
PV_NORM = 0
PV_GN = 56
PV_SW = 64
CF_EPS = CF_PVEC + 66
CF_ONE = CF_EPS + 3
CF_N = CF_ONE + 1


class Prog:
    def __init__(self, nc):
        self.nc = nc
        self.ops = []
        self.dma_sems = {}

    def op(self, eng, fn, reads=(), writes=()):
        self.ops.append(dict(eng=eng, fn=fn, reads=list(reads), writes=list(writes), dma=None))

    def dma(self, eng, fn, slot, reads=(), writes=()):
        self.ops.append(dict(eng=eng, fn=fn, reads=list(reads), writes=list(writes), dma=slot))

    def build(self):
        nc = self.nc
        ops = self.ops
        engs = ['pe', 'act', 'dve', 'pool', 'sp']
        last_w = {}
        readers = {}
        for i, o in enumerate(ops):
            deps = set()
            key = ('dma', i) if o['dma'] is not None else o['eng']
            for r in o['reads']:
                if r in last_w:
                    deps.add(last_w[r])
                if r[0] in ('ps', 'pt'):
                    for k2, i2 in readers.get(r, {}).items():
                        if k2 != key:
                            deps.add(i2)
            for w in o['writes']:
                if w in last_w:
                    deps.add(last_w[w])
                rd = readers.get(w)
                if rd:
                    deps.update(rd.values())
            deps.discard(i)
            o['deps'] = deps
            for r in o['reads']:
                readers.setdefault(r, {})[key] = i
            for w in o['writes']:
                last_w[w] = i
                readers[w] = {}
        signal = set()
        for i, o in enumerate(ops):
            for d in o['deps']:
                p = ops[d]
                if p['dma'] is not None:
                    continue
                if p['eng'] != o['eng'] or (SAME_ENGINE_SYNC and o['eng'] != 'pe'):
                    signal.add(d)
        sem = {e: nc.alloc_semaphore('s_' + e) for e in engs}
        cnt = {e: 0 for e in engs}
        for i, o in enumerate(ops):
            if o['dma'] is not None:
                slot = o['dma']
                if slot not in self.dma_sems:
                    self.dma_sems[slot] = [nc.alloc_semaphore('d_' + str(slot)), 0]
                ent = self.dma_sems[slot]
                ent[1] += 16
                o['tok'] = (ent[0], ent[1])
            elif i in signal:
                cnt[o['eng']] += 1
                o['tok'] = (sem[o['eng']], cnt[o['eng']])
            else:
                o['tok'] = None
        waited = {e: {} for e in engs}
        for i, o in enumerate(ops):
            need = {}
            for d in o['deps']:
                p = ops[d]
                if p['dma'] is None and p['eng'] == o['eng'] and not (SAME_ENGINE_SYNC and o['eng'] != 'pe'):
                    continue
                tok = p['tok']
                assert tok is not None
                k = tok[0]
                if need.get(k, (None, 0))[1] < tok[1]:
                    need[k] = tok
            ws = []
            for k, tok in need.items():
                if waited[o['eng']].get(k, 0) < tok[1]:
                    waited[o['eng']][k] = tok[1]
                    ws.append(tok)
            o['waits'] = ws
        per = {e: [o for o in ops if o['eng'] == e] for e in engs}
        self.stats = {e: len(per[e]) for e in engs}
        self.stats['signals'] = dict(cnt)

        def run(e, lst):
            for o in lst:
                for (s, v) in o['waits']:
                    e.wait_ge(s, v)
                if o['fn'] is None:
                    continue
                ins = o['fn'](e)
                if o['tok'] is not None:
                    ins.then_inc(o['tok'][0], 16 if o['dma'] is not None else 1)

        with nc.Block() as block:
            @block.tensor
            def _(e):
                run(e, per['pe'])

            @block.scalar
            def _(e):
                run(e, per['act'])

            @block.vector
            def _(e):
                run(e, per['dve'])

            @block.gpsimd
            def _(e):
                run(e, per['pool'])

            @block.sync
            def _(e):
                run(e, per['sp'])


def build_program(stop=None, dump=None):
    nc = bass.Bass("TRN2", target_bir_lowering=False)
    P = Prog(nc)

    def din(name, shape, dt=F32):
        return nc.dram_tensor(name, list(shape), dt, kind="ExternalInput").ap()

    xT_d = din("xT", [128, 8, SEQ])
    meta_d = din("metaT", [128, 8, NMETA])
    cb_d = din("cb", [128, CB_N], BF16)
    cf_d = din("cf", [128, CF_N])
    lam_d = din("lam", [128, 2 * 4 * 64])
    tabs_d = din("tabs", [4, 128, TP])
    wgu_d = din("wgu", [2, 2, NF, 128, 2048])
    wd_d = din("wd", [2, 2, 2, 8, 128, NFH * 128])
    wmix_d = din("wmix", [2, 8, 2, 128, 2048])
    wout_d = din("wout", [2, 8, 128, 1024])
    out_d = nc.dram_tensor("outT", [128, 8, SEQ], F32, kind="ExternalOutput").ap()
    dbg_d = None
    if dump is not None:
        dbg_d = nc.dram_tensor("dbg", [128, 8, TP], F32, kind="ExternalOutput").ap()

    sb = nc.alloc_sbuf_tensor
    hT = sb("hT", [128, 8, TP], F32)
    xnT = sb("xnT", [128, 8, TP], BF16)
    SL = sb("slots", [128, NSLOT, TP], BF16)
    wsl = [sb(f"wsl{i}", [128, WCOLS], BF16) for i in range(NW)]
    tabt = [sb(f"tab{i}", [128, 2, 512], F32) for i in range(2)]
    cb = sb("cbs", [128, CB_N], BF16)
    cf = sb("cfs", [128, CF_N], F32)
    lamin = sb("lamin", [128, 512], F32)
    lamw = sb("lamw", [128, 64], F32)
    lamv = sb("lamv", [128, 8], F32)
    NR = 3
    sqr = [sb(f"sq{i}", [128, 512], BF16) for i in range(NR)]
    fa = [sb(f"fa{i}", [128, 512], F32) for i in range(4)]
    fb = [sb(f"fb{i}", [128, 512], F32) for i in range(4)]
    ba = [sb(f"ba{i}", [128, 512], BF16) for i in range(6)]
    Rf = [sb(f"Rf{i}", [128, 128], F32) for i in range(2)]

    PS = [nc.alloc_psum_tensor(f"ps{i}", [128, 512], F32) for i in range(7)]
    PT = nc.alloc_psum_tensor("pst", [128, 1024], BF16)

    SBANK = [(PS[0], ('ps', 0)), (PS[1], ('ps', 1)), (PT[:, :].bitcast(F32), ('pt',))]
    ctr = {}

    def rr(name, n):
        v = ctr.get(name, 0)
        ctr[name] = v + 1
        return v % n

    def ring(name, lst):
        i = rr(name, len(lst))
        return lst[i], (name, i)

    ones = cb[:, CB_ONES:CB_ONES + 128]
    ones_pad = cb[:, CB_ONESPAD:CB_ONESPAD + 128]
    ident = cb[:, CB_IDENT:CB_IDENT + 128]
    cmask = cb[:, CB_CMASK:CB_CMASK + 128]
    RC = ('const',)
    epsc = cf[:, CF_EPS:CF_EPS + 1]
    onec = cf[:, CF_ONE:CF_ONE + 1]
    epsl = [cf[:, CF_EPS + 1 + l_:CF_EPS + 2 + l_] for l_ in range(2)]

    P.dma('sp', lambda e: e.dma_start(out=cb[:, :], in_=cb_d), 'const', writes=[RC])
    P.dma('sp', lambda e: e.dma_start(out=cf[:, :], in_=cf_d), 'const', writes=[RC])
    P.dma('sp', lambda e: e.dma_start(out=lamin[:, :], in_=lam_d), 'const', writes=[RC])
    hregs = lambda c: [('h', c, g) for g in range(5)]
    P.op('dve', lambda e: e.memset(hT[:, :, 0:PAD], 0.0), writes=[('h', c, 0) for c in range(8)])
    for c in range(8):
        P.dma('sp', lambda e, c=c: e.dma_start(out=hT[:, c, 128:TP], in_=xT_d[:, c, :]), 'xin',
              writes=[('h', c, g) for g in range(1, 5)])
    P.dma('sp', lambda e: e.dma_start(out=hT[:, :, PAD:128], in_=meta_d), 'xin',
          reads=[('h', c, 0) for c in range(8)], writes=[('h', c, 0) for c in range(8)])

    def wload(src_ap, ncols):
        i = rr('w', NW)
        t = wsl[i]
        reg = ('w', i)
        P.dma('pool', lambda e: e.dma_start(out=t[:, 0:ncols], in_=src_ap, max_dma_last_dim=8192),
              ('w', i), writes=[reg])
        return t, reg

    def tload(k, s, n):
        i = rr('tab', 2)
        t = tabt[i]
        regs = [('tab', i, 0), ('tab', i, 1)]
        P.dma('sp', lambda e: e.dma_start(out=t[:, 0, 0:n], in_=tabs_d[2 * k, :, s:s + n]), ('tab', i, 0), writes=[regs[0]])
        P.dma('sp', lambda e: e.dma_start(out=t[:, 1, 0:n], in_=tabs_d[2 * k + 1, :, s:s + n]), ('tab', i, 1), writes=[regs[1]])
        return t, regs

    def rmsnorm_group(pvcol, final, g, s, n):
        ss = PS[6]
        for c in range(8):
            sq, sqreg = ring('sq', sqr)
            P.op('act', lambda e, sq=sq, c=c: e.activation(out=sq[:, 0:n], in_=hT[:, c, s:s + n], func=AF.Square),
                 reads=[('h', c, g)], writes=[sqreg])
            P.op('pe', lambda e, sq=sq, c=c: e.matmul(ss[:, 0:n], lhsT=ones, rhs=sq[:, 0:n], start=(c == 0), stop=(c == 7)),
                 reads=[sqreg, RC], writes=[('ps', 6)])
        rt, rtreg = ring('fa', fa)
        P.op('act', lambda e: e.activation(out=rt[:, 0:n], in_=ss[:, 0:n], func=AF.Ln, scale=1.0 / D, bias=epsc),
             reads=[('ps', 6), RC], writes=[rtreg])
        rs, rsreg = ring('fb', fb)
        P.op('act', lambda e: e.activation(out=rs[:, 0:n], in_=rt[:, 0:n], func=AF.Exp, scale=-0.5), reads=[rtreg], writes=[rsreg])
        for c in range(8):
            gcol = cf[:, CF_PVEC + pvcol + c:CF_PVEC + pvcol + c + 1]
            dst = hT if final else xnT
            wreg_ = ('h', c, g) if final else ('xn', c, g)
            P.op('dve', lambda e, c=c, gcol=gcol, dst=dst: e.scalar_tensor_tensor(
                out=dst[:, c, s:s + n], in0=hT[:, c, s:s + n], scalar=gcol, in1=rs[:, 0:n], op0=ALU.mult, op1=ALU.mult),
                reads=[('h', c, g), rsreg, RC], writes=[wreg_])

    norm_queue = []

    def rmsnorm(pvcol, final=False, ahead=None):
        for g, (s, n) in enumerate(GROUPS):
            norm_queue.append(lambda g=g, s=s, n=n: rmsnorm_group(pvcol, final, g, s, n))
        for _ in range(len(GROUPS) if ahead is None else ahead):
            norm_pop()

    def norm_pop():
        if norm_queue:
            norm_queue.pop(0)()

    NM = 128 - PAD

    def resid_add(ps, psreg, c, g, scale):
        s, n = GROUPS[g]
        if g == 0:
            s, n = PAD, NM
        P.op('dve', lambda e: e.scalar_tensor_tensor(out=hT[:, c, s:s + n], in0=ps[:, 0:n], scalar=scale,
                                                      in1=hT[:, c, s:s + n], op0=ALU.mult, op1=ALU.add),
             reads=[psreg, ('h', c, g)], writes=[('h', c, g)])

    def ffn_gu_group(wt, wreg, fi, g, s, n):
        if g == 0:
            s, n = PAD, NM
        k = rr('ffn_gu', 2)
        gp, up = PS[k], PS[2 + k]
        for kc in range(8):
            P.op('pe', lambda e, kc=kc: e.matmul(gp[:, 0:n], lhsT=wt[:, kc * 128:(kc + 1) * 128],
                                               rhs=xnT[:, kc, s:s + n], start=(kc == 0), stop=(kc == 7)),
                 reads=[wreg, ('xn', kc, g)], writes=[('ps', k)])
        for kc in range(8):
            P.op('pe', lambda e, kc=kc: e.matmul(up[:, 0:n], lhsT=wt[:, 1024 + kc * 128:1024 + (kc + 1) * 128],
                                               rhs=xnT[:, kc, s:s + n], start=(kc == 0), stop=(kc == 7)),
                 reads=[wreg, ('xn', kc, g)], writes=[('ps', 2 + k)])
        sg, sgreg = ring('fa', fa)
        P.op('act', lambda e: e.activation(out=sg[:, 0:n], in_=gp[:, 0:n], func=AF.Silu),
             reads=[('ps', k)], writes=[sgreg])
        P.op('dve', lambda e: e.tensor_tensor(out=SL[:, fi, s:s + n], in0=up[:, 0:n], in1=sg[:, 0:n], op=ALU.mult),
             reads=[('ps', 2 + k), sgreg], writes=[('S', fi, g)])

    def mm_out_group(wt, wreg, nk, c, g, s, n, scale):
        if g == 0:
            s, n = PAD, NM
        k = 4 + rr('ffn_o', 2)
        op_ = PS[k]
        for fi in range(nk):
            P.op('pe', lambda e, fi=fi: e.matmul(op_[:, 0:n], lhsT=wt[:, fi * 128:(fi + 1) * 128],
                                               rhs=SL[:, fi, s:s + n], start=(fi == 0), stop=(fi == nk - 1)),
                 reads=[wreg, ('S', fi, g)], writes=[('ps', k)])
        resid_add(op_, ('ps', k), c, g, scale)

    def ffn(l, j):
        rmsnorm(PV_NORM + (l * 3 + (0 if j == 0 else 2)) * 8)
        for half in range(2):
            for fi in range(NFH):
                wt, wreg = wload(wgu_d[l, j, half * NFH + fi], 2048)
                for g, (s, n) in enumerate(GROUPS):
                    ffn_gu_group(wt, wreg, fi, g, s, n)
                    norm_pop()
            for c in range(8):
                wt, wreg = wload(wd_d[l, j, half, c], NFH * 128)
                for g, (s, n) in enumerate(GROUPS):
                    mm_out_group(wt, wreg, NFH, c, g, s, n, 0.5)

    tabs_for = {}

    rot_pending = []

    def project_rot_group(which, wt, wreg, dst_slot, permcol, g, split=None):
        perm = cb[:, permcol:permcol + 128]
        s, n = GROUPS[g]
        tt, tregs = tabs_for[g]
        pk = rr('proj', 2)
        pp, sp_ = PS[pk], PS[2 + pk]
        for kc in range(8):
            P.op('pe', lambda e, kc=kc: e.matmul(pp[:, 0:n], lhsT=wt[:, which * 1024 + kc * 128:which * 1024 + (kc + 1) * 128],
                                               rhs=xnT[:, kc, s:s + n], start=(kc == 0), stop=(kc == 7)),
                 reads=[wreg, ('xn', kc, g)], writes=[('ps', pk)])
        qb, qbreg = ring('ba', ba)
        P.op('act', lambda e: e.activation(out=qb[:, 0:n], in_=pp[:, 0:n], func=AF.Identity), reads=[('ps', pk)], writes=[qbreg])
        rot_flush()

        def second():
            P.op('pe', lambda e: e.matmul(sp_[:, 0:n], lhsT=perm, rhs=qb[:, 0:n], start=True, stop=True),
                 reads=[qbreg, RC], writes=[('ps', 2 + pk)])
            t1, t1reg = ring('fa', fa)
            t2, t2reg = ring('fb', fb)
            P.op('dve', lambda e: e.tensor_tensor(out=t1[:, 0:n], in0=pp[:, 0:n], in1=tt[:, 0, 0:n], op=ALU.mult),
                 reads=[('ps', pk)] + tregs, writes=[t1reg])
            P.op('dve', lambda e: e.tensor_tensor(out=t2[:, 0:n], in0=sp_[:, 0:n], in1=tt[:, 1, 0:n], op=ALU.mult),
                 reads=[('ps', 2 + pk)] + tregs, writes=[t2reg])
            if split is None:
                P.op(ROT_ADD_ENG, lambda e: e.tensor_tensor(out=SL[:, dst_slot, s:s + n], in0=t1[:, 0:n], in1=t2[:, 0:n], op=ALU.add),
                     reads=[t1reg, t2reg], writes=[('S', dst_slot, g)])
            else:
                for m in range(2):
                    P.op('dve', lambda e, m=m: e.tensor_tensor(out=SL[64 * m:64 * m + 64, split[m], s:s + n], in0=t1[64 * m:64 * m + 64, 0:n],
                                                             in1=t2[64 * m:64 * m + 64, 0:n], op=ALU.add),
                         reads=[t1reg, t2reg], writes=[('S', split[m], g)])
        rot_pending.append(second)

    def rot_flush():
        while rot_pending:
            rot_pending.pop(0)()

    def project_v_group(wt, wreg, vslot, g, s, n):
        pk = rr('proj', 2)
        pp = PS[pk]
        for ci in range(n // 128):
            cs = s + ci * 128
            for kc in range(8):
                P.op('pe', lambda e, kc=kc, ci=ci, cs=cs: e.matmul(pp[:, ci * 128:(ci + 1) * 128], lhsT=xnT[:, kc, cs:cs + 128],
                                                                 rhs=wt[:, kc * 128:(kc + 1) * 128], start=(kc == 0), stop=(kc == 7)),
                     reads=[wreg, ('xn', kc, g)], writes=[('ps', pk)])
        P.op('act', lambda e: e.activation(out=SL[:, vslot, s:s + n], in_=pp[:, 0:n], func=AF.Identity),
             reads=[('ps', pk)], writes=[('S', vslot, g)])

    def project_v(wt, wreg, vslot):
        for g, (s, n) in enumerate(GROUPS):
            project_v_group(wt, wreg, vslot, g, s, n)

    def stats_rstd(src, srcreg, n, scale, bias_col, o0=0):
        sq, sqreg = ring('ba', ba)
        P.op('dve', lambda e: e.tensor_tensor(out=sq[:, o0:n], in0=src[:, o0:n], in1=src[:, o0:n], op=ALU.mult), reads=[srcreg], writes=[sqreg])
        P.op('pe', lambda e: e.matmul(PS[6][:, o0:n], lhsT=ones, rhs=sq[:, o0:n], start=True, stop=True),
             reads=[sqreg, RC], writes=[('ps', 6)])
        sd, sdreg = ring('fb', fb)
        P.op('act', lambda e: e.activation(out=sd[:, o0:n], in_=PS[6][:, o0:n], func=AF.Ln, scale=scale, bias=bias_col),
             reads=[('ps', 6), RC], writes=[sdreg])
        rs, rsreg = ring('fb', fb)
        P.op('act', lambda e: e.activation(out=rs[:, o0:n], in_=sd[:, o0:n], func=AF.Exp, scale=-0.5), reads=[sdreg], writes=[rsreg])
        return rs, rsreg

    def retention_attention(l, h, u, ks, vs, wvg, wvgreg):
        gam = 1.0 - 2.0 ** (-5.0 - h)
        cd = gam ** 128
        dmask = cf[:, CF_DMASK + h * 128:CF_DMASK + (h + 1) * 128]
        qdec = cf[:, CF_QDEC + h * 128:CF_QDEC + (h + 1) * 128]
        kdec = cf[:, CF_KDEC + h:CF_KDEC + h + 1]
        gnw = cf[:, CF_PVEC + PV_GN + l * 4 + h:CF_PVEC + PV_GN + l * 4 + h + 1]
        RBs = 18 - ks
        grp = lambda n: 0 if n == 0 else 1 + (n - 1) // 4
        P.op('dve', lambda e: e.memset(SL[:, RBs, 0:128], 0.0), writes=[('S', RBs, 0)])
        P.op('dve', lambda e: e.memset(Rf[0][:, :], 0.0), writes=[('Rf', 0)])
        def emit_gate(g, bank=6):
            s, n = GROUPS[g]
            gp, greg = (PTF, ('pt',)) if bank == 'ptf' else (PS[bank], ('ps', bank))
            for kc in range(8):
                P.op('pe', lambda e, kc=kc: e.matmul(gp[:, 0:n], lhsT=wvg[:, 1024 + kc * 128:1024 + (kc + 1) * 128],
                                                   rhs=xnT[:, kc, s:s + n], start=(kc == 0), stop=(kc == 7)),
                     reads=[wvgreg, ('xn', kc, g)], writes=[greg])
            sg, sgreg = ring('fb', fb)
            P.op('act', lambda e: e.activation(out=sg[:, 0:n], in_=gp[:, 0:n], func=AF.Exp, scale=-1.0), reads=[greg], writes=[sgreg])
            P.op('act', lambda e: e.activation(out=sg[:, 0:n], in_=sg[:, 0:n], func=AF.Ln, bias=onec), reads=[sgreg, RC], writes=[sgreg])
            P.op('act', lambda e: e.activation(out=sg[:, 0:n], in_=sg[:, 0:n], func=AF.Exp, scale=-1.0), reads=[sgreg], writes=[sgreg])
            gs, gsreg = ring('sq', sqr)
            P.op('dve', lambda e: e.tensor_tensor(out=gs[:, 0:n], in0=gp[:, 0:n], in1=sg[:, 0:n], op=ALU.mult),
                 reads=[greg, sgreg], writes=[gsreg])
            gate[g] = (gs, gsreg)

        for r in range(4):
            for i in range(4):
                n = 4 * r + i
                P.op('pe', lambda e, n=n, i=i: e.transpose(PT[:, i * 128:(i + 1) * 128], SL[:, ks, n * 128:(n + 1) * 128], ident),
                     reads=[('S', ks, grp(n)), RC], writes=[('pt',)])
            kd, kdreg = ring('ba', ba)
            P.op('act', lambda e, kd=kd: e.activation(out=kd[:, 0:512], in_=PT[:, 0:512], func=AF.Identity, scale=kdec),
                 reads=[('pt',), RC], writes=[kdreg])
            for i in range(4):
                n = 4 * r + i
                P.op('pe', lambda e, n=n, i=i, r=r, kd=kd: e.matmul(PS[r][:, i * 128:(i + 1) * 128], lhsT=kd[:, i * 128:(i + 1) * 128],
                                                                  rhs=SL[:, vs, n * 128:(n + 1) * 128], start=True, stop=True),
                     reads=[kdreg, ('S', vs, grp(n))], writes=[('ps', r)])
        gate = {}
        for n in range(16):
            if n in (0, 5, 10):
                emit_gate(n // 5, (6, 4, 5)[n // 5])
            r, i = divmod(n, 4)
            P.op('dve', lambda e, n=n, r=r, i=i: e.scalar_tensor_tensor(out=Rf[(n + 1) % 2][:, :], in0=Rf[n % 2][:, :], scalar=cd,
                                                                      in1=PS[r][:, i * 128:(i + 1) * 128], op0=ALU.mult, op1=ALU.add),
                 reads=[('Rf', n % 2), ('ps', r)], writes=[('Rf', (n + 1) % 2)])
            P.op('act', lambda e, n=n: e.activation(out=SL[:, RBs, (n + 1) * 128:(n + 2) * 128], in_=Rf[(n + 1) % 2][:, :], func=AF.Identity),
                 reads=[('Rf', (n + 1) % 2)], writes=[('S', RBs, grp(n + 1))])
        def emit_ST(g):
            s, n = GROUPS[g]
            ncx = n // 128
            bk = g % 2
            for ci in range(ncx):
                cs = s + ci * 128
                P.op('pe', lambda e, ci=ci, cs=cs: e.matmul(PS[bk][:, ci * 128:(ci + 1) * 128], lhsT=SL[:, ks, cs:cs + 128],
                                                          rhs=SL[:, u, cs:cs + 128], start=True, stop=True),
                     reads=[('S', u, g), ('S', ks, g)], writes=[('ps', bk)])
            v3 = lambda ap: ap.rearrange("p (c i) -> p c i", c=ncx)
            bc = lambda ap: ap.unsqueeze(1).broadcast_to([128, ncx, 128])
            stm, stmreg = ring('ba', ba)
            P.op('dve', lambda e: e.tensor_tensor(out=v3(stm[:, 0:n]), in0=v3(PS[bk][:, 0:n]), in1=bc(dmask), op=ALU.mult),
                 reads=[('ps', bk), RC], writes=[stmreg])
            qd, qdreg = ring('ba', ba)
            P.op('dve', lambda e: e.tensor_tensor(out=v3(qd[:, 0:n]), in0=v3(SL[:, u, s:s + n]), in1=bc(qdec), op=ALU.mult),
                 reads=[('S', u, g), RC], writes=[qdreg])
            return stm, stmreg, qd, qdreg

        def emit_O(g, st):
            stm, stmreg, qd, qdreg = st
            s, n = GROUPS[g]
            OBk = 4 + g % 2
            OB = PS[OBk]
            for ci in range(n // 128):
                cs = s + ci * 128
                P.op('pe', lambda e, ci=ci, cs=cs: e.matmul(OB[:, ci * 128:(ci + 1) * 128], lhsT=SL[:, vs, cs:cs + 128],
                                                          rhs=stm[:, ci * 128:(ci + 1) * 128], start=True, stop=False),
                     reads=[('S', vs, g), stmreg], writes=[('ps', OBk)])
                P.op('pe', lambda e, ci=ci, cs=cs: e.matmul(OB[:, ci * 128:(ci + 1) * 128], lhsT=SL[:, RBs, cs:cs + 128],
                                                          rhs=qd[:, ci * 128:(ci + 1) * 128], start=False, stop=True),
                     reads=[('S', RBs, g), qdreg], writes=[('ps', OBk)])
            post.append([1, lambda: fin1(g, OBk)])
            tick()

        post = []
        PTF = PT[:, :].bitcast(F32)

        def tick(force=False):
            for ent in list(post):
                ent[0] -= 1
                if force or ent[0] <= 0:
                    post.remove(ent)
                    ent[1]()

        def fin1(g, OBk):
            s, n = GROUPS[g]
            OB = PS[OBk]
            cent = cb[:, CB_CENT:CB_CENT + 128]
            ck = 2 + g % 2
            ob, obreg = ring('ba', ba)
            P.op('act', lambda e: e.activation(out=ob[:, 0:n], in_=OB[:, 0:n], func=AF.Identity), reads=[('ps', OBk)], writes=[obreg])
            P.op('pe', lambda e: e.matmul(PS[ck][:, 0:n], lhsT=cent, rhs=ob[:, 0:n], start=True, stop=True),
                 reads=[obreg, RC], writes=[('ps', ck)])
            post.append([1, lambda: fin2(g, ck)])

        def fin2(g, ck):
            s, n = GROUPS[g]
            csq, csqreg = ring('ba', ba)
            P.op('act', lambda e: e.activation(out=csq[:, 0:n], in_=PS[ck][:, 0:n], func=AF.Square), reads=[('ps', ck)], writes=[csqreg])
            P.op('pe', lambda e: e.matmul(PS[6][:, 0:n], lhsT=ones, rhs=csq[:, 0:n], start=True, stop=True),
                 reads=[csqreg, RC], writes=[('ps', 6)])
            post.append([1, lambda: fin3(g, ck)])

        def fin3(g, ck):
            s, n = GROUPS[g]
            rs, rsreg = ring('fb', fb)
            P.op('act', lambda e: e.activation(out=rs[:, 0:n], in_=PS[6][:, 0:n], func=AF.Ln, scale=1.0 / 128, bias=epsc),
                 reads=[('ps', 6), RC], writes=[rsreg])
            P.op('act', lambda e: e.activation(out=rs[:, 0:n], in_=rs[:, 0:n], func=AF.Exp, scale=-0.5), reads=[rsreg], writes=[rsreg])
            fin4(g, ck, rs, rsreg)

        def fin4(g, ck, rs, rsreg):
            s, n = GROUPS[g]
            nrm, nrmreg = ring('fa', fa)
            P.op('dve', lambda e: e.scalar_tensor_tensor(out=nrm[:, 0:n], in0=PS[ck][:, 0:n], scalar=gnw, in1=rs[:, 0:n],
                                                          op0=ALU.mult, op1=ALU.mult),
                 reads=[('ps', ck), rsreg, RC], writes=[nrmreg])
            gs, gsreg = gate[g]
            P.op('dve', lambda e: e.tensor_tensor(out=SL[:, u, s:s + n], in0=nrm[:, 0:n], in1=gs[:, 0:n], op=ALU.mult),
                 reads=[nrmreg, gsreg], writes=[('S', u, g)])
            done.add(g)

        done = set()

        pending = None
        need_gate = [3, 4]
        for g in range(5):
            if need_gate and (need_gate[0] - 3) in done:
                emit_gate(need_gate.pop(0), 'ptf')
            st = emit_ST(g)
            if pending is not None:
                emit_O(*pending)
            pending = (g, st)
        emit_O(*pending)
        while post or need_gate:
            if need_gate and (need_gate[0] - 3) in done:
                emit_gate(need_gate.pop(0), 'ptf')
            tick(force=True)

    def retention_unit(l, h):
        u = h
        ks = 8 + 2 * (u % 2)
        vs = 9
        wqk, wqkreg = wload(wmix_d[l, u, 0], 2048)
        wvg, wvgreg = wload(wmix_d[l, u, 1], 2048)
        for g, (s, n) in enumerate(GROUPS):
            tabs_for[g] = tload(0, s, n)
            project_rot_group(0, wqk, wqkreg, u, CB_PERMR, g)
            project_rot_group(1, wqk, wqkreg, ks, CB_PERMR, g)
            norm_pop()
        rot_flush()
        project_v(wvg, wvgreg, vs)
        if DBG_MIX == 'B':
            return
        retention_attention(l, h, u, ks, vs, wvg, wvgreg)

    def diff_attention(l, u, ks, vs, om):
        neglam = lamv[:, 4 + l:5 + l]
        sw = cf[:, CF_PVEC + PV_SW + l:CF_PVEC + PV_SW + l + 1]
        blocks = []
        for g, (s, n) in enumerate(GROUPS):
            last = (s + n) // 128 - 1
            for m in range(2):
                for jb in range(last + 1):
                    blocks.append((g, m, jb, last))
        odset = {}
        tres = {}
        deferred = []
        post = []

        def emit_S(blk):
            g, m, jb, last = blk
            s, n = GROUPS[g]
            off = max(0, jb * 128 - s)
            nq = n - off
            diag = jb * 128 >= s
            sk = rr('dst', 3)
            SP_, sreg = SBANK[sk]
            kg = 0 if jb == 0 else 1 + (jb - 1) // 4
            P.op('pe', lambda e: e.matmul(SP_[:, 0:nq], lhsT=SL[:, ks[m], jb * 128:(jb + 1) * 128],
                                          rhs=SL[:, u, s + off:s + n], start=True, stop=(not diag)),
                 reads=[('S', ks[m], kg), ('S', u, g)], writes=[sreg])
            if diag:
                P.op('pe', lambda e: e.matmul(SP_[:, 0:128], lhsT=ident, rhs=cmask, start=False, stop=True),
                     reads=[RC], writes=[sreg])
            return sk, off, nq, kg

        def emit_rest(blk, sinfo):
            g, m, jb, last = blk
            sk, off, nq, kg = sinfo
            SP_, sreg = SBANK[sk]
            if (g, m) not in odset:
                odset[(g, m)] = 2 + 2 * rr('dod', 2)
                while any(ent[2] == odset[(g, m)] for ent in post):
                    for ent in list(post):
                        if ent[2] == odset[(g, m)]:
                            post.remove(ent)
                            ent[1]()
            OBk = odset[(g, m)]
            DBk = OBk + 1
            OB, DB = PS[OBk], PS[DBk]
            pt, ptreg = ring('ba', ba)
            P.op('act', lambda e: e.activation(out=pt[:, 0:nq], in_=SP_[:, 0:nq], func=AF.Exp, scale=0.125),
                 reads=[sreg], writes=[ptreg])
            P.op('pe', lambda e: e.matmul(OB[:, off:off + nq], lhsT=SL[:, vs, jb * 128:(jb + 1) * 128], rhs=pt[:, 0:nq],
                                          start=(jb == 0), stop=(jb == last)),
                 reads=[('S', vs, kg), ptreg], writes=[('ps', OBk)])
            P.op('pe', lambda e: e.matmul(DB[:, off:off + nq], lhsT=(ones_pad if jb == 0 else ones), rhs=pt[:, 0:nq],
                                          start=(jb == 0), stop=(jb == last)),
                 reads=[RC, ptreg], writes=[('ps', DBk)])
            if jb == last:
                post.append([2, lambda: normalize_act(g, m, OBk), OBk])

        def normalize_act(g, m, OBk):
            DBk = OBk + 1
            DB = PS[DBk]
            s, n = GROUPS[g]
            o0 = PAD if g == 0 else 0
            r, rreg = ring('fb', fb)
            P.op('act', lambda e: e.activation(out=r[:, o0:n], in_=DB[:, o0:n], func=AF.Ln), reads=[('ps', DBk)], writes=[rreg])
            P.op('act', lambda e: e.activation(out=r[:, o0:n], in_=r[:, o0:n], func=AF.Exp, scale=-1.0), reads=[rreg], writes=[rreg])
            post.append([3, lambda: normalize_dve(g, m, OBk, r, rreg), OBk])

        def normalize_dve(g, m, OBk, r, rreg):
            OB = PS[OBk]
            s, n = GROUPS[g]
            o0 = PAD if g == 0 else 0
            t, treg_ = ring('fa', fa)
            P.op('dve', lambda e: e.tensor_tensor(out=t[:, o0:n], in0=OB[:, o0:n], in1=r[:, o0:n], op=ALU.mult),
                 reads=[('ps', OBk), rreg], writes=[treg_])
            tres[(g, m)] = (t, treg_)
            if m == 1:
                post.append([1, lambda: finalize1(g), None])

        def finalize1(g):
            s, n = GROUPS[g]
            o0 = PAD if g == 0 else 0
            (t0, t0reg), (t1, t1reg) = tres[(g, 0)], tres[(g, 1)]
            o, oreg = ring('fa', fa)
            P.op('dve', lambda e: e.scalar_tensor_tensor(out=o[:, o0:n], in0=t1[:, o0:n], scalar=neglam, in1=t0[:, o0:n],
                                                          op0=ALU.mult, op1=ALU.add),
                 reads=[t0reg, t1reg, ('lamv',)], writes=[oreg])
            sq, sqreg = ring('ba', ba)
            P.op('dve', lambda e: e.tensor_tensor(out=sq[:, o0:n], in0=o[:, o0:n], in1=o[:, o0:n], op=ALU.mult), reads=[oreg], writes=[sqreg])
            P.op('pe', lambda e: e.matmul(PS[6][:, o0:n], lhsT=ones, rhs=sq[:, o0:n], start=True, stop=True),
                 reads=[sqreg, RC], writes=[('ps', 6)])
            post.append([4, lambda: finalize2(g, o, oreg), None])

        def finalize2(g, o, oreg):
            s, n = GROUPS[g]
            o0 = PAD if g == 0 else 0
            rs, rsreg = ring('fb', fb)
            P.op('act', lambda e: e.activation(out=rs[:, o0:n], in_=PS[6][:, o0:n], func=AF.Ln, scale=1.0 / (128 * om * om), bias=epsl[l]),
                 reads=[('ps', 6), RC], writes=[rsreg])
            P.op('act', lambda e: e.activation(out=rs[:, o0:n], in_=rs[:, o0:n], func=AF.Exp, scale=-0.5), reads=[rsreg], writes=[rsreg])
            P.op('dve', lambda e: e.scalar_tensor_tensor(out=SL[:, u, s + o0:s + n], in0=o[:, o0:n], scalar=sw, in1=rs[:, o0:n],
                                                          op0=ALU.mult, op1=ALU.mult),
                 reads=[oreg, rsreg, RC], writes=[('S', u, g)])

        def tick(force=False):
            for ent in list(post):
                ent[0] -= 1
                if force or ent[0] <= 0:
                    post.remove(ent)
                    ent[1]()

        pend = []
        for blk in blocks:
            sinfo = emit_S(blk)
            pend.append((blk, sinfo))
            if len(pend) > 2:
                emit_rest(*pend.pop(0))
                tick()
        while pend:
            emit_rest(*pend.pop(0))
            tick()
        while post:
            tick(force=True)

    def diff_unit(l, h, lam_init):
        u = 4 + h
        ks = (8, 10)
        vs = 9
        wqk, wqkreg = wload(wmix_d[l, u, 0], 2048)
        wv, wvreg = wload(wmix_d[l, u, 1, :, 0:1024], 1024)
        if h == 0:
            P.op('dve', lambda e: e.memset(SL[64:128, ks[0], :], 0.0), writes=[('S', ks[0], g) for g in range(5)])
            P.op('dve', lambda e: e.memset(SL[0:64, ks[1], :], 0.0), writes=[('S', ks[1], g) for g in range(5)])
        for g, (s, n) in enumerate(GROUPS):
            tabs_for[g] = tload(1, s, n)
            project_rot_group(0, wqk, wqkreg, u, CB_PERMD, g)
            project_rot_group(1, wqk, wqkreg, None, CB_PERMD, g, split=ks)
        rot_flush()
        project_v(wv, wvreg, vs)
        diff_attention(l, u, ks, vs, 1.0 - lam_init)

    def lam_compute(l):
        lam_init = 0.8 - 0.6 * math.exp(-0.3 * l)
        for t in range(2):
            a = lamin[:, (l * 4 + 2 * t) * 64:(l * 4 + 2 * t + 1) * 64]
            b = lamin[:, (l * 4 + 2 * t + 1) * 64:(l * 4 + 2 * t + 2) * 64]
            P.op('dve', lambda e, a=a, b=b: e.tensor_tensor(out=lamw[:, :], in0=a, in1=b, op=ALU.mult),
                 reads=[RC, ('lamw',)], writes=[('lamw',)])
            P.op('dve', lambda e, t=t: e.tensor_reduce(out=lamv[:, t:t + 1], in_=lamw[:, :], axis=mybir.AxisListType.X, op=ALU.add),
                 reads=[('lamw',), ('lamv',)], writes=[('lamv',)])
            P.op('act', lambda e, t=t: e.activation(out=lamv[:, 2 + t:3 + t], in_=lamv[:, t:t + 1], func=AF.Exp),
                 reads=[('lamv',)], writes=[('lamv',)])
        P.op('dve', lambda e: e.scalar_tensor_tensor(out=lamv[:, 4 + l:5 + l], in0=lamv[:, 3:4], scalar=-lam_init, in1=lamv[:, 2:3],
                                                      op0=ALU.add, op1=ALU.subtract),
             reads=[('lamv',)], writes=[('lamv',)])
        return lam_init

    def mixer(l):
        rmsnorm(PV_NORM + (l * 3 + 1) * 8)
        lam_init = lam_compute(l)
        if DBG_MIX == 'A':
            return
        for h in range(4):
            retention_unit(l, h)
            if DBG_MIX in ('B', 'B1', 'C', 'R1', 'R2', 'R3'):
                return
        if DBG_MIX == 'D':
            return
        for h in range(4):
            diff_unit(l, h, lam_init)
            if DBG_MIX == 'F':
                return
        for c in range(8):
            wt, wreg = wload(wout_d[l, c], 1024)
            for g, (s, n) in enumerate(GROUPS):
                mm_out_group(wt, wreg, 8, c, g, s, n, 1.0)

    stages = []
    for l in range(2):
        stages += [('ffn', l, 0), ('mix', l), ('ffn', l, 1)]
    nst = len(stages) if stop is None else stop
    for st in stages[:nst]:
        if st[0] == 'ffn':
            ffn(st[1], st[2])
        else:
            mixer(st[1])
    outregs = []
    if stop is None:
        for g, (s, n) in enumerate(GROUPS):
            rmsnorm_group(PV_NORM + 48, True, g, s, n)
            if g >= 1:
                P.dma('sp', lambda e, s=s, n=n: e.dma_start(out=out_d[:, :, s - 128:s - 128 + n], in_=hT[:, :, s:s + n]), 'out',
                      reads=[('h', c, g) for c in range(8)], writes=[('out', g)])
                outregs.append(('out', g))
    else:
        for c in range(8):
            P.dma('sp', lambda e, c=c: e.dma_start(out=out_d[:, c, :], in_=hT[:, c, 128:TP]), 'out',
                  reads=[('h', c, g) for g in range(1, 5)], writes=[('out', c)])
            outregs.append(('out', c))
    if dump == 'h':
        for c in range(8):
            P.dma('sp', lambda e, c=c: e.dma_start(out=dbg_d[:, c, :], in_=hT[:, c, :]), 'out',
                  reads=[('h', c, g) for g in range(5)], writes=[('dbg', c)])
            outregs.append(('dbg', c))
    P.op('sp', None, reads=outregs)
    P.build()
    return nc, P


def _bf(a):
    return np.asarray(a, dtype=np.float32).astype(ml_dtypes.bfloat16)


def _const_tables():
    pos = (np.arange(TP, dtype=np.float32) - np.float32(PAD))
    angle = (np.float32(10000.0) ** (-np.linspace(0.0, 1.0, 64, dtype=np.float32))).astype(np.float32)
    fr = (pos[None, :] * np.repeat(angle, 2)[:, None]).astype(np.float32)
    cosr = np.cos(fr).astype(np.float32)
    sgn = np.where(np.arange(128) % 2 == 0, -1.0, 1.0).astype(np.float32)[:, None]
    sinr = (np.sin(fr) * sgn).astype(np.float32)
    r = 16
    inv = (np.float32(500000.0) ** (-np.arange(0, r, 2, dtype=np.float32) / r)).astype(np.float32)
    cosd = np.ones((128, TP), np.float32)
    sind = np.zeros((128, TP), np.float32)
    for p in range(128):
        dd = p % 64
        if dd < r:
            f = (pos * inv[dd % 8]).astype(np.float32)
            cosd[p] = np.cos(f)
            sind[p] = np.sin(f) * (-1.0 if dd < 8 else 1.0)
    tabs = np.stack([cosr, sinr, cosd, sind]).astype(np.float32)
    cbm = np.zeros((128, CB_N), np.float32)
    cbm[:, CB_ONES:CB_ONES + 128] = 1.0
    cbm[PAD:, CB_ONESPAD:CB_ONESPAD + 128] = 1.0
    cbm[:, CB_IDENT:CB_IDENT + 128] = np.eye(128)
    for m in range(128):
        src = m + 1 if m % 2 == 0 else m - 1
        cbm[src, CB_PERMR + m] = 1.0
        dd = m % 64
        if dd < 8:
            cbm[m + 8, CB_PERMD + m] = 1.0
        elif dd < 16:
            cbm[m - 8, CB_PERMD + m] = 1.0
    jj = np.arange(128)[:, None]
    ii = np.arange(128)[None, :]
    cbm[:, CB_CMASK:CB_CMASK + 128] = np.where(jj <= ii, 0.0, -30000.0)
    cbm[:, CB_CENT:CB_CENT + 128] = np.eye(128) - 1.0 / 128.0
    cfm = np.zeros((128, CF_N), np.float32)
    cfm[:, CF_EPS] = EPS
    cfm[:, CF_ONE] = 1.0
    for l_ in range(2):
        om_ = 1.0 - (0.8 - 0.6 * math.exp(-0.3 * l_))
        cfm[:, CF_EPS + 1 + l_] = EPS / (om_ * om_)
    for h in range(4):
        lg = math.log(1.0 - 2.0 ** (-5.0 - h))
        rel = (ii - jj).astype(np.float64)
        dm = np.where(rel >= 0, np.exp(lg * np.maximum(rel, 0.0)), 0.0) * (128.0 ** -0.5)
        cfm[:, CF_DMASK + h * 128:CF_DMASK + (h + 1) * 128] = dm
        cfm[:, CF_QDEC + h * 128:CF_QDEC + (h + 1) * 128] = np.exp(lg * (np.arange(128) + 1.0))[None, :]
        cfm[:, CF_KDEC + h] = np.exp(lg * (127.0 - np.arange(128))) * (128.0 ** -0.5)
    return tabs, cbm, cfm


def _prep(inputs):
    f = lambda k: np.asarray(inputs[k], dtype=np.float32)
    tabs, cbm, cfm = _const_tables()
    cvec = lambda v: np.ascontiguousarray(v.reshape(-1, 128).T)
    norms = [f("ffn1_norm"), f("mix_norm"), f("ffn2_norm")]
    for l in range(2):
        for w in range(3):
            c0 = CF_PVEC + PV_NORM + (l * 3 + w) * 8
            cfm[:, c0:c0 + 8] = cvec(norms[w][l])
        cfm[:, CF_PVEC + PV_GN + l * 4:CF_PVEC + PV_GN + l * 4 + 4] = cvec(f("ret_gn_w")[l])
        cfm[:, CF_PVEC + PV_SW + l] = f("diff_subln_w")[l]
    cfm[:, CF_PVEC + PV_NORM + 48:CF_PVEC + PV_NORM + 56] = cvec(f("final_norm"))
    lam = np.stack([f("diff_lambda_q1"), f("diff_lambda_k1"), f("diff_lambda_q2"), f("diff_lambda_k2")], axis=1)
    lam = np.ascontiguousarray(np.broadcast_to(lam.reshape(1, 512), (128, 512)))

    def fm_tile(w):
        nc_ = w.shape[1] // 128
        return w.reshape(8, 128, nc_, 128).transpose(2, 1, 0, 3)

    wgu = np.empty((2, 2, NF, 128, 2, 8, 128), np.float32)
    wd = np.empty((2, 2, 2, 8, 128, NFH, 128), np.float32)
    for l in range(2):
        for j, (kg, ku, kd_) in enumerate([("ffn1_w_gate", "ffn1_w_up", "ffn1_w_down"), ("ffn2_w_gate", "ffn2_w_up", "ffn2_w_down")]):
            wgu[l, j, :, :, 0] = fm_tile(f(kg)[l])
            wgu[l, j, :, :, 1] = fm_tile(f(ku)[l])
            wd[l, j] = f(kd_)[l].reshape(2, NFH, 128, 8, 128).transpose(0, 3, 2, 1, 4)
    win = f("w_in")
    wmix = np.zeros((2, 8, 2, 128, 2, 8, 128), np.float32)
    for l in range(2):
        t = fm_tile(win[l])
        for h in range(4):
            wmix[l, h, 0, :, 0] = t[h]
            wmix[l, h, 0, :, 1] = t[4 + h]
            wmix[l, h, 1, :, 0] = t[8 + h]
            wmix[l, h, 1, :, 1] = t[12 + h]
            wmix[l, 4 + h, 0, :, 0] = t[16 + h]
            wmix[l, 4 + h, 0, :, 1] = t[20 + h]
            wmix[l, 4 + h, 1, :, 0] = t[24 + h]
    wout = np.stack([fm_tile(f("w_out")[l]) for l in range(2)])
    shared = {
        "metaT": np.ascontiguousarray(f("meta_tokens").reshape(NMETA, 8, 128).transpose(2, 1, 0)),
        "cb": _bf(cbm), "cf": cfm, "lam": lam, "tabs": tabs,
        "wgu": wgu.reshape(2, 2, NF, 128, 2048), "wd": wd.reshape(2, 2, 2, 8, 128, NFH * 128),
        "wmix": wmix.reshape(2, 8, 2, 128, 2048), "wout": np.ascontiguousarray(wout.reshape(2, 8, 128, 1024)),
    }
    x = f("x")
    xTs = [np.ascontiguousarray(x[b].reshape(SEQ, 8, 128).transpose(2, 1, 0)) for b in range(x.shape[0])]
    return shared, xTs


def run(inputs, cores=None, stop=None, dump=None, trace=False):
    shared, xTs = _prep(inputs)
    cores = list(range(8)) if cores is None else cores
    nc, P = build_program(stop=stop, dump=dump)
    in_maps = [dict(shared, xT=xTs[b]) for b in cores]
    res = run_bass_kernel_spmd(nc, in_maps, core_ids=list(range(len(cores))), trace=trace)
    return res, P


def kernel(**inputs):
    res, _ = run(inputs)
    outs = [np.asarray(r["outT"], dtype=np.float32) for r in res.results]
    out = np.stack([o.transpose(2, 1, 0).reshape(SEQ, D) for o in outs])
    return np.ascontiguousarray(out.astype(np.float32))
```

```python
import math
import numpy as np
import ml_dtypes
import concourse.bass as bass
import concourse.mybir as mybir
from concourse.bass_utils import run_bass_kernel_spmd

F32 = mybir.dt.float32
BF16 = mybir.dt.bfloat16
AF = mybir.ActivationFunctionType
ALU = mybir.AluOpType

D = 1024
SEQ = 2048
NMETA = 16
PAD = 112
TP = 2176
NCH = 17
DFF = 2816
NF = 22
NFH = 11
EPS = 1e-6
GROUPS = [(0, 128)] + [(128 + 512 * g, 512) for g in range(4)]
NSLOT = 11
NW = 4
WCOLS = 2048
SAME_ENGINE_SYNC = True
DBG_HACK = 0
WARM = 0
ROT_ADD_ENG = 'dve'
DBG_MIX = None

CB_ONES, CB_ONESPAD, CB_IDENT, CB_PERMR, CB_PERMD, CB_CMASK, CB_CENT = [i * 128 for i in range(7)]
CB_N = 7 * 128
CF_DMASK = 0
CF_QDEC = 512
CF_KDEC = 1024
CF_PVEC = 1028
PV_NORM = 0
PV_GN = 56
PV_SW = 64
CF_EPS = CF_PVEC + 66
CF_ONE = CF_EPS + 3
CF_N = CF_ONE + 1


class Prog:
    def __init__(self, nc):
        self.nc = nc
        self.ops = []
        self.dma_sems = {}

    def op(self, eng, fn, reads=(), writes=()):
        self.ops.append(dict(eng=eng, fn=fn, reads=list(reads), writes=list(writes), dma=None))

    def dma(self, eng, fn, slot, reads=(), writes=()):
        self.ops.append(dict(eng=eng, fn=fn, reads=list(reads), writes=list(writes), dma=slot))

    def build(self):
        nc = self.nc
        ops = self.ops
        engs = ['pe', 'act', 'dve', 'pool', 'sp']
        last_w = {}
        readers = {}
        for i, o in enumerate(ops):
            deps = set()
            key = ('dma', i) if o['dma'] is not None else o['eng']
            for r in o['reads']:
                if r in last_w:
                    deps.add(last_w[r])
                if r[0] in ('ps', 'pt'):
                    for k2, i2 in readers.get(r, {}).items():
                        if k2 != key:
                            deps.add(i2)
            for w in o['writes']:
                if w in last_w:
                    deps.add(last_w[w])
                rd = readers.get(w)
                if rd:
                    deps.update(rd.values())
            deps.discard(i)
            o['deps'] = deps
            for r in o['reads']:
                readers.setdefault(r, {})[key] = i
            for w in o['writes']:
                last_w[w] = i
                readers[w] = {}
        signal = set()
        for i, o in enumerate(ops):
            for d in o['deps']:
                p = ops[d]
                if p['dma'] is not None:
                    continue
                if p['eng'] != o['eng'] or (SAME_ENGINE_SYNC and o['eng'] != 'pe'):
                    signal.add(d)
        sem = {e: nc.alloc_semaphore('s_' + e) for e in engs}
        cnt = {e: 0 for e in engs}
        for i, o in enumerate(ops):
            if o['dma'] is not None:
                slot = o['dma']
                if slot not in self.dma_sems:
                    self.dma_sems[slot] = [nc.alloc_semaphore('d_' + str(slot)), 0]
                ent = self.dma_sems[slot]
                ent[1] += 16
                o['tok'] = (ent[0], ent[1])
            elif i in signal:
                cnt[o['eng']] += 1
                o['tok'] = (sem[o['eng']], cnt[o['eng']])
            else:
                o['tok'] = None
        waited = {e: {} for e in engs}
        for i, o in enumerate(ops):
            need = {}
            for d in o['deps']:
                p = ops[d]
                if p['dma'] is None and p['eng'] == o['eng'] and not (SAME_ENGINE_SYNC and o['eng'] != 'pe'):
                    continue
                tok = p['tok']
                assert tok is not None
                k = tok[0]
                if need.get(k, (None, 0))[1] < tok[1]:
                    need[k] = tok
            ws = []
            for k, tok in need.items():
                if waited[o['eng']].get(k, 0) < tok[1]:
                    waited[o['eng']][k] = tok[1]
                    ws.append(tok)
            o['waits'] = ws
        per = {e: [o for o in ops if o['eng'] == e] for e in engs}
        self.stats = {e: len(per[e]) for e in engs}
        self.stats['signals'] = dict(cnt)

        def run(e, lst):
            for o in lst:
                for (s, v) in o['waits']:
                    e.wait_ge(s, v)
                if o['fn'] is None:
                    continue
                ins = o['fn'](e)
                if o['tok'] is not None:
                    ins.then_inc(o['tok'][0], 16 if o['dma'] is not None else 1)

        with nc.Block() as block:
            @block.tensor
            def _(e):
                run(e, per['pe'])

            @block.scalar
            def _(e):
                run(e, per['act'])

            @block.vector
            def _(e):
                run(e, per['dve'])

            @block.gpsimd
            def _(e):
                run(e, per['pool'])

            @block.sync
            def _(e):
                run(e, per['sp'])


def build_program(stop=None, dump=None):
    nc = bass.Bass("TRN2", target_bir_lowering=False)
    P = Prog(nc)

    def din(name, shape, dt=F32):
        return nc.dram_tensor(name, list(shape), dt, kind="ExternalInput").ap()

    xT_d = din("xT", [128, 8, SEQ])
    meta_d = din("metaT", [128, 8, NMETA])
    cb_d = din("cb", [128, CB_N], BF16)
    cf_d = din("cf", [128, CF_N])
    lam_d = din("lam", [128, 2 * 4 * 64])
    tabs_d = din("tabs", [4, 128, TP])
    wgu_d = din("wgu", [2, 2, NF, 128, 2048])
    wd_d = din("wd", [2, 2, 2, 8, 128, NFH * 128])
    wmix_d = din("wmix", [2, 8, 2, 128, 2048])
    wout_d = din("wout", [2, 8, 128, 1024])
    out_d = nc.dram_tensor("outT", [128, 8, SEQ], F32, kind="ExternalOutput").ap()
    dbg_d = None
    if dump is not None:
        dbg_d = nc.dram_tensor("dbg", [128, 8, TP], F32, kind="ExternalOutput").ap()

    sb = nc.alloc_sbuf_tensor
    hT = sb("hT", [128, 8, TP], F32)
    xnT = sb("xnT", [128, 8, TP], BF16)
    SL = sb("slots", [128, NSLOT, TP], BF16)
    wsl = [sb(f"wsl{i}", [128, WCOLS], BF16) for i in range(NW)]
    tabt = [sb(f"tab{i}", [128, 2, 512], F32) for i in range(2)]
    cb = sb("cbs", [128, CB_N], BF16)
    cf = sb("cfs", [128, CF_N], F32)
    lamin = sb("lamin", [128, 512], F32)
    lamw = sb("lamw", [128, 64], F32)
    lamv = sb("lamv", [128, 8], F32)
    NR = 3
    sqr = [sb(f"sq{i}", [128, 512], BF16) for i in range(NR)]
    fa = [sb(f"fa{i}", [128, 512], F32) for i in range(4)]
    fb = [sb(f"fb{i}", [128, 512], F32) for i in range(4)]
    ba = [sb(f"ba{i}", [128, 512], BF16) for i in range(6)]
    Rf = [sb(f"Rf{i}", [128, 128], F32) for i in range(2)]

    PS = [nc.alloc_psum_tensor(f"ps{i}", [128, 512], F32) for i in range(7)]
    PT = nc.alloc_psum_tensor("pst", [128, 1024], BF16)

    SBANK = [(PS[0], ('ps', 0)), (PS[1], ('ps', 1)), (PT[:, :].bitcast(F32), ('pt',))]
    ctr = {}

    def rr(name, n):
        v = ctr.get(name, 0)
        ctr[name] = v + 1
        return v % n

    def ring(name, lst):
        i = rr(name, len(lst))
        return lst[i], (name, i)

    ones = cb[:, CB_ONES:CB_ONES + 128]
    ones_pad = cb[:, CB_ONESPAD:CB_ONESPAD + 128]
    ident = cb[:, CB_IDENT:CB_IDENT + 128]
    cmask = cb[:, CB_CMASK:CB_CMASK + 128]
    RC = ('const',)
    epsc = cf[:, CF_EPS:CF_EPS + 1]
    onec = cf[:, CF_ONE:CF_ONE + 1]
    epsl = [cf[:, CF_EPS + 1 + l_:CF_EPS + 2 + l_] for l_ in range(2)]

    P.dma('sp', lambda e: e.dma_start(out=cb[:, :], in_=cb_d), 'const', writes=[RC])
    P.dma('sp', lambda e: e.dma_start(out=cf[:, :], in_=cf_d), 'const', writes=[RC])
    P.dma('sp', lambda e: e.dma_start(out=lamin[:, :], in_=lam_d), 'const', writes=[RC])
    hregs = lambda c: [('h', c, g) for g in range(5)]
    P.op('dve', lambda e: e.memset(hT[:, :, 0:PAD], 0.0), writes=[('h', c, 0) for c in range(8)])
    P.dma('sp', lambda e: e.dma_start(out=hT[:, :, PAD:128], in_=meta_d), ('xin', 0),
          reads=[('h', c, 0) for c in range(8)], writes=[('h', c, 0) for c in range(8)])
    for g in range(1, 5):
        s_, n_ = GROUPS[g]
        P.dma('sp', lambda e, s_=s_, n_=n_: e.dma_start(out=hT[:, :, s_:s_ + n_], in_=xT_d[:, :, s_ - 128:s_ - 128 + n_]), ('xin', g),
              writes=[('h', c, g) for c in range(8)])

    def wload(src_ap, ncols):
        i = rr('w', NW)
        t = wsl[i]
        reg = ('w', i)
        P.dma('pool', lambda e: e.dma_start(out=t[:, 0:ncols], in_=src_ap, max_dma_last_dim=8192),
              ('w', i), writes=[reg])
        return t, reg

    def tload(k, s, n):
        i = rr('tab', 2)
        t = tabt[i]
        regs = [('tab', i, 0), ('tab', i, 1)]
        P.dma('sp', lambda e: e.dma_start(out=t[:, 0, 0:n], in_=tabs_d[2 * k, :, s:s + n]), ('tab', i, 0), writes=[regs[0]])
        P.dma('sp', lambda e: e.dma_start(out=t[:, 1, 0:n], in_=tabs_d[2 * k + 1, :, s:s + n]), ('tab', i, 1), writes=[regs[1]])
        return t, regs

    def rmsnorm_group(pvcol, final, g, s, n):
        ss = PS[6]
        for c in range(8):
            sq, sqreg = ring('sq', sqr)
            P.op('act', lambda e, sq=sq, c=c: e.activation(out=sq[:, 0:n], in_=hT[:, c, s:s + n], func=AF.Square),
                 reads=[('h', c, g)], writes=[sqreg])
            P.op('pe', lambda e, sq=sq, c=c: e.matmul(ss[:, 0:n], lhsT=ones, rhs=sq[:, 0:n], start=(c == 0), stop=(c == 7)),
                 reads=[sqreg, RC], writes=[('ps', 6)])
        rt, rtreg = ring('fa', fa)
        P.op('act', lambda e: e.activation(out=rt[:, 0:n], in_=ss[:, 0:n], func=AF.Ln, scale=1.0 / D, bias=epsc),
             reads=[('ps', 6), RC], writes=[rtreg])
        rs, rsreg = ring('fb', fb)
        P.op('act', lambda e: e.activation(out=rs[:, 0:n], in_=rt[:, 0:n], func=AF.Exp, scale=-0.5), reads=[rtreg], writes=[rsreg])
        for c in range(8):
            gcol = cf[:, CF_PVEC + pvcol + c:CF_PVEC + pvcol + c + 1]
            dst = hT if final else xnT
            wreg_ = ('h', c, g) if final else ('xn', c, g)
            P.op('dve', lambda e, c=c, gcol=gcol, dst=dst: e.scalar_tensor_tensor(
                out=dst[:, c, s:s + n], in0=hT[:, c, s:s + n], scalar=gcol, in1=rs[:, 0:n], op0=ALU.mult, op1=ALU.mult),
                reads=[('h', c, g), rsreg, RC], writes=[wreg_])

    norm_queue = []

    def rmsnorm(pvcol, final=False, ahead=None):
        for g, (s, n) in enumerate(GROUPS):
            norm_queue.append(lambda g=g, s=s, n=n: rmsnorm_group(pvcol, final, g, s, n))
        for _ in range(len(GROUPS) if ahead is None else ahead):
            norm_pop()

    def norm_pop():
        if norm_queue:
            norm_queue.pop(0)()

    NM = 128 - PAD

    def resid_add(ps, psreg, c, g, scale):
        s, n = GROUPS[g]
        if g == 0:
            s, n = PAD, NM
        P.op('dve', lambda e: e.scalar_tensor_tensor(out=hT[:, c, s:s + n], in0=ps[:, 0:n], scalar=scale,
                                                      in1=hT[:, c, s:s + n], op0=ALU.mult, op1=ALU.add),
             reads=[psreg, ('h', c, g)], writes=[('h', c, g)])

    def ffn_gu_group(wt, wreg, fi, g, s, n):
        if g == 0:
            s, n = PAD, NM
        k = rr('ffn_gu', 2)
        gp, up = PS[k], PS[2 + k]
        for kc in range(8):
            P.op('pe', lambda e, kc=kc: e.matmul(gp[:, 0:n], lhsT=wt[:, kc * 128:(kc + 1) * 128],
                                               rhs=xnT[:, kc, s:s + n], start=(kc == 0), stop=(kc == 7)),
                 reads=[wreg, ('xn', kc, g)], writes=[('ps', k)])
        for kc in range(8):
            P.op('pe', lambda e, kc=kc: e.matmul(up[:, 0:n], lhsT=wt[:, 1024 + kc * 128:1024 + (kc + 1) * 128],
                                               rhs=xnT[:, kc, s:s + n], start=(kc == 0), stop=(kc == 7)),
                 reads=[wreg, ('xn', kc, g)], writes=[('ps', 2 + k)])
        sg, sgreg = ring('fa', fa)
        P.op('act', lambda e: e.activation(out=sg[:, 0:n], in_=gp[:, 0:n], func=AF.Silu),
             reads=[('ps', k)], writes=[sgreg])
        P.op('dve', lambda e: e.tensor_tensor(out=SL[:, fi, s:s + n], in0=up[:, 0:n], in1=sg[:, 0:n], op=ALU.mult),
             reads=[('ps', 2 + k), sgreg], writes=[('S', fi, g)])

    def mm_out_group(wt, wreg, nk, c, g, s, n, scale):
        if g == 0:
            s, n = PAD, NM
        k = 4 + rr('ffn_o', 2)
        op_ = PS[k]
        for fi in range(nk):
            P.op('pe', lambda e, fi=fi: e.matmul(op_[:, 0:n], lhsT=wt[:, fi * 128:(fi + 1) * 128],
                                               rhs=SL[:, fi, s:s + n], start=(fi == 0), stop=(fi == nk - 1)),
                 reads=[wreg, ('S', fi, g)], writes=[('ps', k)])
        resid_add(op_, ('ps', k), c, g, scale)

    def ffn(l, j):
        rmsnorm(PV_NORM + (l * 3 + (0 if j == 0 else 2)) * 8)
        for half in range(2):
            for fi in range(NFH):
                wt, wreg = wload(wgu_d[l, j, half * NFH + fi], 2048)
                for g, (s, n) in enumerate(GROUPS):
                    ffn_gu_group(wt, wreg, fi, g, s, n)
                    norm_pop()
            for c in range(8):
                wt, wreg = wload(wd_d[l, j, half, c], NFH * 128)
                for g, (s, n) in enumerate(GROUPS):
                    mm_out_group(wt, wreg, NFH, c, g, s, n, 0.5)

    tabs_for = {}

    rot_pending = []

    def project_rot_group(which, wt, wreg, dst_slot, permcol, g, split=None):
        perm = cb[:, permcol:permcol + 128]
        s, n = GROUPS[g]
        tt, tregs = tabs_for[g]
        pk = rr('proj', 2)
        pp, sp_ = PS[pk], PS[2 + pk]
        for kc in range(8):
            P.op('pe', lambda e, kc=kc: e.matmul(pp[:, 0:n], lhsT=wt[:, which * 1024 + kc * 128:which * 1024 + (kc + 1) * 128],
                                               rhs=xnT[:, kc, s:s + n], start=(kc == 0), stop=(kc == 7)),
                 reads=[wreg, ('xn', kc, g)], writes=[('ps', pk)])
        qb, qbreg = ring('ba', ba)
        P.op('act', lambda e: e.activation(out=qb[:, 0:n], in_=pp[:, 0:n], func=AF.Identity), reads=[('ps', pk)], writes=[qbreg])
        rot_flush()

        def second():
            P.op('pe', lambda e: e.matmul(sp_[:, 0:n], lhsT=perm, rhs=qb[:, 0:n], start=True, stop=True),
                 reads=[qbreg, RC], writes=[('ps', 2 + pk)])
            t1, t1reg = ring('fa', fa)
            t2, t2reg = ring('fb', fb)
            P.op('dve', lambda e: e.tensor_tensor(out=t1[:, 0:n], in0=pp[:, 0:n], in1=tt[:, 0, 0:n], op=ALU.mult),
                 reads=[('ps', pk)] + tregs, writes=[t1reg])
            P.op('dve', lambda e: e.tensor_tensor(out=t2[:, 0:n], in0=sp_[:, 0:n], in1=tt[:, 1, 0:n], op=ALU.mult),
                 reads=[('ps', 2 + pk)] + tregs, writes=[t2reg])
            if split is None:
                P.op(ROT_ADD_ENG, lambda e: e.tensor_tensor(out=SL[:, dst_slot, s:s + n], in0=t1[:, 0:n], in1=t2[:, 0:n], op=ALU.add),
                     reads=[t1reg, t2reg], writes=[('S', dst_slot, g)])
            else:
                for m in range(2):
                    P.op('dve', lambda e, m=m: e.tensor_tensor(out=SL[64 * m:64 * m + 64, split[m], s:s + n], in0=t1[64 * m:64 * m + 64, 0:n],
                                                             in1=t2[64 * m:64 * m + 64, 0:n], op=ALU.add),
                         reads=[t1reg, t2reg], writes=[('S', split[m], g)])
        rot_pending.append(second)

    def rot_flush():
        while rot_pending:
            rot_pending.pop(0)()

    def project_v_group(wt, wreg, vslot, g, s, n):
        pk = rr('proj', 2)
        pp = PS[pk]
        for ci in range(n // 128):
            cs = s + ci * 128
            for kc in range(8):
                P.op('pe', lambda e, kc=kc, ci=ci, cs=cs: e.matmul(pp[:, ci * 128:(ci + 1) * 128], lhsT=xnT[:, kc, cs:cs + 128],
                                                                 rhs=wt[:, kc * 128:(kc + 1) * 128], start=(kc == 0), stop=(kc == 7)),
                     reads=[wreg, ('xn', kc, g)], writes=[('ps', pk)])
        P.op('act', lambda e: e.activation(out=SL[:, vslot, s:s + n], in_=pp[:, 0:n], func=AF.Identity),
             reads=[('ps', pk)], writes=[('S', vslot, g)])

    def project_v(wt, wreg, vslot):
        for g, (s, n) in enumerate(GROUPS):
            project_v_group(wt, wreg, vslot, g, s, n)

    def stats_rstd(src, srcreg, n, scale, bias_col, o0=0):
        sq, sqreg = ring('ba', ba)
        P.op('dve', lambda e: e.tensor_tensor(out=sq[:, o0:n], in0=src[:, o0:n], in1=src[:, o0:n], op=ALU.mult), reads=[srcreg], writes=[sqreg])
        P.op('pe', lambda e: e.matmul(PS[6][:, o0:n], lhsT=ones, rhs=sq[:, o0:n], start=True, stop=True),
             reads=[sqreg, RC], writes=[('ps', 6)])
        sd, sdreg = ring('fb', fb)
        P.op('act', lambda e: e.activation(out=sd[:, o0:n], in_=PS[6][:, o0:n], func=AF.Ln, scale=scale, bias=bias_col),
             reads=[('ps', 6), RC], writes=[sdreg])
        rs, rsreg = ring('fb', fb)
        P.op('act', lambda e: e.activation(out=rs[:, o0:n], in_=sd[:, o0:n], func=AF.Exp, scale=-0.5), reads=[sdreg], writes=[rsreg])
        return rs, rsreg

    def retention_attention(l, h, u, ks, vs, wvg, wvgreg):
        gam = 1.0 - 2.0 ** (-5.0 - h)
        cd = gam ** 128
        dmask = cf[:, CF_DMASK + h * 128:CF_DMASK + (h + 1) * 128]
        qdec = cf[:, CF_QDEC + h * 128:CF_QDEC + (h + 1) * 128]
        kdec = cf[:, CF_KDEC + h:CF_KDEC + h + 1]
        gnw = cf[:, CF_PVEC + PV_GN + l * 4 + h:CF_PVEC + PV_GN + l * 4 + h + 1]
        RBs = 18 - ks
        grp = lambda n: 0 if n == 0 else 1 + (n - 1) // 4
        P.op('dve', lambda e: e.memset(SL[:, RBs, 0:128], 0.0), writes=[('S', RBs, 0)])
        P.op('dve', lambda e: e.memset(Rf[0][:, :], 0.0), writes=[('Rf', 0)])
        def emit_gate(g, bank=6):
            s, n = GROUPS[g]
            gp, greg = (PTF, ('pt',)) if bank == 'ptf' else (PS[bank], ('ps', bank))
            for kc in range(8):
                P.op('pe', lambda e, kc=kc: e.matmul(gp[:, 0:n], lhsT=wvg[:, 1024 + kc * 128:1024 + (kc + 1) * 128],
                                                   rhs=xnT[:, kc, s:s + n], start=(kc == 0), stop=(kc == 7)),
                     reads=[wvgreg, ('xn', kc, g)], writes=[greg])
            sg, sgreg = ring('fb', fb)
            P.op('act', lambda e: e.activation(out=sg[:, 0:n], in_=gp[:, 0:n], func=AF.Exp, scale=-1.0), reads=[greg], writes=[sgreg])
            P.op('act', lambda e: e.activation(out=sg[:, 0:n], in_=sg[:, 0:n], func=AF.Ln, bias=onec), reads=[sgreg, RC], writes=[sgreg])
            P.op('act', lambda e: e.activation(out=sg[:, 0:n], in_=sg[:, 0:n], func=AF.Exp, scale=-1.0), reads=[sgreg], writes=[sgreg])
            gs, gsreg = ring('sq', sqr)
            P.op('dve', lambda e: e.tensor_tensor(out=gs[:, 0:n], in0=gp[:, 0:n], in1=sg[:, 0:n], op=ALU.mult),
                 reads=[greg, sgreg], writes=[gsreg])
            gate[g] = (gs, gsreg)

        TB = [(PT, ('pt',)), (PS[6][:, :].bitcast(BF16), ('ps', 6))]
        for r in range(4):
            tb, tbreg = TB[r % 2]
            for i in range(4):
                n = 4 * r + i
                P.op('pe', lambda e, n=n, i=i, tb=tb: e.transpose(tb[:, i * 128:(i + 1) * 128], SL[:, ks, n * 128:(n + 1) * 128], ident),
                     reads=[('S', ks, grp(n)), RC], writes=[tbreg])
            kd, kdreg = ring('ba', ba)
            P.op('act', lambda e, kd=kd, tb=tb: e.activation(out=kd[:, 0:512], in_=tb[:, 0:512], func=AF.Identity, scale=kdec),
                 reads=[tbreg, RC], writes=[kdreg])
            for i in range(4):
                n = 4 * r + i
                P.op('pe', lambda e, n=n, i=i, r=r, kd=kd: e.matmul(PS[r][:, i * 128:(i + 1) * 128], lhsT=kd[:, i * 128:(i + 1) * 128],
                                                                  rhs=SL[:, vs, n * 128:(n + 1) * 128], start=True, stop=True),
                     reads=[kdreg, ('S', vs, grp(n))], writes=[('ps', r)])
        gate = {}
        for n in range(16):
            if n in (0, 5, 10):
                emit_gate(n // 5, (6, 4, 5)[n // 5])
            r, i = divmod(n, 4)
            P.op('dve', lambda e, n=n, r=r, i=i: e.scalar_tensor_tensor(out=Rf[(n + 1) % 2][:, :], in0=Rf[n % 2][:, :], scalar=cd,
                                                                      in1=PS[r][:, i * 128:(i + 1) * 128], op0=ALU.mult, op1=ALU.add),
                 reads=[('Rf', n % 2), ('ps', r)], writes=[('Rf', (n + 1) % 2)])
            P.op('act', lambda e, n=n: e.activation(out=SL[:, RBs, (n + 1) * 128:(n + 2) * 128], in_=Rf[(n + 1) % 2][:, :], func=AF.Identity),
                 reads=[('Rf', (n + 1) % 2)], writes=[('S', RBs, grp(n + 1))])
        def emit_ST(g):
            s, n = GROUPS[g]
            ncx = n // 128
            bk = g % 2
            for ci in range(ncx):
                cs = s + ci * 128
                P.op('pe', lambda e, ci=ci, cs=cs: e.matmul(PS[bk][:, ci * 128:(ci + 1) * 128], lhsT=SL[:, ks, cs:cs + 128],
                                                          rhs=SL[:, u, cs:cs + 128], start=True, stop=True),
                     reads=[('S', u, g), ('S', ks, g)], writes=[('ps', bk)])
            v3 = lambda ap: ap.rearrange("p (c i) -> p c i", c=ncx)
            bc = lambda ap: ap.unsqueeze(1).broadcast_to([128, ncx, 128])
            stm, stmreg = ring('ba', ba)
            P.op('dve', lambda e: e.tensor_tensor(out=v3(stm[:, 0:n]), in0=v3(PS[bk][:, 0:n]), in1=bc(dmask), op=ALU.mult),
                 reads=[('ps', bk), RC], writes=[stmreg])
            qd, qdreg = ring('ba', ba)
            P.op('dve', lambda e: e.tensor_tensor(out=v3(qd[:, 0:n]), in0=v3(SL[:, u, s:s + n]), in1=bc(qdec), op=ALU.mult),
                 reads=[('S', u, g), RC], writes=[qdreg])
            return stm, stmreg, qd, qdreg

        def emit_O(g, st):
            stm, stmreg, qd, qdreg = st
            s, n = GROUPS[g]
            OBk = 4 + g % 2
            OB = PS[OBk]
            for ci in range(n // 128):
                cs = s + ci * 128
                P.op('pe', lambda e, ci=ci, cs=cs: e.matmul(OB[:, ci * 128:(ci + 1) * 128], lhsT=SL[:, vs, cs:cs + 128],
                                                          rhs=stm[:, ci * 128:(ci + 1) * 128], start=True, stop=False),
                     reads=[('S', vs, g), stmreg], writes=[('ps', OBk)])
                P.op('pe', lambda e, ci=ci, cs=cs: e.matmul(OB[:, ci * 128:(ci + 1) * 128], lhsT=SL[:, RBs, cs:cs + 128],
                                                          rhs=qd[:, ci * 128:(ci + 1) * 128], start=False, stop=True),
                     reads=[('S', RBs, g), qdreg], writes=[('ps', OBk)])
            post.append([1, lambda: fin1(g, OBk)])
            tick()

        post = []
        PTF = PT[:, :].bitcast(F32)

        def tick(force=False):
            for ent in list(post):
                ent[0] -= 1
                if force or ent[0] <= 0:
                    post.remove(ent)
                    ent[1]()

        def fin1(g, OBk):
            s, n = GROUPS[g]
            OB = PS[OBk]
            cent = cb[:, CB_CENT:CB_CENT + 128]
            ck = 2 + g % 2
            ob, obreg = ring('ba', ba)
            P.op('act', lambda e: e.activation(out=ob[:, 0:n], in_=OB[:, 0:n], func=AF.Identity), reads=[('ps', OBk)], writes=[obreg])
            P.op('pe', lambda e: e.matmul(PS[ck][:, 0:n], lhsT=cent, rhs=ob[:, 0:n], start=True, stop=True),
                 reads=[obreg, RC], writes=[('ps', ck)])
            post.append([1, lambda: fin2(g, ck)])

        def fin2(g, ck):
            s, n = GROUPS[g]
            csq, csqreg = ring('ba', ba)
            P.op('act', lambda e: e.activation(out=csq[:, 0:n], in_=PS[ck][:, 0:n], func=AF.Square), reads=[('ps', ck)], writes=[csqreg])
            P.op('pe', lambda e: e.matmul(PS[6][:, 0:n], lhsT=ones, rhs=csq[:, 0:n], start=True, stop=True),
                 reads=[csqreg, RC], writes=[('ps', 6)])
            post.append([1, lambda: fin3(g, ck)])

        def fin3(g, ck):
            s, n = GROUPS[g]
            rs, rsreg = ring('fb', fb)
            P.op('act', lambda e: e.activation(out=rs[:, 0:n], in_=PS[6][:, 0:n], func=AF.Ln, scale=1.0 / 128, bias=epsc),
                 reads=[('ps', 6), RC], writes=[rsreg])
            P.op('act', lambda e: e.activation(out=rs[:, 0:n], in_=rs[:, 0:n], func=AF.Exp, scale=-0.5), reads=[rsreg], writes=[rsreg])
            fin4(g, ck, rs, rsreg)

        def fin4(g, ck, rs, rsreg):
            s, n = GROUPS[g]
            nrm, nrmreg = ring('fa', fa)
            P.op('dve', lambda e: e.scalar_tensor_tensor(out=nrm[:, 0:n], in0=PS[ck][:, 0:n], scalar=gnw, in1=rs[:, 0:n],
                                                          op0=ALU.mult, op1=ALU.mult),
                 reads=[('ps', ck), rsreg, RC], writes=[nrmreg])
            gs, gsreg = gate[g]
            P.op('dve', lambda e: e.tensor_tensor(out=SL[:, u, s:s + n], in0=nrm[:, 0:n], in1=gs[:, 0:n], op=ALU.mult),
                 reads=[nrmreg, gsreg], writes=[('S', u, g)])
            done.add(g)

        done = set()

        pending = None
        need_gate = [3, 4]
        for g in range(5):
            if need_gate and (need_gate[0] - 3) in done:
                emit_gate(need_gate.pop(0), 'ptf')
            st = emit_ST(g)
            if pending is not None:
                emit_O(*pending)
            pending = (g, st)
        emit_O(*pending)
        while post or need_gate:
            if need_gate and (need_gate[0] - 3) in done:
                emit_gate(need_gate.pop(0), 'ptf')
            tick(force=True)

    def retention_unit(l, h):
        u = h
        ks = 8 + 2 * (u % 2)
        vs = 9
        wqk, wqkreg = wload(wmix_d[l, u, 0], 2048)
        wvg, wvgreg = wload(wmix_d[l, u, 1], 2048)
        for g, (s, n) in enumerate(GROUPS):
            tabs_for[g] = tload(0, s, n)
            project_rot_group(0, wqk, wqkreg, u, CB_PERMR, g)
            project_rot_group(1, wqk, wqkreg, ks, CB_PERMR, g)
            norm_pop()
        rot_flush()
        project_v(wvg, wvgreg, vs)
        if DBG_MIX == 'B':
            return
        retention_attention(l, h, u, ks, vs, wvg, wvgreg)

    def diff_attention(l, u, ks, vs, om):
        neglam = lamv[:, 4 + l:5 + l]
        sw = cf[:, CF_PVEC + PV_SW + l:CF_PVEC + PV_SW + l + 1]
        blocks = []
        for g, (s, n) in enumerate(GROUPS):
            last = (s + n) // 128 - 1
            for m in range(2):
                for jb in range(last + 1):
                    blocks.append((g, m, jb, last))
        odset = {}
        tres = {}
        deferred = []
        post = []

        def emit_S(blk):
            g, m, jb, last = blk
            s, n = GROUPS[g]
            off = max(0, jb * 128 - s)
            nq = n - off
            diag = jb * 128 >= s
            sk = rr('dst', 3)
            SP_, sreg = SBANK[sk]
            kg = 0 if jb == 0 else 1 + (jb - 1) // 4
            P.op('pe', lambda e: e.matmul(SP_[:, 0:nq], lhsT=SL[:, ks[m], jb * 128:(jb + 1) * 128],
                                          rhs=SL[:, u, s + off:s + n], start=True, stop=(not diag)),
                 reads=[('S', ks[m], kg), ('S', u, g)], writes=[sreg])
            if diag:
                P.op('pe', lambda e: e.matmul(SP_[:, 0:128], lhsT=ident, rhs=cmask, start=False, stop=True),
                     reads=[RC], writes=[sreg])
            return sk, off, nq, kg

        def emit_rest(blk, sinfo):
            g, m, jb, last = blk
            sk, off, nq, kg = sinfo
            SP_, sreg = SBANK[sk]
            if (g, m) not in odset:
                odset[(g, m)] = 2 + 2 * rr('dod', 2)
                while any(ent[2] == odset[(g, m)] for ent in post):
                    for ent in list(post):
                        if ent[2] == odset[(g, m)]:
                            post.remove(ent)
                            ent[1]()
            OBk = odset[(g, m)]
            DBk = OBk + 1
            OB, DB = PS[OBk], PS[DBk]
            pt, ptreg = ring('ba', ba)
            P.op('act', lambda e: e.activation(out=pt[:, 0:nq], in_=SP_[:, 0:nq], func=AF.Exp, scale=0.125),
                 reads=[sreg], writes=[ptreg])
            P.op('pe', lambda e: e.matmul(OB[:, off:off + nq], lhsT=SL[:, vs, jb * 128:(jb + 1) * 128], rhs=pt[:, 0:nq],
                                          start=(jb == 0), stop=(jb == last)),
                 reads=[('S', vs, kg), ptreg], writes=[('ps', OBk)])
            P.op('pe', lambda e: e.matmul(DB[:, off:off + nq], lhsT=(ones_pad if jb == 0 else ones), rhs=pt[:, 0:nq],
                                          start=(jb == 0), stop=(jb == last)),
                 reads=[RC, ptreg], writes=[('ps', DBk)])
            if jb == last:
                post.append([2, lambda: normalize_act(g, m, OBk), OBk])

        def normalize_act(g, m, OBk):
            DBk = OBk + 1
            DB = PS[DBk]
            s, n = GROUPS[g]
            o0 = PAD if g == 0 else 0
            r, rreg = ring('fb', fb)
            P.op('act', lambda e: e.activation(out=r[:, o0:n], in_=DB[:, o0:n], func=AF.Ln), reads=[('ps', DBk)], writes=[rreg])
            P.op('act', lambda e: e.activation(out=r[:, o0:n], in_=r[:, o0:n], func=AF.Exp, scale=-1.0), reads=[rreg], writes=[rreg])
            post.append([3, lambda: normalize_dve(g, m, OBk, r, rreg), OBk])

        def normalize_dve(g, m, OBk, r, rreg):
            OB = PS[OBk]
            s, n = GROUPS[g]
            o0 = PAD if g == 0 else 0
            t, treg_ = ring('fa', fa)
            P.op('dve', lambda e: e.tensor_tensor(out=t[:, o0:n], in0=OB[:, o0:n], in1=r[:, o0:n], op=ALU.mult),
                 reads=[('ps', OBk), rreg], writes=[treg_])
            tres[(g, m)] = (t, treg_)
            if m == 1:
                post.append([1, lambda: finalize1(g), None])

        def finalize1(g):
            s, n = GROUPS[g]
            o0 = PAD if g == 0 else 0
            (t0, t0reg), (t1, t1reg) = tres[(g, 0)], tres[(g, 1)]
            o, oreg = ring('fa', fa)
            P.op('dve', lambda e: e.scalar_tensor_tensor(out=o[:, o0:n], in0=t1[:, o0:n], scalar=neglam, in1=t0[:, o0:n],
                                                          op0=ALU.mult, op1=ALU.add),
                 reads=[t0reg, t1reg, ('lamv',)], writes=[oreg])
            sq, sqreg = ring('ba', ba)
            P.op('dve', lambda e: e.tensor_tensor(out=sq[:, o0:n], in0=o[:, o0:n], in1=o[:, o0:n], op=ALU.mult), reads=[oreg], writes=[sqreg])
            P.op('pe', lambda e: e.matmul(PS[6][:, o0:n], lhsT=ones, rhs=sq[:, o0:n], start=True, stop=True),
                 reads=[sqreg, RC], writes=[('ps', 6)])
            post.append([4, lambda: finalize2(g, o, oreg), None])

        def finalize2(g, o, oreg):
            s, n = GROUPS[g]
            o0 = PAD if g == 0 else 0
            rs, rsreg = ring('fb', fb)
            P.op('act', lambda e: e.activation(out=rs[:, o0:n], in_=PS[6][:, o0:n], func=AF.Ln, scale=1.0 / (128 * om * om), bias=epsl[l]),
                 reads=[('ps', 6), RC], writes=[rsreg])
            P.op('act', lambda e: e.activation(out=rs[:, o0:n], in_=rs[:, o0:n], func=AF.Exp, scale=-0.5), reads=[rsreg], writes=[rsreg])
            P.op('dve', lambda e: e.scalar_tensor_tensor(out=SL[:, u, s + o0:s + n], in0=o[:, o0:n], scalar=sw, in1=rs[:, o0:n],
                                                          op0=ALU.mult, op1=ALU.mult),
                 reads=[oreg, rsreg, RC], writes=[('S', u, g)])

        def tick(force=False):
            for ent in list(post):
                ent[0] -= 1
                if force or ent[0] <= 0:
                    post.remove(ent)
                    ent[1]()

        pend = []
        for blk in blocks:
            sinfo = emit_S(blk)
            pend.append((blk, sinfo))
            if len(pend) > 2:
                emit_rest(*pend.pop(0))
                tick()
        while pend:
            emit_rest(*pend.pop(0))
            tick()
        while post:
            tick(force=True)

    def diff_unit(l, h, lam_init):
        u = 4 + h
        ks = (8, 10)
        vs = 9
        wqk, wqkreg = wload(wmix_d[l, u, 0], 2048)
        wv, wvreg = wload(wmix_d[l, u, 1, :, 0:1024], 1024)
        if h == 0:
            P.op('dve', lambda e: e.memset(SL[64:128, ks[0], :], 0.0), writes=[('S', ks[0], g) for g in range(5)])
            P.op('dve', lambda e: e.memset(SL[0:64, ks[1], :], 0.0), writes=[('S', ks[1], g) for g in range(5)])
        for g, (s, n) in enumerate(GROUPS):
            tabs_for[g] = tload(1, s, n)
            project_rot_group(0, wqk, wqkreg, u, CB_PERMD, g)
            project_rot_group(1, wqk, wqkreg, None, CB_PERMD, g, split=ks)
        rot_flush()
        project_v(wv, wvreg, vs)
        diff_attention(l, u, ks, vs, 1.0 - lam_init)

    def lam_compute(l):
        lam_init = 0.8 - 0.6 * math.exp(-0.3 * l)
        for t in range(2):
            a = lamin[:, (l * 4 + 2 * t) * 64:(l * 4 + 2 * t + 1) * 64]
            b = lamin[:, (l * 4 + 2 * t + 1) * 64:(l * 4 + 2 * t + 2) * 64]
            P.op('dve', lambda e, a=a, b=b: e.tensor_tensor(out=lamw[:, :], in0=a, in1=b, op=ALU.mult),
                 reads=[RC, ('lamw',)], writes=[('lamw',)])
            P.op('dve', lambda e, t=t: e.tensor_reduce(out=lamv[:, t:t + 1], in_=lamw[:, :], axis=mybir.AxisListType.X, op=ALU.add),
                 reads=[('lamw',), ('lamv',)], writes=[('lamv',)])
            P.op('act', lambda e, t=t: e.activation(out=lamv[:, 2 + t:3 + t], in_=lamv[:, t:t + 1], func=AF.Exp),
                 reads=[('lamv',)], writes=[('lamv',)])
        P.op('dve', lambda e: e.scalar_tensor_tensor(out=lamv[:, 4 + l:5 + l], in0=lamv[:, 3:4], scalar=-lam_init, in1=lamv[:, 2:3],
                                                      op0=ALU.add, op1=ALU.subtract),
             reads=[('lamv',)], writes=[('lamv',)])
        return lam_init

    def mixer(l):
        rmsnorm(PV_NORM + (l * 3 + 1) * 8)
        lam_init = lam_compute(l)
        if DBG_MIX == 'A':
            return
        for h in range(4):
            retention_unit(l, h)
            if DBG_MIX in ('B', 'B1', 'C', 'R1', 'R2', 'R3'):
                return
        if DBG_MIX == 'D':
            return
        for h in range(4):
            diff_unit(l, h, lam_init)
            if DBG_MIX == 'F':
                return
        for c in range(8):
            wt, wreg = wload(wout_d[l, c], 1024)
            for g, (s, n) in enumerate(GROUPS):
                mm_out_group(wt, wreg, 8, c, g, s, n, 1.0)

    stages = []
    for l in range(2):
        stages += [('ffn', l, 0), ('mix', l), ('ffn', l, 1)]
    nst = len(stages) if stop is None else stop
    for st in stages[:nst]:
        if st[0] == 'ffn':
            ffn(st[1], st[2])
        else:
            mixer(st[1])
    outregs = []
    if stop is None:
        for g, (s, n) in enumerate(GROUPS):
            rmsnorm_group(PV_NORM + 48, True, g, s, n)
            if g >= 1:
                P.dma('sp', lambda e, s=s, n=n: e.dma_start(out=out_d[:, :, s - 128:s - 128 + n], in_=hT[:, :, s:s + n]), 'out',
                      reads=[('h', c, g) for c in range(8)], writes=[('out', g)])
                outregs.append(('out', g))
    else:
        for c in range(8):
            P.dma('sp', lambda e, c=c: e.dma_start(out=out_d[:, c, :], in_=hT[:, c, 128:TP]), 'out',
                  reads=[('h', c, g) for g in range(1, 5)], writes=[('out', c)])
            outregs.append(('out', c))
    if dump == 'h':
        for c in range(8):
            P.dma('sp', lambda e, c=c: e.dma_start(out=dbg_d[:, c, :], in_=hT[:, c, :]), 'out',
                  reads=[('h', c, g) for g in range(5)], writes=[('dbg', c)])
            outregs.append(('dbg', c))
    P.op('sp', None, reads=outregs)
    P.build()
    return nc, P


def _bf(a):
    return np.asarray(a, dtype=np.float32).astype(ml_dtypes.bfloat16)


def _const_tables():
    pos = (np.arange(TP, dtype=np.float32) - np.float32(PAD))
    angle = (np.float32(10000.0) ** (-np.linspace(0.0, 1.0, 64, dtype=np.float32))).astype(np.float32)
    fr = (pos[None, :] * np.repeat(angle, 2)[:, None]).astype(np.float32)
    cosr = np.cos(fr).astype(np.float32)
    sgn = np.where(np.arange(128) % 2 == 0, -1.0, 1.0).astype(np.float32)[:, None]
    sinr = (np.sin(fr) * sgn).astype(np.float32)
    r = 16
    inv = (np.float32(500000.0) ** (-np.arange(0, r, 2, dtype=np.float32) / r)).astype(np.float32)
    cosd = np.ones((128, TP), np.float32)
    sind = np.zeros((128, TP), np.float32)
    for p in range(128):
        dd = p % 64
        if dd < r:
            f = (pos * inv[dd % 8]).astype(np.float32)
            cosd[p] = np.cos(f)
            sind[p] = np.sin(f) * (-1.0 if dd < 8 else 1.0)
    tabs = np.stack([cosr, sinr, cosd, sind]).astype(np.float32)
    cbm = np.zeros((128, CB_N), np.float32)
    cbm[:, CB_ONES:CB_ONES + 128] = 1.0
    cbm[PAD:, CB_ONESPAD:CB_ONESPAD + 128] = 1.0
    cbm[:, CB_IDENT:CB_IDENT + 128] = np.eye(128)
    for m in range(128):
        src = m + 1 if m % 2 == 0 else m - 1
        cbm[src, CB_PERMR + m] = 1.0
        dd = m % 64
        if dd < 8:
            cbm[m + 8, CB_PERMD + m] = 1.0
        elif dd < 16:
            cbm[m - 8, CB_PERMD + m] = 1.0
    jj = np.arange(128)[:, None]
    ii = np.arange(128)[None, :]
    cbm[:, CB_CMASK:CB_CMASK + 128] = np.where(jj <= ii, 0.0, -30000.0)
    cbm[:, CB_CENT:CB_CENT + 128] = np.eye(128) - 1.0 / 128.0
    cfm = np.zeros((128, CF_N), np.float32)
    cfm[:, CF_EPS] = EPS
    cfm[:, CF_ONE] = 1.0
    for l_ in range(2):
        om_ = 1.0 - (0.8 - 0.6 * math.exp(-0.3 * l_))
        cfm[:, CF_EPS + 1 + l_] = EPS / (om_ * om_)
    for h in range(4):
        lg = math.log(1.0 - 2.0 ** (-5.0 - h))
        rel = (ii - jj).astype(np.float64)
        dm = np.where(rel >= 0, np.exp(lg * np.maximum(rel, 0.0)), 0.0) * (128.0 ** -0.5)
        cfm[:, CF_DMASK + h * 128:CF_DMASK + (h + 1) * 128] = dm
        cfm[:, CF_QDEC + h * 128:CF_QDEC + (h + 1) * 128] = np.exp(lg * (np.arange(128) + 1.0))[None, :]
        cfm[:, CF_KDEC + h] = np.exp(lg * (127.0 - np.arange(128))) * (128.0 ** -0.5)
    return tabs, cbm, cfm


def _prep(inputs):
    f = lambda k: np.asarray(inputs[k], dtype=np.float32)
    tabs, cbm, cfm = _const_tables()
    cvec = lambda v: np.ascontiguousarray(v.reshape(-1, 128).T)
    norms = [f("ffn1_norm"), f("mix_norm"), f("ffn2_norm")]
    for l in range(2):
        for w in range(3):
            c0 = CF_PVEC + PV_NORM + (l * 3 + w) * 8
            cfm[:, c0:c0 + 8] = cvec(norms[w][l])
        cfm[:, CF_PVEC + PV_GN + l * 4:CF_PVEC + PV_GN + l * 4 + 4] = cvec(f("ret_gn_w")[l])
        cfm[:, CF_PVEC + PV_SW + l] = f("diff_subln_w")[l]
    cfm[:, CF_PVEC + PV_NORM + 48:CF_PVEC + PV_NORM + 56] = cvec(f("final_norm"))
    lam = np.stack([f("diff_lambda_q1"), f("diff_lambda_k1"), f("diff_lambda_q2"), f("diff_lambda_k2")], axis=1)
    lam = np.ascontiguousarray(np.broadcast_to(lam.reshape(1, 512), (128, 512)))

    def fm_tile(w):
        nc_ = w.shape[1] // 128
        return w.reshape(8, 128, nc_, 128).transpose(2, 1, 0, 3)

    wgu = np.empty((2, 2, NF, 128, 2, 8, 128), np.float32)
    wd = np.empty((2, 2, 2, 8, 128, NFH, 128), np.float32)
    for l in range(2):
        for j, (kg, ku, kd_) in enumerate([("ffn1_w_gate", "ffn1_w_up", "ffn1_w_down"), ("ffn2_w_gate", "ffn2_w_up", "ffn2_w_down")]):
            wgu[l, j, :, :, 0] = fm_tile(f(kg)[l])
            wgu[l, j, :, :, 1] = fm_tile(f(ku)[l])
            wd[l, j] = f(kd_)[l].reshape(2, NFH, 128, 8, 128).transpose(0, 3, 2, 1, 4)
    win = f("w_in")
    wmix = np.zeros((2, 8, 2, 128, 2, 8, 128), np.float32)
    for l in range(2):
        t = fm_tile(win[l])
        for h in range(4):
            wmix[l, h, 0, :, 0] = t[h]
            wmix[l, h, 0, :, 1] = t[4 + h]
            wmix[l, h, 1, :, 0] = t[8 + h]
            wmix[l, h, 1, :, 1] = t[12 + h]
            wmix[l, 4 + h, 0, :, 0] = t[16 + h]
            wmix[l, 4 + h, 0, :, 1] = t[20 + h]
            wmix[l, 4 + h, 1, :, 0] = t[24 + h]
    wout = np.stack([fm_tile(f("w_out")[l]) for l in range(2)])
    shared = {
        "metaT": np.ascontiguousarray(f("meta_tokens").reshape(NMETA, 8, 128).transpose(2, 1, 0)),
        "cb": _bf(cbm), "cf": cfm, "lam": lam, "tabs": tabs,
        "wgu": wgu.reshape(2, 2, NF, 128, 2048), "wd": wd.reshape(2, 2, 2, 8, 128, NFH * 128),
        "wmix": wmix.reshape(2, 8, 2, 128, 2048), "wout": np.ascontiguousarray(wout.reshape(2, 8, 128, 1024)),
    }
    x = f("x")
    xTs = [np.ascontiguousarray(x[b].reshape(SEQ, 8, 128).transpose(2, 1, 0)) for b in range(x.shape[0])]
    return shared, xTs


def run(inputs, cores=None, stop=None, dump=None, trace=False):
    shared, xTs = _prep(inputs)
    cores = list(range(8)) if cores is None else cores
    nc, P = build_program(stop=stop, dump=dump)
    in_maps = [dict(shared, xT=xTs[b]) for b in cores]
    res = run_bass_kernel_spmd(nc, in_maps, core_ids=list(range(len(cores))), trace=trace)
    return res, P


def kernel(**inputs):
    res, _ = run(inputs)
    outs = [np.asarray(r["outT"], dtype=np.float32) for r in res.results]
    out = np.stack([o.transpose(2, 1, 0).reshape(SEQ, D) for o in outs])
    return np.ascontiguousarray(out.astype(np.float32))
```

```python
import math
import numpy as np
import ml_dtypes
import concourse.bass as bass
import concourse.mybir as mybir
from concourse.bass_utils import run_bass_kernel_spmd

F32 = mybir.dt.float32
BF16 = mybir.dt.bfloat16
AF = mybir.ActivationFunctionType
ALU = mybir.AluOpType

D = 1024
SEQ = 2048
NMETA = 16
PAD = 112
TP = 2176
NCH = 17
DFF = 2816
NF = 22
NFH = 11
EPS = 1e-6
GROUPS = [(0, 128)] + [(128 + 512 * g, 512) for g in range(4)]
NSLOT = 11
NW = 4
WCOLS = 2048
SAME_ENGINE_SYNC = True
DBG_HACK = 0
WARM = 0
ROT_ADD_ENG = 'dve'
DBG_MIX = None

CB_ONES, CB_ONESPAD, CB_IDENT, CB_PERMR, CB_PERMD, CB_CMASK, CB_CENT = [i * 128 for i in range(7)]
CB_N = 7 * 128
CF_DMASK = 0
CF_QDEC = 512
CF_KDEC = 1024
CF_PVEC = 1028
PV_NORM = 0
PV_GN = 56
PV_SW = 64
CF_EPS = CF_PVEC + 66
CF_ONE = CF_EPS + 3
CF_N = CF_ONE + 1


class Prog:
    def __init__(self, nc):
        self.nc = nc
        self.ops = []
        self.dma_sems = {}

    def op(self, eng, fn, reads=(), writes=()):
        self.ops.append(dict(eng=eng, fn=fn, reads=list(reads), writes=list(writes), dma=None))

    def dma(self, eng, fn, slot, reads=(), writes=()):
        self.ops.append(dict(eng=eng, fn=fn, reads=list(reads), writes=list(writes), dma=slot))

    def build(self):
        nc = self.nc
        ops = self.ops
        engs = ['pe', 'act', 'dve', 'pool', 'sp']
        last_w = {}
        readers = {}
        for i, o in enumerate(ops):
            deps = set()
            key = ('dma', i) if o['dma'] is not None else o['eng']
            for r in o['reads']:
                if r in last_w:
                    deps.add(last_w[r])
                if r[0] in ('ps', 'pt'):
                    for k2, i2 in readers.get(r, {}).items():
                        if k2 != key:
                            deps.add(i2)
            for w in o['writes']:
                if w in last_w:
                    deps.add(last_w[w])
                rd = readers.get(w)
                if rd:
                    deps.update(rd.values())
            deps.discard(i)
            o['deps'] = deps
            for r in o['reads']:
                readers.setdefault(r, {})[key] = i
            for w in o['writes']:
                last_w[w] = i
                readers[w] = {}
        signal = set()
        for i, o in enumerate(ops):
            for d in o['deps']:
                p = ops[d]
                if p['dma'] is not None:
                    continue
                if p['eng'] != o['eng'] or (SAME_ENGINE_SYNC and o['eng'] != 'pe'):
                    signal.add(d)
        sem = {e: nc.alloc_semaphore('s_' + e) for e in engs}
        cnt = {e: 0 for e in engs}
        for i, o in enumerate(ops):
            if o['dma'] is not None:
                slot = o['dma']
                if slot not in self.dma_sems:
                    self.dma_sems[slot] = [nc.alloc_semaphore('d_' + str(slot)), 0]
                ent = self.dma_sems[slot]
                ent[1] += 16
                o['tok'] = (ent[0], ent[1])
            elif i in signal:
                cnt[o['eng']] += 1
                o['tok'] = (sem[o['eng']], cnt[o['eng']])
            else:
                o['tok'] = None
        waited = {e: {} for e in engs}
        for i, o in enumerate(ops):
            need = {}
            for d in o['deps']:
                p = ops[d]
                if p['dma'] is None and p['eng'] == o['eng'] and not (SAME_ENGINE_SYNC and o['eng'] != 'pe'):
                    continue
                tok = p['tok']
                assert tok is not None
                k = tok[0]
                if need.get(k, (None, 0))[1] < tok[1]:
                    need[k] = tok
            ws = []
            for k, tok in need.items():
                if waited[o['eng']].get(k, 0) < tok[1]:
                    waited[o['eng']][k] = tok[1]
                    ws.append(tok)
            o['waits'] = ws
        per = {e: [o for o in ops if o['eng'] == e] for e in engs}
        self.stats = {e: len(per[e]) for e in engs}
        self.stats['signals'] = dict(cnt)

        def run(e, lst):
            for o in lst:
                for (s, v) in o['waits']:
                    e.wait_ge(s, v)
                if o['fn'] is None:
                    continue
                ins = o['fn'](e)
                if o['tok'] is not None:
                    ins.then_inc(o['tok'][0], 16 if o['dma'] is not None else 1)

        with nc.Block() as block:
            @block.tensor
            def _(e):
                run(e, per['pe'])

            @block.scalar
            def _(e):
                run(e, per['act'])

            @block.vector
            def _(e):
                run(e, per['dve'])

            @block.gpsimd
            def _(e):
                run(e, per['pool'])

            @block.sync
            def _(e):
                run(e, per['sp'])


def build_program(stop=None, dump=None):
    nc = bass.Bass("TRN2", target_bir_lowering=False)
    P = Prog(nc)

    def din(name, shape, dt=F32):
        return nc.dram_tensor(name, list(shape), dt, kind="ExternalInput").ap()

    xT_d = din("xT", [128, 8, SEQ])
    meta_d = din("metaT", [128, 8, NMETA])
    cb_d = din("cb", [128, CB_N], BF16)
    cf_d = din("cf", [128, CF_N])
    lam_d = din("lam", [128, 2 * 4 * 64])
    tabs_d = din("tabs", [4, 128, TP])
    wgu_d = din("wgu", [2, 2, NF, 128, 2048])
    wd_d = din("wd", [2, 2, 2, 8, 128, NFH * 128])
    wmix_d = din("wmix", [2, 8, 2, 128, 2048])
    wout_d = din("wout", [2, 8, 128, 1024])
    out_d = nc.dram_tensor("outT", [128, 8, SEQ], F32, kind="ExternalOutput").ap()
    dbg_d = None
    if dump is not None:
        dbg_d = nc.dram_tensor("dbg", [128, 8, TP], F32, kind="ExternalOutput").ap()

    sb = nc.alloc_sbuf_tensor
    hT = sb("hT", [128, 8, TP], F32)
    xnT = sb("xnT", [128, 8, TP], BF16)
    SL = sb("slots", [128, NSLOT, TP], BF16)
    wsl = [sb(f"wsl{i}", [128, WCOLS], BF16) for i in range(NW)]
    tabt = [sb(f"tab{i}", [128, 2, 512], F32) for i in range(2)]
    cb = sb("cbs", [128, CB_N], BF16)
    cf = sb("cfs", [128, CF_N], F32)
    lamin = sb("lamin", [128, 512], F32)
    lamw = sb("lamw", [128, 64], F32)
    lamv = sb("lamv", [128, 8], F32)
    NR = 3
    sqr = [sb(f"sq{i}", [128, 512], BF16) for i in range(NR)]
    fa = [sb(f"fa{i}", [128, 512], F32) for i in range(4)]
    fb = [sb(f"fb{i}", [128, 512], F32) for i in range(4)]
    ba = [sb(f"ba{i}", [128, 512], BF16) for i in range(6)]
    Rf = [sb(f"Rf{i}", [128, 128], F32) for i in range(2)]

    PS = [nc.alloc_psum_tensor(f"ps{i}", [128, 512], F32) for i in range(7)]
    PT = nc.alloc_psum_tensor("pst", [128, 1024], BF16)

    SBANK = [(PS[0], ('ps', 0)), (PS[1], ('ps', 1)), (PT[:, :].bitcast(F32), ('pt',))]
    ctr = {}

    def rr(name, n):
        v = ctr.get(name, 0)
        ctr[name] = v + 1
        return v % n

    def ring(name, lst):
        i = rr(name, len(lst))
        return lst[i], (name, i)

    ones = cb[:, CB_ONES:CB_ONES + 128]
    ones_pad = cb[:, CB_ONESPAD:CB_ONESPAD + 128]
    ident = cb[:, CB_IDENT:CB_IDENT + 128]
    cmask = cb[:, CB_CMASK:CB_CMASK + 128]
    RC = ('const',)
    epsc = cf[:, CF_EPS:CF_EPS + 1]
    onec = cf[:, CF_ONE:CF_ONE + 1]
    epsl = [cf[:, CF_EPS + 1 + l_:CF_EPS + 2 + l_] for l_ in range(2)]

    P.dma('sp', lambda e: e.dma_start(out=cb[:, :], in_=cb_d), 'const', writes=[RC])
    P.dma('sp', lambda e: e.dma_start(out=cf[:, :], in_=cf_d), 'const', writes=[RC])
    P.dma('sp', lambda e: e.dma_start(out=lamin[:, :], in_=lam_d), 'const', writes=[RC])
    hregs = lambda c: [('h', c, g) for g in range(5)]
    P.op('dve', lambda e: e.memset(hT[:, :, 0:PAD], 0.0), writes=[('h', c, 0) for c in range(8)])
    P.dma('sp', lambda e: e.dma_start(out=hT[:, :, PAD:128], in_=meta_d), ('xin', 0),
          reads=[('h', c, 0) for c in range(8)], writes=[('h', c, 0) for c in range(8)])
    for g in range(1, 5):
        s_, n_ = GROUPS[g]
        P.dma('sp', lambda e, s_=s_, n_=n_: e.dma_start(out=hT[:, :, s_:s_ + n_], in_=xT_d[:, :, s_ - 128:s_ - 128 + n_]), ('xin', g),
              writes=[('h', c, g) for c in range(8)])

    def wload(src_ap, ncols):
        i = rr('w', NW)
        t = wsl[i]
        reg = ('w', i)
        P.dma('pool', lambda e: e.dma_start(out=t[:, 0:ncols], in_=src_ap, max_dma_last_dim=8192),
              ('w', i), writes=[reg])
        return t, reg

    def tload(k, s, n):
        i = rr('tab', 2)
        t = tabt[i]
        regs = [('tab', i, 0), ('tab', i, 1)]
        P.dma('sp', lambda e: e.dma_start(out=t[:, 0, 0:n], in_=tabs_d[2 * k, :, s:s + n]), ('tab', i, 0), writes=[regs[0]])
        P.dma('sp', lambda e: e.dma_start(out=t[:, 1, 0:n], in_=tabs_d[2 * k + 1, :, s:s + n]), ('tab', i, 1), writes=[regs[1]])
        return t, regs

    def rmsnorm_group(pvcol, final, g, s, n):
        ss = PS[6]
        for c in range(8):
            sq, sqreg = ring('sq', sqr)
            P.op('act', lambda e, sq=sq, c=c: e.activation(out=sq[:, 0:n], in_=hT[:, c, s:s + n], func=AF.Square),
                 reads=[('h', c, g)], writes=[sqreg])
            P.op('pe', lambda e, sq=sq, c=c: e.matmul(ss[:, 0:n], lhsT=ones, rhs=sq[:, 0:n], start=(c == 0), stop=(c == 7)),
                 reads=[sqreg, RC], writes=[('ps', 6)])
        rt, rtreg = ring('fa', fa)
        P.op('act', lambda e: e.activation(out=rt[:, 0:n], in_=ss[:, 0:n], func=AF.Ln, scale=1.0 / D, bias=epsc),
             reads=[('ps', 6), RC], writes=[rtreg])
        rs, rsreg = ring('fb', fb)
        P.op('act', lambda e: e.activation(out=rs[:, 0:n], in_=rt[:, 0:n], func=AF.Exp, scale=-0.5), reads=[rtreg], writes=[rsreg])
        for c in range(8):
            gcol = cf[:, CF_PVEC + pvcol + c:CF_PVEC + pvcol + c + 1]
            dst = hT if final else xnT
            wreg_ = ('h', c, g) if final else ('xn', c, g)
            P.op('dve', lambda e, c=c, gcol=gcol, dst=dst: e.scalar_tensor_tensor(
                out=dst[:, c, s:s + n], in0=hT[:, c, s:s + n], scalar=gcol, in1=rs[:, 0:n], op0=ALU.mult, op1=ALU.mult),
                reads=[('h', c, g), rsreg, RC], writes=[wreg_])

    norm_queue = []

    def rmsnorm(pvcol, final=False, ahead=None):
        for g, (s, n) in enumerate(GROUPS):
            norm_queue.append(lambda g=g, s=s, n=n: rmsnorm_group(pvcol, final, g, s, n))
        for _ in range(len(GROUPS) if ahead is None else ahead):
            norm_pop()

    def norm_pop():
        if norm_queue:
            norm_queue.pop(0)()

    NM = 128 - PAD

    def resid_add(ps, psreg, c, g, scale):
        s, n = GROUPS[g]
        if g == 0:
            s, n = PAD, NM
        P.op('dve', lambda e: e.scalar_tensor_tensor(out=hT[:, c, s:s + n], in0=ps[:, 0:n], scalar=scale,
                                                      in1=hT[:, c, s:s + n], op0=ALU.mult, op1=ALU.add),
             reads=[psreg, ('h', c, g)], writes=[('h', c, g)])

    def ffn_gu_group(wt, wreg, fi, g, s, n):
        if g == 0:
            s, n = PAD, NM
        k = rr('ffn_gu', 2)
        gp, up = PS[k], PS[2 + k]
        for kc in range(8):
            P.op('pe', lambda e, kc=kc: e.matmul(gp[:, 0:n], lhsT=wt[:, kc * 128:(kc + 1) * 128],
                                               rhs=xnT[:, kc, s:s + n], start=(kc == 0), stop=(kc == 7)),
                 reads=[wreg, ('xn', kc, g)], writes=[('ps', k)])
        for kc in range(8):
            P.op('pe', lambda e, kc=kc: e.matmul(up[:, 0:n], lhsT=wt[:, 1024 + kc * 128:1024 + (kc + 1) * 128],
                                               rhs=xnT[:, kc, s:s + n], start=(kc == 0), stop=(kc == 7)),
                 reads=[wreg, ('xn', kc, g)], writes=[('ps', 2 + k)])
        sg, sgreg = ring('fa', fa)
        P.op('act', lambda e: e.activation(out=sg[:, 0:n], in_=gp[:, 0:n], func=AF.Silu),
             reads=[('ps', k)], writes=[sgreg])
        P.op('dve', lambda e: e.tensor_tensor(out=SL[:, fi, s:s + n], in0=up[:, 0:n], in1=sg[:, 0:n], op=ALU.mult),
             reads=[('ps', 2 + k), sgreg], writes=[('S', fi, g)])

    def mm_out_group(wt, wreg, nk, c, g, s, n, scale):
        if g == 0:
            s, n = PAD, NM
        k = 4 + rr('ffn_o', 2)
        op_ = PS[k]
        for fi in range(nk):
            P.op('pe', lambda e, fi=fi: e.matmul(op_[:, 0:n], lhsT=wt[:, fi * 128:(fi + 1) * 128],
                                               rhs=SL[:, fi, s:s + n], start=(fi == 0), stop=(fi == nk - 1)),
                 reads=[wreg, ('S', fi, g)], writes=[('ps', k)])
        resid_add(op_, ('ps', k), c, g, scale)

    def ffn(l, j):
        rmsnorm(PV_NORM + (l * 3 + (0 if j == 0 else 2)) * 8)
        for half in range(2):
            for fi in range(NFH):
                wt, wreg = wload(wgu_d[l, j, half * NFH + fi], 2048)
                for g, (s, n) in enumerate(GROUPS):
                    ffn_gu_group(wt, wreg, fi, g, s, n)
                    norm_pop()
            for c in range(8):
                wt, wreg = wload(wd_d[l, j, half, c], NFH * 128)
                for g, (s, n) in enumerate(GROUPS):
                    mm_out_group(wt, wreg, NFH, c, g, s, n, 0.5)

    tabs_for = {}

    rot_pending = []

    def project_rot_group(which, wt, wreg, dst_slot, permcol, g, split=None):
        perm = cb[:, permcol:permcol + 128]
        s, n = GROUPS[g]
        tt, tregs = tabs_for[g]
        pk = rr('proj', 2)
        pp, sp_ = PS[pk], PS[2 + pk]
        for kc in range(8):
            P.op('pe', lambda e, kc=kc: e.matmul(pp[:, 0:n], lhsT=wt[:, which * 1024 + kc * 128:which * 1024 + (kc + 1) * 128],
                                               rhs=xnT[:, kc, s:s + n], start=(kc == 0), stop=(kc == 7)),
                 reads=[wreg, ('xn', kc, g)], writes=[('ps', pk)])
        qb, qbreg = ring('ba', ba)
        P.op('act', lambda e: e.activation(out=qb[:, 0:n], in_=pp[:, 0:n], func=AF.Identity), reads=[('ps', pk)], writes=[qbreg])
        rot_flush()

        def second():
            P.op('pe', lambda e: e.matmul(sp_[:, 0:n], lhsT=perm, rhs=qb[:, 0:n], start=True, stop=True),
                 reads=[qbreg, RC], writes=[('ps', 2 + pk)])
            t1, t1reg = ring('fa', fa)
            t2, t2reg = ring('fb', fb)
            P.op('dve', lambda e: e.tensor_tensor(out=t1[:, 0:n], in0=pp[:, 0:n], in1=tt[:, 0, 0:n], op=ALU.mult),
                 reads=[('ps', pk)] + tregs, writes=[t1reg])
            P.op('dve', lambda e: e.tensor_tensor(out=t2[:, 0:n], in0=sp_[:, 0:n], in1=tt[:, 1, 0:n], op=ALU.mult),
                 reads=[('ps', 2 + pk)] + tregs, writes=[t2reg])
            if split is None:
                P.op(ROT_ADD_ENG, lambda e: e.tensor_tensor(out=SL[:, dst_slot, s:s + n], in0=t1[:, 0:n], in1=t2[:, 0:n], op=ALU.add),
                     reads=[t1reg, t2reg], writes=[('S', dst_slot, g)])
            else:
                for m in range(2):
                    P.op('dve', lambda e, m=m: e.tensor_tensor(out=SL[64 * m:64 * m + 64, split[m], s:s + n], in0=t1[64 * m:64 * m + 64, 0:n],
                                                             in1=t2[64 * m:64 * m + 64, 0:n], op=ALU.add),
                         reads=[t1reg, t2reg], writes=[('S', split[m], g)])
        rot_pending.append(second)

    def rot_flush():
        while rot_pending:
            rot_pending.pop(0)()

    def project_v_group(wt, wreg, vslot, g, s, n):
        pk = rr('proj', 2)
        pp = PS[pk]
        for ci in range(n // 128):
            cs = s + ci * 128
            for kc in range(8):
                P.op('pe', lambda e, kc=kc, ci=ci, cs=cs: e.matmul(pp[:, ci * 128:(ci + 1) * 128], lhsT=xnT[:, kc, cs:cs + 128],
                                                                 rhs=wt[:, kc * 128:(kc + 1) * 128], start=(kc == 0), stop=(kc == 7)),
                     reads=[wreg, ('xn', kc, g)], writes=[('ps', pk)])
        P.op('act', lambda e: e.activation(out=SL[:, vslot, s:s + n], in_=pp[:, 0:n], func=AF.Identity),
             reads=[('ps', pk)], writes=[('S', vslot, g)])

    def project_v(wt, wreg, vslot):
        for g, (s, n) in enumerate(GROUPS):
            project_v_group(wt, wreg, vslot, g, s, n)

    def stats_rstd(src, srcreg, n, scale, bias_col, o0=0):
        sq, sqreg = ring('ba', ba)
        P.op('dve', lambda e: e.tensor_tensor(out=sq[:, o0:n], in0=src[:, o0:n], in1=src[:, o0:n], op=ALU.mult), reads=[srcreg], writes=[sqreg])
        P.op('pe', lambda e: e.matmul(PS[6][:, o0:n], lhsT=ones, rhs=sq[:, o0:n], start=True, stop=True),
             reads=[sqreg, RC], writes=[('ps', 6)])
        sd, sdreg = ring('fb', fb)
        P.op('act', lambda e: e.activation(out=sd[:, o0:n], in_=PS[6][:, o0:n], func=AF.Ln, scale=scale, bias=bias_col),
             reads=[('ps', 6), RC], writes=[sdreg])
        rs, rsreg = ring('fb', fb)
        P.op('act', lambda e: e.activation(out=rs[:, o0:n], in_=sd[:, o0:n], func=AF.Exp, scale=-0.5), reads=[sdreg], writes=[rsreg])
        return rs, rsreg

    def retention_attention(l, h, u, ks, vs, wvg, wvgreg):
        gam = 1.0 - 2.0 ** (-5.0 - h)
        cd = gam ** 128
        dmask = cf[:, CF_DMASK + h * 128:CF_DMASK + (h + 1) * 128]
        qdec = cf[:, CF_QDEC + h * 128:CF_QDEC + (h + 1) * 128]
        kdec = cf[:, CF_KDEC + h:CF_KDEC + h + 1]
        gnw = cf[:, CF_PVEC + PV_GN + l * 4 + h:CF_PVEC + PV_GN + l * 4 + h + 1]
        RBs = 18 - ks
        grp = lambda n: 0 if n == 0 else 1 + (n - 1) // 4
        P.op('dve', lambda e: e.memset(SL[:, RBs, 0:128], 0.0), writes=[('S', RBs, 0)])
        P.op('dve', lambda e: e.memset(Rf[0][:, :], 0.0), writes=[('Rf', 0)])
        def emit_gate(g, bank=6):
            s, n = GROUPS[g]
            gp, greg = (PTF, ('pt',)) if bank == 'ptf' else (PS[bank], ('ps', bank))
            for kc in range(8):
                P.op('pe', lambda e, kc=kc: e.matmul(gp[:, 0:n], lhsT=wvg[:, 1024 + kc * 128:1024 + (kc + 1) * 128],
                                                   rhs=xnT[:, kc, s:s + n], start=(kc == 0), stop=(kc == 7)),
                     reads=[wvgreg, ('xn', kc, g)], writes=[greg])
            sg, sgreg = ring('fb', fb)
            P.op('act', lambda e: e.activation(out=sg[:, 0:n], in_=gp[:, 0:n], func=AF.Exp, scale=-1.0), reads=[greg], writes=[sgreg])
            P.op('act', lambda e: e.activation(out=sg[:, 0:n], in_=sg[:, 0:n], func=AF.Ln, bias=onec), reads=[sgreg, RC], writes=[sgreg])
            P.op('act', lambda e: e.activation(out=sg[:, 0:n], in_=sg[:, 0:n], func=AF.Exp, scale=-1.0), reads=[sgreg], writes=[sgreg])
            gs, gsreg = ring('sq', sqr)
            P.op('dve', lambda e: e.tensor_tensor(out=gs[:, 0:n], in0=gp[:, 0:n], in1=sg[:, 0:n], op=ALU.mult),
                 reads=[greg, sgreg], writes=[gsreg])
            gate[g] = (gs, gsreg)

        TB = [(PT, ('pt',)), (PS[6][:, :].bitcast(BF16), ('ps', 6))]
        for r in range(4):
            tb, tbreg = TB[r % 2]
            for i in range(4):
                n = 4 * r + i
                P.op('pe', lambda e, n=n, i=i, tb=tb: e.transpose(tb[:, i * 128:(i + 1) * 128], SL[:, ks, n * 128:(n + 1) * 128], ident),
                     reads=[('S', ks, grp(n)), RC], writes=[tbreg])
            kd, kdreg = ring('ba', ba)
            P.op('act', lambda e, kd=kd, tb=tb: e.activation(out=kd[:, 0:512], in_=tb[:, 0:512], func=AF.Identity, scale=kdec),
                 reads=[tbreg, RC], writes=[kdreg])
            for i in range(4):
                n = 4 * r + i
                P.op('pe', lambda e, n=n, i=i, r=r, kd=kd: e.matmul(PS[r][:, i * 128:(i + 1) * 128], lhsT=kd[:, i * 128:(i + 1) * 128],
                                                                  rhs=SL[:, vs, n * 128:(n + 1) * 128], start=True, stop=True),
                     reads=[kdreg, ('S', vs, grp(n))], writes=[('ps', r)])
        gate = {}
        for n in range(16):
            if n in (0, 5, 10):
                emit_gate(n // 5, (6, 4, 5)[n // 5])
            r, i = divmod(n, 4)
            P.op('dve', lambda e, n=n, r=r, i=i: e.scalar_tensor_tensor(out=Rf[(n + 1) % 2][:, :], in0=Rf[n % 2][:, :], scalar=cd,
                                                                      in1=PS[r][:, i * 128:(i + 1) * 128], op0=ALU.mult, op1=ALU.add),
                 reads=[('Rf', n % 2), ('ps', r)], writes=[('Rf', (n + 1) % 2)])
            P.op('act', lambda e, n=n: e.activation(out=SL[:, RBs, (n + 1) * 128:(n + 2) * 128], in_=Rf[(n + 1) % 2][:, :], func=AF.Identity),
                 reads=[('Rf', (n + 1) % 2)], writes=[('S', RBs, grp(n + 1))])
        def emit_ST(g):
            s, n = GROUPS[g]
            ncx = n // 128
            bk = g % 2
            for ci in range(ncx):
                cs = s + ci * 128
                P.op('pe', lambda e, ci=ci, cs=cs: e.matmul(PS[bk][:, ci * 128:(ci + 1) * 128], lhsT=SL[:, ks, cs:cs + 128],
                                                          rhs=SL[:, u, cs:cs + 128], start=True, stop=True),
                     reads=[('S', u, g), ('S', ks, g)], writes=[('ps', bk)])
            v3 = lambda ap: ap.rearrange("p (c i) -> p c i", c=ncx)
            bc = lambda ap: ap.unsqueeze(1).broadcast_to([128, ncx, 128])
            stm, stmreg = ring('ba', ba)
            P.op('dve', lambda e: e.tensor_tensor(out=v3(stm[:, 0:n]), in0=v3(PS[bk][:, 0:n]), in1=bc(dmask), op=ALU.mult),
                 reads=[('ps', bk), RC], writes=[stmreg])
            qd, qdreg = ring('ba', ba)
            P.op('dve', lambda e: e.tensor_tensor(out=v3(qd[:, 0:n]), in0=v3(SL[:, u, s:s + n]), in1=bc(qdec), op=ALU.mult),
                 reads=[('S', u, g), RC], writes=[qdreg])
            return stm, stmreg, qd, qdreg

        def emit_O(g, st):
            stm, stmreg, qd, qdreg = st
            s, n = GROUPS[g]
            OBk = 4 + g % 2
            OB = PS[OBk]
            for ci in range(n // 128):
                cs = s + ci * 128
                P.op('pe', lambda e, ci=ci, cs=cs: e.matmul(OB[:, ci * 128:(ci + 1) * 128], lhsT=SL[:, vs, cs:cs + 128],
                                                          rhs=stm[:, ci * 128:(ci + 1) * 128], start=True, stop=False),
                     reads=[('S', vs, g), stmreg], writes=[('ps', OBk)])
                P.op('pe', lambda e, ci=ci, cs=cs: e.matmul(OB[:, ci * 128:(ci + 1) * 128], lhsT=SL[:, RBs, cs:cs + 128],
                                                          rhs=qd[:, ci * 128:(ci + 1) * 128], start=False, stop=True),
                     reads=[('S', RBs, g), qdreg], writes=[('ps', OBk)])
            post.append([1, lambda: fin1(g, OBk)])
            tick()

        post = []
        PTF = PT[:, :].bitcast(F32)

        def tick(force=False):
            for ent in list(post):
                ent[0] -= 1
                if force or ent[0] <= 0:
                    post.remove(ent)
                    ent[1]()

        def fin1(g, OBk):
            s, n = GROUPS[g]
            OB = PS[OBk]
            cent = cb[:, CB_CENT:CB_CENT + 128]
            ck = 2 + g % 2
            ob, obreg = ring('ba', ba)
            P.op('act', lambda e: e.activation(out=ob[:, 0:n], in_=OB[:, 0:n], func=AF.Identity), reads=[('ps', OBk)], writes=[obreg])
            P.op('pe', lambda e: e.matmul(PS[ck][:, 0:n], lhsT=cent, rhs=ob[:, 0:n], start=True, stop=True),
                 reads=[obreg, RC], writes=[('ps', ck)])
            post.append([1, lambda: fin2(g, ck)])

        def fin2(g, ck):
            s, n = GROUPS[g]
            csq, csqreg = ring('ba', ba)
            P.op('act', lambda e: e.activation(out=csq[:, 0:n], in_=PS[ck][:, 0:n], func=AF.Square), reads=[('ps', ck)], writes=[csqreg])
            P.op('pe', lambda e: e.matmul(PS[6][:, 0:n], lhsT=ones, rhs=csq[:, 0:n], start=True, stop=True),
                 reads=[csqreg, RC], writes=[('ps', 6)])
            post.append([1, lambda: fin3(g, ck)])

        def fin3(g, ck):
            s, n = GROUPS[g]
            rs, rsreg = ring('fb', fb)
            P.op('act', lambda e: e.activation(out=rs[:, 0:n], in_=PS[6][:, 0:n], func=AF.Ln, scale=1.0 / 128, bias=epsc),
                 reads=[('ps', 6), RC], writes=[rsreg])
            P.op('act', lambda e: e.activation(out=rs[:, 0:n], in_=rs[:, 0:n], func=AF.Exp, scale=-0.5), reads=[rsreg], writes=[rsreg])
            fin4(g, ck, rs, rsreg)

        def fin4(g, ck, rs, rsreg):
            s, n = GROUPS[g]
            nrm, nrmreg = ring('fa', fa)
            P.op('dve', lambda e: e.scalar_tensor_tensor(out=nrm[:, 0:n], in0=PS[ck][:, 0:n], scalar=gnw, in1=rs[:, 0:n],
                                                          op0=ALU.mult, op1=ALU.mult),
                 reads=[('ps', ck), rsreg, RC], writes=[nrmreg])
            gs, gsreg = gate[g]
            P.op('dve', lambda e: e.tensor_tensor(out=SL[:, u, s:s + n], in0=nrm[:, 0:n], in1=gs[:, 0:n], op=ALU.mult),
                 reads=[nrmreg, gsreg], writes=[('S', u, g)])
            done.add(g)

        done = set()

        pending = None
        need_gate = [3, 4]
        for g in range(5):
            if need_gate and (need_gate[0] - 3) in done:
                emit_gate(need_gate.pop(0), 'ptf')
            st = emit_ST(g)
            if pending is not None:
                emit_O(*pending)
            pending = (g, st)
        emit_O(*pending)
        while post or need_gate:
            if need_gate and (need_gate[0] - 3) in done:
                emit_gate(need_gate.pop(0), 'ptf')
            tick(force=True)

    def retention_unit(l, h):
        u = h
        ks = 8 + 2 * (u % 2)
        vs = 9
        wqk, wqkreg = wload(wmix_d[l, u, 0], 2048)
        wvg, wvgreg = wload(wmix_d[l, u, 1], 2048)
        for g, (s, n) in enumerate(GROUPS):
            tabs_for[g] = tload(0, s, n)
            project_rot_group(0, wqk, wqkreg, u, CB_PERMR, g)
            project_rot_group(1, wqk, wqkreg, ks, CB_PERMR, g)
            norm_pop()
        rot_flush()
        project_v(wvg, wvgreg, vs)
        if DBG_MIX == 'B':
            return
        retention_attention(l, h, u, ks, vs, wvg, wvgreg)

    def diff_attention(l, u, ks, vs, om):
        neglam = lamv[:, 4 + l:5 + l]
        sw = cf[:, CF_PVEC + PV_SW + l:CF_PVEC + PV_SW + l + 1]
        blocks = []
        for g, (s, n) in enumerate(GROUPS):
            last = (s + n) // 128 - 1
            for m in range(2):
                for jb in range(last + 1):
                    blocks.append((g, m, jb, last))
        odset = {}
        tres = {}
        deferred = []
        post = []

        def emit_S(blk):
            g, m, jb, last = blk
            s, n = GROUPS[g]
            off = max(0, jb * 128 - s)
            nq = n - off
            diag = jb * 128 >= s
            sk = rr('dst', 3)
            SP_, sreg = SBANK[sk]
            kg = 0 if jb == 0 else 1 + (jb - 1) // 4
            P.op('pe', lambda e: e.matmul(SP_[:, 0:nq], lhsT=SL[:, ks[m], jb * 128:(jb + 1) * 128],
                                          rhs=SL[:, u, s + off:s + n], start=True, stop=(not diag)),
                 reads=[('S', ks[m], kg), ('S', u, g)], writes=[sreg])
            if diag:
                P.op('pe', lambda e: e.matmul(SP_[:, 0:128], lhsT=ident, rhs=cmask, start=False, stop=True),
                     reads=[RC], writes=[sreg])
            return sk, off, nq, kg

        def emit_rest(blk, sinfo):
            g, m, jb, last = blk
            sk, off, nq, kg = sinfo
            SP_, sreg = SBANK[sk]
            if (g, m) not in odset:
                odset[(g, m)] = 2 + 2 * rr('dod', 2)
                while any(ent[2] == odset[(g, m)] for ent in post):
                    for ent in list(post):
                        if ent[2] == odset[(g, m)]:
                            post.remove(ent)
                            ent[1]()
            OBk = odset[(g, m)]
            DBk = OBk + 1
            OB, DB = PS[OBk], PS[DBk]
            pt, ptreg = ring('ba', ba)
            P.op('act', lambda e: e.activation(out=pt[:, 0:nq], in_=SP_[:, 0:nq], func=AF.Exp, scale=0.125),
                 reads=[sreg], writes=[ptreg])
            P.op('pe', lambda e: e.matmul(OB[:, off:off + nq], lhsT=SL[:, vs, jb * 128:(jb + 1) * 128], rhs=pt[:, 0:nq],
                                          start=(jb == 0), stop=(jb == last)),
                 reads=[('S', vs, kg), ptreg], writes=[('ps', OBk)])
            P.op('pe', lambda e: e.matmul(DB[:, off:off + nq], lhsT=(ones_pad if jb == 0 else ones), rhs=pt[:, 0:nq],
                                          start=(jb == 0), stop=(jb == last)),
                 reads=[RC, ptreg], writes=[('ps', DBk)])
            if jb == last:
                post.append([2, lambda: normalize_act(g, m, OBk), OBk])

        def normalize_act(g, m, OBk):
            DBk = OBk + 1
            DB = PS[DBk]
            s, n = GROUPS[g]
            o0 = PAD if g == 0 else 0
            r, rreg = ring('fb', fb)
            P.op('act', lambda e: e.activation(out=r[:, o0:n], in_=DB[:, o0:n], func=AF.Ln), reads=[('ps', DBk)], writes=[rreg])
            P.op('act', lambda e: e.activation(out=r[:, o0:n], in_=r[:, o0:n], func=AF.Exp, scale=-1.0), reads=[rreg], writes=[rreg])
            post.append([3, lambda: normalize_dve(g, m, OBk, r, rreg), OBk])

        def normalize_dve(g, m, OBk, r, rreg):
            OB = PS[OBk]
            s, n = GROUPS[g]
            o0 = PAD if g == 0 else 0
            t, treg_ = ring('fa', fa)
            P.op('dve', lambda e: e.tensor_tensor(out=t[:, o0:n], in0=OB[:, o0:n], in1=r[:, o0:n], op=ALU.mult),
                 reads=[('ps', OBk), rreg], writes=[treg_])
            tres[(g, m)] = (t, treg_)
            if m == 1:
                post.append([1, lambda: finalize1(g), None])

        def finalize1(g):
            s, n = GROUPS[g]
            o0 = PAD if g == 0 else 0
            (t0, t0reg), (t1, t1reg) = tres[(g, 0)], tres[(g, 1)]
            o, oreg = ring('fa', fa)
            P.op('dve', lambda e: e.scalar_tensor_tensor(out=o[:, o0:n], in0=t1[:, o0:n], scalar=neglam, in1=t0[:, o0:n],
                                                          op0=ALU.mult, op1=ALU.add),
                 reads=[t0reg, t1reg, ('lamv',)], writes=[oreg])
            sq, sqreg = ring('ba', ba)
            P.op('dve', lambda e: e.tensor_tensor(out=sq[:, o0:n], in0=o[:, o0:n], in1=o[:, o0:n], op=ALU.mult), reads=[oreg], writes=[sqreg])
            P.op('pe', lambda e: e.matmul(PS[6][:, o0:n], lhsT=ones, rhs=sq[:, o0:n], start=True, stop=True),
                 reads=[sqreg, RC], writes=[('ps', 6)])
            post.append([4, lambda: finalize2(g, o, oreg), None])

        def finalize2(g, o, oreg):
            s, n = GROUPS[g]
            o0 = PAD if g == 0 else 0
            rs, rsreg = ring('fb', fb)
            P.op('act', lambda e: e.activation(out=rs[:, o0:n], in_=PS[6][:, o0:n], func=AF.Ln, scale=1.0 / (128 * om * om), bias=epsl[l]),
                 reads=[('ps', 6), RC], writes=[rsreg])
            P.op('act', lambda e: e.activation(out=rs[:, o0:n], in_=rs[:, o0:n], func=AF.Exp, scale=-0.5), reads=[rsreg], writes=[rsreg])
            P.op('dve', lambda e: e.scalar_tensor_tensor(out=SL[:, u, s + o0:s + n], in0=o[:, o0:n], scalar=sw, in1=rs[:, o0:n],
                                                          op0=ALU.mult, op1=ALU.mult),
                 reads=[oreg, rsreg, RC], writes=[('S', u, g)])

        def tick(force=False):
            for ent in list(post):
                ent[0] -= 1
                if force or ent[0] <= 0:
                    post.remove(ent)
                    ent[1]()

        pend = []
        for blk in blocks:
            sinfo = emit_S(blk)
            pend.append((blk, sinfo))
            if len(pend) > 2:
                emit_rest(*pend.pop(0))
                tick()
        while pend:
            emit_rest(*pend.pop(0))
            tick()
        while post:
            tick(force=True)

    def diff_unit(l, h, lam_init):
        u = 4 + h
        ks = (8, 10)
        vs = 9
        wqk, wqkreg = wload(wmix_d[l, u, 0], 2048)
        wv, wvreg = wload(wmix_d[l, u, 1, :, 0:1024], 1024)
        if h == 0:
            P.op('dve', lambda e: e.memset(SL[64:128, ks[0], :], 0.0), writes=[('S', ks[0], g) for g in range(5)])
            P.op('dve', lambda e: e.memset(SL[0:64, ks[1], :], 0.0), writes=[('S', ks[1], g) for g in range(5)])
        for g, (s, n) in enumerate(GROUPS):
            tabs_for[g] = tload(1, s, n)
            project_rot_group(0, wqk, wqkreg, u, CB_PERMD, g)
            project_rot_group(1, wqk, wqkreg, None, CB_PERMD, g, split=ks)
        rot_flush()
        project_v(wv, wvreg, vs)
        diff_attention(l, u, ks, vs, 1.0 - lam_init)

    def lam_compute(l):
        lam_init = 0.8 - 0.6 * math.exp(-0.3 * l)
        for t in range(2):
            a = lamin[:, (l * 4 + 2 * t) * 64:(l * 4 + 2 * t + 1) * 64]
            b = lamin[:, (l * 4 + 2 * t + 1) * 64:(l * 4 + 2 * t + 2) * 64]
            P.op('dve', lambda e, a=a, b=b: e.tensor_tensor(out=lamw[:, :], in0=a, in1=b, op=ALU.mult),
                 reads=[RC, ('lamw',)], writes=[('lamw',)])
            P.op('dve', lambda e, t=t: e.tensor_reduce(out=lamv[:, t:t + 1], in_=lamw[:, :], axis=mybir.AxisListType.X, op=ALU.add),
                 reads=[('lamw',), ('lamv',)], writes=[('lamv',)])
            P.op('act', lambda e, t=t: e.activation(out=lamv[:, 2 + t:3 + t], in_=lamv[:, t:t + 1], func=AF.Exp),
                 reads=[('lamv',)], writes=[('lamv',)])
        P.op('dve', lambda e: e.scalar_tensor_tensor(out=lamv[:, 4 + l:5 + l], in0=lamv[:, 3:4], scalar=-lam_init, in1=lamv[:, 2:3],
                                                      op0=ALU.add, op1=ALU.subtract),
             reads=[('lamv',)], writes=[('lamv',)])
        return lam_init

    def mixer(l):
        rmsnorm(PV_NORM + (l * 3 + 1) * 8)
        lam_init = lam_compute(l)
        if DBG_MIX == 'A':
            return
        for h in range(4):
            retention_unit(l, h)
            if DBG_MIX in ('B', 'B1', 'C', 'R1', 'R2', 'R3'):
                return
        if DBG_MIX == 'D':
            return
        for h in range(4):
            diff_unit(l, h, lam_init)
            if DBG_MIX == 'F':
                return
        for c in range(8):
            wt, wreg = wload(wout_d[l, c], 1024)
            for g, (s, n) in enumerate(GROUPS):
                mm_out_group(wt, wreg, 8, c, g, s, n, 1.0)

    stages = []
    for l in range(2):
        stages += [('ffn', l, 0), ('mix', l), ('ffn', l, 1)]
    nst = len(stages) if stop is None else stop
    for st in stages[:nst]:
        if st[0] == 'ffn':
            ffn(st[1], st[2])
        else:
            mixer(st[1])
    outregs = []
    if stop is None:
        for g, (s, n) in enumerate(GROUPS):
            rmsnorm_group(PV_NORM + 48, True, g, s, n)
            if g >= 1:
                P.dma('sp', lambda e, s=s, n=n: e.dma_start(out=out_d[:, :, s - 128:s - 128 + n], in_=hT[:, :, s:s + n]), 'out',
                      reads=[('h', c, g) for c in range(8)], writes=[('out', g)])
                outregs.append(('out', g))
    else:
        for c in range(8):
            P.dma('sp', lambda e, c=c: e.dma_start(out=out_d[:, c, :], in_=hT[:, c, 128:TP]), 'out',
                  reads=[('h', c, g) for g in range(1, 5)], writes=[('out', c)])
            outregs.append(('out', c))
    if dump == 'h':
        for c in range(8):
            P.dma('sp', lambda e, c=c: e.dma_start(out=dbg_d[:, c, :], in_=hT[:, c, :]), 'out',
                  reads=[('h', c, g) for g in range(5)], writes=[('dbg', c)])
            outregs.append(('dbg', c))
    P.op('sp', None, reads=outregs)
    P.build()
    return nc, P


def _bf(a):
    return np.asarray(a, dtype=np.float32).astype(ml_dtypes.bfloat16)


def _const_tables():
    pos = np.arange(TP, dtype=np.float64) - PAD
    angle = 10000.0 ** (-np.linspace(0.0, 1.0, 64))
    fr = pos[None, :] * np.repeat(angle, 2)[:, None]
    cosr = np.cos(fr)
    sgn = np.where(np.arange(128) % 2 == 0, -1.0, 1.0)[:, None]
    sinr = np.sin(fr) * sgn
    r = 16
    inv = 500000.0 ** (-np.arange(0, r, 2, dtype=np.float64) / r)
    cosd = np.ones((128, TP), np.float64)
    sind = np.zeros((128, TP), np.float64)
    for p in range(128):
        dd = p % 64
        if dd < r:
            f = pos * inv[dd % 8]
            cosd[p] = np.cos(f)
            sind[p] = np.sin(f) * (-1.0 if dd < 8 else 1.0)
    tabs = np.stack([cosr, sinr, cosd, sind]).astype(np.float32)
    cbm = np.zeros((128, CB_N), np.float32)
    cbm[:, CB_ONES:CB_ONES + 128] = 1.0
    cbm[PAD:, CB_ONESPAD:CB_ONESPAD + 128] = 1.0
    cbm[:, CB_IDENT:CB_IDENT + 128] = np.eye(128)
    for m in range(128):
        src = m + 1 if m % 2 == 0 else m - 1
        cbm[src, CB_PERMR + m] = 1.0
        dd = m % 64
        if dd < 8:
            cbm[m + 8, CB_PERMD + m] = 1.0
        elif dd < 16:
            cbm[m - 8, CB_PERMD + m] = 1.0
    jj = np.arange(128)[:, None]
    ii = np.arange(128)[None, :]
    cbm[:, CB_CMASK:CB_CMASK + 128] = np.where(jj <= ii, 0.0, -30000.0)
    cbm[:, CB_CENT:CB_CENT + 128] = np.eye(128) - 1.0 / 128.0
    cfm = np.zeros((128, CF_N), np.float32)
    cfm[:, CF_EPS] = EPS
    cfm[:, CF_ONE] = 1.0
    for l_ in range(2):
        om_ = 1.0 - (0.8 - 0.6 * math.exp(-0.3 * l_))
        cfm[:, CF_EPS + 1 + l_] = EPS / (om_ * om_)
    for h in range(4):
        lg = math.log(1.0 - 2.0 ** (-5.0 - h))
        rel = (ii - jj).astype(np.float64)
        dm = np.where(rel >= 0, np.exp(lg * np.maximum(rel, 0.0)), 0.0) * (128.0 ** -0.5)
        cfm[:, CF_DMASK + h * 128:CF_DMASK + (h + 1) * 128] = dm
        cfm[:, CF_QDEC + h * 128:CF_QDEC + (h + 1) * 128] = np.exp(lg * (np.arange(128) + 1.0))[None, :]
        cfm[:, CF_KDEC + h] = np.exp(lg * (127.0 - np.arange(128))) * (128.0 ** -0.5)
    return tabs, cbm, cfm


def _prep(inputs):
    f = lambda k: np.asarray(inputs[k], dtype=np.float32)
    tabs, cbm, cfm = _const_tables()
    cvec = lambda v: np.ascontiguousarray(v.reshape(-1, 128).T)
    norms = [f("ffn1_norm"), f("mix_norm"), f("ffn2_norm")]
    for l in range(2):
        for w in range(3):
            c0 = CF_PVEC + PV_NORM + (l * 3 + w) * 8
            cfm[:, c0:c0 + 8] = cvec(norms[w][l])
        cfm[:, CF_PVEC + PV_GN + l * 4:CF_PVEC + PV_GN + l * 4 + 4] = cvec(f("ret_gn_w")[l])
        cfm[:, CF_PVEC + PV_SW + l] = f("diff_subln_w")[l]
    cfm[:, CF_PVEC + PV_NORM + 48:CF_PVEC + PV_NORM + 56] = cvec(f("final_norm"))
    lam = np.stack([f("diff_lambda_q1"), f("diff_lambda_k1"), f("diff_lambda_q2"), f("diff_lambda_k2")], axis=1)
    lam = np.ascontiguousarray(np.broadcast_to(lam.reshape(1, 512), (128, 512)))

    def fm_tile(w):
        nc_ = w.shape[1] // 128
        return w.reshape(8, 128, nc_, 128).transpose(2, 1, 0, 3)

    wgu = np.empty((2, 2, NF, 128, 2, 8, 128), np.float32)
    wd = np.empty((2, 2, 2, 8, 128, NFH, 128), np.float32)
    for l in range(2):
        for j, (kg, ku, kd_) in enumerate([("ffn1_w_gate", "ffn1_w_up", "ffn1_w_down"), ("ffn2_w_gate", "ffn2_w_up", "ffn2_w_down")]):
            wgu[l, j, :, :, 0] = fm_tile(f(kg)[l])
            wgu[l, j, :, :, 1] = fm_tile(f(ku)[l])
            wd[l, j] = f(kd_)[l].reshape(2, NFH, 128, 8, 128).transpose(0, 3, 2, 1, 4)
    win = f("w_in")
    wmix = np.zeros((2, 8, 2, 128, 2, 8, 128), np.float32)
    for l in range(2):
        t = fm_tile(win[l])
        for h in range(4):
            wmix[l, h, 0, :, 0] = t[h]
            wmix[l, h, 0, :, 1] = t[4 + h]
            wmix[l, h, 1, :, 0] = t[8 + h]
            wmix[l, h, 1, :, 1] = t[12 + h]
            wmix[l, 4 + h, 0, :, 0] = t[16 + h]
            wmix[l, 4 + h, 0, :, 1] = t[20 + h]
            wmix[l, 4 + h, 1, :, 0] = t[24 + h]
    wout = np.stack([fm_tile(f("w_out")[l]) for l in range(2)])
    shared = {
        "metaT": np.ascontiguousarray(f("meta_tokens").reshape(NMETA, 8, 128).transpose(2, 1, 0)),
        "cb": _bf(cbm), "cf": cfm, "lam": lam, "tabs": tabs,
        "wgu": wgu.reshape(2, 2, NF, 128, 2048), "wd": wd.reshape(2, 2, 2, 8, 128, NFH * 128),
        "wmix": wmix.reshape(2, 8, 2, 128, 2048), "wout": np.ascontiguousarray(wout.reshape(2, 8, 128, 1024)),
    }
    x = f("x")
    xTs = [np.ascontiguousarray(x[b].reshape(SEQ, 8, 128).transpose(2, 1, 0)) for b in range(x.shape[0])]
    return shared, xTs


def run(inputs, cores=None, stop=None, dump=None, trace=False):
    shared, xTs = _prep(inputs)
    cores = list(range(8)) if cores is None else cores
    nc, P = build_program(stop=stop, dump=dump)
    in_maps = [dict(shared, xT=xTs[b]) for b in cores]
    res = run_bass_kernel_spmd(nc, in_maps, core_ids=list(range(len(cores))), trace=trace)
    return res, P


def kernel(**inputs):
    res, _ = run(inputs)
    outs = [np.asarray(r["outT"], dtype=np.float32) for r in res.results]
    out = np.stack([o.transpose(2, 1, 0).reshape(SEQ, D) for o in outs])
    return np.ascontiguousarray(out.astype(np.float32))
```

```python
import math
import numpy as np
import ml_dtypes
import concourse.bass as bass
import concourse.mybir as mybir
from concourse.bass_utils import run_bass_kernel_spmd

F32 = mybir.dt.float32
BF16 = mybir.dt.bfloat16
AF = mybir.ActivationFunctionType
ALU = mybir.AluOpType

D = 1024
SEQ = 2048
NMETA = 16
PAD = 112
TP = 2176
NCH = 17
DFF = 2816
NF = 22
NFH = 11
EPS = 1e-6
GROUPS = [(0, 128)] + [(128 + 512 * g, 512) for g in range(4)]
NSLOT = 11
NW = 4
WCOLS = 2048
SAME_ENGINE_SYNC = True
DBG_HACK = 0
WARM = 0
ROT_ADD_ENG = 'dve'
DBG_MIX = None

CB_ONES, CB_ONESPAD, CB_IDENT, CB_PERMR, CB_PERMD, CB_CMASK, CB_CENT = [i * 128 for i in range(7)]
CB_N = 7 * 128
CF_DMASK = 0
CF_QDEC = 512
CF_KDEC = 1024
CF_PVEC = 1028
PV_NORM = 0
PV_GN = 56
PV_SW = 64
CF_EPS = CF_PVEC + 66
CF_ONE = CF_EPS + 3
CF_N = CF_ONE + 1


class Prog:
    def __init__(self, nc):
        self.nc = nc
        self.ops = []
        self.dma_sems = {}

    def op(self, eng, fn, reads=(), writes=()):
        self.ops.append(dict(eng=eng, fn=fn, reads=list(reads), writes=list(writes), dma=None))

    def dma(self, eng, fn, slot, reads=(), writes=()):
        self.ops.append(dict(eng=eng, fn=fn, reads=list(reads), writes=list(writes), dma=slot))

    def build(self):
        nc = self.nc
        ops = self.ops
        engs = ['pe', 'act', 'dve', 'pool', 'sp']
        last_w = {}
        readers = {}
        for i, o in enumerate(ops):
            deps = set()
            key = ('dma', i) if o['dma'] is not None else o['eng']
            for r in o['reads']:
                if r in last_w:
                    deps.add(last_w[r])
                if r[0] in ('ps', 'pt'):
                    for k2, i2 in readers.get(r, {}).items():
                        if k2 != key:
                            deps.add(i2)
            for w in o['writes']:
                if w in last_w:
                    deps.add(last_w[w])
                rd = readers.get(w)
                if rd:
                    deps.update(rd.values())
            deps.discard(i)
            o['deps'] = deps
            for r in o['reads']:
                readers.setdefault(r, {})[key] = i
            for w in o['writes']:
                last_w[w] = i
                readers[w] = {}
        signal = set()
        for i, o in enumerate(ops):
            for d in o['deps']:
                p = ops[d]
                if p['dma'] is not None:
                    continue
                if p['eng'] != o['eng'] or (SAME_ENGINE_SYNC and o['eng'] != 'pe'):
                    signal.add(d)
        sem = {e: nc.alloc_semaphore('s_' + e) for e in engs}
        cnt = {e: 0 for e in engs}
        for i, o in enumerate(ops):
            if o['dma'] is not None:
                slot = o['dma']
                if slot not in self.dma_sems:
                    self.dma_sems[slot] = [nc.alloc_semaphore('d_' + str(slot)), 0]
                ent = self.dma_sems[slot]
                ent[1] += 16
                o['tok'] = (ent[0], ent[1])
            elif i in signal:
                cnt[o['eng']] += 1
                o['tok'] = (sem[o['eng']], cnt[o['eng']])
            else:
                o['tok'] = None
        waited = {e: {} for e in engs}
        for i, o in enumerate(ops):
            need = {}
            for d in o['deps']:
                p = ops[d]
                if p['dma'] is None and p['eng'] == o['eng'] and not (SAME_ENGINE_SYNC and o['eng'] != 'pe'):
                    continue
                tok = p['tok']
                assert tok is not None
                k = tok[0]
                if need.get(k, (None, 0))[1] < tok[1]:
                    need[k] = tok
            ws = []
            for k, tok in need.items():
                if waited[o['eng']].get(k, 0) < tok[1]:
                    waited[o['eng']][k] = tok[1]
                    ws.append(tok)
            o['waits'] = ws
        per = {e: [o for o in ops if o['eng'] == e] for e in engs}
        self.stats = {e: len(per[e]) for e in engs}
        self.stats['signals'] = dict(cnt)

        def run(e, lst):
            for o in lst:
                for (s, v) in o['waits']:
                    e.wait_ge(s, v)
                if o['fn'] is None:
                    continue
                ins = o['fn'](e)
                if o['tok'] is not None:
                    ins.then_inc(o['tok'][0], 16 if o['dma'] is not None else 1)

        with nc.Block() as block:
            @block.tensor
            def _(e):
                run(e, per['pe'])

            @block.scalar
            def _(e):
                run(e, per['act'])

            @block.vector
            def _(e):
                run(e, per['dve'])

            @block.gpsimd
            def _(e):
                run(e, per['pool'])

            @block.sync
            def _(e):
                run(e, per['sp'])


def build_program(stop=None, dump=None):
    nc = bass.Bass("TRN2", target_bir_lowering=False)
    P = Prog(nc)

    def din(name, shape, dt=F32):
        return nc.dram_tensor(name, list(shape), dt, kind="ExternalInput").ap()

    xT_d = din("xT", [128, 8, SEQ])
    meta_d = din("metaT", [128, 8, NMETA])
    cb_d = din("cb", [128, CB_N], BF16)
    cf_d = din("cf", [128, CF_N])
    lam_d = din("lam", [128, 2 * 4 * 64])
    tabs_d = din("tabs", [4, 128, TP])
    wgu_d = din("wgu", [2, 2, NF, 128, 2048])
    wd_d = din("wd", [2, 2, 2, 8, 128, NFH * 128])
    wmix_d = din("wmix", [2, 8, 2, 128, 2048])
    wout_d = din("wout", [2, 8, 128, 1024])
    out_d = nc.dram_tensor("outT", [128, 8, SEQ], F32, kind="ExternalOutput").ap()
    dbg_d = None
    if dump is not None:
        dbg_d = nc.dram_tensor("dbg", [128, 8, TP], F32, kind="ExternalOutput").ap()

    sb = nc.alloc_sbuf_tensor
    hT = sb("hT", [128, 8, TP], F32)
    xnT = sb("xnT", [128, 8, TP], BF16)
    SL = sb("slots", [128, NSLOT, TP], BF16)
    wsl = [sb(f"wsl{i}", [128, WCOLS], BF16) for i in range(NW)]
    tabt = [sb(f"tab{i}", [128, 2, 512], F32) for i in range(2)]
    cb = sb("cbs", [128, CB_N], BF16)
    cf = sb("cfs", [128, CF_N], F32)
    lamin = sb("lamin", [128, 512], F32)
    lamw = sb("lamw", [128, 64], F32)
    lamv = sb("lamv", [128, 8], F32)
    NR = 3
    sqr = [sb(f"sq{i}", [128, 512], BF16) for i in range(NR)]
    fa = [sb(f"fa{i}", [128, 512], F32) for i in range(4)]
    fb = [sb(f"fb{i}", [128, 512], F32) for i in range(4)]
    ba = [sb(f"ba{i}", [128, 512], BF16) for i in range(6)]
    Rf = [sb(f"Rf{i}", [128, 128], F32) for i in range(2)]

    PS = [nc.alloc_psum_tensor(f"ps{i}", [128, 512], F32) for i in range(7)]
    PT = nc.alloc_psum_tensor("pst", [128, 1024], BF16)

    SBANK = [(PS[0], ('ps', 0)), (PS[1], ('ps', 1)), (PT[:, :].bitcast(F32), ('pt',))]
    ctr = {}

    def rr(name, n):
        v = ctr.get(name, 0)
        ctr[name] = v + 1
        return v % n

    def ring(name, lst):
        i = rr(name, len(lst))
        return lst[i], (name, i)

    ones = cb[:, CB_ONES:CB_ONES + 128]
    ones_pad = cb[:, CB_ONESPAD:CB_ONESPAD + 128]
    ident = cb[:, CB_IDENT:CB_IDENT + 128]
    cmask = cb[:, CB_CMASK:CB_CMASK + 128]
    RC = ('const',)
    epsc = cf[:, CF_EPS:CF_EPS + 1]
    onec = cf[:, CF_ONE:CF_ONE + 1]
    epsl = [cf[:, CF_EPS + 1 + l_:CF_EPS + 2 + l_] for l_ in range(2)]

    P.dma('sp', lambda e: e.dma_start(out=cb[:, :], in_=cb_d), 'const', writes=[RC])
    P.dma('sp', lambda e: e.dma_start(out=cf[:, :], in_=cf_d), 'const', writes=[RC])
    P.dma('sp', lambda e: e.dma_start(out=lamin[:, :], in_=lam_d), 'const', writes=[RC])
    hregs = lambda c: [('h', c, g) for g in range(5)]
    P.op('dve', lambda e: e.memset(hT[:, :, 0:PAD], 0.0), writes=[('h', c, 0) for c in range(8)])
    P.dma('sp', lambda e: e.dma_start(out=hT[:, :, PAD:128], in_=meta_d), ('xin', 0),
          reads=[('h', c, 0) for c in range(8)], writes=[('h', c, 0) for c in range(8)])
    for g in range(1, 5):
        s_, n_ = GROUPS[g]
        P.dma('sp', lambda e, s_=s_, n_=n_: e.dma_start(out=hT[:, :, s_:s_ + n_], in_=xT_d[:, :, s_ - 128:s_ - 128 + n_]), ('xin', g),
              writes=[('h', c, g) for c in range(8)])

    def wload(src_ap, ncols):
        i = rr('w', NW)
        t = wsl[i]
        reg = ('w', i)
        P.dma('pool', lambda e: e.dma_start(out=t[:, 0:ncols], in_=src_ap, max_dma_last_dim=8192),
              ('w', i), writes=[reg])
        return t, reg

    def tload(k, s, n):
        i = rr('tab', 2)
        t = tabt[i]
        regs = [('tab', i, 0), ('tab', i, 1)]
        P.dma('sp', lambda e: e.dma_start(out=t[:, 0, 0:n], in_=tabs_d[2 * k, :, s:s + n]), ('tab', i, 0), writes=[regs[0]])
        P.dma('sp', lambda e: e.dma_start(out=t[:, 1, 0:n], in_=tabs_d[2 * k + 1, :, s:s + n]), ('tab', i, 1), writes=[regs[1]])
        return t, regs

    def rmsnorm_group(pvcol, final, g, s, n):
        ss = PS[6]
        for c in range(8):
            sq, sqreg = ring('sq', sqr)
            P.op('act', lambda e, sq=sq, c=c: e.activation(out=sq[:, 0:n], in_=hT[:, c, s:s + n], func=AF.Square),
                 reads=[('h', c, g)], writes=[sqreg])
            P.op('pe', lambda e, sq=sq, c=c: e.matmul(ss[:, 0:n], lhsT=ones, rhs=sq[:, 0:n], start=(c == 0), stop=(c == 7)),
                 reads=[sqreg, RC], writes=[('ps', 6)])
        rt, rtreg = ring('fa', fa)
        P.op('act', lambda e: e.activation(out=rt[:, 0:n], in_=ss[:, 0:n], func=AF.Ln, scale=1.0 / D, bias=epsc),
             reads=[('ps', 6), RC], writes=[rtreg])
        rs, rsreg = ring('fb', fb)
        P.op('act', lambda e: e.activation(out=rs[:, 0:n], in_=rt[:, 0:n], func=AF.Exp, scale=-0.5), reads=[rtreg], writes=[rsreg])
        for c in range(8):
            gcol = cf[:, CF_PVEC + pvcol + c:CF_PVEC + pvcol + c + 1]
            dst = hT if final else xnT
            wreg_ = ('h', c, g) if final else ('xn', c, g)
            P.op('dve', lambda e, c=c, gcol=gcol, dst=dst: e.scalar_tensor_tensor(
                out=dst[:, c, s:s + n], in0=hT[:, c, s:s + n], scalar=gcol, in1=rs[:, 0:n], op0=ALU.mult, op1=ALU.mult),
                reads=[('h', c, g), rsreg, RC], writes=[wreg_])

    norm_queue = []

    def rmsnorm(pvcol, final=False, ahead=None):
        for g, (s, n) in enumerate(GROUPS):
            norm_queue.append(lambda g=g, s=s, n=n: rmsnorm_group(pvcol, final, g, s, n))
        for _ in range(len(GROUPS) if ahead is None else ahead):
            norm_pop()

    def norm_pop():
        if norm_queue:
            norm_queue.pop(0)()

    NM = 128 - PAD

    def resid_add(ps, psreg, c, g, scale):
        s, n = GROUPS[g]
        if g == 0:
            s, n = PAD, NM
        P.op('dve', lambda e: e.scalar_tensor_tensor(out=hT[:, c, s:s + n], in0=ps[:, 0:n], scalar=scale,
                                                      in1=hT[:, c, s:s + n], op0=ALU.mult, op1=ALU.add),
             reads=[psreg, ('h', c, g)], writes=[('h', c, g)])

    def ffn_gu_group(wt, wreg, fi, g, s, n):
        if g == 0:
            s, n = PAD, NM
        k = rr('ffn_gu', 2)
        gp, up = PS[k], PS[2 + k]
        for kc in range(8):
            P.op('pe', lambda e, kc=kc: e.matmul(gp[:, 0:n], lhsT=wt[:, kc * 128:(kc + 1) * 128],
                                               rhs=xnT[:, kc, s:s + n], start=(kc == 0), stop=(kc == 7)),
                 reads=[wreg, ('xn', kc, g)], writes=[('ps', k)])
        for kc in range(8):
            P.op('pe', lambda e, kc=kc: e.matmul(up[:, 0:n], lhsT=wt[:, 1024 + kc * 128:1024 + (kc + 1) * 128],
                                               rhs=xnT[:, kc, s:s + n], start=(kc == 0), stop=(kc == 7)),
                 reads=[wreg, ('xn', kc, g)], writes=[('ps', 2 + k)])
        sg, sgreg = ring('fa', fa)
        P.op('act', lambda e: e.activation(out=sg[:, 0:n], in_=gp[:, 0:n], func=AF.Silu),
             reads=[('ps', k)], writes=[sgreg])
        P.op('dve', lambda e: e.tensor_tensor(out=SL[:, fi, s:s + n], in0=up[:, 0:n], in1=sg[:, 0:n], op=ALU.mult),
             reads=[('ps', 2 + k), sgreg], writes=[('S', fi, g)])

    def mm_out_group(wt, wreg, nk, c, g, s, n, scale):
        if g == 0:
            s, n = PAD, NM
        k = 4 + rr('ffn_o', 2)
        op_ = PS[k]
        for fi in range(nk):
            P.op('pe', lambda e, fi=fi: e.matmul(op_[:, 0:n], lhsT=wt[:, fi * 128:(fi + 1) * 128],
                                               rhs=SL[:, fi, s:s + n], start=(fi == 0), stop=(fi == nk - 1)),
                 reads=[wreg, ('S', fi, g)], writes=[('ps', k)])
        resid_add(op_, ('ps', k), c, g, scale)

    def ffn(l, j):
        rmsnorm(PV_NORM + (l * 3 + (0 if j == 0 else 2)) * 8)
        for half in range(2):
            for fi in range(NFH):
                wt, wreg = wload(wgu_d[l, j, half * NFH + fi], 2048)
                for g, (s, n) in enumerate(GROUPS):
                    ffn_gu_group(wt, wreg, fi, g, s, n)
                    norm_pop()
            for c in range(8):
                wt, wreg = wload(wd_d[l, j, half, c], NFH * 128)
                for g, (s, n) in enumerate(GROUPS):
                    mm_out_group(wt, wreg, NFH, c, g, s, n, 0.5)

    tabs_for = {}

    rot_pending = []

    def project_rot_group(which, wt, wreg, dst_slot, permcol, g, split=None):
        perm = cb[:, permcol:permcol + 128]
        s, n = GROUPS[g]
        tt, tregs = tabs_for[g]
        pk = rr('proj', 2)
        pp, sp_ = PS[pk], PS[2 + pk]
        for kc in range(8):
            P.op('pe', lambda e, kc=kc: e.matmul(pp[:, 0:n], lhsT=wt[:, which * 1024 + kc * 128:which * 1024 + (kc + 1) * 128],
                                               rhs=xnT[:, kc, s:s + n], start=(kc == 0), stop=(kc == 7)),
                 reads=[wreg, ('xn', kc, g)], writes=[('ps', pk)])
        qb, qbreg = ring('ba', ba)
        P.op('act', lambda e: e.activation(out=qb[:, 0:n], in_=pp[:, 0:n], func=AF.Identity), reads=[('ps', pk)], writes=[qbreg])
        rot_flush()

        def second():
            P.op('pe', lambda e: e.matmul(sp_[:, 0:n], lhsT=perm, rhs=qb[:, 0:n], start=True, stop=True),
                 reads=[qbreg, RC], writes=[('ps', 2 + pk)])
            t1, t1reg = ring('fa', fa)
            t2, t2reg = ring('fb', fb)
            P.op('dve', lambda e: e.tensor_tensor(out=t1[:, 0:n], in0=pp[:, 0:n], in1=tt[:, 0, 0:n], op=ALU.mult),
                 reads=[('ps', pk)] + tregs, writes=[t1reg])
            P.op('dve', lambda e: e.tensor_tensor(out=t2[:, 0:n], in0=sp_[:, 0:n], in1=tt[:, 1, 0:n], op=ALU.mult),
                 reads=[('ps', 2 + pk)] + tregs, writes=[t2reg])
            if split is None:
                P.op(ROT_ADD_ENG, lambda e: e.tensor_tensor(out=SL[:, dst_slot, s:s + n], in0=t1[:, 0:n], in1=t2[:, 0:n], op=ALU.add),
                     reads=[t1reg, t2reg], writes=[('S', dst_slot, g)])
            else:
                kr, krreg = ring('ba', ba)
                P.op('dve', lambda e: e.tensor_tensor(out=kr[:, 0:n], in0=t1[:, 0:n], in1=t2[:, 0:n], op=ALU.add),
                     reads=[t1reg, t2reg], writes=[krreg])
                for m in range(2):
                    P.op('act', lambda e, m=m: e.activation(out=SL[64 * m:64 * m + 64, split[m], s:s + n], in_=kr[64 * m:64 * m + 64, 0:n],
                                                          func=AF.Identity),
                         reads=[krreg], writes=[('S', split[m], g)])
        rot_pending.append(second)

    def rot_flush():
        while rot_pending:
            rot_pending.pop(0)()

    def project_v_group(wt, wreg, vslot, g, s, n):
        pk = rr('proj', 2)
        pp = PS[pk]
        for ci in range(n // 128):
            cs = s + ci * 128
            for kc in range(8):
                P.op('pe', lambda e, kc=kc, ci=ci, cs=cs: e.matmul(pp[:, ci * 128:(ci + 1) * 128], lhsT=xnT[:, kc, cs:cs + 128],
                                                                 rhs=wt[:, kc * 128:(kc + 1) * 128], start=(kc == 0), stop=(kc == 7)),
                     reads=[wreg, ('xn', kc, g)], writes=[('ps', pk)])
        P.op('act', lambda e: e.activation(out=SL[:, vslot, s:s + n], in_=pp[:, 0:n], func=AF.Identity),
             reads=[('ps', pk)], writes=[('S', vslot, g)])

    def project_v(wt, wreg, vslot):
        for g, (s, n) in enumerate(GROUPS):
            project_v_group(wt, wreg, vslot, g, s, n)

    def stats_rstd(src, srcreg, n, scale, bias_col, o0=0):
        sq, sqreg = ring('ba', ba)
        P.op('dve', lambda e: e.tensor_tensor(out=sq[:, o0:n], in0=src[:, o0:n], in1=src[:, o0:n], op=ALU.mult), reads=[srcreg], writes=[sqreg])
        P.op('pe', lambda e: e.matmul(PS[6][:, o0:n], lhsT=ones, rhs=sq[:, o0:n], start=True, stop=True),
             reads=[sqreg, RC], writes=[('ps', 6)])
        sd, sdreg = ring('fb', fb)
        P.op('act', lambda e: e.activation(out=sd[:, o0:n], in_=PS[6][:, o0:n], func=AF.Ln, scale=scale, bias=bias_col),
             reads=[('ps', 6), RC], writes=[sdreg])
        rs, rsreg = ring('fb', fb)
        P.op('act', lambda e: e.activation(out=rs[:, o0:n], in_=sd[:, o0:n], func=AF.Exp, scale=-0.5), reads=[sdreg], writes=[rsreg])
        return rs, rsreg

    def retention_attention(l, h, u, ks, vs, wvg, wvgreg):
        gam = 1.0 - 2.0 ** (-5.0 - h)
        cd = gam ** 128
        dmask = cf[:, CF_DMASK + h * 128:CF_DMASK + (h + 1) * 128]
        qdec = cf[:, CF_QDEC + h * 128:CF_QDEC + (h + 1) * 128]
        kdec = cf[:, CF_KDEC + h:CF_KDEC + h + 1]
        gnw = cf[:, CF_PVEC + PV_GN + l * 4 + h:CF_PVEC + PV_GN + l * 4 + h + 1]
        RBs = 18 - ks
        grp = lambda n: 0 if n == 0 else 1 + (n - 1) // 4
        P.op('dve', lambda e: e.memset(SL[:, RBs, 0:128], 0.0), writes=[('S', RBs, 0)])
        P.op('dve', lambda e: e.memset(Rf[0][:, :], 0.0), writes=[('Rf', 0)])
        def emit_gate(g, bank=6):
            s, n = GROUPS[g]
            gp, greg = (PTF, ('pt',)) if bank == 'ptf' else (PS[bank], ('ps', bank))
            for kc in range(8):
                P.op('pe', lambda e, kc=kc: e.matmul(gp[:, 0:n], lhsT=wvg[:, 1024 + kc * 128:1024 + (kc + 1) * 128],
                                                   rhs=xnT[:, kc, s:s + n], start=(kc == 0), stop=(kc == 7)),
                     reads=[wvgreg, ('xn', kc, g)], writes=[greg])
            sg, sgreg = ring('fb', fb)
            P.op('act', lambda e: e.activation(out=sg[:, 0:n], in_=gp[:, 0:n], func=AF.Exp, scale=-1.0), reads=[greg], writes=[sgreg])
            P.op('act', lambda e: e.activation(out=sg[:, 0:n], in_=sg[:, 0:n], func=AF.Ln, bias=onec), reads=[sgreg, RC], writes=[sgreg])
            P.op('act', lambda e: e.activation(out=sg[:, 0:n], in_=sg[:, 0:n], func=AF.Exp, scale=-1.0), reads=[sgreg], writes=[sgreg])
            gs, gsreg = ring('sq', sqr)
            P.op('dve', lambda e: e.tensor_tensor(out=gs[:, 0:n], in0=gp[:, 0:n], in1=sg[:, 0:n], op=ALU.mult),
                 reads=[greg, sgreg], writes=[gsreg])
            gate[g] = (gs, gsreg)

        TB = [(PT, ('pt',)), (PS[6][:, :].bitcast(BF16), ('ps', 6))]
        for r in range(4):
            tb, tbreg = TB[r % 2]
            for i in range(4):
                n = 4 * r + i
                P.op('pe', lambda e, n=n, i=i, tb=tb: e.transpose(tb[:, i * 128:(i + 1) * 128], SL[:, ks, n * 128:(n + 1) * 128], ident),
                     reads=[('S', ks, grp(n)), RC], writes=[tbreg])
            kd, kdreg = ring('ba', ba)
            P.op('act', lambda e, kd=kd, tb=tb: e.activation(out=kd[:, 0:512], in_=tb[:, 0:512], func=AF.Identity, scale=kdec),
                 reads=[tbreg, RC], writes=[kdreg])
            for i in range(4):
                n = 4 * r + i
                P.op('pe', lambda e, n=n, i=i, r=r, kd=kd: e.matmul(PS[r][:, i * 128:(i + 1) * 128], lhsT=kd[:, i * 128:(i + 1) * 128],
                                                                  rhs=SL[:, vs, n * 128:(n + 1) * 128], start=True, stop=True),
                     reads=[kdreg, ('S', vs, grp(n))], writes=[('ps', r)])
        gate = {}
        for n in range(16):
            if n in (0, 5, 10):
                emit_gate(n // 5, (6, 4, 5)[n // 5])
            r, i = divmod(n, 4)
            P.op('dve', lambda e, n=n, r=r, i=i: e.scalar_tensor_tensor(out=Rf[(n + 1) % 2][:, :], in0=Rf[n % 2][:, :], scalar=cd,
                                                                      in1=PS[r][:, i * 128:(i + 1) * 128], op0=ALU.mult, op1=ALU.add),
                 reads=[('Rf', n % 2), ('ps', r)], writes=[('Rf', (n + 1) % 2)])
            P.op('act', lambda e, n=n: e.activation(out=SL[:, RBs, (n + 1) * 128:(n + 2) * 128], in_=Rf[(n + 1) % 2][:, :], func=AF.Identity),
                 reads=[('Rf', (n + 1) % 2)], writes=[('S', RBs, grp(n + 1))])
        def emit_ST(g):
            s, n = GROUPS[g]
            ncx = n // 128
            bk = g % 2
            for ci in range(ncx):
                cs = s + ci * 128
                P.op('pe', lambda e, ci=ci, cs=cs: e.matmul(PS[bk][:, ci * 128:(ci + 1) * 128], lhsT=SL[:, ks, cs:cs + 128],
                                                          rhs=SL[:, u, cs:cs + 128], start=True, stop=True),
                     reads=[('S', u, g), ('S', ks, g)], writes=[('ps', bk)])
            v3 = lambda ap: ap.rearrange("p (c i) -> p c i", c=ncx)
            bc = lambda ap: ap.unsqueeze(1).broadcast_to([128, ncx, 128])
            stm, stmreg = ring('ba', ba)
            P.op('dve', lambda e: e.tensor_tensor(out=v3(stm[:, 0:n]), in0=v3(PS[bk][:, 0:n]), in1=bc(dmask), op=ALU.mult),
                 reads=[('ps', bk), RC], writes=[stmreg])
            qd, qdreg = ring('ba', ba)
            P.op('dve', lambda e: e.tensor_tensor(out=v3(qd[:, 0:n]), in0=v3(SL[:, u, s:s + n]), in1=bc(qdec), op=ALU.mult),
                 reads=[('S', u, g), RC], writes=[qdreg])
            return stm, stmreg, qd, qdreg

        def emit_O(g, st):
            stm, stmreg, qd, qdreg = st
            s, n = GROUPS[g]
            OBk = 4 + g % 2
            OB = PS[OBk]
            for ci in range(n // 128):
                cs = s + ci * 128
                P.op('pe', lambda e, ci=ci, cs=cs: e.matmul(OB[:, ci * 128:(ci + 1) * 128], lhsT=SL[:, vs, cs:cs + 128],
                                                          rhs=stm[:, ci * 128:(ci + 1) * 128], start=True, stop=False),
                     reads=[('S', vs, g), stmreg], writes=[('ps', OBk)])
                P.op('pe', lambda e, ci=ci, cs=cs: e.matmul(OB[:, ci * 128:(ci + 1) * 128], lhsT=SL[:, RBs, cs:cs + 128],
                                                          rhs=qd[:, ci * 128:(ci + 1) * 128], start=False, stop=True),
                     reads=[('S', RBs, g), qdreg], writes=[('ps', OBk)])
            post.append([1, lambda: fin1(g, OBk)])
            tick()

        post = []
        PTF = PT[:, :].bitcast(F32)

        def tick(force=False):
            for ent in list(post):
                ent[0] -= 1
                if force or ent[0] <= 0:
                    post.remove(ent)
                    ent[1]()

        def fin1(g, OBk):
            s, n = GROUPS[g]
            OB = PS[OBk]
            cent = cb[:, CB_CENT:CB_CENT + 128]
            ck = 2 + g % 2
            ob, obreg = ring('ba', ba)
            P.op('act', lambda e: e.activation(out=ob[:, 0:n], in_=OB[:, 0:n], func=AF.Identity), reads=[('ps', OBk)], writes=[obreg])
            P.op('pe', lambda e: e.matmul(PS[ck][:, 0:n], lhsT=cent, rhs=ob[:, 0:n], start=True, stop=True),
                 reads=[obreg, RC], writes=[('ps', ck)])
            post.append([1, lambda: fin2(g, ck)])

        def fin2(g, ck):
            s, n = GROUPS[g]
            csq, csqreg = ring('ba', ba)
            P.op('act', lambda e: e.activation(out=csq[:, 0:n], in_=PS[ck][:, 0:n], func=AF.Square), reads=[('ps', ck)], writes=[csqreg])
            P.op('pe', lambda e: e.matmul(PS[6][:, 0:n], lhsT=ones, rhs=csq[:, 0:n], start=True, stop=True),
                 reads=[csqreg, RC], writes=[('ps', 6)])
            post.append([1, lambda: fin3(g, ck)])

        def fin3(g, ck):
            s, n = GROUPS[g]
            rs, rsreg = ring('fb', fb)
            P.op('act', lambda e: e.activation(out=rs[:, 0:n], in_=PS[6][:, 0:n], func=AF.Ln, scale=1.0 / 128, bias=epsc),
                 reads=[('ps', 6), RC], writes=[rsreg])
            P.op('act', lambda e: e.activation(out=rs[:, 0:n], in_=rs[:, 0:n], func=AF.Exp, scale=-0.5), reads=[rsreg], writes=[rsreg])
            fin4(g, ck, rs, rsreg)

        def fin4(g, ck, rs, rsreg):
            s, n = GROUPS[g]
            nrm, nrmreg = ring('fa', fa)
            P.op('dve', lambda e: e.scalar_tensor_tensor(out=nrm[:, 0:n], in0=PS[ck][:, 0:n], scalar=gnw, in1=rs[:, 0:n],
                                                          op0=ALU.mult, op1=ALU.mult),
                 reads=[('ps', ck), rsreg, RC], writes=[nrmreg])
            gs, gsreg = gate[g]
            P.op('dve', lambda e: e.tensor_tensor(out=SL[:, u, s:s + n], in0=nrm[:, 0:n], in1=gs[:, 0:n], op=ALU.mult),
                 reads=[nrmreg, gsreg], writes=[('S', u, g)])
            done.add(g)

        done = set()

        pending = None
        need_gate = [3, 4]
        for g in range(5):
            if need_gate and (need_gate[0] - 3) in done:
                emit_gate(need_gate.pop(0), 'ptf')
            st = emit_ST(g)
            if pending is not None:
                emit_O(*pending)
            pending = (g, st)
        emit_O(*pending)
        while post or need_gate:
            if need_gate and (need_gate[0] - 3) in done:
                emit_gate(need_gate.pop(0), 'ptf')
            tick(force=True)

    def retention_unit(l, h):
        u = h
        ks = 8 + 2 * (u % 2)
        vs = 9
        wqk, wqkreg = wload(wmix_d[l, u, 0], 2048)
        wvg, wvgreg = wload(wmix_d[l, u, 1], 2048)
        for g, (s, n) in enumerate(GROUPS):
            tabs_for[g] = tload(0, s, n)
            project_rot_group(0, wqk, wqkreg, u, CB_PERMR, g)
            project_rot_group(1, wqk, wqkreg, ks, CB_PERMR, g)
            norm_pop()
        rot_flush()
        project_v(wvg, wvgreg, vs)
        if DBG_MIX == 'B':
            return
        retention_attention(l, h, u, ks, vs, wvg, wvgreg)

    def diff_attention(l, u, ks, vs, om):
        neglam = lamv[:, 4 + l:5 + l]
        sw = cf[:, CF_PVEC + PV_SW + l:CF_PVEC + PV_SW + l + 1]
        blocks = []
        for g, (s, n) in enumerate(GROUPS):
            last = (s + n) // 128 - 1
            for m in range(2):
                for jb in range(last + 1):
                    blocks.append((g, m, jb, last))
        odset = {}
        tres = {}
        deferred = []
        post = []

        def emit_S(blk):
            g, m, jb, last = blk
            s, n = GROUPS[g]
            off = max(0, jb * 128 - s)
            nq = n - off
            diag = jb * 128 >= s
            sk = rr('dst', 3)
            SP_, sreg = SBANK[sk]
            kg = 0 if jb == 0 else 1 + (jb - 1) // 4
            P.op('pe', lambda e: e.matmul(SP_[:, 0:nq], lhsT=SL[:, ks[m], jb * 128:(jb + 1) * 128],
                                          rhs=SL[:, u, s + off:s + n], start=True, stop=(not diag)),
                 reads=[('S', ks[m], kg), ('S', u, g)], writes=[sreg])
            if diag:
                P.op('pe', lambda e: e.matmul(SP_[:, 0:128], lhsT=ident, rhs=cmask, start=False, stop=True),
                     reads=[RC], writes=[sreg])
            return sk, off, nq, kg

        def emit_rest(blk, sinfo):
            g, m, jb, last = blk
            sk, off, nq, kg = sinfo
            SP_, sreg = SBANK[sk]
            if (g, m) not in odset:
                odset[(g, m)] = 2 + 2 * rr('dod', 2)
                while any(ent[2] == odset[(g, m)] for ent in post):
                    for ent in list(post):
                        if ent[2] == odset[(g, m)]:
                            post.remove(ent)
                            ent[1]()
            OBk = odset[(g, m)]
            DBk = OBk + 1
            OB, DB = PS[OBk], PS[DBk]
            pt, ptreg = ring('ba', ba)
            P.op('act', lambda e: e.activation(out=pt[:, 0:nq], in_=SP_[:, 0:nq], func=AF.Exp, scale=0.125),
                 reads=[sreg], writes=[ptreg])
            P.op('pe', lambda e: e.matmul(OB[:, off:off + nq], lhsT=SL[:, vs, jb * 128:(jb + 1) * 128], rhs=pt[:, 0:nq],
                                          start=(jb == 0), stop=(jb == last)),
                 reads=[('S', vs, kg), ptreg], writes=[('ps', OBk)])
            P.op('pe', lambda e: e.matmul(DB[:, off:off + nq], lhsT=(ones_pad if jb == 0 else ones), rhs=pt[:, 0:nq],
                                          start=(jb == 0), stop=(jb == last)),
                 reads=[RC, ptreg], writes=[('ps', DBk)])
            if jb == last:
                post.append([2, lambda: normalize_act(g, m, OBk), OBk])

        def normalize_act(g, m, OBk):
            DBk = OBk + 1
            DB = PS[DBk]
            s, n = GROUPS[g]
            o0 = PAD if g == 0 else 0
            r, rreg = ring('fb', fb)
            P.op('act', lambda e: e.activation(out=r[:, o0:n], in_=DB[:, o0:n], func=AF.Ln), reads=[('ps', DBk)], writes=[rreg])
            P.op('act', lambda e: e.activation(out=r[:, o0:n], in_=r[:, o0:n], func=AF.Exp, scale=-1.0), reads=[rreg], writes=[rreg])
            post.append([3, lambda: normalize_dve(g, m, OBk, r, rreg), OBk])

        def normalize_dve(g, m, OBk, r, rreg):
            OB = PS[OBk]
            s, n = GROUPS[g]
            o0 = PAD if g == 0 else 0
            t, treg_ = ring('fa', fa)
            P.op('dve', lambda e: e.tensor_tensor(out=t[:, o0:n], in0=OB[:, o0:n], in1=r[:, o0:n], op=ALU.mult),
                 reads=[('ps', OBk), rreg], writes=[treg_])
            tres[(g, m)] = (t, treg_)
            if m == 1:
                post.append([1, lambda: finalize1(g), None])

        def finalize1(g):
            s, n = GROUPS[g]
            o0 = PAD if g == 0 else 0
            (t0, t0reg), (t1, t1reg) = tres[(g, 0)], tres[(g, 1)]
            o, oreg = ring('fa', fa)
            P.op('dve', lambda e: e.scalar_tensor_tensor(out=o[:, o0:n], in0=t1[:, o0:n], scalar=neglam, in1=t0[:, o0:n],
                                                          op0=ALU.mult, op1=ALU.add),
                 reads=[t0reg, t1reg, ('lamv',)], writes=[oreg])
            sq, sqreg = ring('ba', ba)
            P.op('dve', lambda e: e.tensor_tensor(out=sq[:, o0:n], in0=o[:, o0:n], in1=o[:, o0:n], op=ALU.mult), reads=[oreg], writes=[sqreg])
            P.op('pe', lambda e: e.matmul(PS[6][:, o0:n], lhsT=ones, rhs=sq[:, o0:n], start=True, stop=True),
                 reads=[sqreg, RC], writes=[('ps', 6)])
            post.append([4, lambda: finalize2(g, o, oreg), None])

        def finalize2(g, o, oreg):
            s, n = GROUPS[g]
            o0 = PAD if g == 0 else 0
            rs, rsreg = ring('fb', fb)
            P.op('act', lambda e: e.activation(out=rs[:, o0:n], in_=PS[6][:, o0:n], func=AF.Ln, scale=1.0 / (128 * om * om), bias=epsl[l]),
                 reads=[('ps', 6), RC], writes=[rsreg])
            P.op('act', lambda e: e.activation(out=rs[:, o0:n], in_=rs[:, o0:n], func=AF.Exp, scale=-0.5), reads=[rsreg], writes=[rsreg])
            P.op('dve', lambda e: e.scalar_tensor_tensor(out=SL[:, u, s + o0:s + n], in0=o[:, o0:n], scalar=sw, in1=rs[:, o0:n],
                                                          op0=ALU.mult, op1=ALU.mult),
                 reads=[oreg, rsreg, RC], writes=[('S', u, g)])

        def tick(force=False):
            for ent in list(post):
                ent[0] -= 1
                if force or ent[0] <= 0:
                    post.remove(ent)
                    ent[1]()

        pend = []
        for blk in blocks:
            sinfo = emit_S(blk)
            pend.append((blk, sinfo))
            if len(pend) > 2:
                emit_rest(*pend.pop(0))
                tick()
        while pend:
            emit_rest(*pend.pop(0))
            tick()
        while post:
            tick(force=True)

    def diff_unit(l, h, lam_init):
        u = 4 + h
        ks = (8, 10)
        vs = 9
        wqk, wqkreg = wload(wmix_d[l, u, 0], 2048)
        wv, wvreg = wload(wmix_d[l, u, 1, :, 0:1024], 1024)
        if h == 0:
            P.op('dve', lambda e: e.memset(SL[64:128, ks[0], :], 0.0), writes=[('S', ks[0], g) for g in range(5)])
            P.op('dve', lambda e: e.memset(SL[0:64, ks[1], :], 0.0), writes=[('S', ks[1], g) for g in range(5)])
        for g, (s, n) in enumerate(GROUPS):
            tabs_for[g] = tload(1, s, n)
            project_rot_group(0, wqk, wqkreg, u, CB_PERMD, g)
            project_rot_group(1, wqk, wqkreg, None, CB_PERMD, g, split=ks)
        rot_flush()
        project_v(wv, wvreg, vs)
        diff_attention(l, u, ks, vs, 1.0 - lam_init)

    def lam_compute(l):
        lam_init = 0.8 - 0.6 * math.exp(-0.3 * l)
        for t in range(2):
            a = lamin[:, (l * 4 + 2 * t) * 64:(l * 4 + 2 * t + 1) * 64]
            b = lamin[:, (l * 4 + 2 * t + 1) * 64:(l * 4 + 2 * t + 2) * 64]
            P.op('dve', lambda e, a=a, b=b: e.tensor_tensor(out=lamw[:, :], in0=a, in1=b, op=ALU.mult),
                 reads=[RC, ('lamw',)], writes=[('lamw',)])
            P.op('dve', lambda e, t=t: e.tensor_reduce(out=lamv[:, t:t + 1], in_=lamw[:, :], axis=mybir.AxisListType.X, op=ALU.add),
                 reads=[('lamw',), ('lamv',)], writes=[('lamv',)])
            P.op('act', lambda e, t=t: e.activation(out=lamv[:, 2 + t:3 + t], in_=lamv[:, t:t + 1], func=AF.Exp),
                 reads=[('lamv',)], writes=[('lamv',)])
        P.op('dve', lambda e: e.scalar_tensor_tensor(out=lamv[:, 4 + l:5 + l], in0=lamv[:, 3:4], scalar=-lam_init, in1=lamv[:, 2:3],
                                                      op0=ALU.add, op1=ALU.subtract),
             reads=[('lamv',)], writes=[('lamv',)])
        return lam_init

    def mixer(l):
        rmsnorm(PV_NORM + (l * 3 + 1) * 8)
        lam_init = lam_compute(l)
        if DBG_MIX == 'A':
            return
        for h in range(4):
            retention_unit(l, h)
            if DBG_MIX in ('B', 'B1', 'C', 'R1', 'R2', 'R3'):
                return
        if DBG_MIX == 'D':
            return
        for h in range(4):
            diff_unit(l, h, lam_init)
            if DBG_MIX == 'F':
                return
        for c in range(8):
            wt, wreg = wload(wout_d[l, c], 1024)
            for g, (s, n) in enumerate(GROUPS):
                mm_out_group(wt, wreg, 8, c, g, s, n, 1.0)

    stages = []
    for l in range(2):
        stages += [('ffn', l, 0), ('mix', l), ('ffn', l, 1)]
    nst = len(stages) if stop is None else stop
    for st in stages[:nst]:
        if st[0] == 'ffn':
            ffn(st[1], st[2])
        else:
            mixer(st[1])
    outregs = []
    if stop is None:
        for g, (s, n) in enumerate(GROUPS):
            rmsnorm_group(PV_NORM + 48, True, g, s, n)
            if g >= 1:
                P.dma('sp', lambda e, s=s, n=n: e.dma_start(out=out_d[:, :, s - 128:s - 128 + n], in_=hT[:, :, s:s + n]), 'out',
                      reads=[('h', c, g) for c in range(8)], writes=[('out', g)])
                outregs.append(('out', g))
    else:
        for c in range(8):
            P.dma('sp', lambda e, c=c: e.dma_start(out=out_d[:, c, :], in_=hT[:, c, 128:TP]), 'out',
                  reads=[('h', c, g) for g in range(1, 5)], writes=[('out', c)])
            outregs.append(('out', c))
    if dump == 'h':
        for c in range(8):
            P.dma('sp', lambda e, c=c: e.dma_start(out=dbg_d[:, c, :], in_=hT[:, c, :]), 'out',
                  reads=[('h', c, g) for g in range(5)], writes=[('dbg', c)])
            outregs.append(('dbg', c))
    P.op('sp', None, reads=outregs)
    P.build()
    return nc, P


def _bf(a):
    return np.asarray(a, dtype=np.float32).astype(ml_dtypes.bfloat16)


def _const_tables():
    pos = (np.arange(TP, dtype=np.float32) - np.float32(PAD))
    angle = (np.float32(10000.0) ** (-np.linspace(0.0, 1.0, 64, dtype=np.float32))).astype(np.float32)
    fr = (pos[None, :] * np.repeat(angle, 2)[:, None]).astype(np.float32)
    cosr = np.cos(fr).astype(np.float32)
    sgn = np.where(np.arange(128) % 2 == 0, -1.0, 1.0).astype(np.float32)[:, None]
    sinr = (np.sin(fr) * sgn).astype(np.float32)
    r = 16
    inv = (np.float32(500000.0) ** (-np.arange(0, r, 2, dtype=np.float32) / r)).astype(np.float32)
    cosd = np.ones((128, TP), np.float32)
    sind = np.zeros((128, TP), np.float32)
    for p in range(128):
        dd = p % 64
        if dd < r:
            f = (pos * inv[dd % 8]).astype(np.float32)
            cosd[p] = np.cos(f)
            sind[p] = np.sin(f) * (-1.0 if dd < 8 else 1.0)
    tabs = np.stack([cosr, sinr, cosd, sind]).astype(np.float32)
    cbm = np.zeros((128, CB_N), np.float32)
    cbm[:, CB_ONES:CB_ONES + 128] = 1.0
    cbm[PAD:, CB_ONESPAD:CB_ONESPAD + 128] = 1.0
    cbm[:, CB_IDENT:CB_IDENT + 128] = np.eye(128)
    for m in range(128):
        src = m + 1 if m % 2 == 0 else m - 1
        cbm[src, CB_PERMR + m] = 1.0
        dd = m % 64
        if dd < 8:
            cbm[m + 8, CB_PERMD + m] = 1.0
        elif dd < 16:
            cbm[m - 8, CB_PERMD + m] = 1.0
    jj = np.arange(128)[:, None]
    ii = np.arange(128)[None, :]
    cbm[:, CB_CMASK:CB_CMASK + 128] = np.where(jj <= ii, 0.0, -30000.0)
    cbm[:, CB_CENT:CB_CENT + 128] = np.eye(128) - 1.0 / 128.0
    cfm = np.zeros((128, CF_N), np.float32)
    cfm[:, CF_EPS] = EPS
    cfm[:, CF_ONE] = 1.0
    for l_ in range(2):
        om_ = 1.0 - (0.8 - 0.6 * math.exp(-0.3 * l_))
        cfm[:, CF_EPS + 1 + l_] = EPS / (om_ * om_)
    for h in range(4):
        lg = math.log(1.0 - 2.0 ** (-5.0 - h))
        rel = (ii - jj).astype(np.float64)
        dm = np.where(rel >= 0, np.exp(lg * np.maximum(rel, 0.0)), 0.0) * (128.0 ** -0.5)
        cfm[:, CF_DMASK + h * 128:CF_DMASK + (h + 1) * 128] = dm
        cfm[:, CF_QDEC + h * 128:CF_QDEC + (h + 1) * 128] = np.exp(lg * (np.arange(128) + 1.0))[None, :]
        cfm[:, CF_KDEC + h] = np.exp(lg * (127.0 - np.arange(128))) * (128.0 ** -0.5)
    return tabs, cbm, cfm


def _prep(inputs):
    f = lambda k: np.asarray(inputs[k], dtype=np.float32)
    tabs, cbm, cfm = _const_tables()
    cvec = lambda v: np.ascontiguousarray(v.reshape(-1, 128).T)
    norms = [f("ffn1_norm"), f("mix_norm"), f("ffn2_norm")]
    for l in range(2):
        for w in range(3):
            c0 = CF_PVEC + PV_NORM + (l * 3 + w) * 8
            cfm[:, c0:c0 + 8] = cvec(norms[w][l])
        cfm[:, CF_PVEC + PV_GN + l * 4:CF_PVEC + PV_GN + l * 4 + 4] = cvec(f("ret_gn_w")[l])
        cfm[:, CF_PVEC + PV_SW + l] = f("diff_subln_w")[l]
    cfm[:, CF_PVEC + PV_NORM + 48:CF_PVEC + PV_NORM + 56] = cvec(f("final_norm"))
    lam = np.stack([f("diff_lambda_q1"), f("diff_lambda_k1"), f("diff_lambda_q2"), f("diff_lambda_k2")], axis=1)
    lam = np.ascontiguousarray(np.broadcast_to(lam.reshape(1, 512), (128, 512)))

    def fm_tile(w):
        nc_ = w.shape[1] // 128
        return w.reshape(8, 128, nc_, 128).transpose(2, 1, 0, 3)

    wgu = np.empty((2, 2, NF, 128, 2, 8, 128), np.float32)
    wd = np.empty((2, 2, 2, 8, 128, NFH, 128), np.float32)
    for l in range(2):
        for j, (kg, ku, kd_) in enumerate([("ffn1_w_gate", "ffn1_w_up", "ffn1_w_down"), ("ffn2_w_gate", "ffn2_w_up", "ffn2_w_down")]):
            wgu[l, j, :, :, 0] = fm_tile(f(kg)[l])
            wgu[l, j, :, :, 1] = fm_tile(f(ku)[l])
            wd[l, j] = f(kd_)[l].reshape(2, NFH, 128, 8, 128).transpose(0, 3, 2, 1, 4)
    win = f("w_in")
    wmix = np.zeros((2, 8, 2, 128, 2, 8, 128), np.float32)
    for l in range(2):
        t = fm_tile(win[l])
        for h in range(4):
            wmix[l, h, 0, :, 0] = t[h]
            wmix[l, h, 0, :, 1] = t[4 + h]
            wmix[l, h, 1, :, 0] = t[8 + h]
            wmix[l, h, 1, :, 1] = t[12 + h]
            wmix[l, 4 + h, 0, :, 0] = t[16 + h]
            wmix[l, 4 + h, 0, :, 1] = t[20 + h]
            wmix[l, 4 + h, 1, :, 0] = t[24 + h]
    wout = np.stack([fm_tile(f("w_out")[l]) for l in range(2)])
    shared = {
        "metaT": np.ascontiguousarray(f("meta_tokens").reshape(NMETA, 8, 128).transpose(2, 1, 0)),
        "cb": _bf(cbm), "cf": cfm, "lam": lam, "tabs": tabs,
        "wgu": wgu.reshape(2, 2, NF, 128, 2048), "wd": wd.reshape(2, 2, 2, 8, 128, NFH * 128),
        "wmix": wmix.reshape(2, 8, 2, 128, 2048), "wout": np.ascontiguousarray(wout.reshape(2, 8, 128, 1024)),
    }
    x = f("x")
    xTs = [np.ascontiguousarray(x[b].reshape(SEQ, 8, 128).transpose(2, 1, 0)) for b in range(x.shape[0])]
    return shared, xTs


def run(inputs, cores=None, stop=None, dump=None, trace=False):
    shared, xTs = _prep(inputs)
    cores = list(range(8)) if cores is None else cores
    nc, P = build_program(stop=stop, dump=dump)
    in_maps = [dict(shared, xT=xTs[b]) for b in cores]
    res = run_bass_kernel_spmd(nc, in_maps, core_ids=list(range(len(cores))), trace=trace)
    return res, P


def kernel(**inputs):
    res, _ = run(inputs)
    outs = [np.asarray(r["outT"], dtype=np.float32) for r in res.results]
    out = np.stack([o.transpose(2, 1, 0).reshape(SEQ, D) for o in outs])
    return np.ascontiguousarray(out.astype(np.float32))
```

```python
import math
import numpy as np
import ml_dtypes
import concourse.bass as bass
import concourse.mybir as mybir
from concourse.bass_utils import run_bass_kernel_spmd

F32 = mybir.dt.float32
BF16 = mybir.dt.bfloat16
AF = mybir.ActivationFunctionType
ALU = mybir.AluOpType

D = 1024
SEQ = 2048
NMETA = 16
PAD = 112
TP = 2176
NCH = 17
DFF = 2816
NF = 22
NFH = 11
EPS = 1e-6
GROUPS = [(0, 128)] + [(128 + 512 * g, 512) for g in range(4)]
NSLOT = 11
NW = 4
WCOLS = 2048
SAME_ENGINE_SYNC = True
DBG_HACK = 0
WARM = 0
ROT_ADD_ENG = 'dve'
DBG_MIX = None

CB_ONES, CB_ONESPAD, CB_IDENT, CB_PERMR, CB_PERMD, CB_CMASK, CB_CENT = [i * 128 for i in range(7)]
CB_N = 7 * 128
CF_DMASK = 0
CF_QDEC = 512
CF_KDEC = 1024
CF_PVEC = 1028
PV_NORM = 0
PV_GN = 56
PV_SW = 64
CF_EPS = CF_PVEC + 66
CF_ONE = CF_EPS + 3
CF_N = CF_ONE + 1


class Prog:
    def __init__(self, nc):
        self.nc = nc
        self.ops = []
        self.dma_sems = {}

    def op(self, eng, fn, reads=(), writes=()):
        self.ops.append(dict(eng=eng, fn=fn, reads=list(reads), writes=list(writes), dma=None))

    def dma(self, eng, fn, slot, reads=(), writes=()):
        self.ops.append(dict(eng=eng, fn=fn, reads=list(reads), writes=list(writes), dma=slot))

    def build(self):
        nc = self.nc
        ops = self.ops
        engs = ['pe', 'act', 'dve', 'pool', 'sp']
        last_w = {}
        readers = {}
        for i, o in enumerate(ops):
            deps = set()
            key = ('dma', i) if o['dma'] is not None else o['eng']
            for r in o['reads']:
                if r in last_w:
                    deps.add(last_w[r])
                if r[0] in ('ps', 'pt'):
                    for k2, i2 in readers.get(r, {}).items():
                        if k2 != key:
                            deps.add(i2)
            for w in o['writes']:
                if w in last_w:
                    deps.add(last_w[w])
                rd = readers.get(w)
                if rd:
                    deps.update(rd.values())
            deps.discard(i)
            o['deps'] = deps
            for r in o['reads']:
                readers.setdefault(r, {})[key] = i
            for w in o['writes']:
                last_w[w] = i
                readers[w] = {}
        signal = set()
        for i, o in enumerate(ops):
            for d in o['deps']:
                p = ops[d]
                if p['dma'] is not None:
                    continue
                if p['eng'] != o['eng'] or (SAME_ENGINE_SYNC and o['eng'] != 'pe'):
                    signal.add(d)
        sem = {e: nc.alloc_semaphore('s_' + e) for e in engs}
        cnt = {e: 0 for e in engs}
        for i, o in enumerate(ops):
            if o['dma'] is not None:
                slot = o['dma']
                if slot not in self.dma_sems:
                    self.dma_sems[slot] = [nc.alloc_semaphore('d_' + str(slot)), 0]
                ent = self.dma_sems[slot]
                ent[1] += 16
                o['tok'] = (ent[0], ent[1])
            elif i in signal:
                cnt[o['eng']] += 1
                o['tok'] = (sem[o['eng']], cnt[o['eng']])
            else:
                o['tok'] = None
        waited = {e: {} for e in engs}
        for i, o in enumerate(ops):
            need = {}
            for d in o['deps']:
                p = ops[d]
                if p['dma'] is None and p['eng'] == o['eng'] and not (SAME_ENGINE_SYNC and o['eng'] != 'pe'):
                    continue
                tok = p['tok']
                assert tok is not None
                k = tok[0]
                if need.get(k, (None, 0))[1] < tok[1]:
                    need[k] = tok
            ws = []
            for k, tok in need.items():
                if waited[o['eng']].get(k, 0) < tok[1]:
                    waited[o['eng']][k] = tok[1]
                    ws.append(tok)
            o['waits'] = ws
        per = {e: [o for o in ops if o['eng'] == e] for e in engs}
        self.stats = {e: len(per[e]) for e in engs}
        self.stats['signals'] = dict(cnt)

        def run(e, lst):
            for o in lst:
                for (s, v) in o['waits']:
                    e.wait_ge(s, v)
                if o['fn'] is None:
                    continue
                ins = o['fn'](e)
                if o['tok'] is not None:
                    ins.then_inc(o['tok'][0], 16 if o['dma'] is not None else 1)

        with nc.Block() as block:
            @block.tensor
            def _(e):
                run(e, per['pe'])

            @block.scalar
            def _(e):
                run(e, per['act'])

            @block.vector
            def _(e):
                run(e, per['dve'])

            @block.gpsimd
            def _(e):
                run(e, per['pool'])

            @block.sync
            def _(e):
                run(e, per['sp'])


def build_program(stop=None, dump=None):
    nc = bass.Bass("TRN2", target_bir_lowering=False)
    P = Prog(nc)

    def din(name, shape, dt=F32):
        return nc.dram_tensor(name, list(shape), dt, kind="ExternalInput").ap()

    xT_d = din("xT", [128, 8, SEQ])
    meta_d = din("metaT", [128, 8, NMETA])
    cb_d = din("cb", [128, CB_N], BF16)
    cf_d = din("cf", [128, CF_N])
    lam_d = din("lam", [128, 2 * 4 * 64])
    tabs_d = din("tabs", [4, 128, TP])
    wgu_d = din("wgu", [2, 2, NF, 128, 2048])
    wd_d = din("wd", [2, 2, 2, 8, 128, NFH * 128])
    wmix_d = din("wmix", [2, 8, 2, 128, 2048])
    wout_d = din("wout", [2, 8, 128, 1024])
    out_d = nc.dram_tensor("outT", [128, 8, SEQ], F32, kind="ExternalOutput").ap()
    dbg_d = None
    if dump is not None:
        dbg_d = nc.dram_tensor("dbg", [128, 8, TP], F32, kind="ExternalOutput").ap()

    sb = nc.alloc_sbuf_tensor
    hT = sb("hT", [128, 8, TP], F32)
    xnT = sb("xnT", [128, 8, TP], BF16)
    SL = sb("slots", [128, NSLOT, TP], BF16)
    wsl = [sb(f"wsl{i}", [128, WCOLS], BF16) for i in range(NW)]
    tabt = [sb(f"tab{i}", [128, 2, 512], F32) for i in range(2)]
    cb = sb("cbs", [128, CB_N], BF16)
    cf = sb("cfs", [128, CF_N], F32)
    lamin = sb("lamin", [128, 512], F32)
    lamw = sb("lamw", [128, 64], F32)
    lamv = sb("lamv", [128, 8], F32)
    NR = 3
    sqr = [sb(f"sq{i}", [128, 512], BF16) for i in range(NR)]
    fa = [sb(f"fa{i}", [128, 512], F32) for i in range(4)]
    fb = [sb(f"fb{i}", [128, 512], F32) for i in range(4)]
    ba = [sb(f"ba{i}", [128, 512], BF16) for i in range(6)]
    Rf = [sb(f"Rf{i}", [128, 128], F32) for i in range(2)]

    PS = [nc.alloc_psum_tensor(f"ps{i}", [128, 512], F32) for i in range(7)]
    PT = nc.alloc_psum_tensor("pst", [128, 1024], BF16)

    SBANK = [(PS[0], ('ps', 0)), (PS[1], ('ps', 1)), (PT[:, :].bitcast(F32), ('pt',))]
    ctr = {}

    def rr(name, n):
        v = ctr.get(name, 0)
        ctr[name] = v + 1
        return v % n

    def ring(name, lst):
        i = rr(name, len(lst))
        return lst[i], (name, i)

    ones = cb[:, CB_ONES:CB_ONES + 128]
    ones_pad = cb[:, CB_ONESPAD:CB_ONESPAD + 128]
    ident = cb[:, CB_IDENT:CB_IDENT + 128]
    cmask = cb[:, CB_CMASK:CB_CMASK + 128]
    RC = ('const',)
    epsc = cf[:, CF_EPS:CF_EPS + 1]
    onec = cf[:, CF_ONE:CF_ONE + 1]
    epsl = [cf[:, CF_EPS + 1 + l_:CF_EPS + 2 + l_] for l_ in range(2)]

    P.dma('sp', lambda e: e.dma_start(out=cb[:, :], in_=cb_d), 'const', writes=[RC])
    P.dma('sp', lambda e: e.dma_start(out=cf[:, :], in_=cf_d), 'const', writes=[RC])
    P.dma('sp', lambda e: e.dma_start(out=lamin[:, :], in_=lam_d), 'const', writes=[RC])
    hregs = lambda c: [('h', c, g) for g in range(5)]
    P.op('dve', lambda e: e.memset(hT[:, :, 0:PAD], 0.0), writes=[('h', c, 0) for c in range(8)])
    P.op('dve', lambda e: e.memset(xnT[:, :, 0:PAD], 0.0), writes=[('xn', c, 0) for c in range(8)])
    P.dma('sp', lambda e: e.dma_start(out=hT[:, :, PAD:128], in_=meta_d), ('xin', 0),
          reads=[('h', c, 0) for c in range(8)], writes=[('h', c, 0) for c in range(8)])
    for g in range(1, 5):
        s_, n_ = GROUPS[g]
        P.dma('sp', lambda e, s_=s_, n_=n_: e.dma_start(out=hT[:, :, s_:s_ + n_], in_=xT_d[:, :, s_ - 128:s_ - 128 + n_]), ('xin', g),
              writes=[('h', c, g) for c in range(8)])

    def wload(src_ap, ncols):
        i = rr('w', NW)
        t = wsl[i]
        reg = ('w', i)
        P.dma('pool', lambda e: e.dma_start(out=t[:, 0:ncols], in_=src_ap, max_dma_last_dim=8192),
              ('w', i), writes=[reg])
        return t, reg

    def tload(k, s, n):
        i = rr('tab', 2)
        t = tabt[i]
        regs = [('tab', i, 0), ('tab', i, 1)]
        P.dma('sp', lambda e: e.dma_start(out=t[:, 0, 0:n], in_=tabs_d[2 * k, :, s:s + n]), ('tab', i, 0), writes=[regs[0]])
        P.dma('sp', lambda e: e.dma_start(out=t[:, 1, 0:n], in_=tabs_d[2 * k + 1, :, s:s + n]), ('tab', i, 1), writes=[regs[1]])
        return t, regs

    FFG = []
    _a = PAD
    for _j in range(5):
        _n = 413 if _j < 4 else 412
        _gs = [g for g, (s_, n_) in enumerate(GROUPS) if s_ < _a + _n and _a < s_ + n_]
        FFG.append((_a, _n, _gs))
        _a += _n
    assert _a == TP

    def rmsnorm_group(pvcol, final, s, n, gs):
        ss = PS[6]
        for c in range(8):
            sq, sqreg = ring('sq', sqr)
            P.op('act', lambda e, sq=sq, c=c: e.activation(out=sq[:, 0:n], in_=hT[:, c, s:s + n], func=AF.Square),
                 reads=[('h', c, g) for g in gs], writes=[sqreg])
            P.op('pe', lambda e, sq=sq, c=c: e.matmul(ss[:, 0:n], lhsT=ones, rhs=sq[:, 0:n], start=(c == 0), stop=(c == 7)),
                 reads=[sqreg, RC], writes=[('ps', 6)])
        rt, rtreg = ring('fa', fa)
        P.op('act', lambda e: e.activation(out=rt[:, 0:n], in_=ss[:, 0:n], func=AF.Ln, scale=1.0 / D, bias=epsc),
             reads=[('ps', 6), RC], writes=[rtreg])
        rs, rsreg = ring('fb', fb)
        P.op('act', lambda e: e.activation(out=rs[:, 0:n], in_=rt[:, 0:n], func=AF.Exp, scale=-0.5), reads=[rtreg], writes=[rsreg])
        for c in range(8):
            gcol = cf[:, CF_PVEC + pvcol + c:CF_PVEC + pvcol + c + 1]
            dst = hT if final else xnT
            hregs = [('h', c, g) for g in gs]
            wregs = hregs if final else [('xn', c, g) for g in gs]
            P.op('dve', lambda e, c=c, gcol=gcol, dst=dst: e.scalar_tensor_tensor(
                out=dst[:, c, s:s + n], in0=hT[:, c, s:s + n], scalar=gcol, in1=rs[:, 0:n], op0=ALU.mult, op1=ALU.mult),
                reads=hregs + [rsreg, RC] + ([] if final else wregs), writes=wregs)

    norm_queue = []

    def rmsnorm(pvcol, final=False, ahead=None):
        for (a, n, gs) in FFG:
            norm_queue.append(lambda a=a, n=n, gs=gs: rmsnorm_group(pvcol, final, a, n, gs))
        for _ in range(len(FFG) if ahead is None else ahead):
            norm_pop()

    def norm_pop():
        if norm_queue:
            norm_queue.pop(0)()

    def resid_add(ps, psreg, c, a, n, gs, scale):
        regs = [('h', c, g) for g in gs]
        P.op('dve', lambda e: e.scalar_tensor_tensor(out=hT[:, c, a:a + n], in0=ps[:, 0:n], scalar=scale,
                                                      in1=hT[:, c, a:a + n], op0=ALU.mult, op1=ALU.add),
             reads=[psreg] + regs, writes=regs)

    def ffn_gu_group(wt, wreg, fi, a, n, gs):
        k = rr('ffn_gu', 2)
        gp, up = PS[k], PS[2 + k]
        for kc in range(8):
            P.op('pe', lambda e, kc=kc: e.matmul(gp[:, 0:n], lhsT=wt[:, kc * 128:(kc + 1) * 128],
                                               rhs=xnT[:, kc, a:a + n], start=(kc == 0), stop=(kc == 7)),
                 reads=[wreg] + [('xn', kc, g) for g in gs], writes=[('ps', k)])
        for kc in range(8):
            P.op('pe', lambda e, kc=kc: e.matmul(up[:, 0:n], lhsT=wt[:, 1024 + kc * 128:1024 + (kc + 1) * 128],
                                               rhs=xnT[:, kc, a:a + n], start=(kc == 0), stop=(kc == 7)),
                 reads=[wreg] + [('xn', kc, g) for g in gs], writes=[('ps', 2 + k)])
        sg, sgreg = ring('fa', fa)
        P.op('act', lambda e: e.activation(out=sg[:, 0:n], in_=gp[:, 0:n], func=AF.Silu),
             reads=[('ps', k)], writes=[sgreg])
        hregs = [('S', fi, g) for g in gs]
        P.op('dve', lambda e: e.tensor_tensor(out=SL[:, fi, a:a + n], in0=up[:, 0:n], in1=sg[:, 0:n], op=ALU.mult),
             reads=[('ps', 2 + k), sgreg] + hregs, writes=hregs)

    def mm_out_group(wt, wreg, nk, c, a, n, gs, scale):
        k = 4 + rr('ffn_o', 2)
        op_ = PS[k]
        for fi in range(nk):
            P.op('pe', lambda e, fi=fi: e.matmul(op_[:, 0:n], lhsT=wt[:, fi * 128:(fi + 1) * 128],
                                               rhs=SL[:, fi, a:a + n], start=(fi == 0), stop=(fi == nk - 1)),
                 reads=[wreg] + [('S', fi, g) for g in gs], writes=[('ps', k)])
        resid_add(op_, ('ps', k), c, a, n, gs, scale)

    def ffn(l, j):
        rmsnorm(PV_NORM + (l * 3 + (0 if j == 0 else 2)) * 8)
        for half in range(2):
            for fi in range(NFH):
                wt, wreg = wload(wgu_d[l, j, half * NFH + fi], 2048)
                for (a, n, gs) in FFG:
                    ffn_gu_group(wt, wreg, fi, a, n, gs)
                    norm_pop()
            for c in range(8):
                wt, wreg = wload(wd_d[l, j, half, c], NFH * 128)
                for (a, n, gs) in FFG:
                    mm_out_group(wt, wreg, NFH, c, a, n, gs, 0.5)

    tabs_for = {}

    rot_pending = []

    def project_rot_group(which, wt, wreg, dst_slot, permcol, g, split=None):
        perm = cb[:, permcol:permcol + 128]
        s, n = GROUPS[g]
        tt, tregs = tabs_for[g]
        pk = rr('proj', 2)
        pp, sp_ = PS[pk], PS[2 + pk]
        for kc in range(8):
            P.op('pe', lambda e, kc=kc: e.matmul(pp[:, 0:n], lhsT=wt[:, which * 1024 + kc * 128:which * 1024 + (kc + 1) * 128],
                                               rhs=xnT[:, kc, s:s + n], start=(kc == 0), stop=(kc == 7)),
                 reads=[wreg, ('xn', kc, g)], writes=[('ps', pk)])
        qb, qbreg = ring('ba', ba)
        P.op('act', lambda e: e.activation(out=qb[:, 0:n], in_=pp[:, 0:n], func=AF.Identity), reads=[('ps', pk)], writes=[qbreg])
        rot_flush()

        def second():
            P.op('pe', lambda e: e.matmul(sp_[:, 0:n], lhsT=perm, rhs=qb[:, 0:n], start=True, stop=True),
                 reads=[qbreg, RC], writes=[('ps', 2 + pk)])
            t1, t1reg = ring('fa', fa)
            t2, t2reg = ring('fb', fb)
            P.op('dve', lambda e: e.tensor_tensor(out=t1[:, 0:n], in0=pp[:, 0:n], in1=tt[:, 0, 0:n], op=ALU.mult),
                 reads=[('ps', pk)] + tregs, writes=[t1reg])
            P.op('dve', lambda e: e.tensor_tensor(out=t2[:, 0:n], in0=sp_[:, 0:n], in1=tt[:, 1, 0:n], op=ALU.mult),
                 reads=[('ps', 2 + pk)] + tregs, writes=[t2reg])
            if split is None:
                P.op(ROT_ADD_ENG, lambda e: e.tensor_tensor(out=SL[:, dst_slot, s:s + n], in0=t1[:, 0:n], in1=t2[:, 0:n], op=ALU.add),
                     reads=[t1reg, t2reg], writes=[('S', dst_slot, g)])
            else:
                kr, krreg = ring('ba', ba)
                P.op('dve', lambda e: e.tensor_tensor(out=kr[:, 0:n], in0=t1[:, 0:n], in1=t2[:, 0:n], op=ALU.add),
                     reads=[t1reg, t2reg], writes=[krreg])
                for m in range(2):
                    P.op('act', lambda e, m=m: e.activation(out=SL[64 * m:64 * m + 64, split[m], s:s + n], in_=kr[64 * m:64 * m + 64, 0:n],
                                                          func=AF.Identity),
                         reads=[krreg], writes=[('S', split[m], g)])
        rot_pending.append(second)

    def rot_flush():
        while rot_pending:
            rot_pending.pop(0)()

    def project_v_group(wt, wreg, vslot, g, s, n):
        pk = rr('proj', 2)
        pp = PS[pk]
        for ci in range(n // 128):
            cs = s + ci * 128
            for kc in range(8):
                P.op('pe', lambda e, kc=kc, ci=ci, cs=cs: e.matmul(pp[:, ci * 128:(ci + 1) * 128], lhsT=xnT[:, kc, cs:cs + 128],
                                                                 rhs=wt[:, kc * 128:(kc + 1) * 128], start=(kc == 0), stop=(kc == 7)),
                     reads=[wreg, ('xn', kc, g)], writes=[('ps', pk)])
        P.op('act', lambda e: e.activation(out=SL[:, vslot, s:s + n], in_=pp[:, 0:n], func=AF.Identity),
             reads=[('ps', pk)], writes=[('S', vslot, g)])

    def project_v(wt, wreg, vslot):
        for g, (s, n) in enumerate(GROUPS):
            project_v_group(wt, wreg, vslot, g, s, n)

    def stats_rstd(src, srcreg, n, scale, bias_col, o0=0):
        sq, sqreg = ring('ba', ba)
        P.op('dve', lambda e: e.tensor_tensor(out=sq[:, o0:n], in0=src[:, o0:n], in1=src[:, o0:n], op=ALU.mult), reads=[srcreg], writes=[sqreg])
        P.op('pe', lambda e: e.matmul(PS[6][:, o0:n], lhsT=ones, rhs=sq[:, o0:n], start=True, stop=True),
             reads=[sqreg, RC], writes=[('ps', 6)])
        sd, sdreg = ring('fb', fb)
        P.op('act', lambda e: e.activation(out=sd[:, o0:n], in_=PS[6][:, o0:n], func=AF.Ln, scale=scale, bias=bias_col),
             reads=[('ps', 6), RC], writes=[sdreg])
        rs, rsreg = ring('fb', fb)
        P.op('act', lambda e: e.activation(out=rs[:, o0:n], in_=sd[:, o0:n], func=AF.Exp, scale=-0.5), reads=[sdreg], writes=[rsreg])
        return rs, rsreg

    def retention_attention(l, h, u, ks, vs, wvg, wvgreg):
        gam = 1.0 - 2.0 ** (-5.0 - h)
        cd = gam ** 128
        dmask = cf[:, CF_DMASK + h * 128:CF_DMASK + (h + 1) * 128]
        qdec = cf[:, CF_QDEC + h * 128:CF_QDEC + (h + 1) * 128]
        kdec = cf[:, CF_KDEC + h:CF_KDEC + h + 1]
        gnw = cf[:, CF_PVEC + PV_GN + l * 4 + h:CF_PVEC + PV_GN + l * 4 + h + 1]
        RBs = 18 - ks
        grp = lambda n: 0 if n == 0 else 1 + (n - 1) // 4
        P.op('dve', lambda e: e.memset(SL[:, RBs, 0:128], 0.0), writes=[('S', RBs, 0)])
        P.op('dve', lambda e: e.memset(Rf[0][:, :], 0.0), writes=[('Rf', 0)])
        def emit_gate(g, bank=6):
            s, n = GROUPS[g]
            gp, greg = (PTF, ('pt',)) if bank == 'ptf' else (PS[bank], ('ps', bank))
            for kc in range(8):
                P.op('pe', lambda e, kc=kc: e.matmul(gp[:, 0:n], lhsT=wvg[:, 1024 + kc * 128:1024 + (kc + 1) * 128],
                                                   rhs=xnT[:, kc, s:s + n], start=(kc == 0), stop=(kc == 7)),
                     reads=[wvgreg, ('xn', kc, g)], writes=[greg])
            sg, sgreg = ring('fb', fb)
            P.op('act', lambda e: e.activation(out=sg[:, 0:n], in_=gp[:, 0:n], func=AF.Exp, scale=-1.0), reads=[greg], writes=[sgreg])
            P.op('act', lambda e: e.activation(out=sg[:, 0:n], in_=sg[:, 0:n], func=AF.Ln, bias=onec), reads=[sgreg, RC], writes=[sgreg])
            P.op('act', lambda e: e.activation(out=sg[:, 0:n], in_=sg[:, 0:n], func=AF.Exp, scale=-1.0), reads=[sgreg], writes=[sgreg])
            gs, gsreg = ring('sq', sqr)
            P.op('dve', lambda e: e.tensor_tensor(out=gs[:, 0:n], in0=gp[:, 0:n], in1=sg[:, 0:n], op=ALU.mult),
                 reads=[greg, sgreg], writes=[gsreg])
            gate[g] = (gs, gsreg)

        TB = [(PT, ('pt',)), (PS[6][:, :].bitcast(BF16), ('ps', 6))]
        for r in range(4):
            tb, tbreg = TB[r % 2]
            for i in range(4):
                n = 4 * r + i
                P.op('pe', lambda e, n=n, i=i, tb=tb: e.transpose(tb[:, i * 128:(i + 1) * 128], SL[:, ks, n * 128:(n + 1) * 128], ident),
                     reads=[('S', ks, grp(n)), RC], writes=[tbreg])
            kd, kdreg = ring('ba', ba)
            P.op('act', lambda e, kd=kd, tb=tb: e.activation(out=kd[:, 0:512], in_=tb[:, 0:512], func=AF.Identity, scale=kdec),
                 reads=[tbreg, RC], writes=[kdreg])
            for i in range(4):
                n = 4 * r + i
                P.op('pe', lambda e, n=n, i=i, r=r, kd=kd: e.matmul(PS[r][:, i * 128:(i + 1) * 128], lhsT=kd[:, i * 128:(i + 1) * 128],
                                                                  rhs=SL[:, vs, n * 128:(n + 1) * 128], start=True, stop=True),
                     reads=[kdreg, ('S', vs, grp(n))], writes=[('ps', r)])
        gate = {}
        for n in range(16):
            if n in (0, 5, 10):
                emit_gate(n // 5, (6, 4, 5)[n // 5])
            r, i = divmod(n, 4)
            P.op('dve', lambda e, n=n, r=r, i=i: e.scalar_tensor_tensor(out=Rf[(n + 1) % 2][:, :], in0=Rf[n % 2][:, :], scalar=cd,
                                                                      in1=PS[r][:, i * 128:(i + 1) * 128], op0=ALU.mult, op1=ALU.add),
                 reads=[('Rf', n % 2), ('ps', r)], writes=[('Rf', (n + 1) % 2)])
            P.op('act', lambda e, n=n: e.activation(out=SL[:, RBs, (n + 1) * 128:(n + 2) * 128], in_=Rf[(n + 1) % 2][:, :], func=AF.Identity),
                 reads=[('Rf', (n + 1) % 2)], writes=[('S', RBs, grp(n + 1))])
        def emit_ST(g):
            s, n = GROUPS[g]
            ncx = n // 128
            bk = g % 2
            for ci in range(ncx):
                cs = s + ci * 128
                P.op('pe', lambda e, ci=ci, cs=cs: e.matmul(PS[bk][:, ci * 128:(ci + 1) * 128], lhsT=SL[:, ks, cs:cs + 128],
                                                          rhs=SL[:, u, cs:cs + 128], start=True, stop=True),
                     reads=[('S', u, g), ('S', ks, g)], writes=[('ps', bk)])
            v3 = lambda ap: ap.rearrange("p (c i) -> p c i", c=ncx)
            bc = lambda ap: ap.unsqueeze(1).broadcast_to([128, ncx, 128])
            stm, stmreg = ring('ba', ba)
            P.op('dve', lambda e: e.tensor_tensor(out=v3(stm[:, 0:n]), in0=v3(PS[bk][:, 0:n]), in1=bc(dmask), op=ALU.mult),
                 reads=[('ps', bk), RC], writes=[stmreg])
            qd, qdreg = ring('ba', ba)
            P.op('dve', lambda e: e.tensor_tensor(out=v3(qd[:, 0:n]), in0=v3(SL[:, u, s:s + n]), in1=bc(qdec), op=ALU.mult),
                 reads=[('S', u, g), RC], writes=[qdreg])
            return stm, stmreg, qd, qdreg

        def emit_O(g, st):
            stm, stmreg, qd, qdreg = st
            s, n = GROUPS[g]
            OBk = 4 + g % 2
            OB = PS[OBk]
            for ci in range(n // 128):
                cs = s + ci * 128
                P.op('pe', lambda e, ci=ci, cs=cs: e.matmul(OB[:, ci * 128:(ci + 1) * 128], lhsT=SL[:, vs, cs:cs + 128],
                                                          rhs=stm[:, ci * 128:(ci + 1) * 128], start=True, stop=False),
                     reads=[('S', vs, g), stmreg], writes=[('ps', OBk)])
                P.op('pe', lambda e, ci=ci, cs=cs: e.matmul(OB[:, ci * 128:(ci + 1) * 128], lhsT=SL[:, RBs, cs:cs + 128],
                                                          rhs=qd[:, ci * 128:(ci + 1) * 128], start=False, stop=True),
                     reads=[('S', RBs, g), qdreg], writes=[('ps', OBk)])
            post.append([1, lambda: fin1(g, OBk)])
            tick()

        post = []
        PTF = PT[:, :].bitcast(F32)

        def tick(force=False):
            for ent in list(post):
                ent[0] -= 1
                if force or ent[0] <= 0:
                    post.remove(ent)
                    ent[1]()

        def fin1(g, OBk):
            s, n = GROUPS[g]
            OB = PS[OBk]
            cent = cb[:, CB_CENT:CB_CENT + 128]
            ck = 2 + g % 2
            ob, obreg = ring('ba', ba)
            P.op('act', lambda e: e.activation(out=ob[:, 0:n], in_=OB[:, 0:n], func=AF.Identity), reads=[('ps', OBk)], writes=[obreg])
            P.op('pe', lambda e: e.matmul(PS[ck][:, 0:n], lhsT=cent, rhs=ob[:, 0:n], start=True, stop=True),
                 reads=[obreg, RC], writes=[('ps', ck)])
            post.append([1, lambda: fin2(g, ck)])

        def fin2(g, ck):
            s, n = GROUPS[g]
            csq, csqreg = ring('ba', ba)
            P.op('act', lambda e: e.activation(out=csq[:, 0:n], in_=PS[ck][:, 0:n], func=AF.Square), reads=[('ps', ck)], writes=[csqreg])
            P.op('pe', lambda e: e.matmul(PS[6][:, 0:n], lhsT=ones, rhs=csq[:, 0:n], start=True, stop=True),
                 reads=[csqreg, RC], writes=[('ps', 6)])
            post.append([1, lambda: fin3(g, ck)])

        def fin3(g, ck):
            s, n = GROUPS[g]
            rs, rsreg = ring('fb', fb)
            P.op('act', lambda e: e.activation(out=rs[:, 0:n], in_=PS[6][:, 0:n], func=AF.Ln, scale=1.0 / 128, bias=epsc),
                 reads=[('ps', 6), RC], writes=[rsreg])
            P.op('act', lambda e: e.activation(out=rs[:, 0:n], in_=rs[:, 0:n], func=AF.Exp, scale=-0.5), reads=[rsreg], writes=[rsreg])
            fin4(g, ck, rs, rsreg)

        def fin4(g, ck, rs, rsreg):
            s, n = GROUPS[g]
            nrm, nrmreg = ring('fa', fa)
            P.op('dve', lambda e: e.scalar_tensor_tensor(out=nrm[:, 0:n], in0=PS[ck][:, 0:n], scalar=gnw, in1=rs[:, 0:n],
                                                          op0=ALU.mult, op1=ALU.mult),
                 reads=[('ps', ck), rsreg, RC], writes=[nrmreg])
            gs, gsreg = gate[g]
            P.op('dve', lambda e: e.tensor_tensor(out=SL[:, u, s:s + n], in0=nrm[:, 0:n], in1=gs[:, 0:n], op=ALU.mult),
                 reads=[nrmreg, gsreg], writes=[('S', u, g)])
            done.add(g)

        done = set()

        pending = None
        need_gate = [3, 4]
        for g in range(5):
            if need_gate and (need_gate[0] - 3) in done:
                emit_gate(need_gate.pop(0), 'ptf')
            st = emit_ST(g)
            if pending is not None:
                emit_O(*pending)
            pending = (g, st)
        emit_O(*pending)
        while post or need_gate:
            if need_gate and (need_gate[0] - 3) in done:
                emit_gate(need_gate.pop(0), 'ptf')
            tick(force=True)

    def retention_unit(l, h):
        u = h
        ks = 8 + 2 * (u % 2)
        vs = 9
        wqk, wqkreg = wload(wmix_d[l, u, 0], 2048)
        wvg, wvgreg = wload(wmix_d[l, u, 1], 2048)
        for g, (s, n) in enumerate(GROUPS):
            tabs_for[g] = tload(0, s, n)
            project_rot_group(0, wqk, wqkreg, u, CB_PERMR, g)
            project_rot_group(1, wqk, wqkreg, ks, CB_PERMR, g)
            norm_pop()
        rot_flush()
        project_v(wvg, wvgreg, vs)
        if DBG_MIX == 'B':
            return
        retention_attention(l, h, u, ks, vs, wvg, wvgreg)

    def diff_attention(l, u, ks, vs, om):
        neglam = lamv[:, 4 + l:5 + l]
        sw = cf[:, CF_PVEC + PV_SW + l:CF_PVEC + PV_SW + l + 1]
        blocks = []
        for g, (s, n) in enumerate(GROUPS):
            last = (s + n) // 128 - 1
            for m in range(2):
                for jb in range(last + 1):
                    blocks.append((g, m, jb, last))
        odset = {}
        tres = {}
        deferred = []
        post = []

        def emit_S(blk):
            g, m, jb, last = blk
            s, n = GROUPS[g]
            off = max(0, jb * 128 - s)
            nq = n - off
            diag = jb * 128 >= s
            sk = rr('dst', 3)
            SP_, sreg = SBANK[sk]
            kg = 0 if jb == 0 else 1 + (jb - 1) // 4
            P.op('pe', lambda e: e.matmul(SP_[:, 0:nq], lhsT=SL[:, ks[m], jb * 128:(jb + 1) * 128],
                                          rhs=SL[:, u, s + off:s + n], start=True, stop=(not diag)),
                 reads=[('S', ks[m], kg), ('S', u, g)], writes=[sreg])
            if diag:
                P.op('pe', lambda e: e.matmul(SP_[:, 0:128], lhsT=ident, rhs=cmask, start=False, stop=True),
                     reads=[RC], writes=[sreg])
            return sk, off, nq, kg

        def emit_rest(blk, sinfo):
            g, m, jb, last = blk
            sk, off, nq, kg = sinfo
            SP_, sreg = SBANK[sk]
            if (g, m) not in odset:
                odset[(g, m)] = 2 + 2 * rr('dod', 2)
                while any(ent[2] == odset[(g, m)] for ent in post):
                    for ent in list(post):
                        if ent[2] == odset[(g, m)]:
                            post.remove(ent)
                            ent[1]()
            OBk = odset[(g, m)]
            DBk = OBk + 1
            OB, DB = PS[OBk], PS[DBk]
            pt, ptreg = ring('ba', ba)
            P.op('act', lambda e: e.activation(out=pt[:, 0:nq], in_=SP_[:, 0:nq], func=AF.Exp, scale=0.125),
                 reads=[sreg], writes=[ptreg])
            P.op('pe', lambda e: e.matmul(OB[:, off:off + nq], lhsT=SL[:, vs, jb * 128:(jb + 1) * 128], rhs=pt[:, 0:nq],
                                          start=(jb == 0), stop=(jb == last)),
                 reads=[('S', vs, kg), ptreg], writes=[('ps', OBk)])
            P.op('pe', lambda e: e.matmul(DB[:, off:off + nq], lhsT=(ones_pad if jb == 0 else ones), rhs=pt[:, 0:nq],
                                          start=(jb == 0), stop=(jb == last)),
                 reads=[RC, ptreg], writes=[('ps', DBk)])
            if jb == last:
                post.append([2, lambda: normalize_act(g, m, OBk), OBk])

        def normalize_act(g, m, OBk):
            DBk = OBk + 1
            DB = PS[DBk]
            s, n = GROUPS[g]
            o0 = PAD if g == 0 else 0
            r, rreg = ring('fb', fb)
            P.op('act', lambda e: e.activation(out=r[:, o0:n], in_=DB[:, o0:n], func=AF.Ln), reads=[('ps', DBk)], writes=[rreg])
            P.op('act', lambda e: e.activation(out=r[:, o0:n], in_=r[:, o0:n], func=AF.Exp, scale=-1.0), reads=[rreg], writes=[rreg])
            post.append([3, lambda: normalize_dve(g, m, OBk, r, rreg), OBk])

        def normalize_dve(g, m, OBk, r, rreg):
            OB = PS[OBk]
            s, n = GROUPS[g]
            o0 = PAD if g == 0 else 0
            t, treg_ = ring('fa', fa)
            P.op('dve', lambda e: e.tensor_tensor(out=t[:, o0:n], in0=OB[:, o0:n], in1=r[:, o0:n], op=ALU.mult),
                 reads=[('ps', OBk), rreg], writes=[treg_])
            tres[(g, m)] = (t, treg_)
            if m == 1:
                post.append([1, lambda: finalize1(g), None])

        def finalize1(g):
            s, n = GROUPS[g]
            o0 = PAD if g == 0 else 0
            (t0, t0reg), (t1, t1reg) = tres[(g, 0)], tres[(g, 1)]
            o, oreg = ring('fa', fa)
            P.op('dve', lambda e: e.scalar_tensor_tensor(out=o[:, o0:n], in0=t1[:, o0:n], scalar=neglam, in1=t0[:, o0:n],
                                                          op0=ALU.mult, op1=ALU.add),
                 reads=[t0reg, t1reg, ('lamv',)], writes=[oreg])
            sq, sqreg = ring('ba', ba)
            P.op('dve', lambda e: e.tensor_tensor(out=sq[:, o0:n], in0=o[:, o0:n], in1=o[:, o0:n], op=ALU.mult), reads=[oreg], writes=[sqreg])
            P.op('pe', lambda e: e.matmul(PS[6][:, o0:n], lhsT=ones, rhs=sq[:, o0:n], start=True, stop=True),
                 reads=[sqreg, RC], writes=[('ps', 6)])
            post.append([4, lambda: finalize2(g, o, oreg), None])

        def finalize2(g, o, oreg):
            s, n = GROUPS[g]
            o0 = PAD if g == 0 else 0
            rs, rsreg = ring('fb', fb)
            P.op('act', lambda e: e.activation(out=rs[:, o0:n], in_=PS[6][:, o0:n], func=AF.Ln, scale=1.0 / (128 * om * om), bias=epsl[l]),
                 reads=[('ps', 6), RC], writes=[rsreg])
            P.op('act', lambda e: e.activation(out=rs[:, o0:n], in_=rs[:, o0:n], func=AF.Exp, scale=-0.5), reads=[rsreg], writes=[rsreg])
            P.op('dve', lambda e: e.scalar_tensor_tensor(out=SL[:, u, s + o0:s + n], in0=o[:, o0:n], scalar=sw, in1=rs[:, o0:n],
                                                          op0=ALU.mult, op1=ALU.mult),
                 reads=[oreg, rsreg, RC], writes=[('S', u, g)])

        def tick(force=False):
            for ent in list(post):
                ent[0] -= 1
                if force or ent[0] <= 0:
                    post.remove(ent)
                    ent[1]()

        pend = []
        for blk in blocks:
            sinfo = emit_S(blk)
            pend.append((blk, sinfo))
            if len(pend) > 2:
                emit_rest(*pend.pop(0))
                tick()
        while pend:
            emit_rest(*pend.pop(0))
            tick()
        while post:
            tick(force=True)

    def diff_unit(l, h, lam_init):
        u = 4 + h
        ks = (8, 10)
        vs = 9
        wqk, wqkreg = wload(wmix_d[l, u, 0], 2048)
        wv, wvreg = wload(wmix_d[l, u, 1, :, 0:1024], 1024)
        if h == 0:
            P.op('dve', lambda e: e.memset(SL[64:128, ks[0], :], 0.0), writes=[('S', ks[0], g) for g in range(5)])
            P.op('dve', lambda e: e.memset(SL[0:64, ks[1], :], 0.0), writes=[('S', ks[1], g) for g in range(5)])
        for g, (s, n) in enumerate(GROUPS):
            tabs_for[g] = tload(1, s, n)
            project_rot_group(0, wqk, wqkreg, u, CB_PERMD, g)
            project_rot_group(1, wqk, wqkreg, None, CB_PERMD, g, split=ks)
        rot_flush()
        project_v(wv, wvreg, vs)
        diff_attention(l, u, ks, vs, 1.0 - lam_init)

    def lam_compute(l):
        lam_init = 0.8 - 0.6 * math.exp(-0.3 * l)
        for t in range(2):
            a = lamin[:, (l * 4 + 2 * t) * 64:(l * 4 + 2 * t + 1) * 64]
            b = lamin[:, (l * 4 + 2 * t + 1) * 64:(l * 4 + 2 * t + 2) * 64]
            P.op('dve', lambda e, a=a, b=b: e.tensor_tensor(out=lamw[:, :], in0=a, in1=b, op=ALU.mult),
                 reads=[RC, ('lamw',)], writes=[('lamw',)])
            P.op('dve', lambda e, t=t: e.tensor_reduce(out=lamv[:, t:t + 1], in_=lamw[:, :], axis=mybir.AxisListType.X, op=ALU.add),
                 reads=[('lamw',), ('lamv',)], writes=[('lamv',)])
            P.op('act', lambda e, t=t: e.activation(out=lamv[:, 2 + t:3 + t], in_=lamv[:, t:t + 1], func=AF.Exp),
                 reads=[('lamv',)], writes=[('lamv',)])
        P.op('dve', lambda e: e.scalar_tensor_tensor(out=lamv[:, 4 + l:5 + l], in0=lamv[:, 3:4], scalar=-lam_init, in1=lamv[:, 2:3],
                                                      op0=ALU.add, op1=ALU.subtract),
             reads=[('lamv',)], writes=[('lamv',)])
        return lam_init

    def mixer(l):
        rmsnorm(PV_NORM + (l * 3 + 1) * 8)
        lam_init = lam_compute(l)
        if DBG_MIX == 'A':
            return
        for h in range(4):
            retention_unit(l, h)
            if DBG_MIX in ('B', 'B1', 'C', 'R1', 'R2', 'R3'):
                return
        if DBG_MIX == 'D':
            return
        for h in range(4):
            diff_unit(l, h, lam_init)
            if DBG_MIX == 'F':
                return
        for c in range(8):
            wt, wreg = wload(wout_d[l, c], 1024)
            for (a, n, gs) in FFG:
                mm_out_group(wt, wreg, 8, c, a, n, gs, 1.0)

    stages = []
    for l in range(2):
        stages += [('ffn', l, 0), ('mix', l), ('ffn', l, 1)]
    nst = len(stages) if stop is None else stop
    for st in stages[:nst]:
        if st[0] == 'ffn':
            ffn(st[1], st[2])
        else:
            mixer(st[1])
    outregs = []
    if stop is None:
        for j, (a, n, gs) in enumerate(FFG):
            rmsnorm_group(PV_NORM + 48, True, a, n, gs)
            lo = max(a, 128)
            P.dma('sp', lambda e, lo=lo, hi=a + n: e.dma_start(out=out_d[:, :, lo - 128:hi - 128], in_=hT[:, :, lo:hi]), 'out',
                  reads=[('h', c, g) for c in range(8) for g in gs], writes=[('out', j)])
            outregs.append(('out', j))
    else:
        for c in range(8):
            P.dma('sp', lambda e, c=c: e.dma_start(out=out_d[:, c, :], in_=hT[:, c, 128:TP]), 'out',
                  reads=[('h', c, g) for g in range(1, 5)], writes=[('out', c)])
            outregs.append(('out', c))
    if dump == 'h':
        for c in range(8):
            P.dma('sp', lambda e, c=c: e.dma_start(out=dbg_d[:, c, :], in_=hT[:, c, :]), 'out',
                  reads=[('h', c, g) for g in range(5)], writes=[('dbg', c)])
            outregs.append(('dbg', c))
    P.op('sp', None, reads=outregs)
    P.build()
    return nc, P


def _bf(a):
    return np.asarray(a, dtype=np.float32).astype(ml_dtypes.bfloat16)


def _const_tables():
    pos = (np.arange(TP, dtype=np.float32) - np.float32(PAD))
    angle = (np.float32(10000.0) ** (-np.linspace(0.0, 1.0, 64, dtype=np.float32))).astype(np.float32)
    fr = (pos[None, :] * np.repeat(angle, 2)[:, None]).astype(np.float32)
    cosr = np.cos(fr).astype(np.float32)
    sgn = np.where(np.arange(128) % 2 == 0, -1.0, 1.0).astype(np.float32)[:, None]
    sinr = (np.sin(fr) * sgn).astype(np.float32)
    r = 16
    inv = (np.float32(500000.0) ** (-np.arange(0, r, 2, dtype=np.float32) / r)).astype(np.float32)
    cosd = np.ones((128, TP), np.float32)
    sind = np.zeros((128, TP), np.float32)
    for p in range(128):
        dd = p % 64
        if dd < r:
            f = (pos * inv[dd % 8]).astype(np.float32)
            cosd[p] = np.cos(f)
            sind[p] = np.sin(f) * (-1.0 if dd < 8 else 1.0)
    tabs = np.stack([cosr, sinr, cosd, sind]).astype(np.float32)
    cbm = np.zeros((128, CB_N), np.float32)
    cbm[:, CB_ONES:CB_ONES + 128] = 1.0
    cbm[PAD:, CB_ONESPAD:CB_ONESPAD + 128] = 1.0
    cbm[:, CB_IDENT:CB_IDENT + 128] = np.eye(128)
    for m in range(128):
        src = m + 1 if m % 2 == 0 else m - 1
        cbm[src, CB_PERMR + m] = 1.0
        dd = m % 64
        if dd < 8:
            cbm[m + 8, CB_PERMD + m] = 1.0
        elif dd < 16:
            cbm[m - 8, CB_PERMD + m] = 1.0
    jj = np.arange(128)[:, None]
    ii = np.arange(128)[None, :]
    cbm[:, CB_CMASK:CB_CMASK + 128] = np.where(jj <= ii, 0.0, -30000.0)
    cbm[:, CB_CENT:CB_CENT + 128] = np.eye(128) - 1.0 / 128.0
    cfm = np.zeros((128, CF_N), np.float32)
    cfm[:, CF_EPS] = EPS
    cfm[:, CF_ONE] = 1.0
    for l_ in range(2):
        om_ = 1.0 - (0.8 - 0.6 * math.exp(-0.3 * l_))
        cfm[:, CF_EPS + 1 + l_] = EPS / (om_ * om_)
    for h in range(4):
        lg = math.log(1.0 - 2.0 ** (-5.0 - h))
        rel = (ii - jj).astype(np.float64)
        dm = np.where(rel >= 0, np.exp(lg * np.maximum(rel, 0.0)), 0.0) * (128.0 ** -0.5)
        cfm[:, CF_DMASK + h * 128:CF_DMASK + (h + 1) * 128] = dm
        cfm[:, CF_QDEC + h * 128:CF_QDEC + (h + 1) * 128] = np.exp(lg * (np.arange(128) + 1.0))[None, :]
        cfm[:, CF_KDEC + h] = np.exp(lg * (127.0 - np.arange(128))) * (128.0 ** -0.5)
    return tabs, cbm, cfm


def _prep(inputs):
    f = lambda k: np.asarray(inputs[k], dtype=np.float32)
    tabs, cbm, cfm = _const_tables()
    cvec = lambda v: np.ascontiguousarray(v.reshape(-1, 128).T)
    norms = [f("ffn1_norm"), f("mix_norm"), f("ffn2_norm")]
    for l in range(2):
        for w in range(3):
            c0 = CF_PVEC + PV_NORM + (l * 3 + w) * 8
            cfm[:, c0:c0 + 8] = cvec(norms[w][l])
        cfm[:, CF_PVEC + PV_GN + l * 4:CF_PVEC + PV_GN + l * 4 + 4] = cvec(f("ret_gn_w")[l])
        cfm[:, CF_PVEC + PV_SW + l] = f("diff_subln_w")[l]
    cfm[:, CF_PVEC + PV_NORM + 48:CF_PVEC + PV_NORM + 56] = cvec(f("final_norm"))
    lam = np.stack([f("diff_lambda_q1"), f("diff_lambda_k1"), f("diff_lambda_q2"), f("diff_lambda_k2")], axis=1)
    lam = np.ascontiguousarray(np.broadcast_to(lam.reshape(1, 512), (128, 512)))

    def fm_tile(w):
        nc_ = w.shape[1] // 128
        return w.reshape(8, 128, nc_, 128).transpose(2, 1, 0, 3)

    wgu = np.empty((2, 2, NF, 128, 2, 8, 128), np.float32)
    wd = np.empty((2, 2, 2, 8, 128, NFH, 128), np.float32)
    for l in range(2):
        for j, (kg, ku, kd_) in enumerate([("ffn1_w_gate", "ffn1_w_up", "ffn1_w_down"), ("ffn2_w_gate", "ffn2_w_up", "ffn2_w_down")]):
            wgu[l, j, :, :, 0] = fm_tile(f(kg)[l])
            wgu[l, j, :, :, 1] = fm_tile(f(ku)[l])
            wd[l, j] = f(kd_)[l].reshape(2, NFH, 128, 8, 128).transpose(0, 3, 2, 1, 4)
    win = f("w_in")
    wmix = np.zeros((2, 8, 2, 128, 2, 8, 128), np.float32)
    for l in range(2):
        t = fm_tile(win[l])
        for h in range(4):
            wmix[l, h, 0, :, 0] = t[h]
            wmix[l, h, 0, :, 1] = t[4 + h]
            wmix[l, h, 1, :, 0] = t[8 + h]
            wmix[l, h, 1, :, 1] = t[12 + h]
            wmix[l, 4 + h, 0, :, 0] = t[16 + h]
            wmix[l, 4 + h, 0, :, 1] = t[20 + h]
            wmix[l, 4 + h, 1, :, 0] = t[24 + h]
    wout = np.stack([fm_tile(f("w_out")[l]) for l in range(2)])
    shared = {
        "metaT": np.ascontiguousarray(f("meta_tokens").reshape(NMETA, 8, 128).transpose(2, 1, 0)),
        "cb": _bf(cbm), "cf": cfm, "lam": lam, "tabs": tabs,
        "wgu": wgu.reshape(2, 2, NF, 128, 2048), "wd": wd.reshape(2, 2, 2, 8, 128, NFH * 128),
        "wmix": wmix.reshape(2, 8, 2, 128, 2048), "wout": np.ascontiguousarray(wout.reshape(2, 8, 128, 1024)),
    }
    x = f("x")
    xTs = [np.ascontiguousarray(x[b].reshape(SEQ, 8, 128).transpose(2, 1, 0)) for b in range(x.shape[0])]
    return shared, xTs


def run(inputs, cores=None, stop=None, dump=None, trace=False):
    shared, xTs = _prep(inputs)
    cores = list(range(8)) if cores is None else cores
    nc, P = build_program(stop=stop, dump=dump)
    in_maps = [dict(shared, xT=xTs[b]) for b in cores]
    res = run_bass_kernel_spmd(nc, in_maps, core_ids=list(range(len(cores))), trace=trace)
    return res, P


def kernel(**inputs):
    res, _ = run(inputs)
    outs = [np.asarray(r["outT"], dtype=np.float32) for r in res.results]
    out = np.stack([o.transpose(2, 1, 0).reshape(SEQ, D) for o in outs])
    return np.ascontiguousarray(out.astype(np.float32))
```

```python
import math
import numpy as np
import ml_dtypes
import concourse.bass as bass
import concourse.mybir as mybir
from concourse.bass_utils import run_bass_kernel_spmd

F32 = mybir.dt.float32
BF16 = mybir.dt.bfloat16
AF = mybir.ActivationFunctionType
ALU = mybir.AluOpType

D = 1024
SEQ = 2048
NMETA = 16
PAD = 112
TP = 2176
NCH = 17
DFF = 2816
NF = 22
NFH = 11
EPS = 1e-6
GROUPS = [(0, 128)] + [(128 + 512 * g, 512) for g in range(4)]
NSLOT = 11
NW = 4
WCOLS = 2048
SAME_ENGINE_SYNC = True
DBG_HACK = 0
WARM = 0
ROT_ADD_ENG = 'dve'
DBG_MIX = None

CB_ONES, CB_ONESPAD, CB_IDENT, CB_PERMR, CB_PERMD, CB_CMASK, CB_CENT = [i * 128 for i in range(7)]
CB_N = 7 * 128
CF_DMASK = 0
CF_QDEC = 512
CF_KDEC = 1024
CF_PVEC = 1028
PV_NORM = 0
PV_GN = 56
PV_SW = 64
CF_EPS = CF_PVEC + 66
CF_ONE = CF_EPS + 3
CF_N = CF_ONE + 1


class Prog:
    def __init__(self, nc):
        self.nc = nc
        self.ops = []
        self.dma_sems = {}

    def op(self, eng, fn, reads=(), writes=()):
        self.ops.append(dict(eng=eng, fn=fn, reads=list(reads), writes=list(writes), dma=None))

    def dma(self, eng, fn, slot, reads=(), writes=()):
        self.ops.append(dict(eng=eng, fn=fn, reads=list(reads), writes=list(writes), dma=slot))

    def build(self):
        nc = self.nc
        ops = self.ops
        engs = ['pe', 'act', 'dve', 'pool', 'sp']
        last_w = {}
        readers = {}
        for i, o in enumerate(ops):
            deps = set()
            key = ('dma', i) if o['dma'] is not None else o['eng']
            for r in o['reads']:
                if r in last_w:
                    deps.add(last_w[r])
                if r[0] in ('ps', 'pt'):
                    for k2, i2 in readers.get(r, {}).items():
                        if k2 != key:
                            deps.add(i2)
            for w in o['writes']:
                if w in last_w:
                    deps.add(last_w[w])
                rd = readers.get(w)
                if rd:
                    deps.update(rd.values())
            deps.discard(i)
            o['deps'] = deps
            for r in o['reads']:
                readers.setdefault(r, {})[key] = i
            for w in o['writes']:
                last_w[w] = i
                readers[w] = {}
        signal = set()
        for i, o in enumerate(ops):
            for d in o['deps']:
                p = ops[d]
                if p['dma'] is not None:
                    continue
                if p['eng'] != o['eng'] or (SAME_ENGINE_SYNC and o['eng'] != 'pe'):
                    signal.add(d)
        sem = {e: nc.alloc_semaphore('s_' + e) for e in engs}
        cnt = {e: 0 for e in engs}
        for i, o in enumerate(ops):
            if o['dma'] is not None:
                slot = o['dma']
                if slot not in self.dma_sems:
                    self.dma_sems[slot] = [nc.alloc_semaphore('d_' + str(slot)), 0]
                ent = self.dma_sems[slot]
                ent[1] += 16
                o['tok'] = (ent[0], ent[1])
            elif i in signal:
                cnt[o['eng']] += 1
                o['tok'] = (sem[o['eng']], cnt[o['eng']])
            else:
                o['tok'] = None
        waited = {e: {} for e in engs}
        for i, o in enumerate(ops):
            need = {}
            for d in o['deps']:
                p = ops[d]
                if p['dma'] is None and p['eng'] == o['eng'] and not (SAME_ENGINE_SYNC and o['eng'] != 'pe'):
                    continue
                tok = p['tok']
                assert tok is not None
                k = tok[0]
                if need.get(k, (None, 0))[1] < tok[1]:
                    need[k] = tok
            ws = []
            for k, tok in need.items():
                if waited[o['eng']].get(k, 0) < tok[1]:
                    waited[o['eng']][k] = tok[1]
                    ws.append(tok)
            o['waits'] = ws
        per = {e: [o for o in ops if o['eng'] == e] for e in engs}
        self.stats = {e: len(per[e]) for e in engs}
        self.stats['signals'] = dict(cnt)

        def run(e, lst):
            for o in lst:
                for (s, v) in o['waits']:
                    e.wait_ge(s, v)
                if o['fn'] is None:
                    continue
                ins = o['fn'](e)
                if o['tok'] is not None:
                    ins.then_inc(o['tok'][0], 16 if o['dma'] is not None else 1)

        with nc.Block() as block:
            @block.tensor
            def _(e):
                run(e, per['pe'])

            @block.scalar
            def _(e):
                run(e, per['act'])

            @block.vector
            def _(e):
                run(e, per['dve'])

            @block.gpsimd
            def _(e):
                run(e, per['pool'])

            @block.sync
            def _(e):
                run(e, per['sp'])


def build_program(stop=None, dump=None):
    nc = bass.Bass("TRN2", target_bir_lowering=False)
    P = Prog(nc)

    def din(name, shape, dt=F32):
        return nc.dram_tensor(name, list(shape), dt, kind="ExternalInput").ap()

    xT_d = din("xT", [128, 8, SEQ])
    meta_d = din("metaT", [128, 8, NMETA])
    cb_d = din("cb", [128, CB_N], BF16)
    cf_d = din("cf", [128, CF_N])
    lam_d = din("lam", [128, 2 * 4 * 64])
    tabs_d = din("tabs", [4, 128, TP])
    wgu_d = din("wgu", [2, 2, NF, 128, 2048])
    wd_d = din("wd", [2, 2, 2, 8, 128, NFH * 128])
    wmix_d = din("wmix", [2, 8, 2, 128, 2048])
    wout_d = din("wout", [2, 8, 128, 1024])
    out_d = nc.dram_tensor("outT", [128, 8, SEQ], F32, kind="ExternalOutput").ap()
    dbg_d = None
    if dump is not None:
        dbg_d = nc.dram_tensor("dbg", [128, 8, TP], F32, kind="ExternalOutput").ap()

    sb = nc.alloc_sbuf_tensor
    hT = sb("hT", [128, 8, TP], F32)
    xnT = sb("xnT", [128, 8, TP], BF16)
    SL = sb("slots", [128, NSLOT, TP], BF16)
    wsl = [sb(f"wsl{i}", [128, WCOLS], BF16) for i in range(NW)]
    tabt = [sb(f"tab{i}", [128, 2, 512], F32) for i in range(2)]
    cb = sb("cbs", [128, CB_N], BF16)
    cf = sb("cfs", [128, CF_N], F32)
    lamin = sb("lamin", [128, 512], F32)
    lamv = sb("lamv", [128, 8], F32)
    NR = 4
    sqr = [sb(f"sq{i}", [128, 512], BF16) for i in range(NR)]
    fa = [sb(f"fa{i}", [128, 512], F32) for i in range(4)]
    fb = [sb(f"fb{i}", [128, 512], F32) for i in range(4)]
    ba = [sb(f"ba{i}", [128, 512], BF16) for i in range(6)]
    Rf = [sb(f"Rf{i}", [128, 128], F32) for i in range(2)]

    PS = [nc.alloc_psum_tensor(f"ps{i}", [128, 512], F32) for i in range(7)]
    PT = nc.alloc_psum_tensor("pst", [128, 1024], BF16)

    SBANK = [(PS[0], ('ps', 0)), (PS[1], ('ps', 1)), (PT[:, :].bitcast(F32), ('pt',))]
    ctr = {}

    def rr(name, n):
        v = ctr.get(name, 0)
        ctr[name] = v + 1
        return v % n

    def ring(name, lst):
        i = rr(name, len(lst))
        return lst[i], (name, i)

    ones = cb[:, CB_ONES:CB_ONES + 128]
    ones_pad = cb[:, CB_ONESPAD:CB_ONESPAD + 128]
    ident = cb[:, CB_IDENT:CB_IDENT + 128]
    cmask = cb[:, CB_CMASK:CB_CMASK + 128]
    RC = ('const',)
    epsc = cf[:, CF_EPS:CF_EPS + 1]
    onec = cf[:, CF_ONE:CF_ONE + 1]
    epsl = [cf[:, CF_EPS + 1 + l_:CF_EPS + 2 + l_] for l_ in range(2)]

    P.dma('sp', lambda e: e.dma_start(out=cb[:, :], in_=cb_d), 'const', writes=[RC])
    P.dma('sp', lambda e: e.dma_start(out=cf[:, :], in_=cf_d), 'const', writes=[RC])
    P.dma('sp', lambda e: e.dma_start(out=lamin[:, :], in_=lam_d), 'const', writes=[RC])
    hregs = lambda c: [('h', c, g) for g in range(5)]
    P.op('dve', lambda e: e.memset(hT[:, :, 0:PAD], 0.0), writes=[('h', c, 0) for c in range(8)])
    P.dma('sp', lambda e: e.dma_start(out=hT[:, :, PAD:128], in_=meta_d), ('xin', 0),
          reads=[('h', c, 0) for c in range(8)], writes=[('h', c, 0) for c in range(8)])
    for g in range(1, 5):
        s_, n_ = GROUPS[g]
        P.dma('sp', lambda e, s_=s_, n_=n_: e.dma_start(out=hT[:, :, s_:s_ + n_], in_=xT_d[:, :, s_ - 128:s_ - 128 + n_]), ('xin', g),
              writes=[('h', c, g) for c in range(8)])

    def wload(src_ap, ncols):
        i = rr('w', NW)
        t = wsl[i]
        reg = ('w', i)
        P.dma('pool', lambda e: e.dma_start(out=t[:, 0:ncols], in_=src_ap, max_dma_last_dim=8192),
              ('w', i), writes=[reg])
        return t, reg

    def tload(k, s, n):
        i = rr('tab', 2)
        t = tabt[i]
        regs = [('tab', i, 0), ('tab', i, 1)]
        P.dma('sp', lambda e: e.dma_start(out=t[:, 0, 0:n], in_=tabs_d[2 * k, :, s:s + n]), ('tab', i, 0), writes=[regs[0]])
        P.dma('sp', lambda e: e.dma_start(out=t[:, 1, 0:n], in_=tabs_d[2 * k + 1, :, s:s + n]), ('tab', i, 1), writes=[regs[1]])
        return t, regs

    def rmsnorm_group(pvcol, final, g, s, n):
        ss = PS[6]
        for c in range(8):
            sq, sqreg = ring('sq', sqr)
            P.op('act', lambda e, sq=sq, c=c: e.activation(out=sq[:, 0:n], in_=hT[:, c, s:s + n], func=AF.Square),
                 reads=[('h', c, g)], writes=[sqreg])
            P.op('pe', lambda e, sq=sq, c=c: e.matmul(ss[:, 0:n], lhsT=ones, rhs=sq[:, 0:n], start=(c == 0), stop=(c == 7)),
                 reads=[sqreg, RC], writes=[('ps', 6)])
        rt, rtreg = ring('fa', fa)
        P.op('act', lambda e: e.activation(out=rt[:, 0:n], in_=ss[:, 0:n], func=AF.Ln, scale=1.0 / D, bias=epsc),
             reads=[('ps', 6), RC], writes=[rtreg])
        rs, rsreg = ring('fb', fb)
        P.op('act', lambda e: e.activation(out=rs[:, 0:n], in_=rt[:, 0:n], func=AF.Exp, scale=-0.5), reads=[rtreg], writes=[rsreg])
        for c in range(8):
            gcol = cf[:, CF_PVEC + pvcol + c:CF_PVEC + pvcol + c + 1]
            dst = hT if final else xnT
            wreg_ = ('h', c, g) if final else ('xn', c, g)
            P.op('dve', lambda e, c=c, gcol=gcol, dst=dst: e.scalar_tensor_tensor(
                out=dst[:, c, s:s + n], in0=hT[:, c, s:s + n], scalar=gcol, in1=rs[:, 0:n], op0=ALU.mult, op1=ALU.mult),
                reads=[('h', c, g), rsreg, RC], writes=[wreg_])

    norm_queue = []

    def rmsnorm(pvcol, final=False, ahead=None):
        for g, (s, n) in enumerate(GROUPS):
            norm_queue.append(lambda g=g, s=s, n=n: rmsnorm_group(pvcol, final, g, s, n))
        for _ in range(len(GROUPS) if ahead is None else ahead):
            norm_pop()

    def norm_pop():
        if norm_queue:
            norm_queue.pop(0)()

    FFG = []
    _a = PAD
    for _j in range(5):
        _n = 413 if _j < 4 else 412
        _gs = [g for g, (s_, n_) in enumerate(GROUPS) if s_ < _a + _n and _a < s_ + n_]
        FFG.append((_a, _n, _gs))
        _a += _n
    assert _a == TP

    def resid_add(ps, psreg, c, a, n, gs, scale):
        regs = [('h', c, g) for g in gs]
        P.op('dve', lambda e: e.scalar_tensor_tensor(out=hT[:, c, a:a + n], in0=ps[:, 0:n], scalar=scale,
                                                      in1=hT[:, c, a:a + n], op0=ALU.mult, op1=ALU.add),
             reads=[psreg] + regs, writes=regs)

    def ffn_gu_group(wt, wreg, fi, a, n, gs):
        k = rr('ffn_gu', 2)
        gp, up = PS[k], PS[2 + k]
        for kc in range(8):
            P.op('pe', lambda e, kc=kc: e.matmul(gp[:, 0:n], lhsT=wt[:, kc * 128:(kc + 1) * 128],
                                               rhs=xnT[:, kc, a:a + n], start=(kc == 0), stop=(kc == 7)),
                 reads=[wreg] + [('xn', kc, g) for g in gs], writes=[('ps', k)])
        for kc in range(8):
            P.op('pe', lambda e, kc=kc: e.matmul(up[:, 0:n], lhsT=wt[:, 1024 + kc * 128:1024 + (kc + 1) * 128],
                                               rhs=xnT[:, kc, a:a + n], start=(kc == 0), stop=(kc == 7)),
                 reads=[wreg] + [('xn', kc, g) for g in gs], writes=[('ps', 2 + k)])
        sg, sgreg = ring('fa', fa)
        P.op('act', lambda e: e.activation(out=sg[:, 0:n], in_=gp[:, 0:n], func=AF.Silu),
             reads=[('ps', k)], writes=[sgreg])
        hregs = [('S', fi, g) for g in gs]
        P.op('dve', lambda e: e.tensor_tensor(out=SL[:, fi, a:a + n], in0=up[:, 0:n], in1=sg[:, 0:n], op=ALU.mult),
             reads=[('ps', 2 + k), sgreg] + hregs, writes=hregs)

    def mm_out_group(wt, wreg, nk, c, a, n, gs, scale):
        k = 4 + rr('ffn_o', 2)
        op_ = PS[k]
        for fi in range(nk):
            P.op('pe', lambda e, fi=fi: e.matmul(op_[:, 0:n], lhsT=wt[:, fi * 128:(fi + 1) * 128],
                                               rhs=SL[:, fi, a:a + n], start=(fi == 0), stop=(fi == nk - 1)),
                 reads=[wreg] + [('S', fi, g) for g in gs], writes=[('ps', k)])
        resid_add(op_, ('ps', k), c, a, n, gs, scale)

    def ffn(l, j):
        rmsnorm(PV_NORM + (l * 3 + (0 if j == 0 else 2)) * 8)
        for half in range(2):
            for fi in range(NFH):
                wt, wreg = wload(wgu_d[l, j, half * NFH + fi], 2048)
                for (a, n, gs) in FFG:
                    ffn_gu_group(wt, wreg, fi, a, n, gs)
                    norm_pop()
            for c in range(8):
                wt, wreg = wload(wd_d[l, j, half, c], NFH * 128)
                for (a, n, gs) in FFG:
                    mm_out_group(wt, wreg, NFH, c, a, n, gs, 0.5)

    tabs_for = {}

    rot_pending = []

    def project_rot_group(which, wt, wreg, dst_slot, permcol, g, split=None):
        perm = cb[:, permcol:permcol + 128]
        s, n = GROUPS[g]
        tt, tregs = tabs_for[g]
        pk = rr('proj', 2)
        pp, sp_ = PS[pk], PS[2 + pk]
        for kc in range(8):
            P.op('pe', lambda e, kc=kc: e.matmul(pp[:, 0:n], lhsT=wt[:, which * 1024 + kc * 128:which * 1024 + (kc + 1) * 128],
                                               rhs=xnT[:, kc, s:s + n], start=(kc == 0), stop=(kc == 7)),
                 reads=[wreg, ('xn', kc, g)], writes=[('ps', pk)])
        qb, qbreg = ring('ba', ba)
        P.op('act', lambda e: e.activation(out=qb[:, 0:n], in_=pp[:, 0:n], func=AF.Identity), reads=[('ps', pk)], writes=[qbreg])
        rot_flush()

        def second():
            P.op('pe', lambda e: e.matmul(sp_[:, 0:n], lhsT=perm, rhs=qb[:, 0:n], start=True, stop=True),
                 reads=[qbreg, RC], writes=[('ps', 2 + pk)])
            t1, t1reg = ring('fa', fa)
            t2, t2reg = ring('fb', fb)
            P.op('dve', lambda e: e.tensor_tensor(out=t1[:, 0:n], in0=pp[:, 0:n], in1=tt[:, 0, 0:n], op=ALU.mult),
                 reads=[('ps', pk)] + tregs, writes=[t1reg])
            P.op('dve', lambda e: e.tensor_tensor(out=t2[:, 0:n], in0=sp_[:, 0:n], in1=tt[:, 1, 0:n], op=ALU.mult),
                 reads=[('ps', 2 + pk)] + tregs, writes=[t2reg])
            if split is None:
                P.op(ROT_ADD_ENG, lambda e: e.tensor_tensor(out=SL[:, dst_slot, s:s + n], in0=t1[:, 0:n], in1=t2[:, 0:n], op=ALU.add),
                     reads=[t1reg, t2reg], writes=[('S', dst_slot, g)])
            else:
                kr, krreg = ring('ba', ba)
                P.op('dve', lambda e: e.tensor_tensor(out=kr[:, 0:n], in0=t1[:, 0:n], in1=t2[:, 0:n], op=ALU.add),
                     reads=[t1reg, t2reg], writes=[krreg])
                for m in range(2):
                    P.op('act', lambda e, m=m: e.activation(out=SL[64 * m:64 * m + 64, split[m], s:s + n], in_=kr[64 * m:64 * m + 64, 0:n],
                                                          func=AF.Identity),
                         reads=[krreg], writes=[('S', split[m], g)])
        rot_pending.append(second)

    def rot_flush():
        while rot_pending:
            rot_pending.pop(0)()

    def project_v_group(wt, wreg, vslot, g, s, n):
        pk = rr('proj', 2)
        pp = PS[pk]
        for ci in range(n // 128):
            cs = s + ci * 128
            for kc in range(8):
                P.op('pe', lambda e, kc=kc, ci=ci, cs=cs: e.matmul(pp[:, ci * 128:(ci + 1) * 128], lhsT=xnT[:, kc, cs:cs + 128],
                                                                 rhs=wt[:, kc * 128:(kc + 1) * 128], start=(kc == 0), stop=(kc == 7)),
                     reads=[wreg, ('xn', kc, g)], writes=[('ps', pk)])
        P.op('act', lambda e: e.activation(out=SL[:, vslot, s:s + n], in_=pp[:, 0:n], func=AF.Identity),
             reads=[('ps', pk)], writes=[('S', vslot, g)])

    def project_v(wt, wreg, vslot):
        for g, (s, n) in enumerate(GROUPS):
            project_v_group(wt, wreg, vslot, g, s, n)

    def stats_rstd(src, srcreg, n, scale, bias_col, o0=0):
        sq, sqreg = ring('ba', ba)
        P.op('dve', lambda e: e.tensor_tensor(out=sq[:, o0:n], in0=src[:, o0:n], in1=src[:, o0:n], op=ALU.mult), reads=[srcreg], writes=[sqreg])
        P.op('pe', lambda e: e.matmul(PS[6][:, o0:n], lhsT=ones, rhs=sq[:, o0:n], start=True, stop=True),
             reads=[sqreg, RC], writes=[('ps', 6)])
        sd, sdreg = ring('fb', fb)
        P.op('act', lambda e: e.activation(out=sd[:, o0:n], in_=PS[6][:, o0:n], func=AF.Ln, scale=scale, bias=bias_col),
             reads=[('ps', 6), RC], writes=[sdreg])
        rs, rsreg = ring('fb', fb)
        P.op('act', lambda e: e.activation(out=rs[:, o0:n], in_=sd[:, o0:n], func=AF.Exp, scale=-0.5), reads=[sdreg], writes=[rsreg])
        return rs, rsreg

    def retention_attention(l, h, u, ks, vs, wvg, wvgreg):
        gam = 1.0 - 2.0 ** (-5.0 - h)
        cd = gam ** 128
        dmask = cf[:, CF_DMASK + h * 128:CF_DMASK + (h + 1) * 128]
        qdec = cf[:, CF_QDEC + h * 128:CF_QDEC + (h + 1) * 128]
        kdec = cf[:, CF_KDEC + h:CF_KDEC + h + 1]
        gnw = cf[:, CF_PVEC + PV_GN + l * 4 + h:CF_PVEC + PV_GN + l * 4 + h + 1]
        RBs = 18 - ks
        grp = lambda n: 0 if n == 0 else 1 + (n - 1) // 4
        P.op('dve', lambda e: e.memset(SL[:, RBs, 0:128], 0.0), writes=[('S', RBs, 0)])
        P.op('dve', lambda e: e.memset(Rf[0][:, :], 0.0), writes=[('Rf', 0)])
        def emit_gate(g, bank=6):
            s, n = GROUPS[g]
            gp, greg = (PTF, ('pt',)) if bank == 'ptf' else (PS[bank], ('ps', bank))
            for kc in range(8):
                P.op('pe', lambda e, kc=kc: e.matmul(gp[:, 0:n], lhsT=wvg[:, 1024 + kc * 128:1024 + (kc + 1) * 128],
                                                   rhs=xnT[:, kc, s:s + n], start=(kc == 0), stop=(kc == 7)),
                     reads=[wvgreg, ('xn', kc, g)], writes=[greg])
            sg, sgreg = ring('fb', fb)
            P.op('act', lambda e: e.activation(out=sg[:, 0:n], in_=gp[:, 0:n], func=AF.Exp, scale=-1.0), reads=[greg], writes=[sgreg])
            P.op('act', lambda e: e.activation(out=sg[:, 0:n], in_=sg[:, 0:n], func=AF.Ln, bias=onec), reads=[sgreg, RC], writes=[sgreg])
            P.op('act', lambda e: e.activation(out=sg[:, 0:n], in_=sg[:, 0:n], func=AF.Exp, scale=-1.0), reads=[sgreg], writes=[sgreg])
            gs, gsreg = ring('sq', sqr)
            P.op('dve', lambda e: e.tensor_tensor(out=gs[:, 0:n], in0=gp[:, 0:n], in1=sg[:, 0:n], op=ALU.mult),
                 reads=[greg, sgreg], writes=[gsreg])
            gate[g] = (gs, gsreg)

        gate = {}
        for g_ in range(4):
            emit_gate(g_, (2, 3, 4, 5)[g_])
        TB = [(PT, ('pt',)), (PS[6][:, :].bitcast(BF16), ('ps', 6))]
        for r in range(4):
            tb, tbreg = TB[r % 2]
            for i in range(4):
                n = 4 * r + i
                P.op('pe', lambda e, n=n, i=i, tb=tb: e.transpose(tb[:, i * 128:(i + 1) * 128], SL[:, ks, n * 128:(n + 1) * 128], ident),
                     reads=[('S', ks, grp(n)), RC], writes=[tbreg])
            kd, kdreg = ring('ba', ba)
            P.op('act', lambda e, kd=kd, tb=tb: e.activation(out=kd[:, 0:512], in_=tb[:, 0:512], func=AF.Identity, scale=kdec),
                 reads=[tbreg, RC], writes=[kdreg])
            for i in range(4):
                n = 4 * r + i
                P.op('pe', lambda e, n=n, i=i, r=r, kd=kd: e.matmul(PS[r][:, i * 128:(i + 1) * 128], lhsT=kd[:, i * 128:(i + 1) * 128],
                                                                  rhs=SL[:, vs, n * 128:(n + 1) * 128], start=True, stop=True),
                     reads=[kdreg, ('S', vs, grp(n))], writes=[('ps', r)])
        for n in range(16):
            r, i = divmod(n, 4)
            P.op('dve', lambda e, n=n, r=r, i=i: e.scalar_tensor_tensor(out=Rf[(n + 1) % 2][:, :], in0=Rf[n % 2][:, :], scalar=cd,
                                                                      in1=PS[r][:, i * 128:(i + 1) * 128], op0=ALU.mult, op1=ALU.add),
                 reads=[('Rf', n % 2), ('ps', r)], writes=[('Rf', (n + 1) % 2)])
            P.op('act', lambda e, n=n: e.activation(out=SL[:, RBs, (n + 1) * 128:(n + 2) * 128], in_=Rf[(n + 1) % 2][:, :], func=AF.Identity),
                 reads=[('Rf', (n + 1) % 2)], writes=[('S', RBs, grp(n + 1))])
        def emit_ST(g):
            s, n = GROUPS[g]
            ncx = n // 128
            bk = g % 2
            for ci in range(ncx):
                cs = s + ci * 128
                P.op('pe', lambda e, ci=ci, cs=cs: e.matmul(PS[bk][:, ci * 128:(ci + 1) * 128], lhsT=SL[:, ks, cs:cs + 128],
                                                          rhs=SL[:, u, cs:cs + 128], start=True, stop=True),
                     reads=[('S', u, g), ('S', ks, g)], writes=[('ps', bk)])
            v3 = lambda ap: ap.rearrange("p (c i) -> p c i", c=ncx)
            bc = lambda ap: ap.unsqueeze(1).broadcast_to([128, ncx, 128])
            stm, stmreg = ring('ba', ba)
            P.op('dve', lambda e: e.tensor_tensor(out=v3(stm[:, 0:n]), in0=v3(PS[bk][:, 0:n]), in1=bc(dmask), op=ALU.mult),
                 reads=[('ps', bk), RC], writes=[stmreg])
            qd, qdreg = ring('ba', ba)
            P.op('dve', lambda e: e.tensor_tensor(out=v3(qd[:, 0:n]), in0=v3(SL[:, u, s:s + n]), in1=bc(qdec), op=ALU.mult),
                 reads=[('S', u, g), RC], writes=[qdreg])
            return stm, stmreg, qd, qdreg

        def emit_O(g, st):
            stm, stmreg, qd, qdreg = st
            s, n = GROUPS[g]
            OBk = 4 + g % 2
            OB = PS[OBk]
            for ci in range(n // 128):
                cs = s + ci * 128
                P.op('pe', lambda e, ci=ci, cs=cs: e.matmul(OB[:, ci * 128:(ci + 1) * 128], lhsT=SL[:, vs, cs:cs + 128],
                                                          rhs=stm[:, ci * 128:(ci + 1) * 128], start=True, stop=False),
                     reads=[('S', vs, g), stmreg], writes=[('ps', OBk)])
                P.op('pe', lambda e, ci=ci, cs=cs: e.matmul(OB[:, ci * 128:(ci + 1) * 128], lhsT=SL[:, RBs, cs:cs + 128],
                                                          rhs=qd[:, ci * 128:(ci + 1) * 128], start=False, stop=True),
                     reads=[('S', RBs, g), qdreg], writes=[('ps', OBk)])
            post.append([1, lambda: fin1(g, OBk)])
            tick()

        post = []
        PTF = PT[:, :].bitcast(F32)

        def tick(force=False):
            for ent in list(post):
                ent[0] -= 1
                if force or ent[0] <= 0:
                    post.remove(ent)
                    ent[1]()

        def fin1(g, OBk):
            s, n = GROUPS[g]
            OB = PS[OBk]
            cent = cb[:, CB_CENT:CB_CENT + 128]
            ck = 2 + g % 2
            ob, obreg = ring('ba', ba)
            P.op('act', lambda e: e.activation(out=ob[:, 0:n], in_=OB[:, 0:n], func=AF.Identity), reads=[('ps', OBk)], writes=[obreg])
            P.op('pe', lambda e: e.matmul(PS[ck][:, 0:n], lhsT=cent, rhs=ob[:, 0:n], start=True, stop=True),
                 reads=[obreg, RC], writes=[('ps', ck)])
            post.append([1, lambda: fin2(g, ck)])

        def fin2(g, ck):
            s, n = GROUPS[g]
            csq, csqreg = ring('ba', ba)
            P.op('act', lambda e: e.activation(out=csq[:, 0:n], in_=PS[ck][:, 0:n], func=AF.Square), reads=[('ps', ck)], writes=[csqreg])
            P.op('pe', lambda e: e.matmul(PS[6][:, 0:n], lhsT=ones, rhs=csq[:, 0:n], start=True, stop=True),
                 reads=[csqreg, RC], writes=[('ps', 6)])
            post.append([1, lambda: fin3(g, ck)])

        def fin3(g, ck):
            s, n = GROUPS[g]
            rs, rsreg = ring('fb', fb)
            P.op('act', lambda e: e.activation(out=rs[:, 0:n], in_=PS[6][:, 0:n], func=AF.Ln, scale=1.0 / 128, bias=epsc),
                 reads=[('ps', 6), RC], writes=[rsreg])
            P.op('act', lambda e: e.activation(out=rs[:, 0:n], in_=rs[:, 0:n], func=AF.Exp, scale=-0.5), reads=[rsreg], writes=[rsreg])
            fin4(g, ck, rs, rsreg)

        def fin4(g, ck, rs, rsreg):
            s, n = GROUPS[g]
            nrm, nrmreg = ring('fa', fa)
            P.op('dve', lambda e: e.scalar_tensor_tensor(out=nrm[:, 0:n], in0=PS[ck][:, 0:n], scalar=gnw, in1=rs[:, 0:n],
                                                          op0=ALU.mult, op1=ALU.mult),
                 reads=[('ps', ck), rsreg, RC], writes=[nrmreg])
            gs, gsreg = gate[g]
            P.op('dve', lambda e: e.tensor_tensor(out=SL[:, u, s:s + n], in0=nrm[:, 0:n], in1=gs[:, 0:n], op=ALU.mult),
                 reads=[nrmreg, gsreg], writes=[('S', u, g)])
            done.add(g)

        done = set()

        pending = None
        need_gate = [4]
        for g in range(5):
            if need_gate and (need_gate[0] - 4) in done:
                emit_gate(need_gate.pop(0), 'ptf')
            st = emit_ST(g)
            if pending is not None:
                emit_O(*pending)
            pending = (g, st)
        emit_O(*pending)
        while post or need_gate:
            if need_gate and (need_gate[0] - 4) in done:
                emit_gate(need_gate.pop(0), 'ptf')
            tick(force=True)

    def retention_unit(l, h):
        u = h
        ks = 8 + 2 * (u % 2)
        vs = 9
        wqk, wqkreg = wload(wmix_d[l, u, 0], 2048)
        wvg, wvgreg = wload(wmix_d[l, u, 1], 2048)
        for g, (s, n) in enumerate(GROUPS):
            tabs_for[g] = tload(0, s, n)
            project_rot_group(0, wqk, wqkreg, u, CB_PERMR, g)
            project_rot_group(1, wqk, wqkreg, ks, CB_PERMR, g)
            norm_pop()
        rot_flush()
        project_v(wvg, wvgreg, vs)
        if DBG_MIX == 'B':
            return
        retention_attention(l, h, u, ks, vs, wvg, wvgreg)

    def diff_attention(l, u, ks, vs, om):
        neglam = lamv[:, 4 + l:5 + l]
        sw = cf[:, CF_PVEC + PV_SW + l:CF_PVEC + PV_SW + l + 1]
        blocks = []
        for g, (s, n) in enumerate(GROUPS):
            last = (s + n) // 128 - 1
            for m in range(2):
                for jb in range(last + 1):
                    blocks.append((g, m, jb, last))
        odset = {}
        tres = {}
        deferred = []
        post = []

        def emit_S(blk):
            g, m, jb, last = blk
            s, n = GROUPS[g]
            off = max(0, jb * 128 - s)
            nq = n - off
            diag = jb * 128 >= s
            sk = rr('dst', 3)
            SP_, sreg = SBANK[sk]
            kg = 0 if jb == 0 else 1 + (jb - 1) // 4
            P.op('pe', lambda e: e.matmul(SP_[:, 0:nq], lhsT=SL[:, ks[m], jb * 128:(jb + 1) * 128],
                                          rhs=SL[:, u, s + off:s + n], start=True, stop=(not diag)),
                 reads=[('S', ks[m], kg), ('S', u, g)], writes=[sreg])
            if diag:
                P.op('pe', lambda e: e.matmul(SP_[:, 0:128], lhsT=ident, rhs=cmask, start=False, stop=True),
                     reads=[RC], writes=[sreg])
            return sk, off, nq, kg

        def emit_rest(blk, sinfo):
            g, m, jb, last = blk
            sk, off, nq, kg = sinfo
            SP_, sreg = SBANK[sk]
            if (g, m) not in odset:
                odset[(g, m)] = 2 + 2 * rr('dod', 2)
                while any(ent[2] == odset[(g, m)] for ent in post):
                    for ent in list(post):
                        if ent[2] == odset[(g, m)]:
                            post.remove(ent)
                            ent[1]()
            OBk = odset[(g, m)]
            DBk = OBk + 1
            OB, DB = PS[OBk], PS[DBk]
            pt, ptreg = ring('ba', ba)
            P.op('act', lambda e: e.activation(out=pt[:, 0:nq], in_=SP_[:, 0:nq], func=AF.Exp, scale=0.125),
                 reads=[sreg], writes=[ptreg])
            P.op('pe', lambda e: e.matmul(OB[:, off:off + nq], lhsT=SL[:, vs, jb * 128:(jb + 1) * 128], rhs=pt[:, 0:nq],
                                          start=(jb == 0), stop=(jb == last)),
                 reads=[('S', vs, kg), ptreg], writes=[('ps', OBk)])
            P.op('pe', lambda e: e.matmul(DB[:, off:off + nq], lhsT=(ones_pad if jb == 0 else ones), rhs=pt[:, 0:nq],
                                          start=(jb == 0), stop=(jb == last)),
                 reads=[RC, ptreg], writes=[('ps', DBk)])
            if jb == last:
                post.append([2, lambda: normalize_act(g, m, OBk), OBk])

        def normalize_act(g, m, OBk):
            DBk = OBk + 1
            DB = PS[DBk]
            s, n = GROUPS[g]
            o0 = PAD if g == 0 else 0
            r, rreg = ring('fb', fb)
            P.op('act', lambda e: e.activation(out=r[:, o0:n], in_=DB[:, o0:n], func=AF.Ln), reads=[('ps', DBk)], writes=[rreg])
            P.op('act', lambda e: e.activation(out=r[:, o0:n], in_=r[:, o0:n], func=AF.Exp, scale=-1.0), reads=[rreg], writes=[rreg])
            post.append([3, lambda: normalize_dve(g, m, OBk, r, rreg), OBk])

        def normalize_dve(g, m, OBk, r, rreg):
            OB = PS[OBk]
            s, n = GROUPS[g]
            o0 = PAD if g == 0 else 0
            t, treg_ = ring('fa', fa)
            P.op('dve', lambda e: e.tensor_tensor(out=t[:, o0:n], in0=OB[:, o0:n], in1=r[:, o0:n], op=ALU.mult),
                 reads=[('ps', OBk), rreg], writes=[treg_])
            tres[(g, m)] = (t, treg_)
            if m == 1:
                post.append([1, lambda: finalize1(g), None])

        def finalize1(g):
            s, n = GROUPS[g]
            o0 = PAD if g == 0 else 0
            (t0, t0reg), (t1, t1reg) = tres[(g, 0)], tres[(g, 1)]
            o, oreg = ring('fa', fa)
            P.op('dve', lambda e: e.scalar_tensor_tensor(out=o[:, o0:n], in0=t1[:, o0:n], scalar=neglam, in1=t0[:, o0:n],
                                                          op0=ALU.mult, op1=ALU.add),
                 reads=[t0reg, t1reg, ('lamv',)], writes=[oreg])
            sq, sqreg = ring('ba', ba)
            P.op('dve', lambda e: e.tensor_tensor(out=sq[:, o0:n], in0=o[:, o0:n], in1=o[:, o0:n], op=ALU.mult), reads=[oreg], writes=[sqreg])
            P.op('pe', lambda e: e.matmul(PS[6][:, o0:n], lhsT=ones, rhs=sq[:, o0:n], start=True, stop=True),
                 reads=[sqreg, RC], writes=[('ps', 6)])
            post.append([4, lambda: finalize2(g, o, oreg), None])

        def finalize2(g, o, oreg):
            s, n = GROUPS[g]
            o0 = PAD if g == 0 else 0
            rs, rsreg = ring('fb', fb)
            P.op('act', lambda e: e.activation(out=rs[:, o0:n], in_=PS[6][:, o0:n], func=AF.Ln, scale=1.0 / (128 * om * om), bias=epsl[l]),
                 reads=[('ps', 6), RC], writes=[rsreg])
            P.op('act', lambda e: e.activation(out=rs[:, o0:n], in_=rs[:, o0:n], func=AF.Exp, scale=-0.5), reads=[rsreg], writes=[rsreg])
            P.op('dve', lambda e: e.scalar_tensor_tensor(out=SL[:, u, s + o0:s + n], in0=o[:, o0:n], scalar=sw, in1=rs[:, o0:n],
                                                          op0=ALU.mult, op1=ALU.mult),
                 reads=[oreg, rsreg, RC], writes=[('S', u, g)])

        def tick(force=False):
            for ent in list(post):
                ent[0] -= 1
                if force or ent[0] <= 0:
                    post.remove(ent)
                    ent[1]()

        pend = []
        for blk in blocks:
            sinfo = emit_S(blk)
            pend.append((blk, sinfo))
            if len(pend) > 2:
                emit_rest(*pend.pop(0))
                tick()
        while pend:
            emit_rest(*pend.pop(0))
            tick()
        while post:
            tick(force=True)

    def diff_unit(l, h, lam_init):
        u = 4 + h
        ks = (8, 10)
        vs = 9
        wqk, wqkreg = wload(wmix_d[l, u, 0], 2048)
        wv, wvreg = wload(wmix_d[l, u, 1, :, 0:1024], 1024)
        if h == 0:
            P.op('dve', lambda e: e.memset(SL[64:128, ks[0], :], 0.0), writes=[('S', ks[0], g) for g in range(5)])
            P.op('dve', lambda e: e.memset(SL[0:64, ks[1], :], 0.0), writes=[('S', ks[1], g) for g in range(5)])
        for g, (s, n) in enumerate(GROUPS):
            tabs_for[g] = tload(1, s, n)
            project_rot_group(0, wqk, wqkreg, u, CB_PERMD, g)
            project_rot_group(1, wqk, wqkreg, None, CB_PERMD, g, split=ks)
        rot_flush()
        project_v(wv, wvreg, vs)
        diff_attention(l, u, ks, vs, 1.0 - lam_init)

    def lam_compute(l):
        lam_init = 0.8 - 0.6 * math.exp(-0.3 * l)
        for t in range(2):
            a = lamin[:, (l * 4 + 2 * t) * 64:(l * 4 + 2 * t + 1) * 64]
            b = lamin[:, (l * 4 + 2 * t + 1) * 64:(l * 4 + 2 * t + 2) * 64]
            lw, lwreg = ring('fa', fa)
            P.op('dve', lambda e, a=a, b=b, lw=lw: e.tensor_tensor(out=lw[:, 0:64], in0=a, in1=b, op=ALU.mult),
                 reads=[RC], writes=[lwreg])
            P.op('dve', lambda e, t=t, lw=lw: e.tensor_reduce(out=lamv[:, t:t + 1], in_=lw[:, 0:64], axis=mybir.AxisListType.X, op=ALU.add),
                 reads=[lwreg, ('lamv',)], writes=[('lamv',)])
            P.op('act', lambda e, t=t: e.activation(out=lamv[:, 2 + t:3 + t], in_=lamv[:, t:t + 1], func=AF.Exp),
                 reads=[('lamv',)], writes=[('lamv',)])
        P.op('dve', lambda e: e.scalar_tensor_tensor(out=lamv[:, 4 + l:5 + l], in0=lamv[:, 3:4], scalar=-lam_init, in1=lamv[:, 2:3],
                                                      op0=ALU.add, op1=ALU.subtract),
             reads=[('lamv',)], writes=[('lamv',)])
        return lam_init

    def mixer(l):
        rmsnorm(PV_NORM + (l * 3 + 1) * 8)
        lam_init = lam_compute(l)
        if DBG_MIX == 'A':
            return
        for h in range(4):
            retention_unit(l, h)
            if DBG_MIX in ('B', 'B1', 'C', 'R1', 'R2', 'R3'):
                return
        if DBG_MIX == 'D':
            return
        for h in range(4):
            diff_unit(l, h, lam_init)
            if DBG_MIX == 'F':
                return
        for c in range(8):
            wt, wreg = wload(wout_d[l, c], 1024)
            for (a, n, gs) in FFG:
                mm_out_group(wt, wreg, 8, c, a, n, gs, 1.0)

    stages = []
    for l in range(2):
        stages += [('ffn', l, 0), ('mix', l), ('ffn', l, 1)]
    nst = len(stages) if stop is None else stop
    for st in stages[:nst]:
        if st[0] == 'ffn':
            ffn(st[1], st[2])
        else:
            mixer(st[1])
    outregs = []
    if stop is None:
        for g, (s, n) in enumerate(GROUPS):
            rmsnorm_group(PV_NORM + 48, True, g, s, n)
            if g >= 1:
                P.dma('sp', lambda e, s=s, n=n: e.dma_start(out=out_d[:, :, s - 128:s - 128 + n], in_=hT[:, :, s:s + n]), 'out',
                      reads=[('h', c, g) for c in range(8)], writes=[('out', g)])
                outregs.append(('out', g))
    else:
        for c in range(8):
            P.dma('sp', lambda e, c=c: e.dma_start(out=out_d[:, c, :], in_=hT[:, c, 128:TP]), 'out',
                  reads=[('h', c, g) for g in range(1, 5)], writes=[('out', c)])
            outregs.append(('out', c))
    if dump == 'h':
        for c in range(8):
            P.dma('sp', lambda e, c=c: e.dma_start(out=dbg_d[:, c, :], in_=hT[:, c, :]), 'out',
                  reads=[('h', c, g) for g in range(5)], writes=[('dbg', c)])
            outregs.append(('dbg', c))
    P.op('sp', None, reads=outregs)
    P.build()
    return nc, P


def _bf(a):
    return np.asarray(a, dtype=np.float32).astype(ml_dtypes.bfloat16)


def _const_tables():
    pos = (np.arange(TP, dtype=np.float32) - np.float32(PAD))
    angle = (np.float32(10000.0) ** (-np.linspace(0.0, 1.0, 64, dtype=np.float32))).astype(np.float32)
    fr = (pos[None, :] * np.repeat(angle, 2)[:, None]).astype(np.float32)
    cosr = np.cos(fr).astype(np.float32)
    sgn = np.where(np.arange(128) % 2 == 0, -1.0, 1.0).astype(np.float32)[:, None]
    sinr = (np.sin(fr) * sgn).astype(np.float32)
    r = 16
    inv = (np.float32(500000.0) ** (-np.arange(0, r, 2, dtype=np.float32) / r)).astype(np.float32)
    cosd = np.ones((128, TP), np.float32)
    sind = np.zeros((128, TP), np.float32)
    for p in range(128):
        dd = p % 64
        if dd < r:
            f = (pos * inv[dd % 8]).astype(np.float32)
            cosd[p] = np.cos(f)
            sind[p] = np.sin(f) * (-1.0 if dd < 8 else 1.0)
    tabs = np.stack([cosr, sinr, cosd, sind]).astype(np.float32)
    cbm = np.zeros((128, CB_N), np.float32)
    cbm[:, CB_ONES:CB_ONES + 128] = 1.0
    cbm[PAD:, CB_ONESPAD:CB_ONESPAD + 128] = 1.0
    cbm[:, CB_IDENT:CB_IDENT + 128] = np.eye(128)
    for m in range(128):
        src = m + 1 if m % 2 == 0 else m - 1
        cbm[src, CB_PERMR + m] = 1.0
        dd = m % 64
        if dd < 8:
            cbm[m + 8, CB_PERMD + m] = 1.0
        elif dd < 16:
            cbm[m - 8, CB_PERMD + m] = 1.0
    jj = np.arange(128)[:, None]
    ii = np.arange(128)[None, :]
    cbm[:, CB_CMASK:CB_CMASK + 128] = np.where(jj <= ii, 0.0, -30000.0)
    cbm[:, CB_CENT:CB_CENT + 128] = np.eye(128) - 1.0 / 128.0
    cfm = np.zeros((128, CF_N), np.float32)
    cfm[:, CF_EPS] = EPS
    cfm[:, CF_ONE] = 1.0
    for l_ in range(2):
        om_ = 1.0 - (0.8 - 0.6 * math.exp(-0.3 * l_))
        cfm[:, CF_EPS + 1 + l_] = EPS / (om_ * om_)
    for h in range(4):
        lg = math.log(1.0 - 2.0 ** (-5.0 - h))
        rel = (ii - jj).astype(np.float64)
        dm = np.where(rel >= 0, np.exp(lg * np.maximum(rel, 0.0)), 0.0) * (128.0 ** -0.5)
        cfm[:, CF_DMASK + h * 128:CF_DMASK + (h + 1) * 128] = dm
        cfm[:, CF_QDEC + h * 128:CF_QDEC + (h + 1) * 128] = np.exp(lg * (np.arange(128) + 1.0))[None, :]
        cfm[:, CF_KDEC + h] = np.exp(lg * (127.0 - np.arange(128))) * (128.0 ** -0.5)
    return tabs, cbm, cfm


def _prep(inputs):
    f = lambda k: np.asarray(inputs[k], dtype=np.float32)
    tabs, cbm, cfm = _const_tables()
    cvec = lambda v: np.ascontiguousarray(v.reshape(-1, 128).T)
    norms = [f("ffn1_norm"), f("mix_norm"), f("ffn2_norm")]
    for l in range(2):
        for w in range(3):
            c0 = CF_PVEC + PV_NORM + (l * 3 + w) * 8
            cfm[:, c0:c0 + 8] = cvec(norms[w][l])
        cfm[:, CF_PVEC + PV_GN + l * 4:CF_PVEC + PV_GN + l * 4 + 4] = cvec(f("ret_gn_w")[l])
        cfm[:, CF_PVEC + PV_SW + l] = f("diff_subln_w")[l]
    cfm[:, CF_PVEC + PV_NORM + 48:CF_PVEC + PV_NORM + 56] = cvec(f("final_norm"))
    lam = np.stack([f("diff_lambda_q1"), f("diff_lambda_k1"), f("diff_lambda_q2"), f("diff_lambda_k2")], axis=1)
    lam = np.ascontiguousarray(np.broadcast_to(lam.reshape(1, 512), (128, 512)))

    def fm_tile(w):
        nc_ = w.shape[1] // 128
        return w.reshape(8, 128, nc_, 128).transpose(2, 1, 0, 3)

    wgu = np.empty((2, 2, NF, 128, 2, 8, 128), np.float32)
    wd = np.empty((2, 2, 2, 8, 128, NFH, 128), np.float32)
    for l in range(2):
        for j, (kg, ku, kd_) in enumerate([("ffn1_w_gate", "ffn1_w_up", "ffn1_w_down"), ("ffn2_w_gate", "ffn2_w_up", "ffn2_w_down")]):
            wgu[l, j, :, :, 0] = fm_tile(f(kg)[l])
            wgu[l, j, :, :, 1] = fm_tile(f(ku)[l])
            wd[l, j] = f(kd_)[l].reshape(2, NFH, 128, 8, 128).transpose(0, 3, 2, 1, 4)
    win = f("w_in")
    wmix = np.zeros((2, 8, 2, 128, 2, 8, 128), np.float32)
    for l in range(2):
        t = fm_tile(win[l])
        for h in range(4):
            wmix[l, h, 0, :, 0] = t[h]
            wmix[l, h, 0, :, 1] = t[4 + h]
            wmix[l, h, 1, :, 0] = t[8 + h]
            wmix[l, h, 1, :, 1] = t[12 + h]
            wmix[l, 4 + h, 0, :, 0] = t[16 + h]
            wmix[l, 4 + h, 0, :, 1] = t[20 + h]
            wmix[l, 4 + h, 1, :, 0] = t[24 + h]
    wout = np.stack([fm_tile(f("w_out")[l]) for l in range(2)])
    shared = {
        "metaT": np.ascontiguousarray(f("meta_tokens").reshape(NMETA, 8, 128).transpose(2, 1, 0)),
        "cb": _bf(cbm), "cf": cfm, "lam": lam, "tabs": tabs,
        "wgu": wgu.reshape(2, 2, NF, 128, 2048), "wd": wd.reshape(2, 2, 2, 8, 128, NFH * 128),
        "wmix": wmix.reshape(2, 8, 2, 128, 2048), "wout": np.ascontiguousarray(wout.reshape(2, 8, 128, 1024)),
    }
    x = f("x")
    xTs = [np.ascontiguousarray(x[b].reshape(SEQ, 8, 128).transpose(2, 1, 0)) for b in range(x.shape[0])]
    return shared, xTs


def run(inputs, cores=None, stop=None, dump=None, trace=False):
    shared, xTs = _prep(inputs)
    cores = list(range(8)) if cores is None else cores
    nc, P = build_program(stop=stop, dump=dump)
    in_maps = [dict(shared, xT=xTs[b]) for b in cores]
    res = run_bass_kernel_spmd(nc, in_maps, core_ids=list(range(len(cores))), trace=trace)
    return res, P


def kernel(**inputs):
    res, _ = run(inputs)
    outs = [np.asarray(r["outT"], dtype=np.float32) for r in res.results]
    out = np.stack([o.transpose(2, 1, 0).reshape(SEQ, D) for o in outs])
    return np.ascontiguousarray(out.astype(np.float32))
```

```python
import math
import numpy as np
import ml_dtypes
import concourse.bass as bass
import concourse.mybir as mybir
from concourse.bass_utils import run_bass_kernel_spmd

F32 = mybir.dt.float32
BF16 = mybir.dt.bfloat16
AF = mybir.ActivationFunctionType
ALU = mybir.AluOpType

D = 1024
SEQ = 2048
NMETA = 16
PAD = 112
TP = 2176
NCH = 17
DFF = 2816
NF = 22
NFH = 11
EPS = 1e-6
GROUPS = [(0, 128)] + [(128 + 512 * g, 512) for g in range(4)]
NSLOT = 11
NW = 4
WCOLS = 2048
SAME_ENGINE_SYNC = True
DBG_HACK = 0
WARM = 0
ROT_ADD_ENG = 'dve'
DBG_MIX = None

CB_ONES, CB_ONESPAD, CB_IDENT, CB_PERMR, CB_PERMD, CB_CMASK, CB_CENT = [i * 128 for i in range(7)]
CB_N = 7 * 128
CF_DMASK = 0
CF_QDEC = 512
CF_KDEC = 1024
CF_PVEC = 1028
PV_NORM = 0
PV_GN = 56
PV_SW = 64
CF_EPS = CF_PVEC + 66
CF_ONE = CF_EPS + 3
CF_N = CF_ONE + 1


class Prog:
    def __init__(self, nc):
        self.nc = nc
        self.ops = []
        self.dma_sems = {}

    def op(self, eng, fn, reads=(), writes=()):
        self.ops.append(dict(eng=eng, fn=fn, reads=list(reads), writes=list(writes), dma=None))

    def dma(self, eng, fn, slot, reads=(), writes=()):
        self.ops.append(dict(eng=eng, fn=fn, reads=list(reads), writes=list(writes), dma=slot))

    def build(self):
        nc = self.nc
        ops = self.ops
        engs = ['pe', 'act', 'dve', 'pool', 'sp']
        last_w = {}
        readers = {}
        for i, o in enumerate(ops):
            deps = set()
            key = ('dma', i) if o['dma'] is not None else o['eng']
            for r in o['reads']:
                if r in last_w:
                    deps.add(last_w[r])
                if r[0] in ('ps', 'pt'):
                    for k2, i2 in readers.get(r, {}).items():
                        if k2 != key:
                            deps.add(i2)
            for w in o['writes']:
                if w in last_w:
                    deps.add(last_w[w])
                rd = readers.get(w)
                if rd:
                    deps.update(rd.values())
            deps.discard(i)
            o['deps'] = deps
            for r in o['reads']:
                readers.setdefault(r, {})[key] = i
            for w in o['writes']:
                last_w[w] = i
                readers[w] = {}
        signal = set()
        for i, o in enumerate(ops):
            for d in o['deps']:
                p = ops[d]
                if p['dma'] is not None:
                    continue
                if p['eng'] != o['eng'] or (SAME_ENGINE_SYNC and o['eng'] != 'pe'):
                    signal.add(d)
        sem = {e: nc.alloc_semaphore('s_' + e) for e in engs}
        cnt = {e: 0 for e in engs}
        for i, o in enumerate(ops):
            if o['dma'] is not None:
                slot = o['dma']
                if slot not in self.dma_sems:
                    self.dma_sems[slot] = [nc.alloc_semaphore('d_' + str(slot)), 0]
                ent = self.dma_sems[slot]
                ent[1] += 16
                o['tok'] = (ent[0], ent[1])
            elif i in signal:
                cnt[o['eng']] += 1
                o['tok'] = (sem[o['eng']], cnt[o['eng']])
            else:
                o['tok'] = None
        waited = {e: {} for e in engs}
        for i, o in enumerate(ops):
            need = {}
            for d in o['deps']:
                p = ops[d]
                if p['dma'] is None and p['eng'] == o['eng'] and not (SAME_ENGINE_SYNC and o['eng'] != 'pe'):
                    continue
                tok = p['tok']
                assert tok is not None
                k = tok[0]
                if need.get(k, (None, 0))[1] < tok[1]:
                    need[k] = tok
            ws = []
            for k, tok in need.items():
                if waited[o['eng']].get(k, 0) < tok[1]:
                    waited[o['eng']][k] = tok[1]
                    ws.append(tok)
            o['waits'] = ws
        per = {e: [o for o in ops if o['eng'] == e] for e in engs}
        self.stats = {e: len(per[e]) for e in engs}
        self.stats['signals'] = dict(cnt)

        def run(e, lst):
            for o in lst:
                for (s, v) in o['waits']:
                    e.wait_ge(s, v)
                if o['fn'] is None:
                    continue
                ins = o['fn'](e)
                if o['tok'] is not None:
                    ins.then_inc(o['tok'][0], 16 if o['dma'] is not None else 1)

        with nc.Block() as block:
            @block.tensor
            def _(e):
                run(e, per['pe'])

            @block.scalar
            def _(e):
                run(e, per['act'])

            @block.vector
            def _(e):
                run(e, per['dve'])

            @block.gpsimd
            def _(e):
                run(e, per['pool'])

            @block.sync
            def _(e):
                run(e, per['sp'])


def build_program(stop=None, dump=None):
    nc = bass.Bass("TRN2", target_bir_lowering=False)
    P = Prog(nc)

    def din(name, shape, dt=F32):
        return nc.dram_tensor(name, list(shape), dt, kind="ExternalInput").ap()

    xT_d = din("xT", [128, 8, SEQ])
    meta_d = din("metaT", [128, 8, NMETA])
    cb_d = din("cb", [128, CB_N], BF16)
    cf_d = din("cf", [128, CF_N])
    lam_d = din("lam", [128, 2 * 4 * 64])
    tabs_d = din("tabs", [4, 128, TP])
    wgu_d = din("wgu", [2, 2, NF, 128, 2048])
    wd_d = din("wd", [2, 2, 2, 8, 128, NFH * 128])
    wmix_d = din("wmix", [2, 8, 2, 128, 2048])
    wout_d = din("wout", [2, 8, 128, 1024])
    out_d = nc.dram_tensor("outT", [128, 8, SEQ], F32, kind="ExternalOutput").ap()
    dbg_d = None
    if dump is not None:
        dbg_d = nc.dram_tensor("dbg", [128, 8, TP], F32, kind="ExternalOutput").ap()

    sb = nc.alloc_sbuf_tensor
    hT = sb("hT", [128, 8, TP], F32)
    xnT = sb("xnT", [128, 8, TP], BF16)
    SL = sb("slots", [128, NSLOT, TP], BF16)
    wsl = [sb(f"wsl{i}", [128, WCOLS], BF16) for i in range(NW)]
    tabt = [sb(f"tab{i}", [128, 2, 512], F32) for i in range(2)]
    cb = sb("cbs", [128, CB_N], BF16)
    cf = sb("cfs", [128, CF_N], F32)
    lamin = sb("lamin", [128, 256], F32)
    lamv = sb("lamv", [128, 8], F32)
    NR = 5
    sqr = [sb(f"sq{i}", [128, 512], BF16) for i in range(NR)]
    fa = [sb(f"fa{i}", [128, 512], F32) for i in range(4)]
    fb = [sb(f"fb{i}", [128, 512], F32) for i in range(4)]
    ba = [sb(f"ba{i}", [128, 512], BF16) for i in range(6)]
    Rf = [sb(f"Rf{i}", [128, 128], F32) for i in range(2)]

    PS = [nc.alloc_psum_tensor(f"ps{i}", [128, 512], F32) for i in range(7)]
    PT = nc.alloc_psum_tensor("pst", [128, 1024], BF16)

    SBANK = [(PS[0], ('ps', 0)), (PS[1], ('ps', 1)), (PT[:, :].bitcast(F32), ('pt',))]
    ctr = {}

    def rr(name, n):
        v = ctr.get(name, 0)
        ctr[name] = v + 1
        return v % n

    def ring(name, lst):
        i = rr(name, len(lst))
        return lst[i], (name, i)

    ones = cb[:, CB_ONES:CB_ONES + 128]
    ones_pad = cb[:, CB_ONESPAD:CB_ONESPAD + 128]
    ident = cb[:, CB_IDENT:CB_IDENT + 128]
    cmask = cb[:, CB_CMASK:CB_CMASK + 128]
    RC = ('const',)
    epsc = cf[:, CF_EPS:CF_EPS + 1]
    onec = cf[:, CF_ONE:CF_ONE + 1]
    epsl = [cf[:, CF_EPS + 1 + l_:CF_EPS + 2 + l_] for l_ in range(2)]

    P.dma('sp', lambda e: e.dma_start(out=cb[:, :], in_=cb_d), 'const', writes=[RC])
    P.dma('sp', lambda e: e.dma_start(out=cf[:, :], in_=cf_d), 'const', writes=[RC])
    P.dma('sp', lambda e: e.dma_start(out=lamin[:, :], in_=lam_d[:, 0:256]), ('lamin', 0), writes=[('lamin',)])
    hregs = lambda c: [('h', c, g) for g in range(5)]
    P.op('dve', lambda e: e.memset(hT[:, :, 0:PAD], 0.0), writes=[('h', c, 0) for c in range(8)])
    P.dma('sp', lambda e: e.dma_start(out=hT[:, :, PAD:128], in_=meta_d), ('xin', 0),
          reads=[('h', c, 0) for c in range(8)], writes=[('h', c, 0) for c in range(8)])
    for g in range(1, 5):
        s_, n_ = GROUPS[g]
        P.dma('sp', lambda e, s_=s_, n_=n_: e.dma_start(out=hT[:, :, s_:s_ + n_], in_=xT_d[:, :, s_ - 128:s_ - 128 + n_]), ('xin', g),
              writes=[('h', c, g) for c in range(8)])

    def wload(src_ap, ncols):
        i = rr('w', NW)
        t = wsl[i]
        reg = ('w', i)
        P.dma('pool', lambda e: e.dma_start(out=t[:, 0:ncols], in_=src_ap, max_dma_last_dim=8192),
              ('w', i), writes=[reg])
        return t, reg

    def tload(k, s, n):
        i = rr('tab', 2)
        t = tabt[i]
        regs = [('tab', i, 0), ('tab', i, 1)]
        P.dma('sp', lambda e: e.dma_start(out=t[:, 0, 0:n], in_=tabs_d[2 * k, :, s:s + n]), ('tab', i, 0), writes=[regs[0]])
        P.dma('sp', lambda e: e.dma_start(out=t[:, 1, 0:n], in_=tabs_d[2 * k + 1, :, s:s + n]), ('tab', i, 1), writes=[regs[1]])
        return t, regs

    def rmsnorm_group(pvcol, final, g, s, n):
        ss = PS[6]
        for c in range(8):
            sq, sqreg = ring('sq', sqr)
            P.op('act', lambda e, sq=sq, c=c: e.activation(out=sq[:, 0:n], in_=hT[:, c, s:s + n], func=AF.Square),
                 reads=[('h', c, g)], writes=[sqreg])
            P.op('pe', lambda e, sq=sq, c=c: e.matmul(ss[:, 0:n], lhsT=ones, rhs=sq[:, 0:n], start=(c == 0), stop=(c == 7)),
                 reads=[sqreg, RC], writes=[('ps', 6)])
        rt, rtreg = ring('fa', fa)
        P.op('act', lambda e: e.activation(out=rt[:, 0:n], in_=ss[:, 0:n], func=AF.Ln, scale=1.0 / D, bias=epsc),
             reads=[('ps', 6), RC], writes=[rtreg])
        rs, rsreg = ring('fb', fb)
        P.op('act', lambda e: e.activation(out=rs[:, 0:n], in_=rt[:, 0:n], func=AF.Exp, scale=-0.5), reads=[rtreg], writes=[rsreg])
        for c in range(8):
            gcol = cf[:, CF_PVEC + pvcol + c:CF_PVEC + pvcol + c + 1]
            dst = hT if final else xnT
            wreg_ = ('h', c, g) if final else ('xn', c, g)
            P.op('dve', lambda e, c=c, gcol=gcol, dst=dst: e.scalar_tensor_tensor(
                out=dst[:, c, s:s + n], in0=hT[:, c, s:s + n], scalar=gcol, in1=rs[:, 0:n], op0=ALU.mult, op1=ALU.mult),
                reads=[('h', c, g), rsreg, RC], writes=[wreg_])

    norm_queue = []

    def rmsnorm(pvcol, final=False, ahead=None):
        for g, (s, n) in enumerate(GROUPS):
            norm_queue.append(lambda g=g, s=s, n=n: rmsnorm_group(pvcol, final, g, s, n))
        for _ in range(len(GROUPS) if ahead is None else ahead):
            norm_pop()

    def norm_pop():
        if norm_queue:
            norm_queue.pop(0)()

    FFG = []
    _a = PAD
    for _j in range(5):
        _n = 413 if _j < 4 else 412
        _gs = [g for g, (s_, n_) in enumerate(GROUPS) if s_ < _a + _n and _a < s_ + n_]
        FFG.append((_a, _n, _gs))
        _a += _n
    assert _a == TP

    def resid_add(ps, psreg, c, a, n, gs, scale):
        regs = [('h', c, g) for g in gs]
        P.op('dve', lambda e: e.scalar_tensor_tensor(out=hT[:, c, a:a + n], in0=ps[:, 0:n], scalar=scale,
                                                      in1=hT[:, c, a:a + n], op0=ALU.mult, op1=ALU.add),
             reads=[psreg] + regs, writes=regs)

    def ffn_gu_group(wt, wreg, fi, a, n, gs):
        k = rr('ffn_gu', 2)
        gp, up = PS[k], PS[2 + k]
        for kc in range(8):
            P.op('pe', lambda e, kc=kc: e.matmul(gp[:, 0:n], lhsT=wt[:, kc * 128:(kc + 1) * 128],
                                               rhs=xnT[:, kc, a:a + n], start=(kc == 0), stop=(kc == 7)),
                 reads=[wreg] + [('xn', kc, g) for g in gs], writes=[('ps', k)])
        for kc in range(8):
            P.op('pe', lambda e, kc=kc: e.matmul(up[:, 0:n], lhsT=wt[:, 1024 + kc * 128:1024 + (kc + 1) * 128],
                                               rhs=xnT[:, kc, a:a + n], start=(kc == 0), stop=(kc == 7)),
                 reads=[wreg] + [('xn', kc, g) for g in gs], writes=[('ps', 2 + k)])
        sg, sgreg = ring('fa', fa)
        P.op('act', lambda e: e.activation(out=sg[:, 0:n], in_=gp[:, 0:n], func=AF.Silu),
             reads=[('ps', k)], writes=[sgreg])
        hregs = [('S', fi, g) for g in gs]
        P.op('dve', lambda e: e.tensor_tensor(out=SL[:, fi, a:a + n], in0=up[:, 0:n], in1=sg[:, 0:n], op=ALU.mult),
             reads=[('ps', 2 + k), sgreg] + hregs, writes=hregs)

    def mm_out_group(wt, wreg, nk, c, a, n, gs, scale):
        k = 4 + rr('ffn_o', 2)
        op_ = PS[k]
        for fi in range(nk):
            P.op('pe', lambda e, fi=fi: e.matmul(op_[:, 0:n], lhsT=wt[:, fi * 128:(fi + 1) * 128],
                                               rhs=SL[:, fi, a:a + n], start=(fi == 0), stop=(fi == nk - 1)),
                 reads=[wreg] + [('S', fi, g) for g in gs], writes=[('ps', k)])
        resid_add(op_, ('ps', k), c, a, n, gs, scale)

    def ffn(l, j):
        rmsnorm(PV_NORM + (l * 3 + (0 if j == 0 else 2)) * 8)
        for half in range(2):
            for fi in range(NFH):
                wt, wreg = wload(wgu_d[l, j, half * NFH + fi], 2048)
                for (a, n, gs) in FFG:
                    ffn_gu_group(wt, wreg, fi, a, n, gs)
                    norm_pop()
            for c in range(8):
                wt, wreg = wload(wd_d[l, j, half, c], NFH * 128)
                for (a, n, gs) in FFG:
                    mm_out_group(wt, wreg, NFH, c, a, n, gs, 0.5)

    tabs_for = {}

    rot_pending = []

    def project_rot_group(which, wt, wreg, dst_slot, permcol, g, split=None):
        perm = cb[:, permcol:permcol + 128]
        s, n = GROUPS[g]
        tt, tregs = tabs_for[g]
        pk = rr('proj', 2)
        pp, sp_ = PS[pk], PS[2 + pk]
        for kc in range(8):
            P.op('pe', lambda e, kc=kc: e.matmul(pp[:, 0:n], lhsT=wt[:, which * 1024 + kc * 128:which * 1024 + (kc + 1) * 128],
                                               rhs=xnT[:, kc, s:s + n], start=(kc == 0), stop=(kc == 7)),
                 reads=[wreg, ('xn', kc, g)], writes=[('ps', pk)])
        qb, qbreg = ring('ba', ba)
        P.op('act', lambda e: e.activation(out=qb[:, 0:n], in_=pp[:, 0:n], func=AF.Identity), reads=[('ps', pk)], writes=[qbreg])
        rot_flush()

        def second():
            P.op('pe', lambda e: e.matmul(sp_[:, 0:n], lhsT=perm, rhs=qb[:, 0:n], start=True, stop=True),
                 reads=[qbreg, RC], writes=[('ps', 2 + pk)])
            t1, t1reg = ring('fa', fa)
            t2, t2reg = ring('fb', fb)
            P.op('dve', lambda e: e.tensor_tensor(out=t1[:, 0:n], in0=pp[:, 0:n], in1=tt[:, 0, 0:n], op=ALU.mult),
                 reads=[('ps', pk)] + tregs, writes=[t1reg])
            P.op('dve', lambda e: e.tensor_tensor(out=t2[:, 0:n], in0=sp_[:, 0:n], in1=tt[:, 1, 0:n], op=ALU.mult),
                 reads=[('ps', 2 + pk)] + tregs, writes=[t2reg])
            if split is None:
                P.op(ROT_ADD_ENG, lambda e: e.tensor_tensor(out=SL[:, dst_slot, s:s + n], in0=t1[:, 0:n], in1=t2[:, 0:n], op=ALU.add),
                     reads=[t1reg, t2reg], writes=[('S', dst_slot, g)])
            else:
                kr, krreg = ring('ba', ba)
                P.op('dve', lambda e: e.tensor_tensor(out=kr[:, 0:n], in0=t1[:, 0:n], in1=t2[:, 0:n], op=ALU.add),
                     reads=[t1reg, t2reg], writes=[krreg])
                for m in range(2):
                    P.op('act', lambda e, m=m: e.activation(out=SL[64 * m:64 * m + 64, split[m], s:s + n], in_=kr[64 * m:64 * m + 64, 0:n],
                                                          func=AF.Identity),
                         reads=[krreg], writes=[('S', split[m], g)])
        rot_pending.append(second)

    def rot_flush():
        while rot_pending:
            rot_pending.pop(0)()

    def project_v_group(wt, wreg, vslot, g, s, n):
        pk = rr('proj', 2)
        pp = PS[pk]
        for ci in range(n // 128):
            cs = s + ci * 128
            for kc in range(8):
                P.op('pe', lambda e, kc=kc, ci=ci, cs=cs: e.matmul(pp[:, ci * 128:(ci + 1) * 128], lhsT=xnT[:, kc, cs:cs + 128],
                                                                 rhs=wt[:, kc * 128:(kc + 1) * 128], start=(kc == 0), stop=(kc == 7)),
                     reads=[wreg, ('xn', kc, g)], writes=[('ps', pk)])
        P.op('act', lambda e: e.activation(out=SL[:, vslot, s:s + n], in_=pp[:, 0:n], func=AF.Identity),
             reads=[('ps', pk)], writes=[('S', vslot, g)])

    def project_v(wt, wreg, vslot):
        for g, (s, n) in enumerate(GROUPS):
            project_v_group(wt, wreg, vslot, g, s, n)

    def stats_rstd(src, srcreg, n, scale, bias_col, o0=0):
        sq, sqreg = ring('ba', ba)
        P.op('dve', lambda e: e.tensor_tensor(out=sq[:, o0:n], in0=src[:, o0:n], in1=src[:, o0:n], op=ALU.mult), reads=[srcreg], writes=[sqreg])
        P.op('pe', lambda e: e.matmul(PS[6][:, o0:n], lhsT=ones, rhs=sq[:, o0:n], start=True, stop=True),
             reads=[sqreg, RC], writes=[('ps', 6)])
        sd, sdreg = ring('fb', fb)
        P.op('act', lambda e: e.activation(out=sd[:, o0:n], in_=PS[6][:, o0:n], func=AF.Ln, scale=scale, bias=bias_col),
             reads=[('ps', 6), RC], writes=[sdreg])
        rs, rsreg = ring('fb', fb)
        P.op('act', lambda e: e.activation(out=rs[:, o0:n], in_=sd[:, o0:n], func=AF.Exp, scale=-0.5), reads=[sdreg], writes=[rsreg])
        return rs, rsreg

    def retention_attention(l, h, u, ks, vs, wvg, wvgreg):
        gam = 1.0 - 2.0 ** (-5.0 - h)
        cd = gam ** 128
        dmask = cf[:, CF_DMASK + h * 128:CF_DMASK + (h + 1) * 128]
        qdec = cf[:, CF_QDEC + h * 128:CF_QDEC + (h + 1) * 128]
        kdec = cf[:, CF_KDEC + h:CF_KDEC + h + 1]
        gnw = cf[:, CF_PVEC + PV_GN + l * 4 + h:CF_PVEC + PV_GN + l * 4 + h + 1]
        RBs = 18 - ks
        grp = lambda n: 0 if n == 0 else 1 + (n - 1) // 4
        P.op('dve', lambda e: e.memset(SL[:, RBs, 0:128], 0.0), writes=[('S', RBs, 0)])
        P.op('dve', lambda e: e.memset(Rf[0][:, :], 0.0), writes=[('Rf', 0)])
        def emit_gate(g, bank=6):
            s, n = GROUPS[g]
            gp, greg = (PTF, ('pt',)) if bank == 'ptf' else (PS[bank], ('ps', bank))
            for kc in range(8):
                P.op('pe', lambda e, kc=kc: e.matmul(gp[:, 0:n], lhsT=wvg[:, 1024 + kc * 128:1024 + (kc + 1) * 128],
                                                   rhs=xnT[:, kc, s:s + n], start=(kc == 0), stop=(kc == 7)),
                     reads=[wvgreg, ('xn', kc, g)], writes=[greg])
            sg, sgreg = ring('fb', fb)
            P.op('act', lambda e: e.activation(out=sg[:, 0:n], in_=gp[:, 0:n], func=AF.Exp, scale=-1.0), reads=[greg], writes=[sgreg])
            P.op('act', lambda e: e.activation(out=sg[:, 0:n], in_=sg[:, 0:n], func=AF.Ln, bias=onec), reads=[sgreg, RC], writes=[sgreg])
            P.op('act', lambda e: e.activation(out=sg[:, 0:n], in_=sg[:, 0:n], func=AF.Exp, scale=-1.0), reads=[sgreg], writes=[sgreg])
            gs, gsreg = ring('sq', sqr)
            P.op('dve', lambda e: e.tensor_tensor(out=gs[:, 0:n], in0=gp[:, 0:n], in1=sg[:, 0:n], op=ALU.mult),
                 reads=[greg, sgreg], writes=[gsreg])
            gate[g] = (gs, gsreg)

        gate = {}
        for g_ in range(5):
            emit_gate(g_, (2, 3, 4, 5, 6)[g_])
        TB = [(PT, ('pt',)), (PS[6][:, :].bitcast(BF16), ('ps', 6))]
        for r in range(4):
            tb, tbreg = TB[r % 2]
            for i in range(4):
                n = 4 * r + i
                P.op('pe', lambda e, n=n, i=i, tb=tb: e.transpose(tb[:, i * 128:(i + 1) * 128], SL[:, ks, n * 128:(n + 1) * 128], ident),
                     reads=[('S', ks, grp(n)), RC], writes=[tbreg])
            kd, kdreg = ring('ba', ba)
            P.op('act', lambda e, kd=kd, tb=tb: e.activation(out=kd[:, 0:512], in_=tb[:, 0:512], func=AF.Identity, scale=kdec),
                 reads=[tbreg, RC], writes=[kdreg])
            for i in range(4):
                n = 4 * r + i
                P.op('pe', lambda e, n=n, i=i, r=r, kd=kd: e.matmul(PS[r][:, i * 128:(i + 1) * 128], lhsT=kd[:, i * 128:(i + 1) * 128],
                                                                  rhs=SL[:, vs, n * 128:(n + 1) * 128], start=True, stop=True),
                     reads=[kdreg, ('S', vs, grp(n))], writes=[('ps', r)])
        for n in range(16):
            r, i = divmod(n, 4)
            P.op('dve', lambda e, n=n, r=r, i=i: e.scalar_tensor_tensor(out=Rf[(n + 1) % 2][:, :], in0=Rf[n % 2][:, :], scalar=cd,
                                                                      in1=PS[r][:, i * 128:(i + 1) * 128], op0=ALU.mult, op1=ALU.add),
                 reads=[('Rf', n % 2), ('ps', r)], writes=[('Rf', (n + 1) % 2)])
            P.op('act', lambda e, n=n: e.activation(out=SL[:, RBs, (n + 1) * 128:(n + 2) * 128], in_=Rf[(n + 1) % 2][:, :], func=AF.Identity),
                 reads=[('Rf', (n + 1) % 2)], writes=[('S', RBs, grp(n + 1))])
        def emit_ST(g):
            s, n = GROUPS[g]
            ncx = n // 128
            bk = g % 2
            for ci in range(ncx):
                cs = s + ci * 128
                P.op('pe', lambda e, ci=ci, cs=cs: e.matmul(PS[bk][:, ci * 128:(ci + 1) * 128], lhsT=SL[:, ks, cs:cs + 128],
                                                          rhs=SL[:, u, cs:cs + 128], start=True, stop=True),
                     reads=[('S', u, g), ('S', ks, g)], writes=[('ps', bk)])
            v3 = lambda ap: ap.rearrange("p (c i) -> p c i", c=ncx)
            bc = lambda ap: ap.unsqueeze(1).broadcast_to([128, ncx, 128])
            stm, stmreg = ring('ba', ba)
            P.op('dve', lambda e: e.tensor_tensor(out=v3(stm[:, 0:n]), in0=v3(PS[bk][:, 0:n]), in1=bc(dmask), op=ALU.mult),
                 reads=[('ps', bk), RC], writes=[stmreg])
            qd, qdreg = ring('ba', ba)
            P.op('dve', lambda e: e.tensor_tensor(out=v3(qd[:, 0:n]), in0=v3(SL[:, u, s:s + n]), in1=bc(qdec), op=ALU.mult),
                 reads=[('S', u, g), RC], writes=[qdreg])
            return stm, stmreg, qd, qdreg

        def emit_O(g, st):
            stm, stmreg, qd, qdreg = st
            s, n = GROUPS[g]
            OBk = 4 + g % 2
            OB = PS[OBk]
            for ci in range(n // 128):
                cs = s + ci * 128
                P.op('pe', lambda e, ci=ci, cs=cs: e.matmul(OB[:, ci * 128:(ci + 1) * 128], lhsT=SL[:, vs, cs:cs + 128],
                                                          rhs=stm[:, ci * 128:(ci + 1) * 128], start=True, stop=False),
                     reads=[('S', vs, g), stmreg], writes=[('ps', OBk)])
                P.op('pe', lambda e, ci=ci, cs=cs: e.matmul(OB[:, ci * 128:(ci + 1) * 128], lhsT=SL[:, RBs, cs:cs + 128],
                                                          rhs=qd[:, ci * 128:(ci + 1) * 128], start=False, stop=True),
                     reads=[('S', RBs, g), qdreg], writes=[('ps', OBk)])
            post.append([1, lambda: fin1(g, OBk)])
            tick()

        post = []
        PTF = PT[:, :].bitcast(F32)

        def tick(force=False):
            for ent in list(post):
                ent[0] -= 1
                if force or ent[0] <= 0:
                    post.remove(ent)
                    ent[1]()

        def fin1(g, OBk):
            s, n = GROUPS[g]
            OB = PS[OBk]
            cent = cb[:, CB_CENT:CB_CENT + 128]
            ck = 2 + g % 2
            ob, obreg = ring('ba', ba)
            P.op('act', lambda e: e.activation(out=ob[:, 0:n], in_=OB[:, 0:n], func=AF.Identity), reads=[('ps', OBk)], writes=[obreg])
            P.op('pe', lambda e: e.matmul(PS[ck][:, 0:n], lhsT=cent, rhs=ob[:, 0:n], start=True, stop=True),
                 reads=[obreg, RC], writes=[('ps', ck)])
            post.append([1, lambda: fin2(g, ck)])

        def fin2(g, ck):
            s, n = GROUPS[g]
            csq, csqreg = ring('ba', ba)
            P.op('act', lambda e: e.activation(out=csq[:, 0:n], in_=PS[ck][:, 0:n], func=AF.Square), reads=[('ps', ck)], writes=[csqreg])
            P.op('pe', lambda e: e.matmul(PS[6][:, 0:n], lhsT=ones, rhs=csq[:, 0:n], start=True, stop=True),
                 reads=[csqreg, RC], writes=[('ps', 6)])
            post.append([1, lambda: fin3(g, ck)])

        def fin3(g, ck):
            s, n = GROUPS[g]
            rs, rsreg = ring('fb', fb)
            P.op('act', lambda e: e.activation(out=rs[:, 0:n], in_=PS[6][:, 0:n], func=AF.Ln, scale=1.0 / 128, bias=epsc),
                 reads=[('ps', 6), RC], writes=[rsreg])
            P.op('act', lambda e: e.activation(out=rs[:, 0:n], in_=rs[:, 0:n], func=AF.Exp, scale=-0.5), reads=[rsreg], writes=[rsreg])
            fin4(g, ck, rs, rsreg)

        def fin4(g, ck, rs, rsreg):
            s, n = GROUPS[g]
            nrm, nrmreg = ring('fa', fa)
            P.op('dve', lambda e: e.scalar_tensor_tensor(out=nrm[:, 0:n], in0=PS[ck][:, 0:n], scalar=gnw, in1=rs[:, 0:n],
                                                          op0=ALU.mult, op1=ALU.mult),
                 reads=[('ps', ck), rsreg, RC], writes=[nrmreg])
            gs, gsreg = gate[g]
            P.op('dve', lambda e: e.tensor_tensor(out=SL[:, u, s:s + n], in0=nrm[:, 0:n], in1=gs[:, 0:n], op=ALU.mult),
                 reads=[nrmreg, gsreg], writes=[('S', u, g)])
            done.add(g)

        done = set()

        pending = None
        need_gate = []
        for g in range(5):
            if need_gate and (need_gate[0] - 4) in done:
                emit_gate(need_gate.pop(0), 'ptf')
            st = emit_ST(g)
            if pending is not None:
                emit_O(*pending)
            pending = (g, st)
        emit_O(*pending)
        while post or need_gate:
            if need_gate and (need_gate[0] - 4) in done:
                emit_gate(need_gate.pop(0), 'ptf')
            tick(force=True)

    def retention_unit(l, h):
        u = h
        ks = 8 + 2 * (u % 2)
        vs = 9
        wqk, wqkreg = wload(wmix_d[l, u, 0], 2048)
        wvg, wvgreg = wload(wmix_d[l, u, 1], 2048)
        for g, (s, n) in enumerate(GROUPS):
            tabs_for[g] = tload(0, s, n)
            project_rot_group(0, wqk, wqkreg, u, CB_PERMR, g)
            project_rot_group(1, wqk, wqkreg, ks, CB_PERMR, g)
            norm_pop()
        rot_flush()
        project_v(wvg, wvgreg, vs)
        if DBG_MIX == 'B':
            return
        retention_attention(l, h, u, ks, vs, wvg, wvgreg)

    def diff_attention(l, u, ks, vs, om):
        neglam = lamv[:, 4 + l:5 + l]
        sw = cf[:, CF_PVEC + PV_SW + l:CF_PVEC + PV_SW + l + 1]
        blocks = []
        for g, (s, n) in enumerate(GROUPS):
            last = (s + n) // 128 - 1
            for m in range(2):
                for jb in range(last + 1):
                    blocks.append((g, m, jb, last))
        odset = {}
        tres = {}
        deferred = []
        post = []

        def emit_S(blk):
            g, m, jb, last = blk
            s, n = GROUPS[g]
            off = max(0, jb * 128 - s)
            nq = n - off
            diag = jb * 128 >= s
            sk = rr('dst', 3)
            SP_, sreg = SBANK[sk]
            kg = 0 if jb == 0 else 1 + (jb - 1) // 4
            P.op('pe', lambda e: e.matmul(SP_[:, 0:nq], lhsT=SL[:, ks[m], jb * 128:(jb + 1) * 128],
                                          rhs=SL[:, u, s + off:s + n], start=True, stop=(not diag)),
                 reads=[('S', ks[m], kg), ('S', u, g)], writes=[sreg])
            if diag:
                P.op('pe', lambda e: e.matmul(SP_[:, 0:128], lhsT=ident, rhs=cmask, start=False, stop=True),
                     reads=[RC], writes=[sreg])
            return sk, off, nq, kg

        def emit_rest(blk, sinfo):
            g, m, jb, last = blk
            sk, off, nq, kg = sinfo
            SP_, sreg = SBANK[sk]
            if (g, m) not in odset:
                odset[(g, m)] = 2 + 2 * rr('dod', 2)
                while any(ent[2] == odset[(g, m)] for ent in post):
                    for ent in list(post):
                        if ent[2] == odset[(g, m)]:
                            post.remove(ent)
                            ent[1]()
            OBk = odset[(g, m)]
            DBk = OBk + 1
            OB, DB = PS[OBk], PS[DBk]
            pt, ptreg = ring('ba', ba)
            P.op('act', lambda e: e.activation(out=pt[:, 0:nq], in_=SP_[:, 0:nq], func=AF.Exp, scale=0.125),
                 reads=[sreg], writes=[ptreg])
            P.op('pe', lambda e: e.matmul(OB[:, off:off + nq], lhsT=SL[:, vs, jb * 128:(jb + 1) * 128], rhs=pt[:, 0:nq],
                                          start=(jb == 0), stop=(jb == last)),
                 reads=[('S', vs, kg), ptreg], writes=[('ps', OBk)])
            P.op('pe', lambda e: e.matmul(DB[:, off:off + nq], lhsT=(ones_pad if jb == 0 else ones), rhs=pt[:, 0:nq],
                                          start=(jb == 0), stop=(jb == last)),
                 reads=[RC, ptreg], writes=[('ps', DBk)])
            if jb == last:
                post.append([2, lambda: normalize_act(g, m, OBk), OBk])

        def normalize_act(g, m, OBk):
            DBk = OBk + 1
            DB = PS[DBk]
            s, n = GROUPS[g]
            o0 = PAD if g == 0 else 0
            r, rreg = ring('fb', fb)
            P.op('act', lambda e: e.activation(out=r[:, o0:n], in_=DB[:, o0:n], func=AF.Ln), reads=[('ps', DBk)], writes=[rreg])
            P.op('act', lambda e: e.activation(out=r[:, o0:n], in_=r[:, o0:n], func=AF.Exp, scale=-1.0), reads=[rreg], writes=[rreg])
            post.append([3, lambda: normalize_dve(g, m, OBk, r, rreg), OBk])

        def normalize_dve(g, m, OBk, r, rreg):
            OB = PS[OBk]
            s, n = GROUPS[g]
            o0 = PAD if g == 0 else 0
            t, treg_ = ring('fa', fa)
            P.op('dve', lambda e: e.tensor_tensor(out=t[:, o0:n], in0=OB[:, o0:n], in1=r[:, o0:n], op=ALU.mult),
                 reads=[('ps', OBk), rreg], writes=[treg_])
            tres[(g, m)] = (t, treg_)
            if m == 1:
                post.append([1, lambda: finalize1(g), None])

        def finalize1(g):
            s, n = GROUPS[g]
            o0 = PAD if g == 0 else 0
            (t0, t0reg), (t1, t1reg) = tres[(g, 0)], tres[(g, 1)]
            o, oreg = ring('fa', fa)
            P.op('dve', lambda e: e.scalar_tensor_tensor(out=o[:, o0:n], in0=t1[:, o0:n], scalar=neglam, in1=t0[:, o0:n],
                                                          op0=ALU.mult, op1=ALU.add),
                 reads=[t0reg, t1reg, ('lamv',)], writes=[oreg])
            sq, sqreg = ring('ba', ba)
            P.op('dve', lambda e: e.tensor_tensor(out=sq[:, o0:n], in0=o[:, o0:n], in1=o[:, o0:n], op=ALU.mult), reads=[oreg], writes=[sqreg])
            P.op('pe', lambda e: e.matmul(PS[6][:, o0:n], lhsT=ones, rhs=sq[:, o0:n], start=True, stop=True),
                 reads=[sqreg, RC], writes=[('ps', 6)])
            post.append([4, lambda: finalize2(g, o, oreg), None])

        def finalize2(g, o, oreg):
            s, n = GROUPS[g]
            o0 = PAD if g == 0 else 0
            rs, rsreg = ring('fb', fb)
            P.op('act', lambda e: e.activation(out=rs[:, o0:n], in_=PS[6][:, o0:n], func=AF.Ln, scale=1.0 / (128 * om * om), bias=epsl[l]),
                 reads=[('ps', 6), RC], writes=[rsreg])
            P.op('act', lambda e: e.activation(out=rs[:, o0:n], in_=rs[:, o0:n], func=AF.Exp, scale=-0.5), reads=[rsreg], writes=[rsreg])
            P.op('dve', lambda e: e.scalar_tensor_tensor(out=SL[:, u, s + o0:s + n], in0=o[:, o0:n], scalar=sw, in1=rs[:, o0:n],
                                                          op0=ALU.mult, op1=ALU.mult),
                 reads=[oreg, rsreg, RC], writes=[('S', u, g)])

        def tick(force=False):
            for ent in list(post):
                ent[0] -= 1
                if force or ent[0] <= 0:
                    post.remove(ent)
                    ent[1]()

        pend = []
        for blk in blocks:
            sinfo = emit_S(blk)
            pend.append((blk, sinfo))
            if len(pend) > 2:
                emit_rest(*pend.pop(0))
                tick()
        while pend:
            emit_rest(*pend.pop(0))
            tick()
        while post:
            tick(force=True)

    def diff_unit(l, h, lam_init):
        u = 4 + h
        ks = (8, 10)
        vs = 9
        wqk, wqkreg = wload(wmix_d[l, u, 0], 2048)
        wv, wvreg = wload(wmix_d[l, u, 1, :, 0:1024], 1024)
        if h == 0:
            P.op('dve', lambda e: e.memset(SL[64:128, ks[0], :], 0.0), writes=[('S', ks[0], g) for g in range(5)])
            P.op('dve', lambda e: e.memset(SL[0:64, ks[1], :], 0.0), writes=[('S', ks[1], g) for g in range(5)])
        for g, (s, n) in enumerate(GROUPS):
            tabs_for[g] = tload(1, s, n)
            project_rot_group(0, wqk, wqkreg, u, CB_PERMD, g)
            project_rot_group(1, wqk, wqkreg, None, CB_PERMD, g, split=ks)
        rot_flush()
        project_v(wv, wvreg, vs)
        diff_attention(l, u, ks, vs, 1.0 - lam_init)

    def lam_compute(l):
        lam_init = 0.8 - 0.6 * math.exp(-0.3 * l)
        if l == 1:
            P.dma('sp', lambda e: e.dma_start(out=lamin[:, :], in_=lam_d[:, 256:512]), ('lamin', 1),
                  reads=[('lamin',)], writes=[('lamin',)])
        for t in range(2):
            a = lamin[:, (2 * t) * 64:(2 * t + 1) * 64]
            b = lamin[:, (2 * t + 1) * 64:(2 * t + 2) * 64]
            lw, lwreg = ring('fa', fa)
            P.op('dve', lambda e, a=a, b=b, lw=lw: e.tensor_tensor(out=lw[:, 0:64], in0=a, in1=b, op=ALU.mult),
                 reads=[('lamin',)], writes=[lwreg])
            P.op('dve', lambda e, t=t, lw=lw: e.tensor_reduce(out=lamv[:, t:t + 1], in_=lw[:, 0:64], axis=mybir.AxisListType.X, op=ALU.add),
                 reads=[lwreg, ('lamv',)], writes=[('lamv',)])
            P.op('act', lambda e, t=t: e.activation(out=lamv[:, 2 + t:3 + t], in_=lamv[:, t:t + 1], func=AF.Exp),
                 reads=[('lamv',)], writes=[('lamv',)])
        P.op('dve', lambda e: e.scalar_tensor_tensor(out=lamv[:, 4 + l:5 + l], in0=lamv[:, 3:4], scalar=-lam_init, in1=lamv[:, 2:3],
                                                      op0=ALU.add, op1=ALU.subtract),
             reads=[('lamv',)], writes=[('lamv',)])
        return lam_init

    def mixer(l):
        rmsnorm(PV_NORM + (l * 3 + 1) * 8)
        lam_init = lam_compute(l)
        if DBG_MIX == 'A':
            return
        for h in range(4):
            retention_unit(l, h)
            if DBG_MIX in ('B', 'B1', 'C', 'R1', 'R2', 'R3'):
                return
        if DBG_MIX == 'D':
            return
        for h in range(4):
            diff_unit(l, h, lam_init)
            if DBG_MIX == 'F':
                return
        for c in range(8):
            wt, wreg = wload(wout_d[l, c], 1024)
            for (a, n, gs) in FFG:
                mm_out_group(wt, wreg, 8, c, a, n, gs, 1.0)

    stages = []
    for l in range(2):
        stages += [('ffn', l, 0), ('mix', l), ('ffn', l, 1)]
    nst = len(stages) if stop is None else stop
    for st in stages[:nst]:
        if st[0] == 'ffn':
            ffn(st[1], st[2])
        else:
            mixer(st[1])
    outregs = []
    if stop is None:
        for g, (s, n) in enumerate(GROUPS):
            rmsnorm_group(PV_NORM + 48, True, g, s, n)
            if g >= 1:
                P.dma('sp', lambda e, s=s, n=n: e.dma_start(out=out_d[:, :, s - 128:s - 128 + n], in_=hT[:, :, s:s + n]), 'out',
                      reads=[('h', c, g) for c in range(8)], writes=[('out', g)])
                outregs.append(('out', g))
    else:
        for c in range(8):
            P.dma('sp', lambda e, c=c: e.dma_start(out=out_d[:, c, :], in_=hT[:, c, 128:TP]), 'out',
                  reads=[('h', c, g) for g in range(1, 5)], writes=[('out', c)])
            outregs.append(('out', c))
    if dump == 'h':
        for c in range(8):
            P.dma('sp', lambda e, c=c: e.dma_start(out=dbg_d[:, c, :], in_=hT[:, c, :]), 'out',
                  reads=[('h', c, g) for g in range(5)], writes=[('dbg', c)])
            outregs.append(('dbg', c))
    P.op('sp', None, reads=outregs)
    P.build()
    return nc, P


def _bf(a):
    return np.asarray(a, dtype=np.float32).astype(ml_dtypes.bfloat16)


def _const_tables():
    pos = (np.arange(TP, dtype=np.float32) - np.float32(PAD))
    angle = (np.float32(10000.0) ** (-np.linspace(0.0, 1.0, 64, dtype=np.float32))).astype(np.float32)
    fr = (pos[None, :] * np.repeat(angle, 2)[:, None]).astype(np.float32)
    cosr = np.cos(fr).astype(np.float32)
    sgn = np.where(np.arange(128) % 2 == 0, -1.0, 1.0).astype(np.float32)[:, None]
    sinr = (np.sin(fr) * sgn).astype(np.float32)
    r = 16
    inv = (np.float32(500000.0) ** (-np.arange(0, r, 2, dtype=np.float32) / r)).astype(np.float32)
    cosd = np.ones((128, TP), np.float32)
    sind = np.zeros((128, TP), np.float32)
    for p in range(128):
        dd = p % 64
        if dd < r:
            f = (pos * inv[dd % 8]).astype(np.float32)
            cosd[p] = np.cos(f)
            sind[p] = np.sin(f) * (-1.0 if dd < 8 else 1.0)
    tabs = np.stack([cosr, sinr, cosd, sind]).astype(np.float32)
    cbm = np.zeros((128, CB_N), np.float32)
    cbm[:, CB_ONES:CB_ONES + 128] = 1.0
    cbm[PAD:, CB_ONESPAD:CB_ONESPAD + 128] = 1.0
    cbm[:, CB_IDENT:CB_IDENT + 128] = np.eye(128)
    for m in range(128):
        src = m + 1 if m % 2 == 0 else m - 1
        cbm[src, CB_PERMR + m] = 1.0
        dd = m % 64
        if dd < 8:
            cbm[m + 8, CB_PERMD + m] = 1.0
        elif dd < 16:
            cbm[m - 8, CB_PERMD + m] = 1.0
    jj = np.arange(128)[:, None]
    ii = np.arange(128)[None, :]
    cbm[:, CB_CMASK:CB_CMASK + 128] = np.where(jj <= ii, 0.0, -30000.0)
    cbm[:, CB_CENT:CB_CENT + 128] = np.eye(128) - 1.0 / 128.0
    cfm = np.zeros((128, CF_N), np.float32)
    cfm[:, CF_EPS] = EPS
    cfm[:, CF_ONE] = 1.0
    for l_ in range(2):
        om_ = 1.0 - (0.8 - 0.6 * math.exp(-0.3 * l_))
        cfm[:, CF_EPS + 1 + l_] = EPS / (om_ * om_)
    for h in range(4):
        lg = math.log(1.0 - 2.0 ** (-5.0 - h))
        rel = (ii - jj).astype(np.float64)
        dm = np.where(rel >= 0, np.exp(lg * np.maximum(rel, 0.0)), 0.0) * (128.0 ** -0.5)
        cfm[:, CF_DMASK + h * 128:CF_DMASK + (h + 1) * 128] = dm
        cfm[:, CF_QDEC + h * 128:CF_QDEC + (h + 1) * 128] = np.exp(lg * (np.arange(128) + 1.0))[None, :]
        cfm[:, CF_KDEC + h] = np.exp(lg * (127.0 - np.arange(128))) * (128.0 ** -0.5)
    return tabs, cbm, cfm


def _prep(inputs):
    f = lambda k: np.asarray(inputs[k], dtype=np.float32)
    tabs, cbm, cfm = _const_tables()
    cvec = lambda v: np.ascontiguousarray(v.reshape(-1, 128).T)
    norms = [f("ffn1_norm"), f("mix_norm"), f("ffn2_norm")]
    for l in range(2):
        for w in range(3):
            c0 = CF_PVEC + PV_NORM + (l * 3 + w) * 8
            cfm[:, c0:c0 + 8] = cvec(norms[w][l])
        cfm[:, CF_PVEC + PV_GN + l * 4:CF_PVEC + PV_GN + l * 4 + 4] = cvec(f("ret_gn_w")[l])
        cfm[:, CF_PVEC + PV_SW + l] = f("diff_subln_w")[l]
    cfm[:, CF_PVEC + PV_NORM + 48:CF_PVEC + PV_NORM + 56] = cvec(f("final_norm"))
    lam = np.stack([f("diff_lambda_q1"), f("diff_lambda_k1"), f("diff_lambda_q2"), f("diff_lambda_k2")], axis=1)
    lam = np.ascontiguousarray(np.broadcast_to(lam.reshape(1, 512), (128, 512)))

    def fm_tile(w):
        nc_ = w.shape[1] // 128
        return w.reshape(8, 128, nc_, 128).transpose(2, 1, 0, 3)

    wgu = np.empty((2, 2, NF, 128, 2, 8, 128), np.float32)
    wd = np.empty((2, 2, 2, 8, 128, NFH, 128), np.float32)
    for l in range(2):
        for j, (kg, ku, kd_) in enumerate([("ffn1_w_gate", "ffn1_w_up", "ffn1_w_down"), ("ffn2_w_gate", "ffn2_w_up", "ffn2_w_down")]):
            wgu[l, j, :, :, 0] = fm_tile(f(kg)[l])
            wgu[l, j, :, :, 1] = fm_tile(f(ku)[l])
            wd[l, j] = f(kd_)[l].reshape(2, NFH, 128, 8, 128).transpose(0, 3, 2, 1, 4)
    win = f("w_in")
    wmix = np.zeros((2, 8, 2, 128, 2, 8, 128), np.float32)
    for l in range(2):
        t = fm_tile(win[l])
        for h in range(4):
            wmix[l, h, 0, :, 0] = t[h]
            wmix[l, h, 0, :, 1] = t[4 + h]
            wmix[l, h, 1, :, 0] = t[8 + h]
            wmix[l, h, 1, :, 1] = t[12 + h]
            wmix[l, 4 + h, 0, :, 0] = t[16 + h]
            wmix[l, 4 + h, 0, :, 1] = t[20 + h]
            wmix[l, 4 + h, 1, :, 0] = t[24 + h]
    wout = np.stack([fm_tile(f("w_out")[l]) for l in range(2)])
    shared = {
        "metaT": np.ascontiguousarray(f("meta_tokens").reshape(NMETA, 8, 128).transpose(2, 1, 0)),
        "cb": _bf(cbm), "cf": cfm, "lam": lam, "tabs": tabs,
        "wgu": wgu.reshape(2, 2, NF, 128, 2048), "wd": wd.reshape(2, 2, 2, 8, 128, NFH * 128),
        "wmix": wmix.reshape(2, 8, 2, 128, 2048), "wout": np.ascontiguousarray(wout.reshape(2, 8, 128, 1024)),
    }
    x = f("x")
    xTs = [np.ascontiguousarray(x[b].reshape(SEQ, 8, 128).transpose(2, 1, 0)) for b in range(x.shape[0])]
    return shared, xTs


def run(inputs, cores=None, stop=None, dump=None, trace=False):
    shared, xTs = _prep(inputs)
    cores = list(range(8)) if cores is None else cores
    nc, P = build_program(stop=stop, dump=dump)
    in_maps = [dict(shared, xT=xTs[b]) for b in cores]
    res = run_bass_kernel_spmd(nc, in_maps, core_ids=list(range(len(cores))), trace=trace)
    return res, P


def kernel(**inputs):
    res, _ = run(inputs)
    outs = [np.asarray(r["outT"], dtype=np.float32) for r in res.results]
    out = np.stack([o.transpose(2, 1, 0).reshape(SEQ, D) for o in outs])
    return np.ascontiguousarray(out.astype(np.float32))
```
